# Optimizing a Trainium2 kernel written in Bass

```python
import jax
import jax.numpy as jnp
from jax import lax
import numpy as np

D_MODEL = 1024
BATCH = 2
SEQ = 8192
DEPTH = 2

CTX_LEN = 256
GRID_W = 64
BLOCK = 128
WINDOW = 128
ROPE_BASE = 10000.0
EPS = 1e-6
NEG_INF = -1e30

H_A = 8
HKV_A = 2
G_A = H_A // HKV_A
HD_A = 64
SCALE_A = HD_A ** -0.5

H_B = 8
Q_LORA = 384
KV_LORA = 256
NOPE_B = 64
ROPE_B = 32
V_B = 64
SCALE_B = (NOPE_B + ROPE_B) ** -0.5

W_C = 512
CONV_K = 3

W_BRANCH = 512

IN_SIZES = (H_A * HD_A, HKV_A * HD_A, HKV_A * HD_A, W_BRANCH,
            Q_LORA, KV_LORA, ROPE_B, W_BRANCH,
            W_C, W_C, W_C, W_BRANCH,
            D_MODEL, D_MODEL, D_MODEL)
D_IN = sum(IN_SIZES)

kernel_name = "hybrid_gated_branch_diffusion_trunk"


def rms_norm(x, g):
    xf = x.astype(jnp.float32)
    y = xf * lax.rsqrt(jnp.mean(xf * xf, axis=-1, keepdims=True) + EPS)
    return (y * g.astype(jnp.float32)).astype(x.dtype)


def axial_rope_angles(rows, rot_dim):
    row = jnp.repeat(jnp.arange(rows, dtype=jnp.float32), GRID_W)
    col = jnp.tile(jnp.arange(GRID_W, dtype=jnp.float32), rows)
    axis_dim = rot_dim // 2
    inv = ROPE_BASE ** (-jnp.arange(0, axis_dim, 2, dtype=jnp.float32) / axis_dim)
    ang = jnp.concatenate([row[:, None] * inv, col[:, None] * inv], axis=-1)
    return jnp.cos(ang), jnp.sin(ang)


def apply_rope(x, cos, sin):
    xf = x.astype(jnp.float32)
    x1, x2 = jnp.split(xf, 2, axis=-1)
    c = cos[None, :, None, :]
    s = sin[None, :, None, :]
    return jnp.concatenate([x1 * c - x2 * s, x2 * c + x1 * s], axis=-1).astype(x.dtype)


def in_proj(h, w_in):
    splits = np.cumsum(IN_SIZES)[:-1].tolist()
    return jnp.split(h @ w_in, splits, axis=-1)


def window_attn_latent(q, k, v, k_ctx, v_ctx, sink):
    B, S = q.shape[:2]
    nb = S // BLOCK
    qb = q.reshape(B, nb, BLOCK, HKV_A, G_A, HD_A)
    pad = ((0, 0), (BLOCK, BLOCK), (0, 0), (0, 0))
    kp = jnp.pad(k, pad).reshape(B, nb + 2, BLOCK, HKV_A, HD_A)
    vp = jnp.pad(v, pad).reshape(B, nb + 2, BLOCK, HKV_A, HD_A)
    kw = jnp.concatenate([kp[:, :-2], kp[:, 1:-1], kp[:, 2:]], axis=2)
    vw = jnp.concatenate([vp[:, :-2], vp[:, 1:-1], vp[:, 2:]], axis=2)
    s_loc = jnp.einsum('bnqhgd,bnkhd->bnhgqk', qb, kw).astype(jnp.float32) * SCALE_A
    qi = jnp.arange(BLOCK)[:, None] + BLOCK
    kj = jnp.arange(3 * BLOCK)[None, :]
    band = jnp.abs(kj - qi) <= WINDOW
    kabs = jnp.arange(nb)[:, None] * BLOCK + jnp.arange(3 * BLOCK)[None, :] - BLOCK
    valid = (kabs >= 0) & (kabs < S)
    mask = band[None, :, :] & valid[:, None, :]
    s_loc = jnp.where(mask[None, :, None, None], s_loc, NEG_INF)
    s_ctx = jnp.einsum('bnqhgd,blhd->bnhgql', qb, k_ctx).astype(jnp.float32) * SCALE_A
    s_sink = jnp.broadcast_to(sink.astype(jnp.float32).reshape(1, 1, HKV_A, G_A, 1, 1),
                              s_loc.shape[:-1] + (1,))
    p = jax.nn.softmax(jnp.concatenate([s_loc, s_ctx, s_sink], axis=-1), axis=-1)
    p_loc = p[..., :3 * BLOCK].astype(v.dtype)
    p_ctx = p[..., 3 * BLOCK:3 * BLOCK + k_ctx.shape[1]].astype(v.dtype)
    o = (jnp.einsum('bnhgqk,bnkhd->bnqhgd', p_loc, vw)
         + jnp.einsum('bnhgql,blhd->bnqhgd', p_ctx, v_ctx))
    return o.reshape(B, S, H_A * HD_A)


def ctx_gqa_attn(q, k, v, sink):
    B, L = q.shape[:2]
    qg = q.reshape(B, L, HKV_A, G_A, HD_A)
    s = jnp.einsum('blhgd,bmhd->bhglm', qg, k).astype(jnp.float32) * SCALE_A
    sk = jnp.broadcast_to(sink.astype(jnp.float32).reshape(1, HKV_A, G_A, 1, 1), (B, HKV_A, G_A, L, 1))
    p = jax.nn.softmax(jnp.concatenate([s, sk], axis=-1), axis=-1)[..., :L]
    o = jnp.einsum('bhglm,bmhd->blhgd', p.astype(v.dtype), v)
    return o.reshape(B, L, H_A * HD_A)


def mla_project(q_lat, kv_lat, g_qa, w_qb, g_kva, w_kvb):
    B, T = q_lat.shape[:2]
    q = (rms_norm(q_lat, g_qa) @ w_qb).reshape(B, T, H_B, NOPE_B + ROPE_B)
    kv = (rms_norm(kv_lat, g_kva) @ w_kvb).reshape(B, T, H_B, NOPE_B + V_B)
    return q[..., :NOPE_B], q[..., NOPE_B:], kv[..., :NOPE_B], kv[..., NOPE_B:]


def mla_attn_latent(qn, qr, kn, kr, v, kn_c, kr_c, v_c):
    B, S = qn.shape[:2]
    nb = S // BLOCK
    keys_n = jnp.concatenate([kn, kn_c], axis=1)
    keys_r = jnp.concatenate([kr, kr_c], axis=1)
    vals = jnp.concatenate([v, v_c], axis=1)

    def block(qs):
        q_n, q_r = qs
        s = (jnp.einsum('bqhd,bkhd->bhqk', q_n, keys_n)
             + jnp.einsum('bqhd,bkd->bhqk', q_r, keys_r))
        p = jax.nn.softmax(s.astype(jnp.float32) * SCALE_B, axis=-1)
        return jnp.einsum('bhqk,bkhd->bqhd', p.astype(vals.dtype), vals)

    qn_b = qn.reshape(B, nb, BLOCK, H_B, NOPE_B).swapaxes(0, 1)
    qr_b = qr.reshape(B, nb, BLOCK, H_B, ROPE_B).swapaxes(0, 1)
    o = lax.map(block, (qn_b, qr_b))
    return o.swapaxes(0, 1).reshape(B, S, H_B * V_B)


def ctx_mla_attn(qn, qr, kn, kr, v):
    B, L = qn.shape[:2]
    s = jnp.einsum('blhd,bmhd->bhlm', qn, kn) + jnp.einsum('blhd,bmd->bhlm', qr, kr)
    p = jax.nn.softmax(s.astype(jnp.float32) * SCALE_B, axis=-1)
    o = jnp.einsum('bhlm,bmhd->blhd', p.astype(v.dtype), v)
    return o.reshape(B, L, H_B * V_B)


def short_gated_conv(b_gate, c_gate, u, conv_w):
    z = jnp.pad(c_gate * u, ((0, 0), (1, 1), (0, 0)))
    y = conv_w[0] * z[:, :-2] + conv_w[1] * z[:, 1:-1] + conv_w[2] * z[:, 2:]
    return b_gate * y


def merge_branches(ya, yb, yc, za, zb, zc, ga, gb, gc, w_branch, w_o):
    m = (jax.nn.sigmoid(ga) * ((ya * jax.nn.silu(za)) @ w_branch[0])
         + jax.nn.sigmoid(gb) * ((yb * jax.nn.silu(zb)) @ w_branch[1])
         + jax.nn.sigmoid(gc) * ((yc * jax.nn.silu(zc)) @ w_branch[2]))
    return m @ w_o


def mixer_layer(x, ctx, mod_x, mod_c, rope_a, rope_b, g_pre, g_post, w_in, sink,
                g_qa, w_qb, g_kva, w_kvb, conv_w, w_branch, w_o, update_ctx):
    B, S, _ = x.shape
    L = ctx.shape[1]
    shift_x, scale_x, gate_x = [m[:, None, :] for m in jnp.split(mod_x, 3, axis=-1)]
    shift_c, scale_c, gate_c = jnp.split(mod_c, 3, axis=-1)
    hx = rms_norm(x, g_pre) * (1.0 + scale_x) + shift_x
    hc = rms_norm(ctx, g_pre) * (1.0 + scale_c) + shift_c
    (qa_x, ka_x, va_x, za_x, qlb_x, kvlb_x, krb_x, zb_x,
     bc_x, cc_x, uc_x, zc_x, ga_x, gb_x, gc_x) = in_proj(hx, w_in)
    (qa_c, ka_c, va_c, za_c, qlb_c, kvlb_c, krb_c, zb_c,
     bc_c, cc_c, uc_c, zc_c, ga_c, gb_c, gc_c) = in_proj(hc, w_in)

    cos_a, sin_a = rope_a
    qa = apply_rope(qa_x.reshape(B, S, H_A, HD_A), cos_a, sin_a)
    ka = apply_rope(ka_x.reshape(B, S, HKV_A, HD_A), cos_a, sin_a)
    va = va_x.reshape(B, S, HKV_A, HD_A)
    kac = ka_c.reshape(B, L, HKV_A, HD_A)
    vac = va_c.reshape(B, L, HKV_A, HD_A)
    ya_x = window_attn_latent(qa, ka, va, kac, vac, sink)

    cos_b, sin_b = rope_b
    qn_x, qr_x, kn_x, vb_x = mla_project(qlb_x, kvlb_x, g_qa, w_qb, g_kva, w_kvb)
    qr_x = apply_rope(qr_x, cos_b, sin_b)
    kr_x = apply_rope(krb_x[:, :, None, :], cos_b, sin_b)[:, :, 0]
    qn_c, qr_c, kn_c, vb_c = mla_project(qlb_c, kvlb_c, g_qa, w_qb, g_kva, w_kvb)
    yb_x = mla_attn_latent(qn_x, qr_x, kn_x, kr_x, vb_x, kn_c, krb_c, vb_c)

    yc_x = short_gated_conv(bc_x, cc_x, uc_x, conv_w)

    out_x = merge_branches(ya_x, yb_x, yc_x, za_x, zb_x, zc_x, ga_x, gb_x, gc_x, w_branch, w_o)
    x = x + gate_x * rms_norm(out_x, g_post)

    if update_ctx:
        ya_c = ctx_gqa_attn(qa_c.reshape(B, L, H_A, HD_A), kac, vac, sink)
        yb_c = ctx_mla_attn(qn_c, qr_c, kn_c, krb_c, vb_c)
        yc_c = short_gated_conv(bc_c, cc_c, uc_c, conv_w)
        out_c = merge_branches(ya_c, yb_c, yc_c, za_c, zb_c, zc_c, ga_c, gb_c, gc_c, w_branch, w_o)
        ctx = ctx + gate_c * rms_norm(out_c, g_post)
    return x, ctx


def setup_inputs(seed: int = 0) -> dict:
    key = jax.random.key(seed)
    ks = jax.random.split(key, 17)
    f32 = jnp.float32

    def nrm(k, shape, scale):
        return jax.random.normal(k, shape, f32) * scale

    return {
        "x": nrm(ks[0], (BATCH, SEQ, D_MODEL), 1.0),
        "c": nrm(ks[1], (BATCH, D_MODEL), 1.0),
        "ctx": nrm(ks[2], (BATCH, CTX_LEN, D_MODEL), 1.0),
        "c_ctx": nrm(ks[3], (D_MODEL,), 1.0),
        "w_mod": nrm(ks[4], (DEPTH, D_MODEL, 3 * D_MODEL), 0.5 * D_MODEL ** -0.5),
        "b_mod": nrm(ks[5], (DEPTH, 3 * D_MODEL), 0.01),
        "g_pre": 1.0 + nrm(ks[6], (DEPTH, D_MODEL), 0.05),
        "g_post": 1.0 + nrm(ks[7], (DEPTH, D_MODEL), 0.05),
        "w_in": nrm(ks[8], (DEPTH, D_MODEL, D_IN), D_MODEL ** -0.5),
        "sink": nrm(ks[9], (DEPTH, H_A), 1.0),
        "g_qa": 1.0 + nrm(ks[10], (DEPTH, Q_LORA), 0.05),
        "w_qb": nrm(ks[11], (DEPTH, Q_LORA, H_B * (NOPE_B + ROPE_B)), Q_LORA ** -0.5),
        "g_kva": 1.0 + nrm(ks[12], (DEPTH, KV_LORA), 0.05),
        "w_kvb": nrm(ks[13], (DEPTH, KV_LORA, H_B * (NOPE_B + V_B)), KV_LORA ** -0.5),
        "conv_w": nrm(ks[14], (DEPTH, CONV_K, W_C), CONV_K ** -0.5),
        "w_branch": nrm(ks[15], (DEPTH, 3, W_BRANCH, D_MODEL), W_BRANCH ** -0.5),
        "w_o": nrm(ks[16], (DEPTH, D_MODEL, D_MODEL), D_MODEL ** -0.5),
    }


def reference(x, c, ctx, c_ctx, w_mod, b_mod, g_pre, g_post, w_in, sink,
              g_qa, w_qb, g_kva, w_kvb, conv_w, w_branch, w_o):
    rows = x.shape[1] // GRID_W
    rope_a = axial_rope_angles(rows, HD_A)
    rope_b = axial_rope_angles(rows, ROPE_B)
    sc = jax.nn.silu(c)
    scc = jax.nn.silu(c_ctx)
    for i in range(DEPTH):
        mod_x = sc @ w_mod[i] + b_mod[i]
        mod_c = scc @ w_mod[i] + b_mod[i]
        x, ctx = mixer_layer(x, ctx, mod_x, mod_c, rope_a, rope_b, g_pre[i], g_post[i],
                             w_in[i], sink[i], g_qa[i], w_qb[i], g_kva[i], w_kvb[i],
                             conv_w[i], w_branch[i], w_o[i], update_ctx=(i < DEPTH - 1))
    return x
```

```python
import os
import numpy as np
import ml_dtypes
import concourse.bass as bass
import concourse.mybir as mybir
from concourse.bass_utils import run_bass_kernel_spmd

F32 = mybir.dt.float32
BF16 = mybir.dt.bfloat16
AF = mybir.ActivationFunctionType
ALU = mybir.AluOpType
AP = bass.AP

D = 1024
S = 8192
L = 256
DEPTH = 2
NCORE = 8
T = 2048
TT = T + L
GRID_W = 64
EPS = 1e-6
SCALE_A = 64 ** -0.5
SCALE_B = 96 ** -0.5
NEG = -30000.0
DBG_TILES = 0
CHUNKS = [(0, 512), (512, 512), (1024, 512), (1536, 512), (2048, 256)]

O_QA, O_KA, O_VA, O_ZA = 0, 512, 640, 768
O_QL, O_KVL, O_KR, O_ZB = 1280, 1664, 1920, 1952
O_BC, O_CC, O_UC, O_ZC = 2464, 2976, 3488, 4000
O_GA, O_GB, O_GC = 4512, 5536, 6560


def _win_perm():
    cols = []
    off = {}

    def add(name, idx):
        off[name] = len(cols)
        cols.extend(list(idx))

    r64 = np.arange(64)
    rot64 = np.concatenate([r64[32:], r64[:32]])
    r32 = np.arange(32)
    rot32 = np.concatenate([r32[16:], r32[:16]])
    ka = []
    for g in range(2):
        base = O_KA + g * 64
        ka += list(base + r64) + list(base + r64)
        ka += list(base + rot64) + list(base + rot64)
    add("KA", ka)
    add("VKK", list(O_VA + np.arange(128)) + list(O_KVL + np.arange(256))
        + list(O_KR + r32) + list(O_KR + rot32))
    add("ZH", list(O_CC + np.arange(512)) + list(O_UC + np.arange(512)))
    add("QL", list(O_QL + np.arange(384)))
    qa = []
    for j in range(4):
        for e in range(2):
            qa += list(O_QA + (2 * j + e) * 64 + r64)
        for e in range(2):
            qa += list(O_QA + (2 * j + e) * 64 + rot64)
    add("QA", qa)
    add("ZA", list(O_ZA + np.arange(512)))
    add("ZB", list(O_ZB + np.arange(512)))
    cc = []
    for ct in range(4):
        for o in (O_BC, O_CC, O_UC, O_ZC):
            cc += list(o + ct * 128 + np.arange(128))
    add("C", cc)
    gg = []
    for mt in range(8):
        for o in (O_GA, O_GB, O_GC):
            gg += list(o + mt * 128 + np.arange(128))
    add("G", gg)
    return np.asarray(cols, dtype=np.int64), off


WIN_PERM, WOFF = _win_perm()
NCW = len(WIN_PERM)

PC_KVN = 0
PC_N1 = 4096
PC_KR = 0
PC_KAH = 512
PC_VAH = 1024
PC_ZH = 1280
PC_N2 = 1296


class Res:
    __slots__ = ("name", "writers", "readers", "excl", "last")

    def __init__(self, name, excl=False):
        self.name = name
        self.writers = []
        self.readers = []
        self.excl = excl
        self.last = {}


class Op:
    __slots__ = ("eng", "fn", "deps", "kind", "signaled", "sem", "val", "idx")


def _prune(lst):
    out = []
    seen = set()
    for o in reversed(lst):
        if o.kind != "c":
            out.append(o)
        elif o.eng not in seen:
            seen.add(o.eng)
            out.append(o)
    out.reverse()
    return out


class Sched:
    ENG = ("pe", "act", "dve", "pool", "sp")

    def __init__(self):
        self.prog = {e: [] for e in self.ENG}
        self.all = []
        self.pending_dma = []

    def op(self, eng, fn, r=(), w=(), pw=(), kind="c", extra=()):
        o = Op()
        o.eng, o.fn, o.kind, o.signaled, o.sem, o.val = eng, fn, kind, False, None, 0
        o.idx = len(self.all)
        deps = set(x for x in extra if x is not None)
        allres = list(r) + list(w) + list(pw)
        r = [x for x in r if not x.excl]
        w = [x for x in w if not x.excl]
        pw = [x for x in pw if not x.excl]
        for res in allres:
            if res.excl:
                for e2, o2 in res.last.items():
                    if e2 != eng:
                        deps.add(o2)
                res.last[eng] = o
        for res in r:
            deps.update(res.writers)
        for res in w:
            deps.update(res.writers)
            deps.update(res.readers)
        for res in pw:
            deps.update(res.readers)
        for res in r:
            res.readers.append(o)
            if len(res.readers) > 12:
                res.readers = _prune(res.readers)
        for res in w:
            res.writers = [o]
            res.readers = []
        for res in pw:
            if res.readers:
                res.writers = [o]
                res.readers = []
            else:
                res.writers.append(o)
                if len(res.writers) > 12:
                    res.writers = _prune(res.writers)
        deps.discard(o)
        o.deps = deps
        self.prog[eng].append(o)
        self.all.append(o)
        if kind != "c":
            self.pending_dma.append(o)
        return o

    def dma(self, eng, out, in_, r=(), w=(), pw=(), extra=(), **kw):
        return self.op(eng, lambda e: e.dma_start(out=out, in_=in_, **kw), r=r, w=w, pw=pw,
                       kind="d", extra=extra)

    def barrier(self):
        last = [self.prog[e][-1] for e in self.ENG if self.prog[e]]
        last += self.pending_dma
        self.pending_dma = []
        for e in self.ENG:
            self.op(e, None, extra=last)

    def emit(self, nc, stack):
        NS = 20
        for o in self.all:
            for d in o.deps:
                if d.kind == "c" and d.eng == "pe" and o.eng == "pe" and o.kind == "c":
                    continue
                d.signaled = True
        esem = {e: stack.enter_context(nc.semaphore("s_" + e)) for e in self.ENG}
        dsem = {e: [stack.enter_context(nc.semaphore("d_%s%d" % (e, i))) for i in range(NS)]
                for e in ("sp", "pool", "act")}
        ccsem = stack.enter_context(nc.semaphore("s_cc"))
        cnt = {e: 0 for e in self.ENG}
        dcnt = {e: 0 for e in dsem}
        dval = {}
        prev = {}
        ccn = 0
        for e in self.ENG:
            for o in self.prog[e]:
                if o.kind == "c":
                    if o.signaled and o.fn is not None:
                        cnt[e] += 1
                        o.sem, o.val = esem[e], cnt[e]
                    elif o.fn is None:
                        o.sem, o.val = None, 0
                elif o.kind == "d":
                    s = dsem[e][dcnt[e] % NS]
                    dcnt[e] += 1
                    prev[o] = dval.get(id(s), 0)
                    dval[id(s)] = prev[o] + 16
                    o.sem, o.val = s, dval[id(s)]
                else:
                    ccn += 1
                    o.sem, o.val = ccsem, ccn
        self.n_inst = {e: len(self.prog[e]) for e in self.ENG}
        block = stack.enter_context(nc.Block())
        sched = self

        def run(e, engine):
            waited = {}
            for o in sched.prog[e]:
                waits = {}
                for d in o.deps:
                    if d.kind == "c" and d.eng == "pe" and o.eng == "pe" and o.kind == "c":
                        continue
                    if d.sem is None:
                        continue
                    k = id(d.sem)
                    if k not in waits or waits[k][1] < d.val:
                        waits[k] = (d.sem, d.val)
                if o.kind == "d" and prev[o] > 0:
                    k = id(o.sem)
                    if k not in waits or waits[k][1] < prev[o]:
                        waits[k] = (o.sem, prev[o])
                for k, (s, v) in waits.items():
                    if waited.get(k, 0) < v:
                        engine.wait_ge(s, v)
                        waited[k] = v
                if o.fn is None:
                    continue
                inst = o.fn(engine)
                if o.kind == "d":
                    inst.then_inc(o.sem, 16)
                elif o.kind == "cc":
                    inst.then_inc(o.sem)
                elif o.signaled:
                    inst.then_inc(o.sem, 1)

        @block.tensor
        def _(te):
            run("pe", te)

        @block.scalar
        def _(sc):
            run("act", sc)

        @block.vector
        def _(ve):
            run("dve", ve)

        @block.gpsimd
        def _(gp):
            run("pool", gp)

        @block.sync
        def _(sy):
            run("sp", sy)


def _rope_tables(rank):
    g = rank * T + np.arange(T)
    row = (g // GRID_W).astype(np.float64)
    col = (g % GRID_W).astype(np.float64)

    def tab(rot_dim):
        axis_dim = rot_dim // 2
        inv = 10000.0 ** (-np.arange(0, axis_dim, 2, dtype=np.float64) / axis_dim)
        ang = np.concatenate([row[:, None] * inv, col[:, None] * inv], axis=-1)
        half = rot_dim // 2
        d = np.arange(rot_dim)
        c = np.cos(ang)[:, d % half].T
        s = np.sin(ang)[:, d % half].T
        s = np.where((d < half)[:, None], -s, s)
        cfull = np.ones((rot_dim, TT)); sfull = np.zeros((rot_dim, TT))
        cfull[:, :T] = c; sfull[:, :T] = s
        return cfull, sfull

    ca, sa = tab(64)
    cb, sb = tab(32)
    cosA = np.concatenate([ca, ca], 0).astype(np.float32)
    sinA = np.concatenate([sa, sa], 0).astype(np.float32)
    cosB = np.zeros((128, TT), np.float32); sinB = np.zeros((128, TT), np.float32)
    cosB[0:32] = cb; cosB[64:96] = cb
    sinB[0:32] = sb; sinB[64:96] = sb
    return cosA, sinA, cosB, sinB


def _masks(rank):
    kk = np.arange(128)[:, None]
    qq = np.arange(128)[None, :]
    mp = np.where(kk >= qq, 0.0, NEG).astype(np.float32)
    mn = np.where(kk <= qq, 0.0, NEG).astype(np.float32)
    allneg = np.full((128, 128), NEG, np.float32)
    m = np.stack([np.tile(mp, (1, 4)), np.tile(mn, (1, 4)),
                  np.tile(mp if rank > 0 else allneg, (1, 4)),
                  np.tile(mn if rank < 3 else allneg, (1, 4))], axis=1)
    return m.astype(ml_dtypes.bfloat16)


def _fm(v, k):
    return np.ascontiguousarray(np.asarray(v, np.float32).reshape(k, 128).T)


def prepare_inputs(x, c, ctx, c_ctx, w_mod, b_mod, g_pre, g_post, w_in, sink,
                   g_qa, w_qb, g_kva, w_kvb, conv_w, w_branch, w_o):
    f = lambda a: np.asarray(a, np.float32)
    x, c, ctx, c_ctx = f(x), f(c), f(ctx), f(c_ctx)
    w_mod, b_mod, g_pre, g_post = f(w_mod), f(b_mod), f(g_pre), f(g_post)
    w_in, sink, g_qa, w_qb, g_kva, w_kvb = f(w_in), f(sink), f(g_qa), f(w_qb), f(g_kva), f(w_kvb)
    conv_w, w_branch, w_o = f(conv_w), f(w_branch), f(w_o)
    shared = {}
    shared["wmod"] = np.ascontiguousarray(w_mod)
    shared["bmodT"] = np.ascontiguousarray(np.stack([_fm(b_mod[l, :2048], 16) for l in range(DEPTH)], 1))
    shared["bmodg"] = np.ascontiguousarray(np.stack([np.stack([b_mod[l, 2048:], b_mod[l, 2048:]], 0)
                                                     for l in range(DEPTH)], 1))
    shared["gpreT"] = np.ascontiguousarray(np.stack([_fm(g_pre[l], 8) for l in range(DEPTH)], 1))
    shared["gpostb"] = np.ascontiguousarray(np.broadcast_to(g_post[:, None, :], (DEPTH, 128, D)))
    shared["win"] = np.ascontiguousarray(w_in[:, :, WIN_PERM])
    shared["sinkb"] = np.ascontiguousarray(np.broadcast_to(sink[None, :, :], (128, DEPTH, 8)))
    shared["gqaT"] = np.ascontiguousarray(np.stack([_fm(g_qa[l], 3) for l in range(DEPTH)], 1))
    shared["gkvaT"] = np.ascontiguousarray(np.stack([_fm(g_kva[l], 2) for l in range(DEPTH)], 1))
    r32 = np.arange(32)
    qcols = []
    for h in range(8):
        base = h * 96
        qcols += list(base + np.arange(96))
        qcols += list(base + np.arange(64)) + list(base + 64 + np.concatenate([r32[16:], r32[:16]]))
    shared["wqb"] = np.ascontiguousarray(w_qb[:, :, np.asarray(qcols)])
    kcols = [h * 128 + i for h in range(8) for i in range(64)]
    vcols = [h * 128 + 64 + i for h in range(8) for i in range(64)]
    shared["wkvb"] = np.ascontiguousarray(w_kvb[:, :, np.asarray(kcols + vcols)])
    cw = np.zeros((128, DEPTH, 4, 3), np.float32)
    for l in range(DEPTH):
        for k in range(3):
            cw[:, l, :, k] = conv_w[l, k].reshape(4, 128).T
    shared["convw"] = cw
    shared["wbr"] = np.ascontiguousarray(w_branch)
    shared["wo"] = np.ascontiguousarray(w_o)
    shared["ident"] = np.eye(128, dtype=np.float32)
    shared["identb"] = np.eye(128, dtype=np.float32).astype(ml_dtypes.bfloat16)
    shared["onesb"] = np.ones((128, 128), np.float32).astype(ml_dtypes.bfloat16)
    sel = np.zeros((2, 2, 128), np.float32)
    sel[0, 0, :] = 1.0
    sel[1, 1, :] = 1.0
    shared["sel"] = sel
    in_maps = []
    for core in range(NCORE):
        b, r = core // 4, core % 4
        m = dict(shared)
        m["x"] = np.ascontiguousarray(x[b, r * T:(r + 1) * T])
        m["ctx"] = np.ascontiguousarray(ctx[b])
        cs = np.zeros((128, 8, 2), np.float32)
        cs[:, :, 0] = _fm(c[b], 8)
        cs[:, :, 1] = _fm(c_ctx, 8)
        m["cs"] = cs.reshape(128, 16)
        cosA, sinA, cosB, sinB = _rope_tables(r)
        m["cosA"], m["sinA"], m["cosB"], m["sinB"] = cosA, sinA, cosB, sinB
        m["masks"] = _masks(r)
        oh = np.zeros((128, 8), np.float32)
        if r > 0:
            oh[:, r - 1] = 1.0
        if r < 3:
            oh[:, 4 + r + 1] = 1.0
        m["oh"] = oh
        in_maps.append(m)
    return in_maps


class _K:
    pass


def build(upto=99, dbg=()):
    from contextlib import ExitStack
    nc = bass.Bass("TRN2", target_bir_lowering=False)
    S_ = Sched()
    k = _K()
    stack = ExitStack()
    with stack:
        _build_body(nc, S_, k, stack, upto, dbg)
        _program(k, S_, upto)
        S_.barrier()
        for (dd, ap_) in k.dumps:
            S_.dma("sp", dd.ap(), ap_)
        S_.barrier()
        S_.emit(nc, stack)
    k.S = S_
    build.last = k
    return nc


def _din(nc, name, shape, dt=F32):
    return nc.dram_tensor(name, list(shape), dt, kind="ExternalInput")


def _build_body(nc, S_, k, stack, upto, dbg):
    sb = lambda name, shape, dt: stack.enter_context(nc.sbuf_tensor("s_" + name, list(shape), dt))
    d_x = _din(nc, "x", [T, D]); d_ctx = _din(nc, "ctx", [L, D]); d_cs = _din(nc, "cs", [128, 16])
    d_wmod = _din(nc, "wmod", [DEPTH, D, 3 * D]); d_bmodT = _din(nc, "bmodT", [128, DEPTH, 16])
    d_bmodg = _din(nc, "bmodg", [2, DEPTH, D]); d_gpreT = _din(nc, "gpreT", [128, DEPTH, 8])
    d_gpostb = _din(nc, "gpostb", [DEPTH, 128, D]); d_win = _din(nc, "win", [DEPTH, D, NCW])
    d_sinkb = _din(nc, "sinkb", [128, DEPTH, 8]); d_gqaT = _din(nc, "gqaT", [128, DEPTH, 3])
    d_gkvaT = _din(nc, "gkvaT", [128, DEPTH, 2]); d_wqb = _din(nc, "wqb", [DEPTH, 384, 1536])
    d_wkvb = _din(nc, "wkvb", [DEPTH, 256, 1024]); d_convw = _din(nc, "convw", [128, DEPTH, 4, 3])
    d_wbr = _din(nc, "wbr", [DEPTH, 3, 512, D]); d_wo = _din(nc, "wo", [DEPTH, D, D])
    d_tab = [_din(nc, n, [128, TT]) for n in ("cosA", "sinA", "cosB", "sinB")]
    d_masks = _din(nc, "masks", [128, 4, 512], BF16); d_oh = _din(nc, "oh", [128, 8])
    d_ident = _din(nc, "ident", [128, 128]); d_identb = _din(nc, "identb", [128, 128], BF16)
    d_onesb = _din(nc, "onesb", [128, 128], BF16); d_sel = _din(nc, "sel", [2, 2, 128])
    d_y = nc.dram_tensor("y", [T, D], F32, kind="ExternalOutput")
    d_snd1 = nc.dram_tensor("snd1", [128, PC_N1], BF16)
    d_rcv1 = nc.dram_tensor("rcv1", [512, PC_N1], BF16)
    d_snd = nc.dram_tensor("snd2", [128, PC_N2], BF16)
    d_rcv = nc.dram_tensor("rcv2", [512, PC_N2], BF16)
    d_x1 = nc.dram_tensor("x1", [T, D], F32)

    ident = sb("ident", [128, 128], F32); identb = sb("identb", [128, 128], BF16)
    onesb = sb("onesb", [128, 128], BF16); sel = sb("sel", [2, 256], F32)
    masks = sb("masksb", [128, 4, 512], BF16); oh = sb("oh", [128, 8], F32)
    cs = sb("cs", [128, 16], F32); scs = sb("scs", [128, 16], BF16)
    bmodT = sb("bmodT", [128, DEPTH, 16], F32)
    gpreT = sb("gpreT", [128, DEPTH, 8], F32); sinkb = sb("sinkb", [128, DEPTH, 8], F32)
    esink = sb("esink", [128, DEPTH, 8], F32)
    gqaT = sb("gqaT", [128, DEPTH, 3], F32); gkvaT = sb("gkvaT", [128, DEPTH, 2], F32)
    convw = sb("convw", [128, DEPTH, 4, 3], F32)
    modT = sb("modT", [128, 16, 2], F32)
    Amod = sb("Amod", [128, DEPTH, 8, 2], F32); Bmod = sb("Bmod", [128, DEPTH, 8, 2], F32)
    small = sb("small", [128, 16], F32)
    epsc = sb("epsc", [128, 1], F32)
    hxT = sb("hxT", [128, 8, TT], BF16)
    R1 = sb("R1", [128, 9216], BF16)
    R2 = sb("R2", [128, 18432], BF16)
    R3 = sb("R3", [128, 20736], BF16)
    WB = [sb("wb%d" % i, [128, 8, 512], BF16) for i in range(3)]
    wbrb = sb("wbrb", [128, 3, 4, 128], BF16)
    TMP = [sb("tmp%d" % i, [128, 512], F32) for i in range(8)]
    sqb = sb("sqb", [128, 3, 512], BF16); qnb = sb("qnb", [128, 3, 512], BF16)
    junk = sqb[:, 0:2, :].rearrange("p a b -> p (a b)")
    xhl = sb("xhl", [128, 2, 1024], BF16)
    tabs = [sb("tab%d" % i, [128, 512], F32) for i in range(4)]
    kvst = sb("kvst", [128, 2, 512], BF16); krst = sb("krst", [32, 512], BF16)
    zh = sb("zh", [128, 8], F32); zhalo = sb("zhalo", [128, 4, 2], F32)
    zhb = sb("zhb", [128, 16], BF16); hzf = sb("hzf", [128, 4, 8], F32)
    PS = stack.enter_context(nc.psum_tensor("ps", [128, 8, 512], F32))

    def v32(R, off_bf, n_f32):
        return R[:, off_bf:off_bf + 2 * n_f32].bitcast(F32)
    xt = [v32(R1, 0, 1024), v32(R1, 2048, 1024)]
    xn = v32(R1, 4096, 1024)
    xold = v32(R1, 6144, 1024)
    wqb = R1[:, 0:4608].rearrange("p (j c) -> p j c", j=3)
    hal_k = R1[:, 4608:4608 + 2048].rearrange("p (r c) -> p r c", r=4)
    hal_v = R1[:, 6656:6656 + 1024].rearrange("p (r c) -> p r c", r=4)
    hal_z = R1[:, 7680:7680 + 64].rearrange("p (r c) -> p r c", r=4)
    hal_zf = R1[:, 7680:7680 + 64].bitcast(F32).rearrange("p (r c) -> p r c", r=4)
    Y_B = R1[:, 0:9216].rearrange("p (j t) -> p j t", j=4)
    qT_B = R2[:, 0:18432].rearrange("p (h t) -> p h t", h=8)
    Y_A = R2[:, 0:9216].rearrange("p (j t) -> p j t", j=4)
    Y_C = R2[:, 9216:18432].rearrange("p (j t) -> p j t", j=4)
    G_x = v32(R2, 0, 1024); G_c = v32(R2, 2048, 1024)
    o3 = 0
    kT_A = R3[:, o3:o3 + 5120].rearrange("p (g t) -> p g t", g=2); o3 += 5120
    V_A = R3[:, o3:o3 + 7680].rearrange("p (t g c) -> p t g c", t=20, g=2); o3 += 7680
    o3m = o3
    wkvb = R3[:, o3:o3 + 2048].rearrange("p (j c) -> p j c", j=2); o3 += 2048
    kvbuf = []
    for i in range(2):
        kvbuf.append(R3[:, o3:o3 + 1024].rearrange("p (j c) -> p j c", j=2)); o3 += 1024
    Kbuf = []
    for i in range(2):
        Kbuf.append(R3[:, o3:o3 + 512]); o3 += 512
    Vbuf = []
    for i in range(2):
        Vbuf.append(R3[:, o3:o3 + 768].rearrange("p (t c) -> p t c", t=4)); o3 += 768
    Kc = R3[:, o3:o3 + 256]; o3 += 256
    Vc = R3[:, o3:o3 + 384].rearrange("p (t c) -> p t c", t=2); o3 += 384
    kvnc = R3[:, o3:o3 + 512].rearrange("p (j c) -> p j c", j=2); o3 += 512
    assert o3 <= 20736, o3
    qA = R3[:, o3m:o3m + 2048].rearrange("p (j c) -> p j c", j=4)
    sza = R3[:, o3m + 2048:o3m + 4096].rearrange("p (j c) -> p j c", j=4)
    zrow = R3[:, o3m:o3m + 4640].bitcast(F32)
    mT = R3[:, 0:18432].rearrange("p (j t) -> p j t", j=8)
    pbuf = [TMP[4 + i][:, :].bitcast(BF16)[:, 0:512] for i in range(4)]

    R = lambda n: Res(n)
    r_const = R("const")
    PB = [Res("pb%d" % i, excl=True) for i in range(8)]
    r_hx = [R("hx%d" % c) for c in range(5)]
    r_wb = [R("wb%d" % i) for i in range(3)]
    r_tmp = [R("tmp%d" % i) for i in range(8)]
    r_tab = [R("tab%d" % i) for i in range(4)]
    r_misc = {}
    r_xt = [R("xt0"), R("xt1")]

    def rm(name):
        if name not in r_misc:
            r_misc[name] = Res(name)
        return r_misc[name]

    def mm(out, lhsT, rhs, start, stop, r=(), pw=(), w=()):
        return S_.op("pe", lambda e: e.matmul(out, lhsT=lhsT, rhs=rhs, start=start, stop=stop), r=r, pw=pw, w=w)

    def act(out, in_, func, r=(), w=(), pw=(), scale=None, bias=None, accum_out=None):
        kw = {}
        if scale is not None:
            kw["scale"] = scale
        if bias is not None:
            kw["bias"] = bias
        if accum_out is not None:
            kw["accum_out"] = accum_out
        return S_.op("act", lambda e: e.activation(out=out, in_=in_, func=func, **kw), r=r, w=w, pw=pw)

    def tt(eng, out, in0, in1, op, r=(), w=(), pw=()):
        return S_.op(eng, lambda e: e.tensor_tensor(out=out, in0=in0, in1=in1, op=op), r=r, w=w, pw=pw)

    def ts(eng, out, in0, s1, op0, s2=None, op1=None, r=(), w=(), pw=()):
        if op1 is None:
            return S_.op(eng, lambda e: e.tensor_scalar(out=out, in0=in0, scalar1=s1, scalar2=None, op0=op0),
                         r=r, w=w, pw=pw)
        return S_.op(eng, lambda e: e.tensor_scalar(out=out, in0=in0, scalar1=s1, scalar2=s2, op0=op0, op1=op1),
                     r=r, w=w, pw=pw)

    def stt(out, in0, scalar, in1, op0, op1, r=(), w=(), pw=()):
        return S_.op("dve", lambda e: e.scalar_tensor_tensor(out=out, in0=in0, scalar=scalar, in1=in1,
                                                            op0=op0, op1=op1), r=r, w=w, pw=pw)

    def cp(eng, out, in_, r=(), w=(), pw=()):
        if eng == "act":
            return act(out, in_, AF.Copy, r=r, w=w, pw=pw)
        return S_.op(eng, lambda e: e.tensor_copy(out=out, in_=in_), r=r, w=w, pw=pw)

    wb_next = [0]

    def load_w(src_ap, ncols, kparts=8):
        i = wb_next[0] % 3
        wb_next[0] += 1
        dst = WB[i][:, 0:kparts, 0:ncols]
        S_.dma("pool", dst, src_ap, w=[r_wb[i]])
        return WB[i], r_wb[i]

    def win_src(l, name, c0, ncols):
        a = d_win.ap()[l].rearrange("(k p) c -> p k c", p=128)
        o = WOFF[name] + c0
        return a[:, :, o:o + ncols]

    pb_next = [0]

    def bank():
        b = pb_next[0] % 8
        pb_next[0] += 1
        return b

    k.dumps = []

    def dump(name, ap_sbuf, shape=None, dt=None):
        if name in dbg:
            ap_ = ap_sbuf if isinstance(ap_sbuf, AP) else ap_sbuf[:]
            dd = nc.dram_tensor("dbg_" + name, list(ap_.shape), ap_.dtype, kind="ExternalOutput")
            k.dumps.append((dd, ap_))

    for (dst, src) in ((ident[:], d_ident.ap()), (identb[:], d_identb.ap()), (onesb[:], d_onesb.ap()),
                       (sel[:], d_sel.ap().rearrange("k w m -> k (w m)")), (masks[:], d_masks.ap()),
                       (oh[:], d_oh.ap()), (cs[:], d_cs.ap()), (bmodT[:], d_bmodT.ap()),
                       (gpreT[:], d_gpreT.ap()), (sinkb[:], d_sinkb.ap()),
                       (gqaT[:], d_gqaT.ap()), (gkvaT[:], d_gkvaT.ap()), (convw[:], d_convw.ap())):
        S_.dma("sp", dst, src, pw=[r_const])
    S_.op("pool", lambda e: e.memset(epsc[:], EPS), pw=[r_const])
    S_.barrier()
    act(scs[:], cs[:], AF.Silu, w=[rm("scs")])
    act(esink[:], sinkb[:], AF.Exp, w=[rm("esink")])
    scs3 = scs[:].rearrange("p (k s) -> p k s", s=2)
    for l in range(DEPTH):
        wsrc = d_wmod.ap()[l].rearrange("(k p) c -> p k c", p=128)
        bm = bank()
        for j in range(16):
            if j % 4 == 0:
                wt, rw = load_w(wsrc[:, :, (j // 4) * 512:(j // 4) * 512 + 512], 512)
            for kk in range(8):
                mm(PS[:, bm, j * 2:j * 2 + 2], wt[:, kk, (j % 4) * 128:(j % 4) * 128 + 128], scs3[:, kk, :],
                   kk == 0, kk == 7, r=[rw, rm("scs")], pw=[PB[bm]])
        bmb = AP(bmodT[:].tensor, bmodT[:, l, :].offset, [list(bmodT[:].ap[0]), [1, 16], [0, 2]])
        tt("dve", modT[:], PS[:, bm, 0:32].rearrange("p (j s) -> p j s", s=2), bmb, ALU.add,
           r=[PB[bm]], w=[rm("modT")])
        ts("dve", modT[:, 8:16, :], modT[:, 8:16, :], 1.0, ALU.add, r=[], w=[rm("modT")])
        gpb = AP(gpreT[:].tensor, gpreT[:, l, :].offset, [list(gpreT[:].ap[0]), [1, 8], [0, 2]])
        tt("dve", Amod[:, l], modT[:, 8:16, :], gpb, ALU.mult, r=[rm("modT")], pw=[rm("AB")])
        cp("dve", Bmod[:, l], modT[:, 0:8, :], r=[rm("modT")], pw=[rm("AB")])
    dump("Amod", Amod[:], [128, DEPTH * 16], F32)
    dump("Bmod", Bmod[:], [128, DEPTH * 16], F32)
    S_.barrier()
    k.__dict__.update(locals())


def _stage1_tile(k, S_, l, i, xtile, r_xt):
    PS, PB, hxT, ident = k.PS, k.PB, k.hxT, k.ident
    act, ts, mm, rm = k.act, k.ts, k.mm, k.rm
    s = 0 if i < 16 else 1
    c = i // 4 if i < 16 else 4
    col0 = i * 128
    ssq = k.small[:, 0:1]
    lnv = k.small[:, 1:2]
    rstd = k.small[:, 2:3]
    act(k.junk, xtile, AF.Square, r=[r_xt], w=[rm("junk"), rm("ssq")], accum_out=ssq)
    act(lnv, ssq, AF.Ln, r=[rm("ssq")], w=[rm("lnv")], scale=1.0 / D, bias=k.epsc[:, 0:1])
    act(rstd, lnv, AF.Exp, r=[rm("lnv")], w=[rm("rstd")], scale=-0.5)
    ts("dve", k.xn, xtile, rstd, ALU.mult, r=[r_xt, rm("rstd")], w=[rm("xn")])
    import os
    if os.environ.get("BISECT") == "1":
        return
    hi, lo = k.xhl[:, 0, :], k.xhl[:, 1, :]
    k.cp("pool", hi, k.xn, r=[rm("xn")], w=[rm("xhi")])
    k.tt("dve", lo, k.xn, hi, ALU.subtract, r=[rm("xn"), rm("xhi")], w=[rm("xlo")])
    if os.environ.get("BISECT") == "2":
        return
    b0 = k.bank()
    b1 = k.bank()
    for kk in range(8):
        bb = b0 if kk < 4 else b1
        o_ = PS[:, bb, (kk % 4) * 128:(kk % 4) * 128 + 128]
        mm(o_, hi[:, kk * 128:(kk + 1) * 128], k.identb[:], True, False, r=[rm("xhi"), k.r_const], pw=[PB[bb]])
        mm(o_, lo[:, kk * 128:(kk + 1) * 128], k.identb[:], False, True, r=[rm("xlo"), k.r_const], pw=[PB[bb]])
    if os.environ.get("BISECT") == "3":
        return
    for kk in range(8):
        bb = b0 if kk < 4 else b1
        src = PS[:, bb, (kk % 4) * 128:(kk % 4) * 128 + 128]
        dst = hxT[:, kk, col0:col0 + 128]
        if kk < 4:
            act(dst, src, AF.Identity, r=[PB[bb], rm("AB")], pw=[k.r_hx[c]],
                scale=k.Amod[:, l, kk, s:s + 1], bias=k.Bmod[:, l, kk, s:s + 1])
        else:
            ts("dve", dst, src, k.Amod[:, l, kk, s:s + 1], ALU.mult, s2=k.Bmod[:, l, kk, s:s + 1], op1=ALU.add,
               r=[PB[bb], rm("AB")], pw=[k.r_hx[c]])


def _phaseA(k, S_, l):
    for i in range(getattr(k, "ntilesA", 18)):
        src = k.d_x.ap()[i * 128:(i + 1) * 128, :] if i < 16 else k.d_ctx.ap()[(i - 16) * 128:(i - 15) * 128, :]
        j = i % 2
        S_.dma("sp", k.xt[j], src, w=[k.r_xt[j]])
        _stage1_tile(k, S_, l, i, k.xt[j], k.r_xt[j])


def _load_tabs(k, S_, c0, n, which=(0, 1, 2, 3)):
    for ti in which:
        S_.dma("sp", k.tabs[ti][:, 0:n], k.d_tab[ti].ap()[:, c0:c0 + n], w=[k.r_tab[ti]])


def _rstd_bcast(k, S_, banks, nj, n, inv_n, out_tmp, r_out):
    PS, PB = k.PS, k.PB
    for j in range(nj):
        k.act(k.sqb[:, j, 0:n], PS[:, banks[j], 0:n], AF.Square, r=[PB[banks[j]]], pw=[k.rm("sqb")])
    bs = k.bank()
    for j in range(nj):
        k.mm(PS[:, bs, 0:n], k.onesb[:], k.sqb[:, j, 0:n], j == 0, j == nj - 1, r=[k.rm("sqb"), k.r_const],
             pw=[PB[bs]])
    k.act(out_tmp[:, 0:n], PS[:, bs, 0:n], AF.Ln, r=[PB[bs]], w=[r_out], scale=inv_n, bias=k.epsc[:, 0:1])
    k.act(out_tmp[:, 0:n], out_tmp[:, 0:n], AF.Exp, r=[], w=[r_out], scale=-0.5)


def _phaseB_kv(k, S_, l):
    PS, PB, hxT = k.PS, k.PB, k.hxT
    mm, act, tt, ts, stt, cp, rm = k.mm, k.act, k.tt, k.ts, k.stt, k.cp, k.rm
    TMP, r_tmp = k.TMP, k.r_tmp
    S_.op("pool", lambda e: e.memset(k.V_A[:, :, :, 0:64], 1.0), pw=[rm("VA")])
    S_.op("pool", lambda e: e.memset(k.V_A[:, :, :, 128:192], 1.0), pw=[rm("VA")])
    for i in range(2):
        S_.op("pool", (lambda vb: (lambda e: e.memset(vb[:, :, 0:64], 1.0)))(k.Vbuf[i]), pw=[rm("Vbuf%d" % i)])
        S_.op("pool", (lambda vb: (lambda e: e.memset(vb[:, :, 128:192], 1.0)))(k.Vbuf[i]), pw=[rm("Vbuf%d" % i)])
    S_.op("pool", lambda e: e.memset(k.Vc[:, :, 0:64], 1.0), pw=[rm("Vc")])
    S_.op("pool", lambda e: e.memset(k.Vc[:, :, 128:192], 1.0), pw=[rm("Vc")])
    wka, r_wka = k.load_w(k.win_src(l, "KA", 0, 512), 512)
    wvk, r_wvk = k.load_w(k.win_src(l, "VKK", 0, 448), 448)
    for c, (c0, n) in enumerate(CHUNKS):
        _load_tabs(k, S_, c0, n)
        rhs = [hxT[:, kk, c0:c0 + n] for kk in range(8)]
        kdst0 = 128 + c0 if c < 4 else 2304
        for g in range(2):
            bq, br = k.bank(), k.bank()
            for ti, bb in ((2 * g, bq), (2 * g + 1, br)):
                for kk in range(8):
                    mm(PS[:, bb, 0:n], wka[:, kk, ti * 128:(ti + 1) * 128], rhs[kk], kk == 0, kk == 7,
                       r=[r_wka, k.r_hx[c]], pw=[PB[bb]])
            tt("dve", TMP[0][:, 0:n], PS[:, bq, 0:n], k.tabs[0][:, 0:n], ALU.mult, r=[PB[bq], k.r_tab[0]], w=[r_tmp[0]])
            tt("dve", TMP[1][:, 0:n], PS[:, br, 0:n], k.tabs[1][:, 0:n], ALU.mult, r=[PB[br], k.r_tab[1]], w=[r_tmp[1]])
            tt("dve", k.kT_A[:, g, kdst0:kdst0 + n], TMP[0][:, 0:n], TMP[1][:, 0:n], ALU.add,
               r=[r_tmp[0], r_tmp[1]], pw=[rm("kTA")])
        bv = k.bank()
        nt = n // 128
        for t_ in range(nt):
            for kk in range(8):
                mm(PS[:, bv, t_ * 128:(t_ + 1) * 128], hxT[:, kk, c0 + t_ * 128:c0 + (t_ + 1) * 128], wvk[:, kk, 0:128],
                   kk == 0, kk == 7, r=[r_wvk, k.r_hx[c]], pw=[PB[bv]])
        vt0 = 1 + c * 4 if c < 4 else 18
        cp("dve", k.V_A[:, vt0:vt0 + nt, :, 64:128], PS[:, bv, 0:n].rearrange("p (t g d) -> p t g d", t=nt, g=2),
           r=[PB[bv]], pw=[rm("VA")])
        bk = [k.bank(), k.bank()]
        for j in range(2):
            for kk in range(8):
                mm(PS[:, bk[j], 0:n], wvk[:, kk, 128 + j * 128:256 + j * 128], rhs[kk], kk == 0, kk == 7,
                   r=[r_wvk, k.r_hx[c]], pw=[PB[bk[j]]])
        _rstd_bcast(k, S_, bk, 2, n, 1.0 / 256, TMP[2], r_tmp[2])
        for j in range(2):
            dst = k.kvst[:, j, 0:n] if c < 4 else k.kvnc[:, j, 0:n]
            stt(dst, PS[:, bk[j], 0:n], k.gkvaT[:, l, j:j + 1], TMP[2][:, 0:n], ALU.mult, ALU.mult,
                r=[PB[bk[j]], r_tmp[2]], pw=[rm("kvst") if c < 4 else rm("kvnc")])
        if c < 4:
            S_.dma("sp", k.d_snd1.ap()[:, PC_KVN:PC_KVN + 4096].rearrange("p (j t) -> p j t", j=2)[:, :, c0:c0 + n],
                   k.kvst[:, :, 0:n], r=[rm("kvst")], pw=[rm("snd1")])
        b1, b2 = k.bank(), k.bank()
        for (bb, co) in ((b1, 384), (b2, 416)):
            for kk in range(8):
                mm(PS[0:32, bb, 0:n], wvk[:, kk, co:co + 32], rhs[kk], kk == 0, kk == 7,
                   r=[r_wvk, k.r_hx[c]], pw=[PB[bb]])
        tt("dve", TMP[0][0:32, 0:n], PS[0:32, b1, 0:n], k.tabs[2][0:32, 0:n], ALU.mult, r=[PB[b1], k.r_tab[2]], w=[r_tmp[0]])
        tt("dve", TMP[1][0:32, 0:n], PS[0:32, b2, 0:n], k.tabs[3][0:32, 0:n], ALU.mult, r=[PB[b2], k.r_tab[3]], w=[r_tmp[1]])
        tt("dve", k.krst[:, 0:n], TMP[0][0:32, 0:n], TMP[1][0:32, 0:n], ALU.add, r=[r_tmp[0], r_tmp[1]], w=[rm("krst")])
        if c < 4:
            S_.dma("sp", k.d_snd.ap()[c * 32:(c + 1) * 32, PC_KR:PC_KR + 512], k.krst[:, 0:n], r=[rm("krst")], pw=[rm("snd")])
        else:
            S_.dma("sp", k.Kc[64:96, 0:n], k.krst[:, 0:n], r=[rm("krst")], pw=[rm("Kc")])
    bz = k.bank()
    for which in range(2):
        wz, r_wz = k.load_w(k.win_src(l, "ZH", which * 512, 512), 512)
        for ct in range(4):
            col = (which * 4 + ct) * 2
            for kk in range(8):
                mm(PS[:, bz, col:col + 2], wz[:, kk, ct * 128:(ct + 1) * 128], hxT[:, kk, 0:2048:2047],
                   kk == 0, kk == 7, r=[r_wz, k.r_hx[0], k.r_hx[3]], pw=[PB[bz]])
    cp("act", TMP[3][:, 0:8], PS[:, bz, 0:8], r=[PB[bz]], w=[r_tmp[3]])
    tt("dve", k.zh[:], TMP[3][:, 0:8], PS[:, bz, 8:16], ALU.mult, r=[PB[bz], r_tmp[3]], w=[rm("zh")])
    snd = k.d_snd.ap()
    for fl, off in ((0, 128), (1, 2048)):
        S_.dma("sp", snd[:, PC_KAH:PC_KAH + 512].rearrange("p (g f t) -> p g f t", g=2, f=2)[:, :, fl, :],
               k.kT_A[:, :, off:off + 128], r=[rm("kTA")], pw=[rm("snd")])
    for fl, vt in ((0, 1), (1, 16)):
        S_.dma("sp", snd[:, PC_VAH + fl * 128:PC_VAH + (fl + 1) * 128].rearrange("p (g d) -> p g d", g=2),
               k.V_A[:, vt, :, 64:128], r=[rm("VA")], pw=[rm("snd")])
    cp("dve", k.zhb[:, 0:8], k.zh[:], r=[rm("zh")], w=[rm("zhb")])
    tt("dve", k.zhb[:, 8:16], k.zh[:], k.zhb[:, 0:8], ALU.subtract, r=[rm("zh")], w=[rm("zhb")])
    S_.dma("sp", snd[:, PC_ZH:PC_ZH + 16], k.zhb[:], r=[rm("zhb")], pw=[rm("snd")])
    if os.environ.get("NOCC") == "1":
        return
    S_.op("pool", lambda e: e.collective_compute("AllGather", ALU.bypass,
                                                 replica_groups=[[0, 1, 2, 3], [4, 5, 6, 7]],
                                                 ins=[k.d_snd1.ap().opt()], outs=[k.d_rcv1.ap().opt()]),
          r=[rm("snd1")], w=[rm("rcv1")], kind="cc")
    S_.op("pool", lambda e: e.collective_compute("AllGather", ALU.bypass,
                                                 replica_groups=[[0, 1, 2, 3], [4, 5, 6, 7]],
                                                 ins=[k.d_snd.ap().opt()], outs=[k.d_rcv.ap().opt()]),
          r=[rm("snd")], w=[rm("rcv")], kind="cc")


def _phaseB_halo(k, S_, l):
    rm, stt, ts = k.rm, k.stt, k.ts
    rcv = k.d_rcv.ap().rearrange("(r p) c -> p r c", p=128)
    S_.dma("sp", k.hal_k, rcv[:, :, PC_KAH:PC_KAH + 512], r=[rm("rcv")], w=[rm("halk")])
    S_.dma("sp", k.hal_v, rcv[:, :, PC_VAH:PC_VAH + 256], r=[rm("rcv")], w=[rm("halv")])
    S_.dma("sp", k.hal_z, rcv[:, :, PC_ZH:PC_ZH + 16], r=[rm("rcv")], w=[rm("halz")])
    oh = k.oh

    def select(dst, srcs, ohbase, tmp, r_t, rsrc, rdst):
        ts("dve", tmp, srcs[0], oh[:, ohbase:ohbase + 1], ALU.mult, r=[rsrc, k.r_const], w=[r_t])
        for rr in range(1, 4):
            last = rr == 3
            stt(dst if last else tmp, srcs[rr], oh[:, ohbase + rr:ohbase + rr + 1], tmp, ALU.mult, ALU.add,
                r=[rsrc, k.r_const] + ([] if not last else [r_t]), w=([r_t] if not last else []),
                pw=([rdst] if last else []))

    hk = lambda rr, fl: k.hal_k[:, rr, :].rearrange("p (g f t) -> p g f t", g=2, f=2)[:, :, fl, :]
    t3 = lambda i: k.TMP[i][:, 0:256].rearrange("p (g t) -> p g t", g=2)
    select(k.kT_A[:, :, 0:128], [hk(rr, 1) for rr in range(4)], 0, t3(0), k.r_tmp[0], rm("halk"), rm("kTA"))
    select(k.kT_A[:, :, 2176:2304], [hk(rr, 0) for rr in range(4)], 4, t3(1), k.r_tmp[1], rm("halk"), rm("kTA"))
    hv = lambda rr, fl: k.hal_v[:, rr, fl * 128:(fl + 1) * 128].rearrange("p (g d) -> p g d", g=2)
    t4 = lambda i: k.TMP[i][:, 0:128].rearrange("p (g d) -> p g d", g=2)
    select(k.V_A[:, 0, :, 64:128], [hv(rr, 1) for rr in range(4)], 0, t4(2), k.r_tmp[2], rm("halv"), rm("VA"))
    select(k.V_A[:, 17, :, 64:128], [hv(rr, 0) for rr in range(4)], 4, t4(3), k.r_tmp[3], rm("halv"), rm("VA"))
    k.tt("dve", k.hzf[:], k.hal_z[:, :, 0:8], k.hal_z[:, :, 8:16], ALU.add, r=[rm("halz")], w=[rm("hzf")])
    hz = lambda rr, fl: k.hzf[:, rr, :].rearrange("p (c f) -> p c f", f=2)[:, :, fl]
    select(k.zhalo[:, :, 0], [hz(rr, 1) for rr in range(4)], 0, k.TMP[0][:, 256:260], k.r_tmp[0], rm("hzf"), rm("zhalo"))
    select(k.zhalo[:, :, 1], [hz(rr, 0) for rr in range(4)], 4, k.TMP[1][:, 256:260], k.r_tmp[1], rm("hzf"), rm("zhalo"))


def _phaseB_qb(k, S_, l, nchunks):
    PS, PB, hxT = k.PS, k.PB, k.hxT
    mm, act, tt, stt, cp, rm = k.mm, k.act, k.tt, k.stt, k.cp, k.rm
    TMP, r_tmp = k.TMP, k.r_tmp
    S_.dma("pool", k.wqb, k.d_wqb.ap()[l].rearrange("(j p) c -> p j c", p=128), w=[rm("wqb")])
    wql, r_wql = k.load_w(k.win_src(l, "QL", 0, 384), 384)
    for c in range(nchunks):
        c0, n = CHUNKS[c]
        _load_tabs(k, S_, c0, n, which=(2, 3))
        bq = [k.bank(), k.bank(), k.bank()]
        for j in range(3):
            for kk in range(8):
                mm(PS[:, bq[j], 0:n], wql[:, kk, j * 128:(j + 1) * 128], hxT[:, kk, c0:c0 + n], kk == 0, kk == 7,
                   r=[r_wql, k.r_hx[c]], pw=[PB[bq[j]]])
        _rstd_bcast(k, S_, bq, 3, n, 1.0 / 384, TMP[2], r_tmp[2])
        for j in range(3):
            stt(k.qnb[:, j, 0:n], PS[:, bq[j], 0:n], k.gqaT[:, l, j:j + 1], TMP[2][:, 0:n], ALU.mult, ALU.mult,
                r=[PB[bq[j]], r_tmp[2]], pw=[rm("qnb")])
        for h in range(8):
            b1, b2 = k.bank(), k.bank()
            for (bb, co) in ((b1, h * 192), (b2, h * 192 + 96)):
                for j in range(3):
                    mm(PS[0:96, bb, 0:n], k.wqb[:, j, co:co + 96], k.qnb[:, j, 0:n], j == 0, j == 2,
                       r=[rm("wqb"), rm("qnb")], pw=[PB[bb]])
            cp("act", k.qT_B[0:64, h, c0:c0 + n], PS[0:64, b1, 0:n], r=[PB[b1]], pw=[rm("qTB")])
            ta, tb = (0, 1) if h % 2 == 0 else (3, 5)
            tt("dve", TMP[ta][64:96, 0:n], PS[64:96, b1, 0:n], k.tabs[2][64:96, 0:n], ALU.mult,
               r=[PB[b1], k.r_tab[2]], w=[r_tmp[ta]])
            tt("dve", TMP[tb][64:96, 0:n], PS[64:96, b2, 0:n], k.tabs[3][64:96, 0:n], ALU.mult,
               r=[PB[b2], k.r_tab[3]], w=[r_tmp[tb]])
            tt("dve", k.qT_B[64:96, h, c0:c0 + n], TMP[ta][64:96, 0:n], TMP[tb][64:96, 0:n], ALU.add,
               r=[r_tmp[ta], r_tmp[tb]], pw=[rm("qTB")])


def _attn_core(k, S_, items, scale):
    PS, PB = k.PS, k.PB
    pend = None
    for it in items:
        if it.get("before") is not None:
            it["before"]()
        sb0, nb = it["sb"]
        pb_i = k.sctr % 4
        k.sctr += 1
        pws = [PB[sb0 + i] for i in range(nb)]
        first = True
        if it.get("mask") is not None:
            for (o_ap, m_ap) in it["mask"]:
                k.mm(o_ap, k.identb[:], m_ap, True, False, r=[k.r_const], pw=pws)
            first = False
        for (o_ap, lhsT, rhs) in it["s_mms"]:
            k.mm(o_ap, lhsT, rhs, first, True, r=it["r"], pw=pws)
        pv_ = it["p_view"](k.pbuf[pb_i])
        k.act(pv_, it["s_view"], AF.Exp, r=pws, w=[k.r_pbuf[pb_i]], scale=scale)
        if pend is not None:
            pend()

        def mk(it=it, pb_i=pb_i):
            def f():
                st = it["start"]
                npv = len(it["pv"])
                for pi, (o_ap, lhsT, rhs_fn) in enumerate(it["pv"]):
                    k.mm(o_ap, lhsT, rhs_fn(k.pbuf[pb_i]), st, it["stop"] and pi == npv - 1,
                         r=it["rv"] + [k.r_pbuf[pb_i]], pw=[PB[it["ob"]]])
                    st = False
            return f
        pend = mk()
        if it.get("after") is not None:
            it["after"]()
    if pend is not None:
        pend()


def _phaseC_mla(k, S_, l, with_ctx_q):
    PS, PB = k.PS, k.PB
    mm, act, tt, cp, rm = k.mm, k.act, k.tt, k.cp, k.rm
    S_.dma("pool", k.wkvb, k.d_wkvb.ap()[l].rearrange("(j p) c -> p j c", p=128), w=[rm("wkvb")])
    rcv = k.d_rcv.ap()
    rcv1 = k.d_rcv1.ap()
    EB = 7
    kchunks = [(rr, cc) for rr in range(4) for cc in range(4)] + [None]
    rec = k.TMP[0]
    for h in range(8):
        e = h % 2
        vsl = slice(64, 192) if e == 0 else slice(0, 128)
        o_lo, r_lo = (0, 64) if e == 0 else (64, 0)

        def load_kv(idx, h=h):
            if idx >= len(kchunks) or kchunks[idx] is None:
                return
            rr, cc = kchunks[idx]
            i = idx % 2
            S_.dma("sp", k.kvbuf[i], rcv1[rr * 128:(rr + 1) * 128, PC_KVN:PC_KVN + 4096].rearrange(
                "p (j t) -> p j t", j=2)[:, :, cc * 512:(cc + 1) * 512], r=[rm("rcv1")], w=[rm("kvbuf%d" % i)])

        def load_kr(idx, h=h):
            if idx >= len(kchunks) or kchunks[idx] is None:
                return
            rr, cc = kchunks[idx]
            i = idx % 2
            S_.dma("sp", k.Kbuf[i][64:96, :], rcv[rr * 128 + cc * 32:rr * 128 + cc * 32 + 32, PC_KR:PC_KR + 512],
                   r=[rm("rcv")], pw=[rm("Kbuf%d" % i)])

        def expand_k(idx, h=h):
            kc = kchunks[idx]
            i = idx % 2
            src, rs, n, dst, rd = ((k.kvbuf[i], rm("kvbuf%d" % i), 512, k.Kbuf[i], rm("Kbuf%d" % i)) if kc is not None
                                   else (k.kvnc, rm("kvnc"), 256, k.Kc, rm("Kc")))
            for j in range(2):
                mm(PS[0:64, EB, 0:n], k.wkvb[:, j, h * 64:(h + 1) * 64], src[:, j, 0:n], j == 0, j == 1,
                   r=[rm("wkvb"), rs], pw=[PB[EB]])
            cp("dve", dst[0:64, 0:n], PS[0:64, EB, 0:n], r=[PB[EB]], pw=[rd])

        def expand_v(idx, h=h):
            kc = kchunks[idx]
            i = idx % 2
            src, rs, nt, dst, rd = ((k.kvbuf[i], rm("kvbuf%d" % i), 4, k.Vbuf[i], rm("Vbuf%d" % i)) if kc is not None
                                    else (k.kvnc, rm("kvnc"), 2, k.Vc, rm("Vc")))
            for t_ in range(nt):
                for j in range(2):
                    mm(PS[:, EB, t_ * 64:(t_ + 1) * 64], src[:, j, t_ * 128:(t_ + 1) * 128],
                       k.wkvb[:, j, 512 + h * 64:512 + (h + 1) * 64], j == 0, j == 1,
                       r=[rm("wkvb"), rs], pw=[PB[EB]])
            cp("dve", dst[:, 0:nt, 64:128], PS[:, EB, 0:nt * 64].rearrange("p (t d) -> p t d", t=nt),
               r=[PB[EB]], pw=[rd])

        load_kv(0)
        load_kv(1)
        load_kr(0)
        expand_k(0)
        expand_v(0)
        items = []
        for idx, kc in enumerate(kchunks):
            i = idx % 2
            if kc is not None:
                Kt, rK, Vt, rV, nt = k.Kbuf[i], rm("Kbuf%d" % i), k.Vbuf[i], rm("Vbuf%d" % i), 4
            else:
                Kt, rK, Vt, rV, nt = k.Kc, rm("Kc"), k.Vc, rm("Vc"), 2
            cnt = 0
            for qc in range(4):
                for kt in range(nt):
                    sbank = 4 + (len(items) % 3)
                    it = dict(sb=(sbank, 1),
                              s_mms=[(PS[:, sbank, 0:512], Kt[0:96, kt * 128:(kt + 1) * 128],
                                      k.qT_B[0:96, h, qc * 512:(qc + 1) * 512])],
                              s_view=PS[:, sbank, 0:512], p_view=(lambda p: p[:, 0:512]),
                              r=[rK, rm("qTB")],
                              pv=[(PS[:, qc, 0:512], Vt[:, kt, vsl], (lambda p: p[:, 0:512]))], rv=[rV],
                              ob=qc, start=(idx == 0 and kt == 0), stop=(idx == len(kchunks) - 1 and kt == nt - 1))
                    cnt += 1
                    if cnt == 1:
                        def bef(idx=idx):
                            load_kv(idx + 2)
                            load_kr(idx + 1)
                        it["before"] = bef
                    if idx + 1 < len(kchunks):
                        if cnt == 2:
                            it["after"] = (lambda idx=idx: expand_k(idx + 1))
                        elif cnt == 6:
                            it["after"] = (lambda idx=idx: expand_v(idx + 1))
                    items.append(it)
        _attn_core(k, S_, items, SCALE_B)
        for qc in range(4):
            S_.op("dve", (lambda o, i_: (lambda e_: e_.reciprocal(out=o, in_=i_)))(
                rec[o_lo:o_lo + 64, 0:512], PS[r_lo:r_lo + 64, qc, 0:512]), r=[PB[qc]], w=[k.r_tmp[0]])
            tt("dve", k.Y_B[o_lo:o_lo + 64, h // 2, qc * 512:(qc + 1) * 512], PS[o_lo:o_lo + 64, qc, 0:512],
               rec[o_lo:o_lo + 64, 0:512], ALU.mult, r=[PB[qc], k.r_tmp[0]], pw=[rm("YB")])
        if with_ctx_q:
            items = []
            for kt in range(2):
                sbank = 4 + kt
                items.append(dict(sb=(sbank, 1),
                                  s_mms=[(PS[:, sbank, 0:256], k.Kc[0:96, kt * 128:(kt + 1) * 128],
                                          k.qT_B[0:96, h, 2048:2304])],
                                  s_view=PS[:, sbank, 0:256], p_view=(lambda p: p[:, 0:256]),
                                  r=[rm("Kc"), rm("qTB")],
                                  pv=[(PS[:, 6, 0:256], k.Vc[:, kt, vsl], (lambda p: p[:, 0:256]))], rv=[rm("Vc")],
                                  ob=6, start=(kt == 0), stop=(kt == 1)))
            _attn_core(k, S_, items, SCALE_B)
            S_.op("dve", (lambda o, i_: (lambda e_: e_.reciprocal(out=o, in_=i_)))(
                rec[o_lo:o_lo + 64, 0:256], PS[r_lo:r_lo + 64, 6, 0:256]), r=[PB[6]], w=[k.r_tmp[0]])
            tt("dve", k.Y_B[o_lo:o_lo + 64, h // 2, 2048:2304], PS[o_lo:o_lo + 64, 6, 0:256],
               rec[o_lo:o_lo + 64, 0:256], ALU.mult, r=[PB[6], k.r_tmp[0]], pw=[rm("YB")])


def _phaseD_A(k, S_, l, nchunks):
    PS, PB, hxT = k.PS, k.PB, k.hxT
    mm, act, tt, ts, rm = k.mm, k.act, k.tt, k.ts, k.rm
    TMP, r_tmp = k.TMP, k.r_tmp
    wq = [k.load_w(k.win_src(l, "QA", 0, 512), 512), k.load_w(k.win_src(l, "QA", 512, 512), 512)]
    wza, r_wza = k.load_w(k.win_src(l, "ZA", 0, 512), 512)
    gctr = [0]
    for c in range(nchunks):
        c0, n = CHUNKS[c]
        _load_tabs(k, S_, c0, n, which=(0, 1))
        for j in range(4):
            wt, rw = wq[j // 2]
            base = (j % 2) * 256
            bq, br = k.bank(), k.bank()
            for (bb, co) in ((bq, base), (br, base + 128)):
                for kk in range(8):
                    mm(PS[:, bb, 0:n], wt[:, kk, co:co + 128], hxT[:, kk, c0:c0 + n], kk == 0, kk == 7,
                       r=[rw, k.r_hx[c]], pw=[PB[bb]])
            tt("dve", TMP[0][:, 0:n], PS[:, bq, 0:n], k.tabs[0][:, 0:n], ALU.mult, r=[PB[bq], k.r_tab[0]], w=[r_tmp[0]])
            tt("dve", TMP[1][:, 0:n], PS[:, br, 0:n], k.tabs[1][:, 0:n], ALU.mult, r=[PB[br], k.r_tab[1]], w=[r_tmp[1]])
            tt("dve", k.qA[:, j, 0:n], TMP[0][:, 0:n], TMP[1][:, 0:n], ALU.add, r=[r_tmp[0], r_tmp[1]], pw=[rm("qA")])
        for jt in range(4):
            bb = k.bank()
            for kk in range(8):
                mm(PS[:, bb, 0:n], wza[:, kk, jt * 128:(jt + 1) * 128], hxT[:, kk, c0:c0 + n], kk == 0, kk == 7,
                   r=[r_wza, k.r_hx[c]], pw=[PB[bb]])
            act(k.sza[:, jt, 0:n], PS[:, bb, 0:n], AF.Silu, r=[PB[bb]], pw=[rm("sza")])
        items = []
        groups = []
        for qb in range(n // 128):
            for g in range(2):
                if c < 4:
                    qbg = c * 4 + qb
                    tiles = [(qbg * 128, qbg, 2 if qbg == 0 else 0), ((qbg + 1) * 128, qbg + 1, None),
                             ((qbg + 2) * 128, qbg + 2, 3 if qbg == 15 else 1), (2304, 18, None), (2432, 19, None)]
                else:
                    tiles = [(2304, 18, None), (2432, 19, None)]
                gi = gctr[0]
                gctr[0] += 1
                ob = gi % 4
                sb0 = 4 + 2 * (gi % 2)
                groups.append((qb, g, ob, gi))
                for ti, (koff, vt, mi) in enumerate(tiles):
                    sb0 = 4 + 2 * ((len(items)) % 2)
                    it = dict(sb=(sb0, 2),
                              s_mms=[(PS[:, sb0 + e, 0:256], k.kT_A[e * 64:(e + 1) * 64, g, koff:koff + 128],
                                      k.qA[e * 64:(e + 1) * 64, 2 * g:2 * g + 2, qb * 128:(qb + 1) * 128])
                                     for e in range(2)],
                              mask=(None if mi is None else [(PS[:, sb0 + e, 0:256], k.masks[:, mi, 0:256])
                                                             for e in range(2)]),
                              s_view=PS[:, sb0:sb0 + 2, 0:256],
                              p_view=(lambda p: p[:, 0:512].rearrange("p (e c) -> p e c", e=2)),
                              r=[rm("kTA"), rm("qA")],
                              pv=[(PS[:, ob, 0:256], k.V_A[:, vt, g, 64:192], (lambda p: p[:, 0:256])),
                                  (PS[:, ob, 256:512], k.V_A[:, vt, g, 0:128], (lambda p: p[:, 256:512]))],
                              rv=[rm("VA")], ob=ob, start=(ti == 0), stop=(ti == len(tiles) - 1))
                    items.append(it)

        def normalize(qb, g, ob, gi, c0=c0):
            ta, tb = (TMP[0], TMP[1]) if gi % 2 == 0 else (TMP[2], TMP[3])
            ra, rb = (r_tmp[0], r_tmp[1]) if gi % 2 == 0 else (r_tmp[2], r_tmp[3])
            tok0 = c0 + qb * 128
            for e in range(2):
                o_lo, r_lo = (0, 64) if e == 0 else (64, 0)
                cs_ = slice(e * 256, (e + 1) * 256)
                for jj in range(2):
                    h = 4 * g + 2 * jj + e
                    cj = slice(e * 256 + jj * 128, e * 256 + (jj + 1) * 128)
                    ts("dve", ta[o_lo:o_lo + 64, cj], PS[r_lo:r_lo + 64, ob, cj], k.esink[r_lo:r_lo + 64, l, h:h + 1],
                       ALU.add, r=[PB[ob], rm("esink")], pw=[ra])
                S_.op("dve", (lambda o, i_: (lambda e_: e_.reciprocal(out=o, in_=i_)))(
                    ta[o_lo:o_lo + 64, cs_], ta[o_lo:o_lo + 64, cs_]), r=[], w=[ra])
                tt("dve", tb[o_lo:o_lo + 64, cs_], PS[o_lo:o_lo + 64, ob, cs_], ta[o_lo:o_lo + 64, cs_], ALU.mult,
                   r=[PB[ob], ra], pw=[rb])
                tt("dve", k.Y_A[o_lo:o_lo + 64, 2 * g:2 * g + 2, tok0:tok0 + 128],
                   tb[o_lo:o_lo + 64, cs_].rearrange("p (j q) -> p j q", j=2),
                   k.sza[o_lo:o_lo + 64, 2 * g:2 * g + 2, qb * 128:(qb + 1) * 128], ALU.mult,
                   r=[rb, rm("sza")], pw=[rm("YA")])

        per = len(items) // len(groups)
        for gidx in range(1, len(groups)):
            items[gidx * per]["after"] = (lambda a=groups[gidx - 1]: normalize(*a))
        _attn_core(k, S_, items, SCALE_A)
        normalize(*groups[-1])


def _phaseD_zb(k, S_, l, nchunks):
    PS, PB, hxT = k.PS, k.PB, k.hxT
    wzb, r_wzb = k.load_w(k.win_src(l, "ZB", 0, 512), 512)
    for c in range(nchunks):
        c0, n = CHUNKS[c]
        for jt in range(4):
            bb = k.bank()
            for kk in range(8):
                k.mm(PS[:, bb, 0:n], wzb[:, kk, jt * 128:(jt + 1) * 128], hxT[:, kk, c0:c0 + n], kk == 0, kk == 7,
                     r=[r_wzb, k.r_hx[c]], pw=[PB[bb]])
            sq = k.sqb[:, jt % 3, 0:n]
            k.act(sq, PS[:, bb, 0:n], AF.Silu, r=[PB[bb]], w=[k.rm("sqb%d" % (jt % 3))])
            k.tt("dve", k.Y_B[:, jt, c0:c0 + n], k.Y_B[:, jt, c0:c0 + n], sq, ALU.mult,
                 r=[k.rm("sqb%d" % (jt % 3))], pw=[k.rm("YB")])


def _phaseD_C(k, S_, l, nchunks):
    PS, PB, hxT = k.PS, k.PB, k.hxT
    mm, act, tt, ts, stt, cp, rm = k.mm, k.act, k.tt, k.ts, k.stt, k.cp, k.rm
    TMP, r_tmp = k.TMP, k.r_tmp
    zrow = k.zrow
    for ct in range(4):
        wc, r_wc = k.load_w(k.win_src(l, "C", ct * 512, 512), 512)
        cp("dve", zrow[:, 0:1], k.zhalo[:, ct, 0:1], r=[rm("zhalo")], pw=[rm("zrow")])
        cp("dve", zrow[:, 2049:2050], k.zhalo[:, ct, 1:2], r=[rm("zhalo")], pw=[rm("zrow")])
        S_.op("dve", lambda e: e.memset(zrow[:, 2050:2051], 0.0), pw=[rm("zrow")])
        S_.op("dve", lambda e: e.memset(zrow[:, 2307:2308], 0.0), pw=[rm("zrow")])
        zo = lambda c: (1 + CHUNKS[c][0]) if c < 4 else 2051
        for c in range(nchunks):
            c0, n = CHUNKS[c]
            bcc, buc = k.bank(), k.bank()
            for (bb, co) in ((bcc, 128), (buc, 256)):
                for kk in range(8):
                    mm(PS[:, bb, 0:n], wc[:, kk, co:co + 128], hxT[:, kk, c0:c0 + n], kk == 0, kk == 7,
                       r=[r_wc, k.r_hx[c]], pw=[PB[bb]])
            cp("act", TMP[0][:, 0:n], PS[:, bcc, 0:n], r=[PB[bcc]], w=[r_tmp[0]])
            tt("dve", zrow[:, zo(c):zo(c) + n], TMP[0][:, 0:n], PS[:, buc, 0:n], ALU.mult,
               r=[r_tmp[0], PB[buc]], pw=[rm("zrow")])
        for c in range(nchunks):
            c0, n = CHUNKS[c]
            z0 = zo(c)
            bbc, bzc = k.bank(), k.bank()
            for (bb, co) in ((bbc, 0), (bzc, 384)):
                for kk in range(8):
                    mm(PS[:, bb, 0:n], wc[:, kk, co:co + 128], hxT[:, kk, c0:c0 + n], kk == 0, kk == 7,
                       r=[r_wc, k.r_hx[c]], pw=[PB[bb]])
            act(k.sqb[:, 0, 0:n], PS[:, bzc, 0:n], AF.Silu, r=[PB[bzc]], w=[rm("sqb0")])
            ts("dve", TMP[1][:, 0:n], zrow[:, z0 - 1:z0 - 1 + n], k.convw[:, l, ct, 0:1], ALU.mult,
               r=[rm("zrow")], w=[r_tmp[1]])
            stt(TMP[1][:, 0:n], zrow[:, z0:z0 + n], k.convw[:, l, ct, 1:2], TMP[1][:, 0:n], ALU.mult, ALU.add,
                r=[rm("zrow")], w=[r_tmp[1]])
            stt(TMP[1][:, 0:n], zrow[:, z0 + 1:z0 + 1 + n], k.convw[:, l, ct, 2:3], TMP[1][:, 0:n], ALU.mult, ALU.add,
                r=[rm("zrow")], w=[r_tmp[1]])
            tt("dve", TMP[2][:, 0:n], TMP[1][:, 0:n], PS[:, bbc, 0:n], ALU.mult, r=[r_tmp[1], PB[bbc]], w=[r_tmp[2]])
            tt("dve", k.Y_C[:, ct, c0:c0 + n], TMP[2][:, 0:n], k.sqb[:, 0, 0:n], ALU.mult,
               r=[r_tmp[2], rm("sqb0")], pw=[rm("YC")])


def _phaseE_merge(k, S_, l, nchunks):
    PS, PB, hxT = k.PS, k.PB, k.hxT
    mm, act, tt, rm = k.mm, k.act, k.tt, k.rm
    TMP, r_tmp = k.TMP, k.r_tmp
    Y = [k.Y_A, k.Y_B, k.Y_C]
    rY = [rm("YA"), rm("YB"), rm("YC")]
    for mt in range(8):
        wg, r_wg = k.load_w(k.win_src(l, "G", mt * 384, 384), 384)
        S_.dma("pool", k.wbrb[:], k.d_wbr.ap()[l].rearrange("b (kc p) c -> p b kc c", p=128)[:, :, :, mt * 128:(mt + 1) * 128],
               w=[rm("wbrb")])
        for c in range(nchunks):
            c0, n = CHUNKS[c]
            bp = [k.bank() for _ in range(3)]
            bg = [k.bank() for _ in range(3)]
            for br in range(3):
                for kc in range(4):
                    mm(PS[:, bp[br], 0:n], k.wbrb[:, br, kc, :], Y[br][:, kc, c0:c0 + n], kc == 0, kc == 3,
                       r=[rm("wbrb"), rY[br]], pw=[PB[bp[br]]])
            for br in range(3):
                for kk in range(8):
                    mm(PS[:, bg[br], 0:n], wg[:, kk, br * 128:(br + 1) * 128], hxT[:, kk, c0:c0 + n], kk == 0, kk == 7,
                       r=[r_wg, k.r_hx[c]], pw=[PB[bg[br]]])
            for br in range(3):
                act(TMP[br][:, 0:n], PS[:, bg[br], 0:n], AF.Sigmoid, r=[PB[bg[br]]], w=[r_tmp[br]])
            for br in range(3):
                tt("dve", TMP[br][:, 0:n], TMP[br][:, 0:n], PS[:, bp[br], 0:n], ALU.mult, r=[PB[bp[br]]], w=[r_tmp[br]])
            tt("dve", TMP[0][:, 0:n], TMP[0][:, 0:n], TMP[1][:, 0:n], ALU.add, r=[r_tmp[1]], w=[r_tmp[0]])
            tt("dve", k.mT[:, mt, c0:c0 + n], TMP[0][:, 0:n], TMP[2][:, 0:n], ALU.add, r=[r_tmp[0], r_tmp[2]],
               pw=[rm("mT")])


def _phaseF_out(k, S_, l, ntiles):
    PS, PB = k.PS, k.PB
    mm, act, tt, ts, stt, rm = k.mm, k.act, k.tt, k.ts, k.stt, k.rm
    TMP, r_tmp = k.TMP, k.r_tmp
    last = (l == DEPTH - 1)
    wsrc = k.d_wmod.ap()[l].rearrange("(k p) c -> p k c", p=128)
    scs3 = k.scs[:].rearrange("p (k s) -> p k s", s=2)
    for n_ in range(2):
        wt, rw = k.load_w(wsrc[:, :, 2048 + n_ * 512:2048 + n_ * 512 + 512], 512)
        bg = k.bank()
        for kk in range(8):
            mm(PS[0:2, bg, 0:512], scs3[:, kk, :], wt[:, kk, :], kk == 0, kk == 7, r=[rw, rm("scs")], pw=[PB[bg]])
        S_.dma("sp", TMP[3][0:2, 0:512], k.d_bmodg.ap()[:, l, n_ * 512:(n_ + 1) * 512], w=[r_tmp[3]])
        tt("dve", TMP[2][0:2, 0:512], PS[0:2, bg, 0:512], TMP[3][0:2, 0:512], ALU.add, r=[PB[bg], r_tmp[3]], w=[r_tmp[2]])
        for which, Gt, rG in ((0, k.G_x, rm("Gx")), (1, k.G_c, rm("Gc"))):
            if which == 1 and ntiles <= 16:
                continue
            hs = slice(n_ * 512, (n_ + 1) * 512)
            S_.dma("sp", Gt[:, hs], k.d_gpostb.ap()[l][:, hs], pw=[rG])
            bb = k.bank()
            mm(PS[:, bb, 0:512], k.sel[0:2, which * 128:(which + 1) * 128], TMP[2][0:2, 0:512],
               True, True, r=[r_tmp[2], k.r_const], pw=[PB[bb]])
            tt("dve", Gt[:, hs], Gt[:, hs], PS[:, bb, 0:512], ALU.mult, r=[PB[bb], rG], pw=[rG])
    wo0, r_wo0 = k.load_w(k.d_wo.ap()[l].rearrange("(k p) c -> p k c", p=128)[:, :, 0:512], 512)
    wo1, r_wo1 = k.load_w(k.d_wo.ap()[l].rearrange("(k p) c -> p k c", p=128)[:, :, 512:1024], 512)
    wo = [(wo0, r_wo0), (wo1, r_wo1)]
    sm = k.small
    for i in range(ntiles):
        j = i % 2
        isx = i < 16
        Gt, rG = (k.G_x, rm("Gx")) if isx else (k.G_c, rm("Gc"))
        if isx:
            src = (k.d_x.ap() if l == 0 else k.d_x1.ap())[i * 128:(i + 1) * 128, :]
        else:
            src = k.d_ctx.ap()[(i - 16) * 128:(i - 15) * 128, :]
        S_.dma("sp", k.xold, src, r=([rm("x1d")] if (l > 0 and isx) else []), w=[rm("xold")])
        bo = [k.bank(), k.bank()]
        for hf in range(2):
            for kk in range(8):
                mm(PS[:, bo[hf], 0:512], k.mT[:, kk, i * 128:(i + 1) * 128], wo[hf][0][:, kk, :], kk == 0, kk == 7,
                   r=[wo[hf][1], rm("mT")], pw=[PB[bo[hf]]])
        for hf in range(2):
            act(k.junk[:, 0:512], PS[:, bo[hf], 0:512], AF.Square, r=[PB[bo[hf]]], w=[rm("junk"), rm("ssq2%d" % hf)],
                accum_out=sm[:, 4 + hf:5 + hf])
        tt("dve", sm[:, 6:7], sm[:, 4:5], sm[:, 5:6], ALU.add, r=[rm("ssq20"), rm("ssq21")], w=[rm("ssq2")])
        act(sm[:, 7:8], sm[:, 6:7], AF.Ln, r=[rm("ssq2")], w=[rm("ln2")], scale=1.0 / D, bias=k.epsc[:, 0:1])
        act(sm[:, 8:9], sm[:, 7:8], AF.Exp, r=[rm("ln2")], w=[rm("rstd2")], scale=-0.5)
        for hf in range(2):
            hs = slice(hf * 512, (hf + 1) * 512)
            stt(TMP[hf][:, 0:512], PS[:, bo[hf], 0:512], sm[:, 8:9], Gt[:, hs], ALU.mult, ALU.mult,
                r=[PB[bo[hf]], rm("rstd2"), rG], w=[r_tmp[hf]])
            tt("dve", k.xt[j][:, hs], TMP[hf][:, 0:512], k.xold[:, hs], ALU.add, r=[r_tmp[hf], rm("xold")],
               pw=[k.r_xt[j]])
        if isx:
            dst = (k.d_y.ap() if last else k.d_x1.ap())[i * 128:(i + 1) * 128, :]
            S_.dma("sp", dst, k.xt[j], r=[k.r_xt[j]], pw=[rm("yd") if last else rm("x1d")])
        if not last:
            _stage1_tile(k, S_, l + 1, i, k.xt[j], k.r_xt[j])


def _program(k, S_, upto):
    k.sctr = 0
    k.r_pbuf = [Res("pbuf%d" % i) for i in range(4)]
    if upto <= 0:
        return
    if upto == 1 and DBG_TILES:
        k.ntilesA = DBG_TILES
    _phaseA(k, S_, 0)
    k.dump("hxT0", k.hxT[:], [128, 8 * TT], BF16)
    k.dump("xn", k.xn)
    if upto <= 1:
        return
    for l in range(DEPTH):
        nch = 5 if l == 0 else 4
        S_.barrier()
        _phaseB_kv(k, S_, l)
        _phaseB_qb(k, S_, l, nch)
        _phaseB_halo(k, S_, l)
        if l == 0:
            k.dump("kTA", k.kT_A, [128, 5120], BF16)
            k.dump("VA", k.V_A, [128, 7680], BF16)
            k.dump("qTB", k.qT_B[0:96])
            k.dump("kvnc", k.kvnc, [128, 512], BF16)
            k.dump("zhalo", k.zhalo[:], [128, 8], F32)
        if upto <= 2 and l == 0:
            return
        S_.barrier()
        _phaseC_mla(k, S_, l, with_ctx_q=(l == 0))
        if upto <= 3 and l == 0:
            k.dump("YB", k.Y_B, [128, 4 * TT], BF16)
            return
        S_.barrier()
        _phaseD_A(k, S_, l, nch)
        _phaseD_zb(k, S_, l, nch)
        if upto <= 4 and l == 0:
            k.dump("YB", k.Y_B, [128, 4 * TT], BF16)
            k.dump("YA", k.Y_A, [128, 4 * TT], BF16)
            return
        S_.barrier()
        _phaseD_C(k, S_, l, nch)
        if upto <= 5 and l == 0:
            k.dump("YC", k.Y_C, [128, 4 * TT], BF16)
            return
        S_.barrier()
        _phaseE_merge(k, S_, l, nch)
        if upto <= 6 and l == 0:
            k.dump("mT", k.mT, [128, 8 * TT], BF16)
            return
        S_.barrier()
        _phaseF_out(k, S_, l, 18 if l == 0 else 16)
        if upto <= 7 and l == 0:
            k.dump("hxT1", k.hxT[:], [128, 8 * TT], BF16)
            return


def kernel(**inputs):
    in_maps = prepare_inputs(**inputs)
    nc = build()
    res = run_bass_kernel_spmd(nc, in_maps, core_ids=list(range(NCORE)))
    out = np.zeros((2, S, D), np.float32)
    for core in range(NCORE):
        b, r = divmod(core, 4)
        out[b, r * T:(r + 1) * T] = np.asarray(res.results[core]["y"], np.float32)
    return out
```

```python
import os
import numpy as np
import ml_dtypes
import concourse.bass as bass
import concourse.mybir as mybir
from concourse.bass_utils import run_bass_kernel_spmd

F32 = mybir.dt.float32
BF16 = mybir.dt.bfloat16
AF = mybir.ActivationFunctionType
ALU = mybir.AluOpType
AP = bass.AP

D = 1024
S = 8192
L = 256
DEPTH = 2
NCORE = 8
T = 2048
TT = T + L
GRID_W = 64
EPS = 1e-6
SCALE_A = 64 ** -0.5
SCALE_B = 96 ** -0.5
NEG = -30000.0
DBG_TILES = 0
CHUNKS = [(0, 512), (512, 512), (1024, 512), (1536, 512), (2048, 256)]

O_QA, O_KA, O_VA, O_ZA = 0, 512, 640, 768
O_QL, O_KVL, O_KR, O_ZB = 1280, 1664, 1920, 1952
O_BC, O_CC, O_UC, O_ZC = 2464, 2976, 3488, 4000
O_GA, O_GB, O_GC = 4512, 5536, 6560


def _win_perm():
    cols = []
    off = {}

    def add(name, idx):
        off[name] = len(cols)
        cols.extend(list(idx))

    r64 = np.arange(64)
    rot64 = np.concatenate([r64[32:], r64[:32]])
    r32 = np.arange(32)
    rot32 = np.concatenate([r32[16:], r32[:16]])
    ka = []
    for g in range(2):
        base = O_KA + g * 64
        ka += list(base + r64) + list(base + r64)
        ka += list(base + rot64) + list(base + rot64)
    add("KA", ka)
    add("VKK", list(O_VA + np.arange(128)) + list(O_KVL + np.arange(256))
        + list(O_KR + r32) + list(O_KR + rot32))
    add("ZH", list(O_CC + np.arange(512)) + list(O_UC + np.arange(512)))
    add("QL", list(O_QL + np.arange(384)))
    qa = []
    for j in range(4):
        for e in range(2):
            qa += list(O_QA + (2 * j + e) * 64 + r64)
        for e in range(2):
            qa += list(O_QA + (2 * j + e) * 64 + rot64)
    add("QA", qa)
    add("ZA", list(O_ZA + np.arange(512)))
    add("ZB", list(O_ZB + np.arange(512)))
    cc = []
    for ct in range(4):
        for o in (O_BC, O_CC, O_UC, O_ZC):
            cc += list(o + ct * 128 + np.arange(128))
    add("C", cc)
    gg = []
    for mt in range(8):
        for o in (O_GA, O_GB, O_GC):
            gg += list(o + mt * 128 + np.arange(128))
    add("G", gg)
    return np.asarray(cols, dtype=np.int64), off


WIN_PERM, WOFF = _win_perm()
NCW = len(WIN_PERM)

PC_KVN = 0
PC_N1 = 4096
PC_KR = 0
PC_KAH = 512
PC_VAH = 1024
PC_ZH = 1280
PC_N2 = 1296


class Res:
    __slots__ = ("name", "writers", "readers", "excl", "last")

    def __init__(self, name, excl=False):
        self.name = name
        self.writers = []
        self.readers = []
        self.excl = excl
        self.last = {}


class Op:
    __slots__ = ("eng", "fn", "deps", "kind", "signaled", "sem", "val", "idx")


def _prune(lst):
    out = []
    seen = set()
    for o in reversed(lst):
        if o.kind != "c":
            out.append(o)
        elif o.eng not in seen:
            seen.add(o.eng)
            out.append(o)
    out.reverse()
    return out


class Sched:
    ENG = ("pe", "act", "dve", "pool", "sp")

    def __init__(self):
        self.prog = {e: [] for e in self.ENG}
        self.all = []
        self.pending_dma = []

    def op(self, eng, fn, r=(), w=(), pw=(), kind="c", extra=()):
        o = Op()
        o.eng, o.fn, o.kind, o.signaled, o.sem, o.val = eng, fn, kind, False, None, 0
        o.idx = len(self.all)
        deps = set(x for x in extra if x is not None)
        allres = list(r) + list(w) + list(pw)
        r = [x for x in r if not x.excl]
        w = [x for x in w if not x.excl]
        pw = [x for x in pw if not x.excl]
        for res in allres:
            if res.excl:
                for e2, o2 in res.last.items():
                    if e2 != eng:
                        deps.add(o2)
                res.last[eng] = o
        for res in r:
            deps.update(res.writers)
        for res in w:
            deps.update(res.writers)
            deps.update(res.readers)
        for res in pw:
            deps.update(res.readers)
        for res in r:
            res.readers.append(o)
            if len(res.readers) > 12:
                res.readers = _prune(res.readers)
        for res in w:
            res.writers = [o]
            res.readers = []
        for res in pw:
            if res.readers:
                res.writers = [o]
                res.readers = []
            else:
                res.writers.append(o)
                if len(res.writers) > 12:
                    res.writers = _prune(res.writers)
        deps.discard(o)
        o.deps = deps
        self.prog[eng].append(o)
        self.all.append(o)
        if kind != "c":
            self.pending_dma.append(o)
        return o

    def dma(self, eng, out, in_, r=(), w=(), pw=(), extra=(), **kw):
        return self.op(eng, lambda e: e.dma_start(out=out, in_=in_, **kw), r=r, w=w, pw=pw,
                       kind="d", extra=extra)

    def barrier(self):
        last = [self.prog[e][-1] for e in self.ENG if self.prog[e]]
        last += self.pending_dma
        self.pending_dma = []
        for e in self.ENG:
            self.op(e, None, extra=last)

    def emit(self, nc, stack):
        NS = 20
        for o in self.all:
            for d in o.deps:
                if d.kind == "c" and d.eng == "pe" and o.eng == "pe" and o.kind == "c":
                    continue
                d.signaled = True
        esem = {e: stack.enter_context(nc.semaphore("s_" + e)) for e in self.ENG}
        dsem = {e: [stack.enter_context(nc.semaphore("d_%s%d" % (e, i))) for i in range(NS)]
                for e in ("sp", "pool", "act")}
        ccsem = stack.enter_context(nc.semaphore("s_cc"))
        cnt = {e: 0 for e in self.ENG}
        dcnt = {e: 0 for e in dsem}
        dval = {}
        prev = {}
        ccn = 0
        for e in self.ENG:
            for o in self.prog[e]:
                if o.kind == "c":
                    if o.signaled and o.fn is not None:
                        cnt[e] += 1
                        o.sem, o.val = esem[e], cnt[e]
                    elif o.fn is None:
                        o.sem, o.val = None, 0
                elif o.kind == "d":
                    s = dsem[e][dcnt[e] % NS]
                    dcnt[e] += 1
                    prev[o] = dval.get(id(s), 0)
                    dval[id(s)] = prev[o] + 16
                    o.sem, o.val = s, dval[id(s)]
                else:
                    ccn += 1
                    o.sem, o.val = ccsem, ccn
        self.n_inst = {e: len(self.prog[e]) for e in self.ENG}
        block = stack.enter_context(nc.Block())
        sched = self

        def run(e, engine):
            waited = {}
            for o in sched.prog[e]:
                waits = {}
                for d in o.deps:
                    if d.kind == "c" and d.eng == "pe" and o.eng == "pe" and o.kind == "c":
                        continue
                    if d.sem is None:
                        continue
                    k = id(d.sem)
                    if k not in waits or waits[k][1] < d.val:
                        waits[k] = (d.sem, d.val)
                if o.kind == "d" and prev[o] > 0:
                    k = id(o.sem)
                    if k not in waits or waits[k][1] < prev[o]:
                        waits[k] = (o.sem, prev[o])
                for k, (s, v) in waits.items():
                    if waited.get(k, 0) < v:
                        engine.wait_ge(s, v)
                        waited[k] = v
                if o.fn is None:
                    continue
                inst = o.fn(engine)
                if o.kind == "d":
                    inst.then_inc(o.sem, 16)
                elif o.kind == "cc":
                    inst.then_inc(o.sem)
                elif o.signaled:
                    inst.then_inc(o.sem, 1)

        @block.tensor
        def _(te):
            run("pe", te)

        @block.scalar
        def _(sc):
            run("act", sc)

        @block.vector
        def _(ve):
            run("dve", ve)

        @block.gpsimd
        def _(gp):
            run("pool", gp)

        @block.sync
        def _(sy):
            run("sp", sy)


def _rope_tables(rank):
    g = rank * T + np.arange(T)
    row = (g // GRID_W).astype(np.float64)
    col = (g % GRID_W).astype(np.float64)

    def tab(rot_dim):
        axis_dim = rot_dim // 2
        inv = 10000.0 ** (-np.arange(0, axis_dim, 2, dtype=np.float64) / axis_dim)
        ang = np.concatenate([row[:, None] * inv, col[:, None] * inv], axis=-1)
        half = rot_dim // 2
        d = np.arange(rot_dim)
        c = np.cos(ang)[:, d % half].T
        s = np.sin(ang)[:, d % half].T
        s = np.where((d < half)[:, None], -s, s)
        cfull = np.ones((rot_dim, TT)); sfull = np.zeros((rot_dim, TT))
        cfull[:, :T] = c; sfull[:, :T] = s
        return cfull, sfull

    ca, sa = tab(64)
    cb, sb = tab(32)
    cosA = np.concatenate([ca, ca], 0).astype(np.float32)
    sinA = np.concatenate([sa, sa], 0).astype(np.float32)
    cosB = np.zeros((128, TT), np.float32); sinB = np.zeros((128, TT), np.float32)
    cosB[0:32] = cb; cosB[64:96] = cb
    sinB[0:32] = sb; sinB[64:96] = sb
    return cosA, sinA, cosB, sinB


def _masks(rank):
    kk = np.arange(128)[:, None]
    qq = np.arange(128)[None, :]
    mp = np.where(kk >= qq, 0.0, NEG).astype(np.float32)
    mn = np.where(kk <= qq, 0.0, NEG).astype(np.float32)
    allneg = np.full((128, 128), NEG, np.float32)
    m = np.stack([np.tile(mp, (1, 4)), np.tile(mn, (1, 4)),
                  np.tile(mp if rank > 0 else allneg, (1, 4)),
                  np.tile(mn if rank < 3 else allneg, (1, 4))], axis=1)
    return m.astype(ml_dtypes.bfloat16)


def _fm(v, k):
    return np.ascontiguousarray(np.asarray(v, np.float32).reshape(k, 128).T)


def prepare_inputs(x, c, ctx, c_ctx, w_mod, b_mod, g_pre, g_post, w_in, sink,
                   g_qa, w_qb, g_kva, w_kvb, conv_w, w_branch, w_o):
    f = lambda a: np.asarray(a, np.float32)
    x, c, ctx, c_ctx = f(x), f(c), f(ctx), f(c_ctx)
    w_mod, b_mod, g_pre, g_post = f(w_mod), f(b_mod), f(g_pre), f(g_post)
    w_in, sink, g_qa, w_qb, g_kva, w_kvb = f(w_in), f(sink), f(g_qa), f(w_qb), f(g_kva), f(w_kvb)
    conv_w, w_branch, w_o = f(conv_w), f(w_branch), f(w_o)
    shared = {}
    shared["wmod"] = np.ascontiguousarray(w_mod)
    shared["bmodT"] = np.ascontiguousarray(np.stack([_fm(b_mod[l, :2048], 16) for l in range(DEPTH)], 1))
    shared["bmodg"] = np.ascontiguousarray(np.stack([np.stack([b_mod[l, 2048:], b_mod[l, 2048:]], 0)
                                                     for l in range(DEPTH)], 1))
    shared["gpreT"] = np.ascontiguousarray(np.stack([_fm(g_pre[l], 8) for l in range(DEPTH)], 1))
    shared["gpostb"] = np.ascontiguousarray(np.broadcast_to(g_post[:, None, :], (DEPTH, 128, D)))
    shared["win"] = np.ascontiguousarray(w_in[:, :, WIN_PERM])
    shared["sinkb"] = np.ascontiguousarray(np.broadcast_to(sink[None, :, :], (128, DEPTH, 8)))
    shared["gqaT"] = np.ascontiguousarray(np.stack([_fm(g_qa[l], 3) for l in range(DEPTH)], 1))
    shared["gkvaT"] = np.ascontiguousarray(np.stack([_fm(g_kva[l], 2) for l in range(DEPTH)], 1))
    r32 = np.arange(32)
    qcols = []
    for h in range(8):
        base = h * 96
        qcols += list(base + np.arange(96))
        qcols += list(base + np.arange(64)) + list(base + 64 + np.concatenate([r32[16:], r32[:16]]))
    shared["wqb"] = np.ascontiguousarray(w_qb[:, :, np.asarray(qcols)])
    kcols = [h * 128 + i for h in range(8) for i in range(64)]
    vcols = [h * 128 + 64 + i for h in range(8) for i in range(64)]
    shared["wkvb"] = np.ascontiguousarray(w_kvb[:, :, np.asarray(kcols + vcols)])
    cw = np.zeros((128, DEPTH, 4, 3), np.float32)
    for l in range(DEPTH):
        for k in range(3):
            cw[:, l, :, k] = conv_w[l, k].reshape(4, 128).T
    shared["convw"] = cw
    shared["wbr"] = np.ascontiguousarray(w_branch)
    shared["wo"] = np.ascontiguousarray(w_o)
    shared["ident"] = np.eye(128, dtype=np.float32)
    shared["identb"] = np.eye(128, dtype=np.float32).astype(ml_dtypes.bfloat16)
    shared["onesb"] = np.ones((128, 128), np.float32).astype(ml_dtypes.bfloat16)
    sel = np.zeros((2, 2, 128), np.float32)
    sel[0, 0, :] = 1.0
    sel[1, 1, :] = 1.0
    shared["sel"] = sel
    in_maps = []
    for core in range(NCORE):
        b, r = core // 4, core % 4
        m = dict(shared)
        m["x"] = np.ascontiguousarray(x[b, r * T:(r + 1) * T])
        m["ctx"] = np.ascontiguousarray(ctx[b])
        cs = np.zeros((128, 8, 2), np.float32)
        cs[:, :, 0] = _fm(c[b], 8)
        cs[:, :, 1] = _fm(c_ctx, 8)
        m["cs"] = cs.reshape(128, 16)
        cosA, sinA, cosB, sinB = _rope_tables(r)
        m["cosA"], m["sinA"], m["cosB"], m["sinB"] = cosA, sinA, cosB, sinB
        m["masks"] = _masks(r)
        oh = np.zeros((128, 8), np.float32)
        if r > 0:
            oh[:, r - 1] = 1.0
        if r < 3:
            oh[:, 4 + r + 1] = 1.0
        m["oh"] = oh
        in_maps.append(m)
    return in_maps


class _K:
    pass


def build(upto=99, dbg=()):
    from contextlib import ExitStack
    nc = bass.Bass("TRN2", target_bir_lowering=False)
    S_ = Sched()
    k = _K()
    stack = ExitStack()
    with stack:
        _build_body(nc, S_, k, stack, upto, dbg)
        _program(k, S_, upto)
        S_.barrier()
        for (dd, ap_) in k.dumps:
            S_.dma("sp", dd.ap(), ap_)
        S_.barrier()
        S_.emit(nc, stack)
    k.S = S_
    build.last = k
    return nc


def _din(nc, name, shape, dt=F32):
    return nc.dram_tensor(name, list(shape), dt, kind="ExternalInput")


def _build_body(nc, S_, k, stack, upto, dbg):
    sb = lambda name, shape, dt: stack.enter_context(nc.sbuf_tensor("s_" + name, list(shape), dt))
    d_x = _din(nc, "x", [T, D]); d_ctx = _din(nc, "ctx", [L, D]); d_cs = _din(nc, "cs", [128, 16])
    d_wmod = _din(nc, "wmod", [DEPTH, D, 3 * D]); d_bmodT = _din(nc, "bmodT", [128, DEPTH, 16])
    d_bmodg = _din(nc, "bmodg", [2, DEPTH, D]); d_gpreT = _din(nc, "gpreT", [128, DEPTH, 8])
    d_gpostb = _din(nc, "gpostb", [DEPTH, 128, D]); d_win = _din(nc, "win", [DEPTH, D, NCW])
    d_sinkb = _din(nc, "sinkb", [128, DEPTH, 8]); d_gqaT = _din(nc, "gqaT", [128, DEPTH, 3])
    d_gkvaT = _din(nc, "gkvaT", [128, DEPTH, 2]); d_wqb = _din(nc, "wqb", [DEPTH, 384, 1536])
    d_wkvb = _din(nc, "wkvb", [DEPTH, 256, 1024]); d_convw = _din(nc, "convw", [128, DEPTH, 4, 3])
    d_wbr = _din(nc, "wbr", [DEPTH, 3, 512, D]); d_wo = _din(nc, "wo", [DEPTH, D, D])
    d_tab = [_din(nc, n, [128, TT]) for n in ("cosA", "sinA", "cosB", "sinB")]
    d_masks = _din(nc, "masks", [128, 4, 512], BF16); d_oh = _din(nc, "oh", [128, 8])
    d_ident = _din(nc, "ident", [128, 128]); d_identb = _din(nc, "identb", [128, 128], BF16)
    d_onesb = _din(nc, "onesb", [128, 128], BF16); d_sel = _din(nc, "sel", [2, 2, 128])
    d_y = nc.dram_tensor("y", [T, D], F32, kind="ExternalOutput")
    d_snd1 = nc.dram_tensor("snd1", [128, PC_N1], BF16)
    d_rcv1 = nc.dram_tensor("rcv1", [512, PC_N1], BF16)
    d_snd = nc.dram_tensor("snd2", [128, PC_N2], BF16)
    d_rcv = nc.dram_tensor("rcv2", [512, PC_N2], BF16)
    d_x1 = nc.dram_tensor("x1", [T, D], F32)

    ident = sb("ident", [128, 128], F32); identb = sb("identb", [128, 128], BF16)
    onesb = sb("onesb", [128, 128], BF16); sel = sb("sel", [2, 256], F32)
    masks = sb("masksb", [128, 4, 512], BF16); oh = sb("oh", [128, 8], F32)
    cs = sb("cs", [128, 16], F32); scs = sb("scs", [128, 16], BF16)
    bmodT = sb("bmodT", [128, DEPTH, 16], F32)
    gpreT = sb("gpreT", [128, DEPTH, 8], F32); sinkb = sb("sinkb", [128, DEPTH, 8], F32)
    esink = sb("esink", [128, DEPTH, 8], F32)
    gqaT = sb("gqaT", [128, DEPTH, 3], F32); gkvaT = sb("gkvaT", [128, DEPTH, 2], F32)
    convw = sb("convw", [128, DEPTH, 4, 3], F32)
    modT = sb("modT", [128, 16, 2], F32)
    Amod = sb("Amod", [128, DEPTH, 8, 2], F32); Bmod = sb("Bmod", [128, DEPTH, 8, 2], F32)
    small = sb("small", [128, 16], F32)
    epsc = sb("epsc", [128, 1], F32)
    hxT = sb("hxT", [128, 8, TT], BF16)
    R1 = sb("R1", [128, 9216], BF16)
    R2 = sb("R2", [128, 18432], BF16)
    R3 = sb("R3", [128, 20736], BF16)
    WB = [sb("wb%d" % i, [128, 8, 512], BF16) for i in range(3)]
    wbrb = sb("wbrb", [128, 3, 4, 128], BF16)
    TMP = [sb("tmp%d" % i, [128, 512], F32) for i in range(8)]
    sqb = sb("sqb", [128, 3, 512], BF16); qnb = sb("qnb", [128, 3, 512], BF16)
    junk = sqb[:, 0:2, :].rearrange("p a b -> p (a b)")
    xhl = sb("xhl", [128, 2, 1024], BF16)
    tabs = [sb("tab%d" % i, [128, 512], F32) for i in range(4)]
    kvst = sb("kvst", [128, 2, 512], BF16); krst = sb("krst", [32, 512], BF16)
    zh = sb("zh", [128, 8], F32); zhalo = sb("zhalo", [128, 4, 2], F32)
    zhb = sb("zhb", [128, 16], BF16); hzf = sb("hzf", [128, 4, 8], F32)
    PS = stack.enter_context(nc.psum_tensor("ps", [128, 8, 512], F32))

    def v32(R, off_bf, n_f32):
        return R[:, off_bf:off_bf + 2 * n_f32].bitcast(F32)
    xt = [v32(R1, 0, 1024), v32(R1, 2048, 1024)]
    xn = v32(R1, 4096, 1024)
    xold = v32(R1, 6144, 1024)
    wqb = R1[:, 0:4608].rearrange("p (j c) -> p j c", j=3)
    hal_k = R1[:, 4608:4608 + 2048].rearrange("p (r c) -> p r c", r=4)
    hal_v = R1[:, 6656:6656 + 1024].rearrange("p (r c) -> p r c", r=4)
    hal_z = R1[:, 7680:7680 + 64].rearrange("p (r c) -> p r c", r=4)
    hal_zf = R1[:, 7680:7680 + 64].bitcast(F32).rearrange("p (r c) -> p r c", r=4)
    Y_B = R1[:, 0:9216].rearrange("p (j t) -> p j t", j=4)
    qT_B = R2[:, 0:18432].rearrange("p (h t) -> p h t", h=8)
    Y_A = R2[:, 0:9216].rearrange("p (j t) -> p j t", j=4)
    Y_C = R2[:, 9216:18432].rearrange("p (j t) -> p j t", j=4)
    G_x = v32(R2, 0, 1024); G_c = v32(R2, 2048, 1024)
    o3 = 0
    kT_A = R3[:, o3:o3 + 5120].rearrange("p (g t) -> p g t", g=2); o3 += 5120
    V_A = R3[:, o3:o3 + 7680].rearrange("p (t g c) -> p t g c", t=20, g=2); o3 += 7680
    o3m = o3
    wkvb = R3[:, o3:o3 + 2048].rearrange("p (j c) -> p j c", j=2); o3 += 2048
    kvbuf = []
    for i in range(2):
        kvbuf.append(R3[:, o3:o3 + 1024].rearrange("p (j c) -> p j c", j=2)); o3 += 1024
    Kbuf = []
    for i in range(2):
        Kbuf.append(R3[:, o3:o3 + 512]); o3 += 512
    Vbuf = []
    for i in range(2):
        Vbuf.append(R3[:, o3:o3 + 768].rearrange("p (t c) -> p t c", t=4)); o3 += 768
    Kc = R3[:, o3:o3 + 256]; o3 += 256
    Vc = R3[:, o3:o3 + 384].rearrange("p (t c) -> p t c", t=2); o3 += 384
    kvnc = R3[:, o3:o3 + 512].rearrange("p (j c) -> p j c", j=2); o3 += 512
    assert o3 <= 20736, o3
    qA = R3[:, o3m:o3m + 2048].rearrange("p (j c) -> p j c", j=4)
    sza = R3[:, o3m + 2048:o3m + 4096].rearrange("p (j c) -> p j c", j=4)
    zrow = R3[:, o3m:o3m + 4640].bitcast(F32)
    mT = R3[:, 0:18432].rearrange("p (j t) -> p j t", j=8)
    pbuf = [TMP[4 + i][:, :].bitcast(BF16)[:, 0:512] for i in range(4)]

    R = lambda n: Res(n)
    r_const = R("const")
    PB = [Res("pb%d" % i, excl=True) for i in range(8)]
    r_hx = [R("hx%d" % c) for c in range(5)]
    r_wb = [R("wb%d" % i) for i in range(3)]
    r_tmp = [R("tmp%d" % i) for i in range(8)]
    r_tab = [R("tab%d" % i) for i in range(4)]
    r_misc = {}
    r_xt = [R("xt0"), R("xt1")]

    def rm(name):
        if name not in r_misc:
            r_misc[name] = Res(name)
        return r_misc[name]

    def mm(out, lhsT, rhs, start, stop, r=(), pw=(), w=()):
        return S_.op("pe", lambda e: e.matmul(out, lhsT=lhsT, rhs=rhs, start=start, stop=stop), r=r, pw=pw, w=w)

    def act(out, in_, func, r=(), w=(), pw=(), scale=None, bias=None, accum_out=None):
        kw = {}
        if scale is not None:
            kw["scale"] = scale
        if bias is not None:
            kw["bias"] = bias
        if accum_out is not None:
            kw["accum_out"] = accum_out
        return S_.op("act", lambda e: e.activation(out=out, in_=in_, func=func, **kw), r=r, w=w, pw=pw)

    def tt(eng, out, in0, in1, op, r=(), w=(), pw=()):
        return S_.op(eng, lambda e: e.tensor_tensor(out=out, in0=in0, in1=in1, op=op), r=r, w=w, pw=pw)

    def ts(eng, out, in0, s1, op0, s2=None, op1=None, r=(), w=(), pw=()):
        if op1 is None:
            return S_.op(eng, lambda e: e.tensor_scalar(out=out, in0=in0, scalar1=s1, scalar2=None, op0=op0),
                         r=r, w=w, pw=pw)
        return S_.op(eng, lambda e: e.tensor_scalar(out=out, in0=in0, scalar1=s1, scalar2=s2, op0=op0, op1=op1),
                     r=r, w=w, pw=pw)

    def stt(out, in0, scalar, in1, op0, op1, r=(), w=(), pw=()):
        return S_.op("dve", lambda e: e.scalar_tensor_tensor(out=out, in0=in0, scalar=scalar, in1=in1,
                                                            op0=op0, op1=op1), r=r, w=w, pw=pw)

    def cp(eng, out, in_, r=(), w=(), pw=()):
        if eng == "act":
            return act(out, in_, AF.Copy, r=r, w=w, pw=pw)
        return S_.op(eng, lambda e: e.tensor_copy(out=out, in_=in_), r=r, w=w, pw=pw)

    wb_next = [0]

    def load_w(src_ap, ncols, kparts=8):
        i = wb_next[0] % 3
        wb_next[0] += 1
        dst = WB[i][:, 0:kparts, 0:ncols]
        S_.dma("pool", dst, src_ap, w=[r_wb[i]])
        return WB[i], r_wb[i]

    def win_src(l, name, c0, ncols):
        a = d_win.ap()[l].rearrange("(k p) c -> p k c", p=128)
        o = WOFF[name] + c0
        return a[:, :, o:o + ncols]

    pb_next = [0]

    def bank():
        b = pb_next[0] % 8
        pb_next[0] += 1
        return b

    k.dumps = []

    def dump(name, ap_sbuf, shape=None, dt=None):
        if name in dbg:
            ap_ = ap_sbuf if isinstance(ap_sbuf, AP) else ap_sbuf[:]
            dd = nc.dram_tensor("dbg_" + name, list(ap_.shape), ap_.dtype, kind="ExternalOutput")
            k.dumps.append((dd, ap_))

    for (dst, src) in ((ident[:], d_ident.ap()), (identb[:], d_identb.ap()), (onesb[:], d_onesb.ap()),
                       (sel[:], d_sel.ap().rearrange("k w m -> k (w m)")), (masks[:], d_masks.ap()),
                       (oh[:], d_oh.ap()), (cs[:], d_cs.ap()), (bmodT[:], d_bmodT.ap()),
                       (gpreT[:], d_gpreT.ap()), (sinkb[:], d_sinkb.ap()),
                       (gqaT[:], d_gqaT.ap()), (gkvaT[:], d_gkvaT.ap()), (convw[:], d_convw.ap())):
        S_.dma("sp", dst, src, pw=[r_const])
    S_.op("pool", lambda e: e.memset(epsc[:], EPS), pw=[r_const])
    S_.barrier()
    act(scs[:], cs[:], AF.Silu, w=[rm("scs")])
    act(esink[:], sinkb[:], AF.Exp, w=[rm("esink")])
    scs3 = scs[:].rearrange("p (k s) -> p k s", s=2)
    for l in range(DEPTH):
        wsrc = d_wmod.ap()[l].rearrange("(k p) c -> p k c", p=128)
        bm = bank()
        for j in range(16):
            if j % 4 == 0:
                wt, rw = load_w(wsrc[:, :, (j // 4) * 512:(j // 4) * 512 + 512], 512)
            for kk in range(8):
                mm(PS[:, bm, j * 2:j * 2 + 2], wt[:, kk, (j % 4) * 128:(j % 4) * 128 + 128], scs3[:, kk, :],
                   kk == 0, kk == 7, r=[rw, rm("scs")], pw=[PB[bm]])
        bmb = AP(bmodT[:].tensor, bmodT[:, l, :].offset, [list(bmodT[:].ap[0]), [1, 16], [0, 2]])
        tt("dve", modT[:], PS[:, bm, 0:32].rearrange("p (j s) -> p j s", s=2), bmb, ALU.add,
           r=[PB[bm]], w=[rm("modT")])
        ts("dve", modT[:, 8:16, :], modT[:, 8:16, :], 1.0, ALU.add, r=[], w=[rm("modT")])
        gpb = AP(gpreT[:].tensor, gpreT[:, l, :].offset, [list(gpreT[:].ap[0]), [1, 8], [0, 2]])
        tt("dve", Amod[:, l], modT[:, 8:16, :], gpb, ALU.mult, r=[rm("modT")], pw=[rm("AB")])
        cp("dve", Bmod[:, l], modT[:, 0:8, :], r=[rm("modT")], pw=[rm("AB")])
    dump("Amod", Amod[:], [128, DEPTH * 16], F32)
    dump("Bmod", Bmod[:], [128, DEPTH * 16], F32)
    S_.barrier()
    k.__dict__.update(locals())


def _stage1_tile(k, S_, l, i, xtile, r_xt):
    PS, PB, hxT, ident = k.PS, k.PB, k.hxT, k.ident
    act, ts, mm, rm = k.act, k.ts, k.mm, k.rm
    s = 0 if i < 16 else 1
    c = i // 4 if i < 16 else 4
    col0 = i * 128
    ssq = k.small[:, 0:1]
    lnv = k.small[:, 1:2]
    rstd = k.small[:, 2:3]
    act(k.junk, xtile, AF.Square, r=[r_xt], w=[rm("junk"), rm("ssq")], accum_out=ssq)
    act(lnv, ssq, AF.Ln, r=[rm("ssq")], w=[rm("lnv")], scale=1.0 / D, bias=k.epsc[:, 0:1])
    act(rstd, lnv, AF.Exp, r=[rm("lnv")], w=[rm("rstd")], scale=-0.5)
    ts("dve", k.xn, xtile, rstd, ALU.mult, r=[r_xt, rm("rstd")], w=[rm("xn")])
    import os
    if os.environ.get("BISECT") == "1":
        return
    hi, lo = k.xhl[:, 0, :], k.xhl[:, 1, :]
    k.cp("pool", hi, k.xn, r=[rm("xn")], w=[rm("xhi")])
    k.tt("dve", lo, k.xn, hi, ALU.subtract, r=[rm("xn"), rm("xhi")], w=[rm("xlo")])
    if os.environ.get("BISECT") == "2":
        return
    b0 = k.bank()
    b1 = k.bank()
    for kk in range(8):
        bb = b0 if kk < 4 else b1
        o_ = PS[:, bb, (kk % 4) * 128:(kk % 4) * 128 + 128]
        mm(o_, hi[:, kk * 128:(kk + 1) * 128], k.identb[:], True, False, r=[rm("xhi"), k.r_const], pw=[PB[bb]])
        mm(o_, lo[:, kk * 128:(kk + 1) * 128], k.identb[:], False, True, r=[rm("xlo"), k.r_const], pw=[PB[bb]])
    if os.environ.get("BISECT") == "3":
        return
    for kk in range(8):
        bb = b0 if kk < 4 else b1
        src = PS[:, bb, (kk % 4) * 128:(kk % 4) * 128 + 128]
        dst = hxT[:, kk, col0:col0 + 128]
        if kk < 4:
            act(dst, src, AF.Identity, r=[PB[bb], rm("AB")], pw=[k.r_hx[c]],
                scale=k.Amod[:, l, kk, s:s + 1], bias=k.Bmod[:, l, kk, s:s + 1])
        else:
            ts("dve", dst, src, k.Amod[:, l, kk, s:s + 1], ALU.mult, s2=k.Bmod[:, l, kk, s:s + 1], op1=ALU.add,
               r=[PB[bb], rm("AB")], pw=[k.r_hx[c]])


def _phaseA(k, S_, l):
    for i in range(getattr(k, "ntilesA", 18)):
        src = k.d_x.ap()[i * 128:(i + 1) * 128, :] if i < 16 else k.d_ctx.ap()[(i - 16) * 128:(i - 15) * 128, :]
        j = i % 2
        S_.dma("sp", k.xt[j], src, w=[k.r_xt[j]])
        _stage1_tile(k, S_, l, i, k.xt[j], k.r_xt[j])


def _load_tabs(k, S_, c0, n, which=(0, 1, 2, 3)):
    for ti in which:
        S_.dma("sp", k.tabs[ti][:, 0:n], k.d_tab[ti].ap()[:, c0:c0 + n], w=[k.r_tab[ti]])


def _rstd_bcast(k, S_, banks, nj, n, inv_n, out_tmp, r_out):
    PS, PB = k.PS, k.PB
    for j in range(nj):
        k.act(k.sqb[:, j, 0:n], PS[:, banks[j], 0:n], AF.Square, r=[PB[banks[j]]], pw=[k.rm("sqb")])
    bs = k.bank()
    for j in range(nj):
        k.mm(PS[:, bs, 0:n], k.onesb[:], k.sqb[:, j, 0:n], j == 0, j == nj - 1, r=[k.rm("sqb"), k.r_const],
             pw=[PB[bs]])
    k.act(out_tmp[:, 0:n], PS[:, bs, 0:n], AF.Ln, r=[PB[bs]], w=[r_out], scale=inv_n, bias=k.epsc[:, 0:1])
    k.act(out_tmp[:, 0:n], out_tmp[:, 0:n], AF.Exp, r=[], w=[r_out], scale=-0.5)


def _phaseB_kv(k, S_, l):
    PS, PB, hxT = k.PS, k.PB, k.hxT
    mm, act, tt, ts, stt, cp, rm = k.mm, k.act, k.tt, k.ts, k.stt, k.cp, k.rm
    TMP, r_tmp = k.TMP, k.r_tmp
    S_.op("pool", lambda e: e.memset(k.V_A[:, :, :, 0:64], 1.0), pw=[rm("VA")])
    S_.op("pool", lambda e: e.memset(k.V_A[:, :, :, 128:192], 1.0), pw=[rm("VA")])
    for i in range(2):
        S_.op("pool", (lambda vb: (lambda e: e.memset(vb[:, :, 0:64], 1.0)))(k.Vbuf[i]), pw=[rm("Vbuf%d" % i)])
        S_.op("pool", (lambda vb: (lambda e: e.memset(vb[:, :, 128:192], 1.0)))(k.Vbuf[i]), pw=[rm("Vbuf%d" % i)])
    S_.op("pool", lambda e: e.memset(k.Vc[:, :, 0:64], 1.0), pw=[rm("Vc")])
    S_.op("pool", lambda e: e.memset(k.Vc[:, :, 128:192], 1.0), pw=[rm("Vc")])
    wka, r_wka = k.load_w(k.win_src(l, "KA", 0, 512), 512)
    wvk, r_wvk = k.load_w(k.win_src(l, "VKK", 0, 448), 448)
    for c, (c0, n) in enumerate(CHUNKS):
        _load_tabs(k, S_, c0, n)
        rhs = [hxT[:, kk, c0:c0 + n] for kk in range(8)]
        kdst0 = 128 + c0 if c < 4 else 2304
        for g in range(2):
            bq, br = k.bank(), k.bank()
            for ti, bb in ((2 * g, bq), (2 * g + 1, br)):
                for kk in range(8):
                    mm(PS[:, bb, 0:n], wka[:, kk, ti * 128:(ti + 1) * 128], rhs[kk], kk == 0, kk == 7,
                       r=[r_wka, k.r_hx[c]], pw=[PB[bb]])
            tt("dve", TMP[0][:, 0:n], PS[:, bq, 0:n], k.tabs[0][:, 0:n], ALU.mult, r=[PB[bq], k.r_tab[0]], w=[r_tmp[0]])
            tt("dve", TMP[1][:, 0:n], PS[:, br, 0:n], k.tabs[1][:, 0:n], ALU.mult, r=[PB[br], k.r_tab[1]], w=[r_tmp[1]])
            tt("dve", k.kT_A[:, g, kdst0:kdst0 + n], TMP[0][:, 0:n], TMP[1][:, 0:n], ALU.add,
               r=[r_tmp[0], r_tmp[1]], pw=[rm("kTA")])
        bv = k.bank()
        nt = n // 128
        for t_ in range(nt):
            for kk in range(8):
                mm(PS[:, bv, t_ * 128:(t_ + 1) * 128], hxT[:, kk, c0 + t_ * 128:c0 + (t_ + 1) * 128], wvk[:, kk, 0:128],
                   kk == 0, kk == 7, r=[r_wvk, k.r_hx[c]], pw=[PB[bv]])
        vt0 = 1 + c * 4 if c < 4 else 18
        cp("dve", k.V_A[:, vt0:vt0 + nt, :, 64:128], PS[:, bv, 0:n].rearrange("p (t g d) -> p t g d", t=nt, g=2),
           r=[PB[bv]], pw=[rm("VA")])
        bk = [k.bank(), k.bank()]
        for j in range(2):
            for kk in range(8):
                mm(PS[:, bk[j], 0:n], wvk[:, kk, 128 + j * 128:256 + j * 128], rhs[kk], kk == 0, kk == 7,
                   r=[r_wvk, k.r_hx[c]], pw=[PB[bk[j]]])
        _rstd_bcast(k, S_, bk, 2, n, 1.0 / 256, TMP[2], r_tmp[2])
        for j in range(2):
            dst = k.kvst[:, j, 0:n] if c < 4 else k.kvnc[:, j, 0:n]
            stt(dst, PS[:, bk[j], 0:n], k.gkvaT[:, l, j:j + 1], TMP[2][:, 0:n], ALU.mult, ALU.mult,
                r=[PB[bk[j]], r_tmp[2]], pw=[rm("kvst") if c < 4 else rm("kvnc")])
        if c < 4:
            S_.dma("sp", k.d_snd1.ap()[:, PC_KVN:PC_KVN + 4096].rearrange("p (j t) -> p j t", j=2)[:, :, c0:c0 + n],
                   k.kvst[:, :, 0:n], r=[rm("kvst")], pw=[rm("snd1")])
        b1, b2 = k.bank(), k.bank()
        for (bb, co) in ((b1, 384), (b2, 416)):
            for kk in range(8):
                mm(PS[0:32, bb, 0:n], wvk[:, kk, co:co + 32], rhs[kk], kk == 0, kk == 7,
                   r=[r_wvk, k.r_hx[c]], pw=[PB[bb]])
        tt("dve", TMP[0][0:32, 0:n], PS[0:32, b1, 0:n], k.tabs[2][0:32, 0:n], ALU.mult, r=[PB[b1], k.r_tab[2]], w=[r_tmp[0]])
        tt("dve", TMP[1][0:32, 0:n], PS[0:32, b2, 0:n], k.tabs[3][0:32, 0:n], ALU.mult, r=[PB[b2], k.r_tab[3]], w=[r_tmp[1]])
        tt("dve", k.krst[:, 0:n], TMP[0][0:32, 0:n], TMP[1][0:32, 0:n], ALU.add, r=[r_tmp[0], r_tmp[1]], w=[rm("krst")])
        if c < 4:
            S_.dma("sp", k.d_snd.ap()[c * 32:(c + 1) * 32, PC_KR:PC_KR + 512], k.krst[:, 0:n], r=[rm("krst")], pw=[rm("snd")])
        else:
            S_.dma("sp", k.Kc[64:96, 0:n], k.krst[:, 0:n], r=[rm("krst")], pw=[rm("Kc")])
    bz = k.bank()
    for which in range(2):
        wz, r_wz = k.load_w(k.win_src(l, "ZH", which * 512, 512), 512)
        for ct in range(4):
            col = (which * 4 + ct) * 2
            for kk in range(8):
                mm(PS[:, bz, col:col + 2], wz[:, kk, ct * 128:(ct + 1) * 128], hxT[:, kk, 0:2048:2047],
                   kk == 0, kk == 7, r=[r_wz, k.r_hx[0], k.r_hx[3]], pw=[PB[bz]])
    cp("act", TMP[3][:, 0:8], PS[:, bz, 0:8], r=[PB[bz]], w=[r_tmp[3]])
    tt("dve", k.zh[:], TMP[3][:, 0:8], PS[:, bz, 8:16], ALU.mult, r=[PB[bz], r_tmp[3]], w=[rm("zh")])
    snd = k.d_snd.ap()
    for fl, off in ((0, 128), (1, 2048)):
        S_.dma("sp", snd[:, PC_KAH:PC_KAH + 512].rearrange("p (g f t) -> p g f t", g=2, f=2)[:, :, fl, :],
               k.kT_A[:, :, off:off + 128], r=[rm("kTA")], pw=[rm("snd")])
    for fl, vt in ((0, 1), (1, 16)):
        S_.dma("sp", snd[:, PC_VAH + fl * 128:PC_VAH + (fl + 1) * 128].rearrange("p (g d) -> p g d", g=2),
               k.V_A[:, vt, :, 64:128], r=[rm("VA")], pw=[rm("snd")])
    cp("dve", k.zhb[:, 0:8], k.zh[:], r=[rm("zh")], w=[rm("zhb")])
    tt("dve", k.zhb[:, 8:16], k.zh[:], k.zhb[:, 0:8], ALU.subtract, r=[rm("zh")], w=[rm("zhb")])
    S_.dma("sp", snd[:, PC_ZH:PC_ZH + 16], k.zhb[:], r=[rm("zhb")], pw=[rm("snd")])
    if os.environ.get("NOCC") == "1":
        return
    S_.op("pool", lambda e: e.collective_compute("AllGather", ALU.bypass,
                                                 replica_groups=[[0, 1, 2, 3], [4, 5, 6, 7]],
                                                 ins=[k.d_snd1.ap().opt()], outs=[k.d_rcv1.ap().opt()]),
          r=[rm("snd1")], w=[rm("rcv1")], kind="cc")
    S_.op("pool", lambda e: e.collective_compute("AllGather", ALU.bypass,
                                                 replica_groups=[[0, 1, 2, 3], [4, 5, 6, 7]],
                                                 ins=[k.d_snd.ap().opt()], outs=[k.d_rcv.ap().opt()]),
          r=[rm("snd")], w=[rm("rcv")], kind="cc")


def _phaseB_halo(k, S_, l):
    rm, stt, ts = k.rm, k.stt, k.ts
    rcv = k.d_rcv.ap().rearrange("(r p) c -> p r c", p=128)
    S_.dma("sp", k.hal_k, rcv[:, :, PC_KAH:PC_KAH + 512], r=[rm("rcv")], w=[rm("halk")])
    S_.dma("sp", k.hal_v, rcv[:, :, PC_VAH:PC_VAH + 256], r=[rm("rcv")], w=[rm("halv")])
    S_.dma("sp", k.hal_z, rcv[:, :, PC_ZH:PC_ZH + 16], r=[rm("rcv")], w=[rm("halz")])
    oh = k.oh

    def select(dst, srcs, ohbase, tmp, r_t, rsrc, rdst):
        ts("dve", tmp, srcs[0], oh[:, ohbase:ohbase + 1], ALU.mult, r=[rsrc, k.r_const], w=[r_t])
        for rr in range(1, 4):
            last = rr == 3
            stt(dst if last else tmp, srcs[rr], oh[:, ohbase + rr:ohbase + rr + 1], tmp, ALU.mult, ALU.add,
                r=[rsrc, k.r_const] + ([] if not last else [r_t]), w=([r_t] if not last else []),
                pw=([rdst] if last else []))

    hk = lambda rr, fl: k.hal_k[:, rr, :].rearrange("p (g f t) -> p g f t", g=2, f=2)[:, :, fl, :]
    t3 = lambda i: k.TMP[i][:, 0:256].rearrange("p (g t) -> p g t", g=2)
    select(k.kT_A[:, :, 0:128], [hk(rr, 1) for rr in range(4)], 0, t3(0), k.r_tmp[0], rm("halk"), rm("kTA"))
    select(k.kT_A[:, :, 2176:2304], [hk(rr, 0) for rr in range(4)], 4, t3(1), k.r_tmp[1], rm("halk"), rm("kTA"))
    hv = lambda rr, fl: k.hal_v[:, rr, fl * 128:(fl + 1) * 128].rearrange("p (g d) -> p g d", g=2)
    t4 = lambda i: k.TMP[i][:, 0:128].rearrange("p (g d) -> p g d", g=2)
    select(k.V_A[:, 0, :, 64:128], [hv(rr, 1) for rr in range(4)], 0, t4(2), k.r_tmp[2], rm("halv"), rm("VA"))
    select(k.V_A[:, 17, :, 64:128], [hv(rr, 0) for rr in range(4)], 4, t4(3), k.r_tmp[3], rm("halv"), rm("VA"))
    k.tt("dve", k.hzf[:], k.hal_z[:, :, 0:8], k.hal_z[:, :, 8:16], ALU.add, r=[rm("halz")], w=[rm("hzf")])
    hz = lambda rr, fl: k.hzf[:, rr, :].rearrange("p (c f) -> p c f", f=2)[:, :, fl]
    select(k.zhalo[:, :, 0], [hz(rr, 1) for rr in range(4)], 0, k.TMP[0][:, 256:260], k.r_tmp[0], rm("hzf"), rm("zhalo"))
    select(k.zhalo[:, :, 1], [hz(rr, 0) for rr in range(4)], 4, k.TMP[1][:, 256:260], k.r_tmp[1], rm("hzf"), rm("zhalo"))


def _phaseB_qb(k, S_, l, nchunks):
    PS, PB, hxT = k.PS, k.PB, k.hxT
    mm, act, tt, stt, cp, rm = k.mm, k.act, k.tt, k.stt, k.cp, k.rm
    TMP, r_tmp = k.TMP, k.r_tmp
    S_.dma("pool", k.wqb, k.d_wqb.ap()[l].rearrange("(j p) c -> p j c", p=128), w=[rm("wqb")])
    wql, r_wql = k.load_w(k.win_src(l, "QL", 0, 384), 384)
    for c in range(nchunks):
        c0, n = CHUNKS[c]
        _load_tabs(k, S_, c0, n, which=(2, 3))
        bq = [k.bank(), k.bank(), k.bank()]
        for j in range(3):
            for kk in range(8):
                mm(PS[:, bq[j], 0:n], wql[:, kk, j * 128:(j + 1) * 128], hxT[:, kk, c0:c0 + n], kk == 0, kk == 7,
                   r=[r_wql, k.r_hx[c]], pw=[PB[bq[j]]])
        _rstd_bcast(k, S_, bq, 3, n, 1.0 / 384, TMP[2], r_tmp[2])
        for j in range(3):
            stt(k.qnb[:, j, 0:n], PS[:, bq[j], 0:n], k.gqaT[:, l, j:j + 1], TMP[2][:, 0:n], ALU.mult, ALU.mult,
                r=[PB[bq[j]], r_tmp[2]], pw=[rm("qnb")])
        for h in range(8):
            b1, b2 = k.bank(), k.bank()
            for (bb, co) in ((b1, h * 192), (b2, h * 192 + 96)):
                for j in range(3):
                    mm(PS[0:96, bb, 0:n], k.wqb[:, j, co:co + 96], k.qnb[:, j, 0:n], j == 0, j == 2,
                       r=[rm("wqb"), rm("qnb")], pw=[PB[bb]])
            cp("act", k.qT_B[0:64, h, c0:c0 + n], PS[0:64, b1, 0:n], r=[PB[b1]], pw=[rm("qTB")])
            ta, tb = (0, 1) if h % 2 == 0 else (3, 5)
            tt("dve", TMP[ta][64:96, 0:n], PS[64:96, b1, 0:n], k.tabs[2][64:96, 0:n], ALU.mult,
               r=[PB[b1], k.r_tab[2]], w=[r_tmp[ta]])
            tt("dve", TMP[tb][64:96, 0:n], PS[64:96, b2, 0:n], k.tabs[3][64:96, 0:n], ALU.mult,
               r=[PB[b2], k.r_tab[3]], w=[r_tmp[tb]])
            tt("dve", k.qT_B[64:96, h, c0:c0 + n], TMP[ta][64:96, 0:n], TMP[tb][64:96, 0:n], ALU.add,
               r=[r_tmp[ta], r_tmp[tb]], pw=[rm("qTB")])


def _attn_core(k, S_, items, scale):
    PS, PB = k.PS, k.PB
    LOOK = 2
    pend = []
    for it in items:
        if it.get("before") is not None:
            it["before"]()
        sb0, nb = it["sb"]
        pb_i = k.sctr % 4
        k.sctr += 1
        pws = [PB[sb0 + i] for i in range(nb)]
        first = True
        if it.get("mask") is not None:
            for (o_ap, m_ap) in it["mask"]:
                k.mm(o_ap, k.identb[:], m_ap, True, False, r=[k.r_const], pw=pws)
            first = False
        for (o_ap, lhsT, rhs) in it["s_mms"]:
            k.mm(o_ap, lhsT, rhs, first, True, r=it["r"], pw=pws)
        pv_ = it["p_view"](k.pbuf[pb_i])
        k.act(pv_, it["s_view"], AF.Exp, r=pws, w=[k.r_pbuf[pb_i]], scale=scale)

        def mk(it=it, pb_i=pb_i):
            def f():
                st = it["start"]
                npv = len(it["pv"])
                for pi, (o_ap, lhsT, rhs_fn) in enumerate(it["pv"]):
                    k.mm(o_ap, lhsT, rhs_fn(k.pbuf[pb_i]), st, it["stop"] and pi == npv - 1,
                         r=it["rv"] + [k.r_pbuf[pb_i]], pw=[PB[it["ob"]]])
                    st = False
                if it.get("after_pv") is not None:
                    it["after_pv"]()
            return f
        pend.append(mk())
        if len(pend) > LOOK:
            pend.pop(0)()
        if it.get("after") is not None:
            it["after"]()
    while pend:
        pend.pop(0)()


def _phaseC_mla(k, S_, l, with_ctx_q):
    PS, PB = k.PS, k.PB
    mm, act, tt, cp, rm = k.mm, k.act, k.tt, k.cp, k.rm
    S_.dma("pool", k.wkvb, k.d_wkvb.ap()[l].rearrange("(j p) c -> p j c", p=128), w=[rm("wkvb")])
    rcv = k.d_rcv.ap()
    rcv1 = k.d_rcv1.ap()
    EB = 7
    kchunks = [(rr, cc) for rr in range(4) for cc in range(4)] + [None]
    rec = k.TMP[0]
    for h in range(8):
        e = h % 2
        vsl = slice(64, 192) if e == 0 else slice(0, 128)
        o_lo, r_lo = (0, 64) if e == 0 else (64, 0)

        def load_kv(idx, h=h):
            if idx >= len(kchunks) or kchunks[idx] is None:
                return
            rr, cc = kchunks[idx]
            i = idx % 2
            S_.dma("sp", k.kvbuf[i], rcv1[rr * 128:(rr + 1) * 128, PC_KVN:PC_KVN + 4096].rearrange(
                "p (j t) -> p j t", j=2)[:, :, cc * 512:(cc + 1) * 512], r=[rm("rcv1")], w=[rm("kvbuf%d" % i)])

        def load_kr(idx, h=h):
            if idx >= len(kchunks) or kchunks[idx] is None:
                return
            rr, cc = kchunks[idx]
            i = idx % 2
            S_.dma("sp", k.Kbuf[i][64:96, :], rcv[rr * 128 + cc * 32:rr * 128 + cc * 32 + 32, PC_KR:PC_KR + 512],
                   r=[rm("rcv")], pw=[rm("Kbuf%d" % i)])

        def expand_k(idx, h=h):
            kc = kchunks[idx]
            i = idx % 2
            src, rs, n, dst, rd = ((k.kvbuf[i], rm("kvbuf%d" % i), 512, k.Kbuf[i], rm("Kbuf%d" % i)) if kc is not None
                                   else (k.kvnc, rm("kvnc"), 256, k.Kc, rm("Kc")))
            for j in range(2):
                mm(PS[0:64, EB, 0:n], k.wkvb[:, j, h * 64:(h + 1) * 64], src[:, j, 0:n], j == 0, j == 1,
                   r=[rm("wkvb"), rs], pw=[PB[EB]])
            cp("dve", dst[0:64, 0:n], PS[0:64, EB, 0:n], r=[PB[EB]], pw=[rd])

        def expand_v(idx, h=h):
            kc = kchunks[idx]
            i = idx % 2
            src, rs, nt, dst, rd = ((k.kvbuf[i], rm("kvbuf%d" % i), 4, k.Vbuf[i], rm("Vbuf%d" % i)) if kc is not None
                                    else (k.kvnc, rm("kvnc"), 2, k.Vc, rm("Vc")))
            for t_ in range(nt):
                for j in range(2):
                    mm(PS[:, EB, t_ * 64:(t_ + 1) * 64], src[:, j, t_ * 128:(t_ + 1) * 128],
                       k.wkvb[:, j, 512 + h * 64:512 + (h + 1) * 64], j == 0, j == 1,
                       r=[rm("wkvb"), rs], pw=[PB[EB]])
            cp("dve", dst[:, 0:nt, 64:128], PS[:, EB, 0:nt * 64].rearrange("p (t d) -> p t d", t=nt),
               r=[PB[EB]], pw=[rd])

        load_kv(0)
        load_kv(1)
        load_kr(0)
        expand_k(0)
        expand_v(0)
        items = []
        for idx, kc in enumerate(kchunks):
            i = idx % 2
            if kc is not None:
                Kt, rK, Vt, rV, nt = k.Kbuf[i], rm("Kbuf%d" % i), k.Vbuf[i], rm("Vbuf%d" % i), 4
            else:
                Kt, rK, Vt, rV, nt = k.Kc, rm("Kc"), k.Vc, rm("Vc"), 2
            cnt = 0
            for qc in range(4):
                for kt in range(nt):
                    sbank = 4 + (len(items) % 3)
                    it = dict(sb=(sbank, 1),
                              s_mms=[(PS[:, sbank, 0:512], Kt[0:96, kt * 128:(kt + 1) * 128],
                                      k.qT_B[0:96, h, qc * 512:(qc + 1) * 512])],
                              s_view=PS[:, sbank, 0:512], p_view=(lambda p: p[:, 0:512]),
                              r=[rK, rm("qTB")],
                              pv=[(PS[:, qc, 0:512], Vt[:, kt, vsl], (lambda p: p[:, 0:512]))], rv=[rV],
                              ob=qc, start=(idx == 0 and kt == 0), stop=(idx == len(kchunks) - 1 and kt == nt - 1))
                    cnt += 1
                    if cnt == 1:
                        def bef(idx=idx):
                            load_kv(idx + 2)
                            load_kr(idx + 1)
                        it["before"] = bef
                    if idx + 1 < len(kchunks):
                        if cnt == 2:
                            it["after"] = (lambda idx=idx: expand_k(idx + 1))
                        elif cnt == 6:
                            it["after"] = (lambda idx=idx: expand_v(idx + 1))
                    items.append(it)
        _attn_core(k, S_, items, SCALE_B)
        for qc in range(4):
            act(rec[o_lo:o_lo + 64, 0:512], PS[r_lo:r_lo + 64, qc, 0:512], AF.Ln, r=[PB[qc]], w=[k.r_tmp[0]])
            act(rec[o_lo:o_lo + 64, 0:512], rec[o_lo:o_lo + 64, 0:512], AF.Exp, r=[], w=[k.r_tmp[0]], scale=-1.0)
            tt("dve", k.Y_B[o_lo:o_lo + 64, h // 2, qc * 512:(qc + 1) * 512], PS[o_lo:o_lo + 64, qc, 0:512],
               rec[o_lo:o_lo + 64, 0:512], ALU.mult, r=[PB[qc], k.r_tmp[0]], pw=[rm("YB")])
        if with_ctx_q:
            items = []
            for kt in range(2):
                sbank = 4 + kt
                items.append(dict(sb=(sbank, 1),
                                  s_mms=[(PS[:, sbank, 0:256], k.Kc[0:96, kt * 128:(kt + 1) * 128],
                                          k.qT_B[0:96, h, 2048:2304])],
                                  s_view=PS[:, sbank, 0:256], p_view=(lambda p: p[:, 0:256]),
                                  r=[rm("Kc"), rm("qTB")],
                                  pv=[(PS[:, 6, 0:256], k.Vc[:, kt, vsl], (lambda p: p[:, 0:256]))], rv=[rm("Vc")],
                                  ob=6, start=(kt == 0), stop=(kt == 1)))
            _attn_core(k, S_, items, SCALE_B)
            act(rec[o_lo:o_lo + 64, 0:256], PS[r_lo:r_lo + 64, 6, 0:256], AF.Ln, r=[PB[6]], w=[k.r_tmp[0]])
            act(rec[o_lo:o_lo + 64, 0:256], rec[o_lo:o_lo + 64, 0:256], AF.Exp, r=[], w=[k.r_tmp[0]], scale=-1.0)
            tt("dve", k.Y_B[o_lo:o_lo + 64, h // 2, 2048:2304], PS[o_lo:o_lo + 64, 6, 0:256],
               rec[o_lo:o_lo + 64, 0:256], ALU.mult, r=[PB[6], k.r_tmp[0]], pw=[rm("YB")])


def _phaseD_A(k, S_, l, nchunks):
    PS, PB, hxT = k.PS, k.PB, k.hxT
    mm, act, tt, ts, rm = k.mm, k.act, k.tt, k.ts, k.rm
    TMP, r_tmp = k.TMP, k.r_tmp
    wq = [k.load_w(k.win_src(l, "QA", 0, 512), 512), k.load_w(k.win_src(l, "QA", 512, 512), 512)]
    wza, r_wza = k.load_w(k.win_src(l, "ZA", 0, 512), 512)
    gctr = [0]
    for c in range(nchunks):
        c0, n = CHUNKS[c]
        _load_tabs(k, S_, c0, n, which=(0, 1))
        for j in range(4):
            wt, rw = wq[j // 2]
            base = (j % 2) * 256
            bq, br = k.bank(), k.bank()
            for (bb, co) in ((bq, base), (br, base + 128)):
                for kk in range(8):
                    mm(PS[:, bb, 0:n], wt[:, kk, co:co + 128], hxT[:, kk, c0:c0 + n], kk == 0, kk == 7,
                       r=[rw, k.r_hx[c]], pw=[PB[bb]])
            tt("dve", TMP[0][:, 0:n], PS[:, bq, 0:n], k.tabs[0][:, 0:n], ALU.mult, r=[PB[bq], k.r_tab[0]], w=[r_tmp[0]])
            tt("dve", TMP[1][:, 0:n], PS[:, br, 0:n], k.tabs[1][:, 0:n], ALU.mult, r=[PB[br], k.r_tab[1]], w=[r_tmp[1]])
            tt("dve", k.qA[:, j, 0:n], TMP[0][:, 0:n], TMP[1][:, 0:n], ALU.add, r=[r_tmp[0], r_tmp[1]], pw=[rm("qA")])
        for jt in range(4):
            bb = k.bank()
            for kk in range(8):
                mm(PS[:, bb, 0:n], wza[:, kk, jt * 128:(jt + 1) * 128], hxT[:, kk, c0:c0 + n], kk == 0, kk == 7,
                   r=[r_wza, k.r_hx[c]], pw=[PB[bb]])
            act(k.sza[:, jt, 0:n], PS[:, bb, 0:n], AF.Silu, r=[PB[bb]], pw=[rm("sza")])
        items = []
        groups = []
        for qb in range(n // 128):
            for g in range(2):
                if c < 4:
                    qbg = c * 4 + qb
                    tiles = [(qbg * 128, qbg, 2 if qbg == 0 else 0), ((qbg + 1) * 128, qbg + 1, None),
                             ((qbg + 2) * 128, qbg + 2, 3 if qbg == 15 else 1), (2304, 18, None), (2432, 19, None)]
                else:
                    tiles = [(2304, 18, None), (2432, 19, None)]
                gi = gctr[0]
                gctr[0] += 1
                ob = gi % 4
                sb0 = 4 + 2 * (gi % 2)
                groups.append((qb, g, ob, gi))
                for ti, (koff, vt, mi) in enumerate(tiles):
                    sb0 = 4 + 2 * ((len(items)) % 2)
                    it = dict(sb=(sb0, 2),
                              s_mms=[(PS[:, sb0 + e, 0:256], k.kT_A[e * 64:(e + 1) * 64, g, koff:koff + 128],
                                      k.qA[e * 64:(e + 1) * 64, 2 * g:2 * g + 2, qb * 128:(qb + 1) * 128])
                                     for e in range(2)],
                              mask=(None if mi is None else [(PS[:, sb0 + e, 0:256], k.masks[:, mi, 0:256])
                                                             for e in range(2)]),
                              s_view=PS[:, sb0:sb0 + 2, 0:256],
                              p_view=(lambda p: p[:, 0:512].rearrange("p (e c) -> p e c", e=2)),
                              r=[rm("kTA"), rm("qA")],
                              pv=[(PS[:, ob, 0:256], k.V_A[:, vt, g, 64:192], (lambda p: p[:, 0:256])),
                                  (PS[:, ob, 256:512], k.V_A[:, vt, g, 0:128], (lambda p: p[:, 256:512]))],
                              rv=[rm("VA")], ob=ob, start=(ti == 0), stop=(ti == len(tiles) - 1))
                    items.append(it)

        def normalize(qb, g, ob, gi, c0=c0):
            ta, tb = (TMP[0], TMP[1]) if gi % 2 == 0 else (TMP[2], TMP[3])
            ra, rb = (r_tmp[0], r_tmp[1]) if gi % 2 == 0 else (r_tmp[2], r_tmp[3])
            tok0 = c0 + qb * 128
            for e in range(2):
                o_lo, r_lo = (0, 64) if e == 0 else (64, 0)
                cs_ = slice(e * 256, (e + 1) * 256)
                for jj in range(2):
                    h = 4 * g + 2 * jj + e
                    cj = slice(e * 256 + jj * 128, e * 256 + (jj + 1) * 128)
                    ts("dve", ta[o_lo:o_lo + 64, cj], PS[r_lo:r_lo + 64, ob, cj], k.esink[r_lo:r_lo + 64, l, h:h + 1],
                       ALU.add, r=[PB[ob], rm("esink")], pw=[ra])
                act(ta[o_lo:o_lo + 64, cs_], ta[o_lo:o_lo + 64, cs_], AF.Ln, r=[], w=[ra])
                act(ta[o_lo:o_lo + 64, cs_], ta[o_lo:o_lo + 64, cs_], AF.Exp, r=[], w=[ra], scale=-1.0)
                tt("dve", tb[o_lo:o_lo + 64, cs_], PS[o_lo:o_lo + 64, ob, cs_], ta[o_lo:o_lo + 64, cs_], ALU.mult,
                   r=[PB[ob], ra], pw=[rb])
                tt("dve", k.Y_A[o_lo:o_lo + 64, 2 * g:2 * g + 2, tok0:tok0 + 128],
                   tb[o_lo:o_lo + 64, cs_].rearrange("p (j q) -> p j q", j=2),
                   k.sza[o_lo:o_lo + 64, 2 * g:2 * g + 2, qb * 128:(qb + 1) * 128], ALU.mult,
                   r=[rb, rm("sza")], pw=[rm("YA")])

        per = len(items) // len(groups)
        for gidx in range(len(groups)):
            items[gidx * per + per - 1]["after_pv"] = (lambda a=groups[gidx]: normalize(*a))
        _attn_core(k, S_, items, SCALE_A)


def _phaseD_zb(k, S_, l, nchunks):
    PS, PB, hxT = k.PS, k.PB, k.hxT
    wzb, r_wzb = k.load_w(k.win_src(l, "ZB", 0, 512), 512)
    for c in range(nchunks):
        c0, n = CHUNKS[c]
        for jt in range(4):
            bb = k.bank()
            for kk in range(8):
                k.mm(PS[:, bb, 0:n], wzb[:, kk, jt * 128:(jt + 1) * 128], hxT[:, kk, c0:c0 + n], kk == 0, kk == 7,
                     r=[r_wzb, k.r_hx[c]], pw=[PB[bb]])
            sq = k.sqb[:, jt % 3, 0:n]
            k.act(sq, PS[:, bb, 0:n], AF.Silu, r=[PB[bb]], w=[k.rm("sqb%d" % (jt % 3))])
            k.tt("dve", k.Y_B[:, jt, c0:c0 + n], k.Y_B[:, jt, c0:c0 + n], sq, ALU.mult,
                 r=[k.rm("sqb%d" % (jt % 3))], pw=[k.rm("YB")])


def _phaseD_C(k, S_, l, nchunks):
    PS, PB, hxT = k.PS, k.PB, k.hxT
    mm, act, tt, ts, stt, cp, rm = k.mm, k.act, k.tt, k.ts, k.stt, k.cp, k.rm
    TMP, r_tmp = k.TMP, k.r_tmp
    zrow = k.zrow
    for ct in range(4):
        wc, r_wc = k.load_w(k.win_src(l, "C", ct * 512, 512), 512)
        cp("dve", zrow[:, 0:1], k.zhalo[:, ct, 0:1], r=[rm("zhalo")], pw=[rm("zrow")])
        cp("dve", zrow[:, 2049:2050], k.zhalo[:, ct, 1:2], r=[rm("zhalo")], pw=[rm("zrow")])
        S_.op("dve", lambda e: e.memset(zrow[:, 2050:2051], 0.0), pw=[rm("zrow")])
        S_.op("dve", lambda e: e.memset(zrow[:, 2307:2308], 0.0), pw=[rm("zrow")])
        zo = lambda c: (1 + CHUNKS[c][0]) if c < 4 else 2051
        for c in range(nchunks):
            c0, n = CHUNKS[c]
            bcc, buc = k.bank(), k.bank()
            for (bb, co) in ((bcc, 128), (buc, 256)):
                for kk in range(8):
                    mm(PS[:, bb, 0:n], wc[:, kk, co:co + 128], hxT[:, kk, c0:c0 + n], kk == 0, kk == 7,
                       r=[r_wc, k.r_hx[c]], pw=[PB[bb]])
            cp("act", TMP[0][:, 0:n], PS[:, bcc, 0:n], r=[PB[bcc]], w=[r_tmp[0]])
            tt("dve", zrow[:, zo(c):zo(c) + n], TMP[0][:, 0:n], PS[:, buc, 0:n], ALU.mult,
               r=[r_tmp[0], PB[buc]], pw=[rm("zrow")])
        for c in range(nchunks):
            c0, n = CHUNKS[c]
            z0 = zo(c)
            bbc, bzc = k.bank(), k.bank()
            for (bb, co) in ((bbc, 0), (bzc, 384)):
                for kk in range(8):
                    mm(PS[:, bb, 0:n], wc[:, kk, co:co + 128], hxT[:, kk, c0:c0 + n], kk == 0, kk == 7,
                       r=[r_wc, k.r_hx[c]], pw=[PB[bb]])
            act(k.sqb[:, 0, 0:n], PS[:, bzc, 0:n], AF.Silu, r=[PB[bzc]], w=[rm("sqb0")])
            ts("dve", TMP[1][:, 0:n], zrow[:, z0 - 1:z0 - 1 + n], k.convw[:, l, ct, 0:1], ALU.mult,
               r=[rm("zrow")], w=[r_tmp[1]])
            stt(TMP[1][:, 0:n], zrow[:, z0:z0 + n], k.convw[:, l, ct, 1:2], TMP[1][:, 0:n], ALU.mult, ALU.add,
                r=[rm("zrow")], w=[r_tmp[1]])
            stt(TMP[1][:, 0:n], zrow[:, z0 + 1:z0 + 1 + n], k.convw[:, l, ct, 2:3], TMP[1][:, 0:n], ALU.mult, ALU.add,
                r=[rm("zrow")], w=[r_tmp[1]])
            tt("dve", TMP[2][:, 0:n], TMP[1][:, 0:n], PS[:, bbc, 0:n], ALU.mult, r=[r_tmp[1], PB[bbc]], w=[r_tmp[2]])
            tt("dve", k.Y_C[:, ct, c0:c0 + n], TMP[2][:, 0:n], k.sqb[:, 0, 0:n], ALU.mult,
               r=[r_tmp[2], rm("sqb0")], pw=[rm("YC")])


def _phaseE_merge(k, S_, l, nchunks):
    PS, PB, hxT = k.PS, k.PB, k.hxT
    mm, act, tt, rm = k.mm, k.act, k.tt, k.rm
    TMP, r_tmp = k.TMP, k.r_tmp
    Y = [k.Y_A, k.Y_B, k.Y_C]
    rY = [rm("YA"), rm("YB"), rm("YC")]
    for mt in range(8):
        wg, r_wg = k.load_w(k.win_src(l, "G", mt * 384, 384), 384)
        S_.dma("pool", k.wbrb[:], k.d_wbr.ap()[l].rearrange("b (kc p) c -> p b kc c", p=128)[:, :, :, mt * 128:(mt + 1) * 128],
               w=[rm("wbrb")])
        for c in range(nchunks):
            c0, n = CHUNKS[c]
            bp = [k.bank() for _ in range(3)]
            bg = [k.bank() for _ in range(3)]
            for br in range(3):
                for kc in range(4):
                    mm(PS[:, bp[br], 0:n], k.wbrb[:, br, kc, :], Y[br][:, kc, c0:c0 + n], kc == 0, kc == 3,
                       r=[rm("wbrb"), rY[br]], pw=[PB[bp[br]]])
            for br in range(3):
                for kk in range(8):
                    mm(PS[:, bg[br], 0:n], wg[:, kk, br * 128:(br + 1) * 128], hxT[:, kk, c0:c0 + n], kk == 0, kk == 7,
                       r=[r_wg, k.r_hx[c]], pw=[PB[bg[br]]])
            for br in range(3):
                act(TMP[br][:, 0:n], PS[:, bg[br], 0:n], AF.Sigmoid, r=[PB[bg[br]]], w=[r_tmp[br]])
            for br in range(3):
                tt("dve", TMP[br][:, 0:n], TMP[br][:, 0:n], PS[:, bp[br], 0:n], ALU.mult, r=[PB[bp[br]]], w=[r_tmp[br]])
            tt("dve", TMP[0][:, 0:n], TMP[0][:, 0:n], TMP[1][:, 0:n], ALU.add, r=[r_tmp[1]], w=[r_tmp[0]])
            tt("dve", k.mT[:, mt, c0:c0 + n], TMP[0][:, 0:n], TMP[2][:, 0:n], ALU.add, r=[r_tmp[0], r_tmp[2]],
               pw=[rm("mT")])


def _phaseF_out(k, S_, l, ntiles):
    PS, PB = k.PS, k.PB
    mm, act, tt, ts, stt, rm = k.mm, k.act, k.tt, k.ts, k.stt, k.rm
    TMP, r_tmp = k.TMP, k.r_tmp
    last = (l == DEPTH - 1)
    wsrc = k.d_wmod.ap()[l].rearrange("(k p) c -> p k c", p=128)
    scs3 = k.scs[:].rearrange("p (k s) -> p k s", s=2)
    for n_ in range(2):
        wt, rw = k.load_w(wsrc[:, :, 2048 + n_ * 512:2048 + n_ * 512 + 512], 512)
        bg = k.bank()
        for kk in range(8):
            mm(PS[0:2, bg, 0:512], scs3[:, kk, :], wt[:, kk, :], kk == 0, kk == 7, r=[rw, rm("scs")], pw=[PB[bg]])
        S_.dma("sp", TMP[3][0:2, 0:512], k.d_bmodg.ap()[:, l, n_ * 512:(n_ + 1) * 512], w=[r_tmp[3]])
        tt("dve", TMP[2][0:2, 0:512], PS[0:2, bg, 0:512], TMP[3][0:2, 0:512], ALU.add, r=[PB[bg], r_tmp[3]], w=[r_tmp[2]])
        for which, Gt, rG in ((0, k.G_x, rm("Gx")), (1, k.G_c, rm("Gc"))):
            if which == 1 and ntiles <= 16:
                continue
            hs = slice(n_ * 512, (n_ + 1) * 512)
            S_.dma("sp", Gt[:, hs], k.d_gpostb.ap()[l][:, hs], pw=[rG])
            bb = k.bank()
            mm(PS[:, bb, 0:512], k.sel[0:2, which * 128:(which + 1) * 128], TMP[2][0:2, 0:512],
               True, True, r=[r_tmp[2], k.r_const], pw=[PB[bb]])
            tt("dve", Gt[:, hs], Gt[:, hs], PS[:, bb, 0:512], ALU.mult, r=[PB[bb], rG], pw=[rG])
    wo0, r_wo0 = k.load_w(k.d_wo.ap()[l].rearrange("(k p) c -> p k c", p=128)[:, :, 0:512], 512)
    wo1, r_wo1 = k.load_w(k.d_wo.ap()[l].rearrange("(k p) c -> p k c", p=128)[:, :, 512:1024], 512)
    wo = [(wo0, r_wo0), (wo1, r_wo1)]
    sm = k.small
    for i in range(ntiles):
        j = i % 2
        isx = i < 16
        Gt, rG = (k.G_x, rm("Gx")) if isx else (k.G_c, rm("Gc"))
        if isx:
            src = (k.d_x.ap() if l == 0 else k.d_x1.ap())[i * 128:(i + 1) * 128, :]
        else:
            src = k.d_ctx.ap()[(i - 16) * 128:(i - 15) * 128, :]
        S_.dma("sp", k.xold, src, r=([rm("x1d")] if (l > 0 and isx) else []), w=[rm("xold")])
        bo = [k.bank(), k.bank()]
        for hf in range(2):
            for kk in range(8):
                mm(PS[:, bo[hf], 0:512], k.mT[:, kk, i * 128:(i + 1) * 128], wo[hf][0][:, kk, :], kk == 0, kk == 7,
                   r=[wo[hf][1], rm("mT")], pw=[PB[bo[hf]]])
        for hf in range(2):
            act(k.junk[:, 0:512], PS[:, bo[hf], 0:512], AF.Square, r=[PB[bo[hf]]], w=[rm("junk"), rm("ssq2%d" % hf)],
                accum_out=sm[:, 4 + hf:5 + hf])
        tt("dve", sm[:, 6:7], sm[:, 4:5], sm[:, 5:6], ALU.add, r=[rm("ssq20"), rm("ssq21")], w=[rm("ssq2")])
        act(sm[:, 7:8], sm[:, 6:7], AF.Ln, r=[rm("ssq2")], w=[rm("ln2")], scale=1.0 / D, bias=k.epsc[:, 0:1])
        act(sm[:, 8:9], sm[:, 7:8], AF.Exp, r=[rm("ln2")], w=[rm("rstd2")], scale=-0.5)
        for hf in range(2):
            hs = slice(hf * 512, (hf + 1) * 512)
            stt(TMP[hf][:, 0:512], PS[:, bo[hf], 0:512], sm[:, 8:9], Gt[:, hs], ALU.mult, ALU.mult,
                r=[PB[bo[hf]], rm("rstd2"), rG], w=[r_tmp[hf]])
            tt("dve", k.xt[j][:, hs], TMP[hf][:, 0:512], k.xold[:, hs], ALU.add, r=[r_tmp[hf], rm("xold")],
               pw=[k.r_xt[j]])
        if isx:
            dst = (k.d_y.ap() if last else k.d_x1.ap())[i * 128:(i + 1) * 128, :]
            S_.dma("sp", dst, k.xt[j], r=[k.r_xt[j]], pw=[rm("yd") if last else rm("x1d")])
        if not last:
            _stage1_tile(k, S_, l + 1, i, k.xt[j], k.r_xt[j])


def _program(k, S_, upto):
    k.sctr = 0
    k.r_pbuf = [Res("pbuf%d" % i) for i in range(4)]
    if upto <= 0:
        return
    if upto == 1 and DBG_TILES:
        k.ntilesA = DBG_TILES
    _phaseA(k, S_, 0)
    k.dump("hxT0", k.hxT[:], [128, 8 * TT], BF16)
    k.dump("xn", k.xn)
    if upto <= 1:
        return
    for l in range(DEPTH):
        nch = 5 if l == 0 else 4
        S_.barrier()
        _phaseB_kv(k, S_, l)
        _phaseB_qb(k, S_, l, nch)
        _phaseB_halo(k, S_, l)
        if l == 0:
            k.dump("kTA", k.kT_A, [128, 5120], BF16)
            k.dump("VA", k.V_A, [128, 7680], BF16)
            k.dump("qTB", k.qT_B[0:96])
            k.dump("kvnc", k.kvnc, [128, 512], BF16)
            k.dump("zhalo", k.zhalo[:], [128, 8], F32)
        if upto <= 2 and l == 0:
            return
        S_.barrier()
        _phaseC_mla(k, S_, l, with_ctx_q=(l == 0))
        if upto <= 3 and l == 0:
            k.dump("YB", k.Y_B, [128, 4 * TT], BF16)
            return
        S_.barrier()
        _phaseD_A(k, S_, l, nch)
        _phaseD_zb(k, S_, l, nch)
        if upto <= 4 and l == 0:
            k.dump("YB", k.Y_B, [128, 4 * TT], BF16)
            k.dump("YA", k.Y_A, [128, 4 * TT], BF16)
            return
        S_.barrier()
        _phaseD_C(k, S_, l, nch)
        if upto <= 5 and l == 0:
            k.dump("YC", k.Y_C, [128, 4 * TT], BF16)
            return
        S_.barrier()
        _phaseE_merge(k, S_, l, nch)
        if upto <= 6 and l == 0:
            k.dump("mT", k.mT, [128, 8 * TT], BF16)
            return
        S_.barrier()
        _phaseF_out(k, S_, l, 18 if l == 0 else 16)
        if upto <= 7 and l == 0:
            k.dump("hxT1", k.hxT[:], [128, 8 * TT], BF16)
            return


def kernel(**inputs):
    in_maps = prepare_inputs(**inputs)
    nc = build()
    res = run_bass_kernel_spmd(nc, in_maps, core_ids=list(range(NCORE)))
    out = np.zeros((2, S, D), np.float32)
    for core in range(NCORE):
        b, r = divmod(core, 4)
        out[b, r * T:(r + 1) * T] = np.asarray(res.results[core]["y"], np.float32)
    return out
```

```python
import os
import numpy as np
import ml_dtypes
import concourse.bass as bass
import concourse.mybir as mybir
from concourse.bass_utils import run_bass_kernel_spmd

F32 = mybir.dt.float32
BF16 = mybir.dt.bfloat16
AF = mybir.ActivationFunctionType
ALU = mybir.AluOpType
AP = bass.AP

D = 1024
S = 8192
L = 256
DEPTH = 2
NCORE = 8
T = 2048
TT = T + L
GRID_W = 64
EPS = 1e-6
SCALE_A = 64 ** -0.5
SCALE_B = 96 ** -0.5
NEG = -30000.0
DBG_TILES = 0
CHUNKS = [(0, 512), (512, 512), (1024, 512), (1536, 512), (2048, 256)]

O_QA, O_KA, O_VA, O_ZA = 0, 512, 640, 768
O_QL, O_KVL, O_KR, O_ZB = 1280, 1664, 1920, 1952
O_BC, O_CC, O_UC, O_ZC = 2464, 2976, 3488, 4000
O_GA, O_GB, O_GC = 4512, 5536, 6560


def _win_perm():
    cols = []
    off = {}

    def add(name, idx):
        off[name] = len(cols)
        cols.extend(list(idx))

    r64 = np.arange(64)
    rot64 = np.concatenate([r64[32:], r64[:32]])
    r32 = np.arange(32)
    rot32 = np.concatenate([r32[16:], r32[:16]])
    ka = []
    for g in range(2):
        base = O_KA + g * 64
        ka += list(base + r64) + list(base + r64)
        ka += list(base + rot64) + list(base + rot64)
    add("KA", ka)
    add("VKK", list(O_VA + np.arange(128)) + list(O_KVL + np.arange(256))
        + list(O_KR + r32) + list(O_KR + rot32))
    add("ZH", list(O_CC + np.arange(512)) + list(O_UC + np.arange(512)))
    add("QL", list(O_QL + np.arange(384)))
    qa = []
    for j in range(4):
        for e in range(2):
            qa += list(O_QA + (2 * j + e) * 64 + r64)
        for e in range(2):
            qa += list(O_QA + (2 * j + e) * 64 + rot64)
    add("QA", qa)
    add("ZA", list(O_ZA + np.arange(512)))
    add("ZB", list(O_ZB + np.arange(512)))
    cc = []
    for ct in range(4):
        for o in (O_BC, O_CC, O_UC, O_ZC):
            cc += list(o + ct * 128 + np.arange(128))
    add("C", cc)
    gg = []
    for mt in range(8):
        for o in (O_GA, O_GB, O_GC):
            gg += list(o + mt * 128 + np.arange(128))
    add("G", gg)
    return np.asarray(cols, dtype=np.int64), off


WIN_PERM, WOFF = _win_perm()
NCW = len(WIN_PERM)

PC_KVN = 0
PC_N1 = 4096
PC_KR = 0
PC_KAH = 512
PC_VAH = 1024
PC_ZH = 1280
PC_N2 = 1296


class Res:
    __slots__ = ("name", "writers", "readers", "excl", "last")

    def __init__(self, name, excl=False):
        self.name = name
        self.writers = []
        self.readers = []
        self.excl = excl
        self.last = {}


class Op:
    __slots__ = ("eng", "fn", "deps", "kind", "signaled", "sem", "val", "idx")


def _prune(lst):
    out = []
    seen = set()
    for o in reversed(lst):
        if o.kind != "c":
            out.append(o)
        elif o.eng not in seen:
            seen.add(o.eng)
            out.append(o)
    out.reverse()
    return out


class Sched:
    ENG = ("pe", "act", "dve", "pool", "sp")

    def __init__(self):
        self.prog = {e: [] for e in self.ENG}
        self.all = []
        self.pending_dma = []

    def op(self, eng, fn, r=(), w=(), pw=(), kind="c", extra=()):
        o = Op()
        o.eng, o.fn, o.kind, o.signaled, o.sem, o.val = eng, fn, kind, False, None, 0
        o.idx = len(self.all)
        deps = set(x for x in extra if x is not None)
        allres = list(r) + list(w) + list(pw)
        r = [x for x in r if not x.excl]
        w = [x for x in w if not x.excl]
        pw = [x for x in pw if not x.excl]
        for res in allres:
            if res.excl:
                for e2, o2 in res.last.items():
                    if e2 != eng:
                        deps.add(o2)
                res.last[eng] = o
        for res in r:
            deps.update(res.writers)
        for res in w:
            deps.update(res.writers)
            deps.update(res.readers)
        for res in pw:
            deps.update(res.readers)
        for res in r:
            res.readers.append(o)
            if len(res.readers) > 12:
                res.readers = _prune(res.readers)
        for res in w:
            res.writers = [o]
            res.readers = []
        for res in pw:
            if res.readers:
                res.writers = [o]
                res.readers = []
            else:
                res.writers.append(o)
                if len(res.writers) > 12:
                    res.writers = _prune(res.writers)
        deps.discard(o)
        o.deps = deps
        self.prog[eng].append(o)
        self.all.append(o)
        if kind != "c":
            self.pending_dma.append(o)
        return o

    def dma(self, eng, out, in_, r=(), w=(), pw=(), extra=(), **kw):
        return self.op(eng, lambda e: e.dma_start(out=out, in_=in_, **kw), r=r, w=w, pw=pw,
                       kind="d", extra=extra)

    def barrier(self):
        last = [self.prog[e][-1] for e in self.ENG if self.prog[e]]
        last += self.pending_dma
        self.pending_dma = []
        for e in self.ENG:
            self.op(e, None, extra=last)

    def emit(self, nc, stack):
        NS = 20
        for o in self.all:
            for d in o.deps:
                if d.kind == "c" and d.eng == "pe" and o.eng == "pe" and o.kind == "c":
                    continue
                d.signaled = True
        esem = {e: stack.enter_context(nc.semaphore("s_" + e)) for e in self.ENG}
        dsem = {e: [stack.enter_context(nc.semaphore("d_%s%d" % (e, i))) for i in range(NS)]
                for e in ("sp", "pool", "act")}
        ccsem = stack.enter_context(nc.semaphore("s_cc"))
        cnt = {e: 0 for e in self.ENG}
        dcnt = {e: 0 for e in dsem}
        dval = {}
        prev = {}
        ccn = 0
        for e in self.ENG:
            for o in self.prog[e]:
                if o.kind == "c":
                    if o.signaled and o.fn is not None:
                        cnt[e] += 1
                        o.sem, o.val = esem[e], cnt[e]
                    elif o.fn is None:
                        o.sem, o.val = None, 0
                elif o.kind == "d":
                    s = dsem[e][dcnt[e] % NS]
                    dcnt[e] += 1
                    prev[o] = dval.get(id(s), 0)
                    dval[id(s)] = prev[o] + 16
                    o.sem, o.val = s, dval[id(s)]
                else:
                    ccn += 1
                    o.sem, o.val = ccsem, ccn
        self.n_inst = {e: len(self.prog[e]) for e in self.ENG}
        block = stack.enter_context(nc.Block())
        sched = self

        def run(e, engine):
            waited = {}
            for o in sched.prog[e]:
                waits = {}
                for d in o.deps:
                    if d.kind == "c" and d.eng == "pe" and o.eng == "pe" and o.kind == "c":
                        continue
                    if d.sem is None:
                        continue
                    k = id(d.sem)
                    if k not in waits or waits[k][1] < d.val:
                        waits[k] = (d.sem, d.val)
                if o.kind == "d" and prev[o] > 0:
                    k = id(o.sem)
                    if k not in waits or waits[k][1] < prev[o]:
                        waits[k] = (o.sem, prev[o])
                for k, (s, v) in waits.items():
                    if waited.get(k, 0) < v:
                        engine.wait_ge(s, v)
                        waited[k] = v
                if o.fn is None:
                    continue
                inst = o.fn(engine)
                if o.kind == "d":
                    inst.then_inc(o.sem, 16)
                elif o.kind == "cc":
                    inst.then_inc(o.sem)
                elif o.signaled:
                    inst.then_inc(o.sem, 1)

        @block.tensor
        def _(te):
            run("pe", te)

        @block.scalar
        def _(sc):
            run("act", sc)

        @block.vector
        def _(ve):
            run("dve", ve)

        @block.gpsimd
        def _(gp):
            run("pool", gp)

        @block.sync
        def _(sy):
            run("sp", sy)


def _rope_tables(rank):
    g = rank * T + np.arange(T)
    row = (g // GRID_W).astype(np.float64)
    col = (g % GRID_W).astype(np.float64)

    def tab(rot_dim):
        axis_dim = rot_dim // 2
        inv = 10000.0 ** (-np.arange(0, axis_dim, 2, dtype=np.float64) / axis_dim)
        ang = np.concatenate([row[:, None] * inv, col[:, None] * inv], axis=-1)
        half = rot_dim // 2
        d = np.arange(rot_dim)
        c = np.cos(ang)[:, d % half].T
        s = np.sin(ang)[:, d % half].T
        s = np.where((d < half)[:, None], -s, s)
        cfull = np.ones((rot_dim, TT)); sfull = np.zeros((rot_dim, TT))
        cfull[:, :T] = c; sfull[:, :T] = s
        return cfull, sfull

    ca, sa = tab(64)
    cb, sb = tab(32)
    cosA = np.concatenate([ca, ca], 0).astype(np.float32)
    sinA = np.concatenate([sa, sa], 0).astype(np.float32)
    cosB = np.zeros((128, TT), np.float32); sinB = np.zeros((128, TT), np.float32)
    cosB[0:32] = cb; cosB[64:96] = cb
    sinB[0:32] = sb; sinB[64:96] = sb
    return cosA, sinA, cosB, sinB


def _masks(rank):
    kk = np.arange(128)[:, None]
    qq = np.arange(128)[None, :]
    mp = np.where(kk >= qq, 0.0, NEG).astype(np.float32)
    mn = np.where(kk <= qq, 0.0, NEG).astype(np.float32)
    allneg = np.full((128, 128), NEG, np.float32)
    m = np.stack([np.tile(mp, (1, 4)), np.tile(mn, (1, 4)),
                  np.tile(mp if rank > 0 else allneg, (1, 4)),
                  np.tile(mn if rank < 3 else allneg, (1, 4))], axis=1)
    return m.astype(ml_dtypes.bfloat16)


def _fm(v, k):
    return np.ascontiguousarray(np.asarray(v, np.float32).reshape(k, 128).T)


def prepare_inputs(x, c, ctx, c_ctx, w_mod, b_mod, g_pre, g_post, w_in, sink,
                   g_qa, w_qb, g_kva, w_kvb, conv_w, w_branch, w_o):
    f = lambda a: np.asarray(a, np.float32)
    x, c, ctx, c_ctx = f(x), f(c), f(ctx), f(c_ctx)
    w_mod, b_mod, g_pre, g_post = f(w_mod), f(b_mod), f(g_pre), f(g_post)
    w_in, sink, g_qa, w_qb, g_kva, w_kvb = f(w_in), f(sink), f(g_qa), f(w_qb), f(g_kva), f(w_kvb)
    conv_w, w_branch, w_o = f(conv_w), f(w_branch), f(w_o)
    shared = {}
    shared["wmod"] = np.ascontiguousarray(w_mod)
    shared["bmodT"] = np.ascontiguousarray(np.stack([_fm(b_mod[l, :2048], 16) for l in range(DEPTH)], 1))
    shared["bmodg"] = np.ascontiguousarray(np.stack([np.stack([b_mod[l, 2048:], b_mod[l, 2048:]], 0)
                                                     for l in range(DEPTH)], 1))
    shared["gpreT"] = np.ascontiguousarray(np.stack([_fm(g_pre[l], 8) for l in range(DEPTH)], 1))
    shared["gpostb"] = np.ascontiguousarray(np.broadcast_to(g_post[:, None, :], (DEPTH, 128, D)))
    shared["win"] = np.ascontiguousarray(w_in[:, :, WIN_PERM])
    shared["sinkb"] = np.ascontiguousarray(np.broadcast_to(sink[None, :, :], (128, DEPTH, 8)))
    shared["gqaT"] = np.ascontiguousarray(np.stack([_fm(g_qa[l], 3) for l in range(DEPTH)], 1))
    shared["gkvaT"] = np.ascontiguousarray(np.stack([_fm(g_kva[l], 2) for l in range(DEPTH)], 1))
    r32 = np.arange(32)
    qcols = []
    for h in range(8):
        base = h * 96
        qcols += list(base + np.arange(96))
        qcols += list(base + np.arange(64)) + list(base + 64 + np.concatenate([r32[16:], r32[:16]]))
    shared["wqb"] = np.ascontiguousarray(w_qb[:, :, np.asarray(qcols)])
    kcols = [h * 128 + i for h in range(8) for i in range(64)]
    vcols = [h * 128 + 64 + i for h in range(8) for i in range(64)]
    shared["wkvb"] = np.ascontiguousarray(w_kvb[:, :, np.asarray(kcols + vcols)])
    cw = np.zeros((128, DEPTH, 4, 3), np.float32)
    for l in range(DEPTH):
        for k in range(3):
            cw[:, l, :, k] = conv_w[l, k].reshape(4, 128).T
    shared["convw"] = cw
    shared["wbr"] = np.ascontiguousarray(w_branch)
    shared["wo"] = np.ascontiguousarray(w_o)
    shared["ident"] = np.eye(128, dtype=np.float32)
    shared["identb"] = np.eye(128, dtype=np.float32).astype(ml_dtypes.bfloat16)
    shared["onesb"] = np.ones((128, 128), np.float32).astype(ml_dtypes.bfloat16)
    sel = np.zeros((2, 2, 128), np.float32)
    sel[0, 0, :] = 1.0
    sel[1, 1, :] = 1.0
    shared["sel"] = sel
    in_maps = []
    for core in range(NCORE):
        b, r = core // 4, core % 4
        m = dict(shared)
        m["x"] = np.ascontiguousarray(x[b, r * T:(r + 1) * T])
        m["ctx"] = np.ascontiguousarray(ctx[b])
        cs = np.zeros((128, 8, 2), np.float32)
        cs[:, :, 0] = _fm(c[b], 8)
        cs[:, :, 1] = _fm(c_ctx, 8)
        m["cs"] = cs.reshape(128, 16)
        cosA, sinA, cosB, sinB = _rope_tables(r)
        m["cosA"], m["sinA"], m["cosB"], m["sinB"] = cosA, sinA, cosB, sinB
        m["masks"] = _masks(r)
        oh = np.zeros((128, 8), np.float32)
        if r > 0:
            oh[:, r - 1] = 1.0
        if r < 3:
            oh[:, 4 + r + 1] = 1.0
        m["oh"] = oh
        in_maps.append(m)
    return in_maps


class _K:
    pass


def build(upto=99, dbg=()):
    from contextlib import ExitStack
    nc = bass.Bass("TRN2", target_bir_lowering=False)
    S_ = Sched()
    k = _K()
    stack = ExitStack()
    with stack:
        _build_body(nc, S_, k, stack, upto, dbg)
        _program(k, S_, upto)
        S_.barrier()
        for (dd, ap_) in k.dumps:
            S_.dma("sp", dd.ap(), ap_)
        S_.barrier()
        S_.emit(nc, stack)
    k.S = S_
    build.last = k
    return nc


def _din(nc, name, shape, dt=F32):
    return nc.dram_tensor(name, list(shape), dt, kind="ExternalInput")


def _build_body(nc, S_, k, stack, upto, dbg):
    sb = lambda name, shape, dt: stack.enter_context(nc.sbuf_tensor("s_" + name, list(shape), dt))
    d_x = _din(nc, "x", [T, D]); d_ctx = _din(nc, "ctx", [L, D]); d_cs = _din(nc, "cs", [128, 16])
    d_wmod = _din(nc, "wmod", [DEPTH, D, 3 * D]); d_bmodT = _din(nc, "bmodT", [128, DEPTH, 16])
    d_bmodg = _din(nc, "bmodg", [2, DEPTH, D]); d_gpreT = _din(nc, "gpreT", [128, DEPTH, 8])
    d_gpostb = _din(nc, "gpostb", [DEPTH, 128, D]); d_win = _din(nc, "win", [DEPTH, D, NCW])
    d_sinkb = _din(nc, "sinkb", [128, DEPTH, 8]); d_gqaT = _din(nc, "gqaT", [128, DEPTH, 3])
    d_gkvaT = _din(nc, "gkvaT", [128, DEPTH, 2]); d_wqb = _din(nc, "wqb", [DEPTH, 384, 1536])
    d_wkvb = _din(nc, "wkvb", [DEPTH, 256, 1024]); d_convw = _din(nc, "convw", [128, DEPTH, 4, 3])
    d_wbr = _din(nc, "wbr", [DEPTH, 3, 512, D]); d_wo = _din(nc, "wo", [DEPTH, D, D])
    d_tab = [_din(nc, n, [128, TT]) for n in ("cosA", "sinA", "cosB", "sinB")]
    d_masks = _din(nc, "masks", [128, 4, 512], BF16); d_oh = _din(nc, "oh", [128, 8])
    d_ident = _din(nc, "ident", [128, 128]); d_identb = _din(nc, "identb", [128, 128], BF16)
    d_onesb = _din(nc, "onesb", [128, 128], BF16); d_sel = _din(nc, "sel", [2, 2, 128])
    d_y = nc.dram_tensor("y", [T, D], F32, kind="ExternalOutput")
    d_snd1 = nc.dram_tensor("snd1", [128, PC_N1], BF16)
    d_rcv1 = nc.dram_tensor("rcv1", [512, PC_N1], BF16)
    d_snd = nc.dram_tensor("snd2", [128, PC_N2], BF16)
    d_rcv = nc.dram_tensor("rcv2", [512, PC_N2], BF16)
    d_x1 = nc.dram_tensor("x1", [T, D], F32)

    ident = sb("ident", [128, 128], F32); identb = sb("identb", [128, 128], BF16)
    onesb = sb("onesb", [128, 128], BF16); sel = sb("sel", [2, 256], F32)
    masks = sb("masksb", [128, 4, 512], BF16); oh = sb("oh", [128, 8], F32)
    cs = sb("cs", [128, 16], F32); scs = sb("scs", [128, 16], BF16)
    bmodT = sb("bmodT", [128, DEPTH, 16], F32)
    gpreT = sb("gpreT", [128, DEPTH, 8], F32); sinkb = sb("sinkb", [128, DEPTH, 8], F32)
    esink = sb("esink", [128, DEPTH, 8], F32)
    gqaT = sb("gqaT", [128, DEPTH, 3], F32); gkvaT = sb("gkvaT", [128, DEPTH, 2], F32)
    convw = sb("convw", [128, DEPTH, 4, 3], F32)
    modT = sb("modT", [128, 16, 2], F32)
    Amod = sb("Amod", [128, DEPTH, 8, 2], F32); Bmod = sb("Bmod", [128, DEPTH, 8, 2], F32)
    small = sb("small", [128, 32], F32)
    epsc = sb("epsc", [128, 1], F32)
    hxT = sb("hxT", [128, 8, TT], BF16)
    R1 = sb("R1", [128, 9216], BF16)
    R2 = sb("R2", [128, 18432], BF16)
    R3 = sb("R3", [128, 20736], BF16)
    WB = [sb("wb%d" % i, [128, 8, 512], BF16) for i in range(3)]
    wbrb2 = [sb("wbrb%d" % i, [128, 3, 4, 128], BF16) for i in range(2)]
    TMP = [sb("tmp%d" % i, [128, 512], F32) for i in range(8)]
    sqb = sb("sqb", [128, 3, 512], BF16); qnb = sb("qnb", [128, 3, 512], BF16)
    junk = sqb[:, 0:2, :].rearrange("p a b -> p (a b)")
    tabs = [sb("tab%d" % i, [128, 512], F32) for i in range(4)]
    kvst = sb("kvst", [128, 2, 512], BF16); krst = sb("krst", [32, 512], BF16)
    zh = sb("zh", [128, 8], F32); zhalo = sb("zhalo", [128, 4, 2], F32)
    zhb = sb("zhb", [128, 16], BF16); hzf = sb("hzf", [128, 4, 8], F32)
    PS = stack.enter_context(nc.psum_tensor("ps", [128, 8, 512], F32))

    def v32(R, off_bf, n_f32):
        return R[:, off_bf:off_bf + 2 * n_f32].bitcast(F32)
    xt = [v32(R1, 0, 1024), v32(R1, 2048, 1024)]
    xnb = [v32(R2, 4096, 1024), v32(R2, 6144, 1024)]
    xoldb = [v32(R2, 8192, 1024), v32(R2, 10240, 1024)]
    xhlb = [R2[:, 12288:14336].rearrange("p (a b) -> p a b", a=2), R2[:, 14336:16384].rearrange("p (a b) -> p a b", a=2)]
    junk2 = [junk, qnb[:, 0:2, :].rearrange("p a b -> p (a b)")]
    wqb = R1[:, 0:4608].rearrange("p (j c) -> p j c", j=3)
    hal_k = R1[:, 4608:4608 + 2048].rearrange("p (r c) -> p r c", r=4)
    hal_v = R1[:, 6656:6656 + 1024].rearrange("p (r c) -> p r c", r=4)
    hal_z = R1[:, 7680:7680 + 64].rearrange("p (r c) -> p r c", r=4)
    hal_zf = R1[:, 7680:7680 + 64].bitcast(F32).rearrange("p (r c) -> p r c", r=4)
    Y_B = R1[:, 0:9216].rearrange("p (j t) -> p j t", j=4)
    qT_B = R2[:, 0:18432].rearrange("p (h t) -> p h t", h=8)
    Y_A = R2[:, 0:9216].rearrange("p (j t) -> p j t", j=4)
    Y_C = R2[:, 9216:18432].rearrange("p (j t) -> p j t", j=4)
    G_x = v32(R2, 0, 1024); G_c = v32(R2, 2048, 1024)
    o3 = 0
    kT_A = R3[:, o3:o3 + 5120].rearrange("p (g t) -> p g t", g=2); o3 += 5120
    V_A = R3[:, o3:o3 + 7680].rearrange("p (t g c) -> p t g c", t=20, g=2); o3 += 7680
    o3m = o3
    wkvb = R3[:, o3:o3 + 2048].rearrange("p (j c) -> p j c", j=2); o3 += 2048
    kvbuf = []
    for i in range(2):
        kvbuf.append(R3[:, o3:o3 + 1024].rearrange("p (j c) -> p j c", j=2)); o3 += 1024
    Kbuf = []
    for i in range(2):
        Kbuf.append(R3[:, o3:o3 + 512]); o3 += 512
    Vbuf = []
    for i in range(2):
        Vbuf.append(R3[:, o3:o3 + 768].rearrange("p (t c) -> p t c", t=4)); o3 += 768
    Kc = R3[:, o3:o3 + 256]; o3 += 256
    Vc = R3[:, o3:o3 + 384].rearrange("p (t c) -> p t c", t=2); o3 += 384
    kvnc = R3[:, o3:o3 + 512].rearrange("p (j c) -> p j c", j=2); o3 += 512
    assert o3 <= 20736, o3
    qA = R3[:, o3m:o3m + 2048].rearrange("p (j c) -> p j c", j=4)
    sza = R3[:, o3m + 2048:o3m + 4096].rearrange("p (j c) -> p j c", j=4)
    zrow = R3[:, o3m:o3m + 4640].bitcast(F32)
    mT = R3[:, 0:18432].rearrange("p (j t) -> p j t", j=8)
    pbuf = [TMP[4 + i][:, :].bitcast(BF16)[:, 0:512] for i in range(4)]

    R = lambda n: Res(n)
    r_const = R("const")
    PB = [Res("pb%d" % i, excl=True) for i in range(8)]
    r_hx = [R("hx%d" % c) for c in range(5)]
    r_wb = [R("wb%d" % i) for i in range(3)]
    r_tmp = [R("tmp%d" % i) for i in range(8)]
    r_tab = [R("tab%d" % i) for i in range(4)]
    r_misc = {}
    r_xt = [R("xt0"), R("xt1")]

    def rm(name):
        if name not in r_misc:
            r_misc[name] = Res(name)
        return r_misc[name]

    def mm(out, lhsT, rhs, start, stop, r=(), pw=(), w=()):
        return S_.op("pe", lambda e: e.matmul(out, lhsT=lhsT, rhs=rhs, start=start, stop=stop), r=r, pw=pw, w=w)

    def act(out, in_, func, r=(), w=(), pw=(), scale=None, bias=None, accum_out=None):
        kw = {}
        if scale is not None:
            kw["scale"] = scale
        if bias is not None:
            kw["bias"] = bias
        if accum_out is not None:
            kw["accum_out"] = accum_out
        return S_.op("act", lambda e: e.activation(out=out, in_=in_, func=func, **kw), r=r, w=w, pw=pw)

    def tt(eng, out, in0, in1, op, r=(), w=(), pw=()):
        return S_.op(eng, lambda e: e.tensor_tensor(out=out, in0=in0, in1=in1, op=op), r=r, w=w, pw=pw)

    def ts(eng, out, in0, s1, op0, s2=None, op1=None, r=(), w=(), pw=()):
        if op1 is None:
            return S_.op(eng, lambda e: e.tensor_scalar(out=out, in0=in0, scalar1=s1, scalar2=None, op0=op0),
                         r=r, w=w, pw=pw)
        return S_.op(eng, lambda e: e.tensor_scalar(out=out, in0=in0, scalar1=s1, scalar2=s2, op0=op0, op1=op1),
                     r=r, w=w, pw=pw)

    def stt(out, in0, scalar, in1, op0, op1, r=(), w=(), pw=()):
        return S_.op("dve", lambda e: e.scalar_tensor_tensor(out=out, in0=in0, scalar=scalar, in1=in1,
                                                            op0=op0, op1=op1), r=r, w=w, pw=pw)

    def cp(eng, out, in_, r=(), w=(), pw=()):
        if eng == "act":
            return act(out, in_, AF.Copy, r=r, w=w, pw=pw)
        return S_.op(eng, lambda e: e.tensor_copy(out=out, in_=in_), r=r, w=w, pw=pw)

    wb_next = [0]

    def load_w(src_ap, ncols, kparts=8):
        i = wb_next[0] % 3
        wb_next[0] += 1
        dst = WB[i][:, 0:kparts, 0:ncols]
        S_.dma("pool", dst, src_ap, w=[r_wb[i]])
        return WB[i], r_wb[i]

    def win_src(l, name, c0, ncols):
        a = d_win.ap()[l].rearrange("(k p) c -> p k c", p=128)
        o = WOFF[name] + c0
        return a[:, :, o:o + ncols]

    pb_next = [0]

    def bank():
        b = pb_next[0] % 8
        pb_next[0] += 1
        return b

    k.dumps = []

    def dump(name, ap_sbuf, shape=None, dt=None):
        if name in dbg:
            ap_ = ap_sbuf if isinstance(ap_sbuf, AP) else ap_sbuf[:]
            dd = nc.dram_tensor("dbg_" + name, list(ap_.shape), ap_.dtype, kind="ExternalOutput")
            k.dumps.append((dd, ap_))

    for (dst, src) in ((ident[:], d_ident.ap()), (identb[:], d_identb.ap()), (onesb[:], d_onesb.ap()),
                       (sel[:], d_sel.ap().rearrange("k w m -> k (w m)")), (masks[:], d_masks.ap()),
                       (oh[:], d_oh.ap()), (cs[:], d_cs.ap()), (bmodT[:], d_bmodT.ap()),
                       (gpreT[:], d_gpreT.ap()), (sinkb[:], d_sinkb.ap()),
                       (gqaT[:], d_gqaT.ap()), (gkvaT[:], d_gkvaT.ap()), (convw[:], d_convw.ap())):
        S_.dma("sp", dst, src, pw=[r_const])
    S_.op("pool", lambda e: e.memset(epsc[:], EPS), pw=[r_const])
    S_.barrier()
    act(scs[:], cs[:], AF.Silu, w=[rm("scs")])
    act(esink[:], sinkb[:], AF.Exp, w=[rm("esink")])
    scs3 = scs[:].rearrange("p (k s) -> p k s", s=2)
    for l in range(DEPTH):
        wsrc = d_wmod.ap()[l].rearrange("(k p) c -> p k c", p=128)
        bm = bank()
        for j in range(16):
            if j % 4 == 0:
                wt, rw = load_w(wsrc[:, :, (j // 4) * 512:(j // 4) * 512 + 512], 512)
            for kk in range(8):
                mm(PS[:, bm, j * 2:j * 2 + 2], wt[:, kk, (j % 4) * 128:(j % 4) * 128 + 128], scs3[:, kk, :],
                   kk == 0, kk == 7, r=[rw, rm("scs")], pw=[PB[bm]])
        bmb = AP(bmodT[:].tensor, bmodT[:, l, :].offset, [list(bmodT[:].ap[0]), [1, 16], [0, 2]])
        tt("dve", modT[:], PS[:, bm, 0:32].rearrange("p (j s) -> p j s", s=2), bmb, ALU.add,
           r=[PB[bm]], w=[rm("modT")])
        ts("dve", modT[:, 8:16, :], modT[:, 8:16, :], 1.0, ALU.add, r=[], w=[rm("modT")])
        gpb = AP(gpreT[:].tensor, gpreT[:, l, :].offset, [list(gpreT[:].ap[0]), [1, 8], [0, 2]])
        tt("dve", Amod[:, l], modT[:, 8:16, :], gpb, ALU.mult, r=[rm("modT")], pw=[rm("AB")])
        cp("dve", Bmod[:, l], modT[:, 0:8, :], r=[rm("modT")], pw=[rm("AB")])
    dump("Amod", Amod[:], [128, DEPTH * 16], F32)
    dump("Bmod", Bmod[:], [128, DEPTH * 16], F32)
    S_.barrier()
    k.__dict__.update(locals())


def _stage1_tile(k, S_, l, i, xtile, r_xt):
    PS, PB, hxT, ident = k.PS, k.PB, k.hxT, k.ident
    act, ts, mm, rm = k.act, k.ts, k.mm, k.rm
    s = 0 if i < 16 else 1
    c = i // 4 if i < 16 else 4
    col0 = i * 128
    p = i % 2
    sm0 = 16 * p
    ssq = k.small[:, sm0 + 0:sm0 + 1]
    lnv = k.small[:, sm0 + 1:sm0 + 2]
    rstd = k.small[:, sm0 + 2:sm0 + 3]
    xn = k.xnb[p]
    rs = lambda n_: rm("%s%d" % (n_, p))
    act(k.junk2[p], xtile, AF.Square, r=[r_xt], w=[rs("junk"), rs("ssq")], accum_out=ssq)
    act(lnv, ssq, AF.Ln, r=[rs("ssq")], w=[rs("lnv")], scale=1.0 / D, bias=k.epsc[:, 0:1])
    act(rstd, lnv, AF.Exp, r=[rs("lnv")], w=[rs("rstd")], scale=-0.5)
    ts("dve", xn, xtile, rstd, ALU.mult, r=[r_xt, rs("rstd")], w=[rs("xn")])
    hi, lo = k.xhlb[p][:, 0, :], k.xhlb[p][:, 1, :]
    k.cp("pool", hi, xn, r=[rs("xn")], w=[rs("xhi")])
    k.tt("dve", lo, xn, hi, ALU.subtract, r=[rs("xn"), rs("xhi")], w=[rs("xlo")])
    b0 = k.bank()
    b1 = k.bank()
    for kk in range(8):
        bb = b0 if kk < 4 else b1
        o_ = PS[:, bb, (kk % 4) * 128:(kk % 4) * 128 + 128]
        mm(o_, hi[:, kk * 128:(kk + 1) * 128], k.identb[:], True, False, r=[rs("xhi"), k.r_const], pw=[PB[bb]])
        mm(o_, lo[:, kk * 128:(kk + 1) * 128], k.identb[:], False, True, r=[rs("xlo"), k.r_const], pw=[PB[bb]])
    for kk in range(8):
        bb = b0 if kk < 4 else b1
        src = PS[:, bb, (kk % 4) * 128:(kk % 4) * 128 + 128]
        dst = hxT[:, kk, col0:col0 + 128]
        if kk < 4:
            act(dst, src, AF.Identity, r=[PB[bb], rm("AB")], pw=[k.r_hx[c]],
                scale=k.Amod[:, l, kk, s:s + 1], bias=k.Bmod[:, l, kk, s:s + 1])
        else:
            ts("dve", dst, src, k.Amod[:, l, kk, s:s + 1], ALU.mult, s2=k.Bmod[:, l, kk, s:s + 1], op1=ALU.add,
               r=[PB[bb], rm("AB")], pw=[k.r_hx[c]])


def _phaseA(k, S_, l):
    for i in range(getattr(k, "ntilesA", 18)):
        src = k.d_x.ap()[i * 128:(i + 1) * 128, :] if i < 16 else k.d_ctx.ap()[(i - 16) * 128:(i - 15) * 128, :]
        j = i % 2
        S_.dma("sp", k.xt[j], src, w=[k.r_xt[j]])
        _stage1_tile(k, S_, l, i, k.xt[j], k.r_xt[j])


def _load_tabs(k, S_, c0, n, which=(0, 1, 2, 3)):
    for ti in which:
        S_.dma("sp", k.tabs[ti][:, 0:n], k.d_tab[ti].ap()[:, c0:c0 + n], w=[k.r_tab[ti]])


def _rstd_bcast(k, S_, banks, nj, n, inv_n, out_tmp, r_out):
    PS, PB = k.PS, k.PB
    for j in range(nj):
        k.act(k.sqb[:, j, 0:n], PS[:, banks[j], 0:n], AF.Square, r=[PB[banks[j]]], pw=[k.rm("sqb")])
    bs = k.bank()
    for j in range(nj):
        k.mm(PS[:, bs, 0:n], k.onesb[:], k.sqb[:, j, 0:n], j == 0, j == nj - 1, r=[k.rm("sqb"), k.r_const],
             pw=[PB[bs]])
    k.act(out_tmp[:, 0:n], PS[:, bs, 0:n], AF.Ln, r=[PB[bs]], w=[r_out], scale=inv_n, bias=k.epsc[:, 0:1])
    k.act(out_tmp[:, 0:n], out_tmp[:, 0:n], AF.Exp, r=[], w=[r_out], scale=-0.5)


def _phaseB_kv(k, S_, l):
    PS, PB, hxT = k.PS, k.PB, k.hxT
    mm, act, tt, ts, stt, cp, rm = k.mm, k.act, k.tt, k.ts, k.stt, k.cp, k.rm
    TMP, r_tmp = k.TMP, k.r_tmp
    S_.op("pool", lambda e: e.memset(k.V_A[:, :, :, 0:64], 1.0), pw=[rm("VA")])
    S_.op("pool", lambda e: e.memset(k.V_A[:, :, :, 128:192], 1.0), pw=[rm("VA")])
    for i in range(2):
        S_.op("pool", (lambda vb: (lambda e: e.memset(vb[:, :, 0:64], 1.0)))(k.Vbuf[i]), pw=[rm("Vbuf%d" % i)])
        S_.op("pool", (lambda vb: (lambda e: e.memset(vb[:, :, 128:192], 1.0)))(k.Vbuf[i]), pw=[rm("Vbuf%d" % i)])
    S_.op("pool", lambda e: e.memset(k.Vc[:, :, 0:64], 1.0), pw=[rm("Vc")])
    S_.op("pool", lambda e: e.memset(k.Vc[:, :, 128:192], 1.0), pw=[rm("Vc")])
    wka, r_wka = k.load_w(k.win_src(l, "KA", 0, 512), 512)
    wvk, r_wvk = k.load_w(k.win_src(l, "VKK", 0, 448), 448)
    for c, (c0, n) in enumerate(CHUNKS):
        _load_tabs(k, S_, c0, n)
        rhs = [hxT[:, kk, c0:c0 + n] for kk in range(8)]
        kdst0 = 128 + c0 if c < 4 else 2304
        for g in range(2):
            bq, br = k.bank(), k.bank()
            for ti, bb in ((2 * g, bq), (2 * g + 1, br)):
                for kk in range(8):
                    mm(PS[:, bb, 0:n], wka[:, kk, ti * 128:(ti + 1) * 128], rhs[kk], kk == 0, kk == 7,
                       r=[r_wka, k.r_hx[c]], pw=[PB[bb]])
            tt("dve", TMP[0][:, 0:n], PS[:, bq, 0:n], k.tabs[0][:, 0:n], ALU.mult, r=[PB[bq], k.r_tab[0]], w=[r_tmp[0]])
            tt("dve", TMP[1][:, 0:n], PS[:, br, 0:n], k.tabs[1][:, 0:n], ALU.mult, r=[PB[br], k.r_tab[1]], w=[r_tmp[1]])
            tt("dve", k.kT_A[:, g, kdst0:kdst0 + n], TMP[0][:, 0:n], TMP[1][:, 0:n], ALU.add,
               r=[r_tmp[0], r_tmp[1]], pw=[rm("kTA")])
        bv = k.bank()
        nt = n // 128
        for t_ in range(nt):
            for kk in range(8):
                mm(PS[:, bv, t_ * 128:(t_ + 1) * 128], hxT[:, kk, c0 + t_ * 128:c0 + (t_ + 1) * 128], wvk[:, kk, 0:128],
                   kk == 0, kk == 7, r=[r_wvk, k.r_hx[c]], pw=[PB[bv]])
        vt0 = 1 + c * 4 if c < 4 else 18
        cp("dve", k.V_A[:, vt0:vt0 + nt, :, 64:128], PS[:, bv, 0:n].rearrange("p (t g d) -> p t g d", t=nt, g=2),
           r=[PB[bv]], pw=[rm("VA")])
        bk = [k.bank(), k.bank()]
        for j in range(2):
            for kk in range(8):
                mm(PS[:, bk[j], 0:n], wvk[:, kk, 128 + j * 128:256 + j * 128], rhs[kk], kk == 0, kk == 7,
                   r=[r_wvk, k.r_hx[c]], pw=[PB[bk[j]]])
        _rstd_bcast(k, S_, bk, 2, n, 1.0 / 256, TMP[2], r_tmp[2])
        for j in range(2):
            dst = k.kvst[:, j, 0:n] if c < 4 else k.kvnc[:, j, 0:n]
            stt(dst, PS[:, bk[j], 0:n], k.gkvaT[:, l, j:j + 1], TMP[2][:, 0:n], ALU.mult, ALU.mult,
                r=[PB[bk[j]], r_tmp[2]], pw=[rm("kvst") if c < 4 else rm("kvnc")])
        if c < 4:
            S_.dma("sp", k.d_snd1.ap()[:, PC_KVN:PC_KVN + 4096].rearrange("p (j t) -> p j t", j=2)[:, :, c0:c0 + n],
                   k.kvst[:, :, 0:n], r=[rm("kvst")], pw=[rm("snd1")])
        b1, b2 = k.bank(), k.bank()
        for (bb, co) in ((b1, 384), (b2, 416)):
            for kk in range(8):
                mm(PS[0:32, bb, 0:n], wvk[:, kk, co:co + 32], rhs[kk], kk == 0, kk == 7,
                   r=[r_wvk, k.r_hx[c]], pw=[PB[bb]])
        tt("dve", TMP[0][0:32, 0:n], PS[0:32, b1, 0:n], k.tabs[2][0:32, 0:n], ALU.mult, r=[PB[b1], k.r_tab[2]], w=[r_tmp[0]])
        tt("dve", TMP[1][0:32, 0:n], PS[0:32, b2, 0:n], k.tabs[3][0:32, 0:n], ALU.mult, r=[PB[b2], k.r_tab[3]], w=[r_tmp[1]])
        tt("dve", k.krst[:, 0:n], TMP[0][0:32, 0:n], TMP[1][0:32, 0:n], ALU.add, r=[r_tmp[0], r_tmp[1]], w=[rm("krst")])
        if c < 4:
            S_.dma("sp", k.d_snd.ap()[c * 32:(c + 1) * 32, PC_KR:PC_KR + 512], k.krst[:, 0:n], r=[rm("krst")], pw=[rm("snd")])
        else:
            S_.dma("sp", k.Kc[64:96, 0:n], k.krst[:, 0:n], r=[rm("krst")], pw=[rm("Kc")])
    bz = k.bank()
    for which in range(2):
        wz, r_wz = k.load_w(k.win_src(l, "ZH", which * 512, 512), 512)
        for ct in range(4):
            col = (which * 4 + ct) * 2
            for kk in range(8):
                mm(PS[:, bz, col:col + 2], wz[:, kk, ct * 128:(ct + 1) * 128], hxT[:, kk, 0:2048:2047],
                   kk == 0, kk == 7, r=[r_wz, k.r_hx[0], k.r_hx[3]], pw=[PB[bz]])
    cp("act", TMP[3][:, 0:8], PS[:, bz, 0:8], r=[PB[bz]], w=[r_tmp[3]])
    tt("dve", k.zh[:], TMP[3][:, 0:8], PS[:, bz, 8:16], ALU.mult, r=[PB[bz], r_tmp[3]], w=[rm("zh")])
    snd = k.d_snd.ap()
    for fl, off in ((0, 128), (1, 2048)):
        S_.dma("sp", snd[:, PC_KAH:PC_KAH + 512].rearrange("p (g f t) -> p g f t", g=2, f=2)[:, :, fl, :],
               k.kT_A[:, :, off:off + 128], r=[rm("kTA")], pw=[rm("snd")])
    for fl, vt in ((0, 1), (1, 16)):
        S_.dma("sp", snd[:, PC_VAH + fl * 128:PC_VAH + (fl + 1) * 128].rearrange("p (g d) -> p g d", g=2),
               k.V_A[:, vt, :, 64:128], r=[rm("VA")], pw=[rm("snd")])
    cp("dve", k.zhb[:, 0:8], k.zh[:], r=[rm("zh")], w=[rm("zhb")])
    tt("dve", k.zhb[:, 8:16], k.zh[:], k.zhb[:, 0:8], ALU.subtract, r=[rm("zh")], w=[rm("zhb")])
    S_.dma("sp", snd[:, PC_ZH:PC_ZH + 16], k.zhb[:], r=[rm("zhb")], pw=[rm("snd")])
    if os.environ.get("NOCC") == "1":
        return
    S_.op("pool", lambda e: e.collective_compute("AllGather", ALU.bypass,
                                                 replica_groups=[[0, 1, 2, 3], [4, 5, 6, 7]],
                                                 ins=[k.d_snd1.ap().opt()], outs=[k.d_rcv1.ap().opt()]),
          r=[rm("snd1")], w=[rm("rcv1")], kind="cc")
    S_.op("pool", lambda e: e.collective_compute("AllGather", ALU.bypass,
                                                 replica_groups=[[0, 1, 2, 3], [4, 5, 6, 7]],
                                                 ins=[k.d_snd.ap().opt()], outs=[k.d_rcv.ap().opt()]),
          r=[rm("snd")], w=[rm("rcv")], kind="cc")


def _phaseB_halo(k, S_, l):
    rm, stt, ts = k.rm, k.stt, k.ts
    rcv = k.d_rcv.ap().rearrange("(r p) c -> p r c", p=128)
    S_.dma("sp", k.hal_k, rcv[:, :, PC_KAH:PC_KAH + 512], r=[rm("rcv")], w=[rm("halk")])
    S_.dma("sp", k.hal_v, rcv[:, :, PC_VAH:PC_VAH + 256], r=[rm("rcv")], w=[rm("halv")])
    S_.dma("sp", k.hal_z, rcv[:, :, PC_ZH:PC_ZH + 16], r=[rm("rcv")], w=[rm("halz")])
    oh = k.oh

    def select(dst, srcs, ohbase, tmp, r_t, rsrc, rdst):
        ts("dve", tmp, srcs[0], oh[:, ohbase:ohbase + 1], ALU.mult, r=[rsrc, k.r_const], w=[r_t])
        for rr in range(1, 4):
            last = rr == 3
            stt(dst if last else tmp, srcs[rr], oh[:, ohbase + rr:ohbase + rr + 1], tmp, ALU.mult, ALU.add,
                r=[rsrc, k.r_const] + ([] if not last else [r_t]), w=([r_t] if not last else []),
                pw=([rdst] if last else []))

    hk = lambda rr, fl: k.hal_k[:, rr, :].rearrange("p (g f t) -> p g f t", g=2, f=2)[:, :, fl, :]
    t3 = lambda i: k.TMP[i][:, 0:256].rearrange("p (g t) -> p g t", g=2)
    select(k.kT_A[:, :, 0:128], [hk(rr, 1) for rr in range(4)], 0, t3(0), k.r_tmp[0], rm("halk"), rm("kTA"))
    select(k.kT_A[:, :, 2176:2304], [hk(rr, 0) for rr in range(4)], 4, t3(1), k.r_tmp[1], rm("halk"), rm("kTA"))
    hv = lambda rr, fl: k.hal_v[:, rr, fl * 128:(fl + 1) * 128].rearrange("p (g d) -> p g d", g=2)
    t4 = lambda i: k.TMP[i][:, 0:128].rearrange("p (g d) -> p g d", g=2)
    select(k.V_A[:, 0, :, 64:128], [hv(rr, 1) for rr in range(4)], 0, t4(2), k.r_tmp[2], rm("halv"), rm("VA"))
    select(k.V_A[:, 17, :, 64:128], [hv(rr, 0) for rr in range(4)], 4, t4(3), k.r_tmp[3], rm("halv"), rm("VA"))
    k.tt("dve", k.hzf[:], k.hal_z[:, :, 0:8], k.hal_z[:, :, 8:16], ALU.add, r=[rm("halz")], w=[rm("hzf")])
    hz = lambda rr, fl: k.hzf[:, rr, :].rearrange("p (c f) -> p c f", f=2)[:, :, fl]
    select(k.zhalo[:, :, 0], [hz(rr, 1) for rr in range(4)], 0, k.TMP[0][:, 256:260], k.r_tmp[0], rm("hzf"), rm("zhalo"))
    select(k.zhalo[:, :, 1], [hz(rr, 0) for rr in range(4)], 4, k.TMP[1][:, 256:260], k.r_tmp[1], rm("hzf"), rm("zhalo"))


def _phaseB_qb(k, S_, l, nchunks):
    PS, PB, hxT = k.PS, k.PB, k.hxT
    mm, act, tt, stt, cp, rm = k.mm, k.act, k.tt, k.stt, k.cp, k.rm
    TMP, r_tmp = k.TMP, k.r_tmp
    S_.dma("pool", k.wqb, k.d_wqb.ap()[l].rearrange("(j p) c -> p j c", p=128), w=[rm("wqb")])
    wql, r_wql = k.load_w(k.win_src(l, "QL", 0, 384), 384)
    for c in range(nchunks):
        c0, n = CHUNKS[c]
        _load_tabs(k, S_, c0, n, which=(2, 3))
        bq = [k.bank(), k.bank(), k.bank()]
        for j in range(3):
            for kk in range(8):
                mm(PS[:, bq[j], 0:n], wql[:, kk, j * 128:(j + 1) * 128], hxT[:, kk, c0:c0 + n], kk == 0, kk == 7,
                   r=[r_wql, k.r_hx[c]], pw=[PB[bq[j]]])
        _rstd_bcast(k, S_, bq, 3, n, 1.0 / 384, TMP[2], r_tmp[2])
        for j in range(3):
            stt(k.qnb[:, j, 0:n], PS[:, bq[j], 0:n], k.gqaT[:, l, j:j + 1], TMP[2][:, 0:n], ALU.mult, ALU.mult,
                r=[PB[bq[j]], r_tmp[2]], pw=[rm("qnb")])
        for h in range(8):
            b1, b2 = k.bank(), k.bank()
            for (bb, co) in ((b1, h * 192), (b2, h * 192 + 96)):
                for j in range(3):
                    mm(PS[0:96, bb, 0:n], k.wqb[:, j, co:co + 96], k.qnb[:, j, 0:n], j == 0, j == 2,
                       r=[rm("wqb"), rm("qnb")], pw=[PB[bb]])
            cp("act", k.qT_B[0:64, h, c0:c0 + n], PS[0:64, b1, 0:n], r=[PB[b1]], pw=[rm("qTB")])
            ta, tb = (0, 1) if h % 2 == 0 else (3, 5)
            tt("dve", TMP[ta][64:96, 0:n], PS[64:96, b1, 0:n], k.tabs[2][64:96, 0:n], ALU.mult,
               r=[PB[b1], k.r_tab[2]], w=[r_tmp[ta]])
            tt("dve", TMP[tb][64:96, 0:n], PS[64:96, b2, 0:n], k.tabs[3][64:96, 0:n], ALU.mult,
               r=[PB[b2], k.r_tab[3]], w=[r_tmp[tb]])
            tt("dve", k.qT_B[64:96, h, c0:c0 + n], TMP[ta][64:96, 0:n], TMP[tb][64:96, 0:n], ALU.add,
               r=[r_tmp[ta], r_tmp[tb]], pw=[rm("qTB")])


def _attn_core(k, S_, items, scale):
    PS, PB = k.PS, k.PB
    LOOK = 2
    pend = []
    for it in items:
        if it.get("before") is not None:
            it["before"]()
        sb0, nb = it["sb"]
        pb_i = k.sctr % 4
        k.sctr += 1
        pws = [PB[sb0 + i] for i in range(nb)]
        first = True
        if it.get("mask") is not None:
            for (o_ap, m_ap) in it["mask"]:
                k.mm(o_ap, k.identb[:], m_ap, True, False, r=[k.r_const], pw=pws)
            first = False
        for (o_ap, lhsT, rhs) in it["s_mms"]:
            k.mm(o_ap, lhsT, rhs, first, True, r=it["r"], pw=pws)
        pv_ = it["p_view"](k.pbuf[pb_i])
        k.act(pv_, it["s_view"], AF.Exp, r=pws, w=[k.r_pbuf[pb_i]], scale=scale)

        def mk(it=it, pb_i=pb_i):
            def f():
                st = it["start"]
                npv = len(it["pv"])
                for pi, (o_ap, lhsT, rhs_fn) in enumerate(it["pv"]):
                    k.mm(o_ap, lhsT, rhs_fn(k.pbuf[pb_i]), st, it["stop"] and pi == npv - 1,
                         r=it["rv"] + [k.r_pbuf[pb_i]], pw=[PB[it["ob"]]])
                    st = False
                if it.get("after_pv") is not None:
                    it["after_pv"]()
            return f
        pend.append(mk())
        if len(pend) > LOOK:
            pend.pop(0)()
        if it.get("after") is not None:
            it["after"]()
    while pend:
        pend.pop(0)()


def _phaseC_mla(k, S_, l, with_ctx_q):
    PS, PB = k.PS, k.PB
    mm, act, tt, cp, rm = k.mm, k.act, k.tt, k.cp, k.rm
    S_.dma("pool", k.wkvb, k.d_wkvb.ap()[l].rearrange("(j p) c -> p j c", p=128), w=[rm("wkvb")])
    rcv = k.d_rcv.ap()
    rcv1 = k.d_rcv1.ap()
    EB = 7
    kchunks = [(rr, cc) for rr in range(4) for cc in range(4)] + [None]
    rec = k.TMP[0]
    def mk_head(h):
        def load_kv(idx):
            if idx >= len(kchunks) or kchunks[idx] is None:
                return
            rr, cc = kchunks[idx]
            i = idx % 2
            S_.dma("sp", k.kvbuf[i], rcv1[rr * 128:(rr + 1) * 128, PC_KVN:PC_KVN + 4096].rearrange(
                "p (j t) -> p j t", j=2)[:, :, cc * 512:(cc + 1) * 512], r=[rm("rcv1")], w=[rm("kvbuf%d" % i)])

        def load_kr(idx):
            if idx >= len(kchunks) or kchunks[idx] is None:
                return
            rr, cc = kchunks[idx]
            i = idx % 2
            S_.dma("sp", k.Kbuf[i][64:96, :], rcv[rr * 128 + cc * 32:rr * 128 + cc * 32 + 32, PC_KR:PC_KR + 512],
                   r=[rm("rcv")], pw=[rm("Kbuf%d" % i)])

        def expand_k(idx):
            kc = kchunks[idx]
            i = idx % 2
            src, rs, n, dst, rd = ((k.kvbuf[i], rm("kvbuf%d" % i), 512, k.Kbuf[i], rm("Kbuf%d" % i)) if kc is not None
                                   else (k.kvnc, rm("kvnc"), 256, k.Kc, rm("Kc")))
            for j in range(2):
                mm(PS[0:64, EB, 0:n], k.wkvb[:, j, h * 64:(h + 1) * 64], src[:, j, 0:n], j == 0, j == 1,
                   r=[rm("wkvb"), rs], pw=[PB[EB]])
            cp("dve", dst[0:64, 0:n], PS[0:64, EB, 0:n], r=[PB[EB]], pw=[rd])

        def expand_v(idx):
            kc = kchunks[idx]
            i = idx % 2
            src, rs, nt, dst, rd = ((k.kvbuf[i], rm("kvbuf%d" % i), 4, k.Vbuf[i], rm("Vbuf%d" % i)) if kc is not None
                                    else (k.kvnc, rm("kvnc"), 2, k.Vc, rm("Vc")))
            for t_ in range(nt):
                for j in range(2):
                    mm(PS[:, EB, t_ * 64:(t_ + 1) * 64], src[:, j, t_ * 128:(t_ + 1) * 128],
                       k.wkvb[:, j, 512 + h * 64:512 + (h + 1) * 64], j == 0, j == 1,
                       r=[rm("wkvb"), rs], pw=[PB[EB]])
            cp("dve", dst[:, 0:nt, 64:128], PS[:, EB, 0:nt * 64].rearrange("p (t d) -> p t d", t=nt),
               r=[PB[EB]], pw=[rd])
        return load_kv, load_kr, expand_k, expand_v

    heads = [mk_head(h) for h in range(8)]
    for h in range(8):
        e = h % 2
        vsl = slice(64, 192) if e == 0 else slice(0, 128)
        o_lo, r_lo = (0, 64) if e == 0 else (64, 0)
        load_kv, load_kr, expand_k, expand_v = heads[h]
        nxt = heads[h + 1] if h + 1 < 8 else None
        if h == 0:
            load_kv(0)
            load_kv(1)
            load_kr(0)
            expand_k(0)
            expand_v(0)
        items = []
        for idx, kc in enumerate(kchunks):
            i = idx % 2
            if kc is not None:
                Kt, rK, Vt, rV, nt = k.Kbuf[i], rm("Kbuf%d" % i), k.Vbuf[i], rm("Vbuf%d" % i), 4
            else:
                Kt, rK, Vt, rV, nt = k.Kc, rm("Kc"), k.Vc, rm("Vc"), 2
            cnt = 0
            for qc in range(4):
                for kt in range(nt):
                    sbank = 4 + (len(items) % 3)
                    it = dict(sb=(sbank, 1),
                              s_mms=[(PS[:, sbank, 0:512], Kt[0:96, kt * 128:(kt + 1) * 128],
                                      k.qT_B[0:96, h, qc * 512:(qc + 1) * 512])],
                              s_view=PS[:, sbank, 0:512], p_view=(lambda p: p[:, 0:512]),
                              r=[rK, rm("qTB")],
                              pv=[(PS[:, qc, 0:512], Vt[:, kt, vsl], (lambda p: p[:, 0:512]))], rv=[rV],
                              ob=qc, start=(idx == 0 and kt == 0), stop=(idx == len(kchunks) - 1 and kt == nt - 1))
                    cnt += 1
                    if cnt == 1:
                        def bef(idx=idx):
                            load_kv(idx + 2)
                            load_kr(idx + 1)
                            if idx == len(kchunks) - 1 and nxt is not None:
                                nxt[0](0)
                                nxt[0](1)
                                nxt[1](0)
                        it["before"] = bef
                    if idx + 1 < len(kchunks):
                        if cnt == 2:
                            it["after"] = (lambda idx=idx: expand_k(idx + 1))
                        elif cnt == 6:
                            it["after"] = (lambda idx=idx: expand_v(idx + 1))
                    elif nxt is not None:
                        if cnt == 2:
                            it["after"] = (lambda: nxt[2](0))
                        elif cnt == 6:
                            it["after"] = (lambda: nxt[3](0))
                    items.append(it)
        _attn_core(k, S_, items, SCALE_B)
        for qc in range(4):
            act(rec[o_lo:o_lo + 64, 0:512], PS[r_lo:r_lo + 64, qc, 0:512], AF.Ln, r=[PB[qc]], w=[k.r_tmp[0]])
            act(rec[o_lo:o_lo + 64, 0:512], rec[o_lo:o_lo + 64, 0:512], AF.Exp, r=[], w=[k.r_tmp[0]], scale=-1.0)
            tt("dve", k.Y_B[o_lo:o_lo + 64, h // 2, qc * 512:(qc + 1) * 512], PS[o_lo:o_lo + 64, qc, 0:512],
               rec[o_lo:o_lo + 64, 0:512], ALU.mult, r=[PB[qc], k.r_tmp[0]], pw=[rm("YB")])
        if with_ctx_q:
            items = []
            for kt in range(2):
                sbank = 4 + kt
                items.append(dict(sb=(sbank, 1),
                                  s_mms=[(PS[:, sbank, 0:256], k.Kc[0:96, kt * 128:(kt + 1) * 128],
                                          k.qT_B[0:96, h, 2048:2304])],
                                  s_view=PS[:, sbank, 0:256], p_view=(lambda p: p[:, 0:256]),
                                  r=[rm("Kc"), rm("qTB")],
                                  pv=[(PS[:, 6, 0:256], k.Vc[:, kt, vsl], (lambda p: p[:, 0:256]))], rv=[rm("Vc")],
                                  ob=6, start=(kt == 0), stop=(kt == 1)))
            _attn_core(k, S_, items, SCALE_B)
            act(rec[o_lo:o_lo + 64, 0:256], PS[r_lo:r_lo + 64, 6, 0:256], AF.Ln, r=[PB[6]], w=[k.r_tmp[0]])
            act(rec[o_lo:o_lo + 64, 0:256], rec[o_lo:o_lo + 64, 0:256], AF.Exp, r=[], w=[k.r_tmp[0]], scale=-1.0)
            tt("dve", k.Y_B[o_lo:o_lo + 64, h // 2, 2048:2304], PS[o_lo:o_lo + 64, 6, 0:256],
               rec[o_lo:o_lo + 64, 0:256], ALU.mult, r=[PB[6], k.r_tmp[0]], pw=[rm("YB")])


def _phaseD_A(k, S_, l, nchunks):
    PS, PB, hxT = k.PS, k.PB, k.hxT
    mm, act, tt, ts, rm = k.mm, k.act, k.tt, k.ts, k.rm
    TMP, r_tmp = k.TMP, k.r_tmp
    wq = [k.load_w(k.win_src(l, "QA", 0, 512), 512), k.load_w(k.win_src(l, "QA", 512, 512), 512)]
    wza, r_wza = k.load_w(k.win_src(l, "ZA", 0, 512), 512)
    gctr = [0]
    for c in range(nchunks):
        c0, n = CHUNKS[c]
        _load_tabs(k, S_, c0, n, which=(0, 1))
        for j in range(4):
            wt, rw = wq[j // 2]
            base = (j % 2) * 256
            bq, br = k.bank(), k.bank()
            for (bb, co) in ((bq, base), (br, base + 128)):
                for kk in range(8):
                    mm(PS[:, bb, 0:n], wt[:, kk, co:co + 128], hxT[:, kk, c0:c0 + n], kk == 0, kk == 7,
                       r=[rw, k.r_hx[c]], pw=[PB[bb]])
            tt("dve", TMP[0][:, 0:n], PS[:, bq, 0:n], k.tabs[0][:, 0:n], ALU.mult, r=[PB[bq], k.r_tab[0]], w=[r_tmp[0]])
            tt("dve", TMP[1][:, 0:n], PS[:, br, 0:n], k.tabs[1][:, 0:n], ALU.mult, r=[PB[br], k.r_tab[1]], w=[r_tmp[1]])
            tt("dve", k.qA[:, j, 0:n], TMP[0][:, 0:n], TMP[1][:, 0:n], ALU.add, r=[r_tmp[0], r_tmp[1]], pw=[rm("qA")])
        for jt in range(4):
            bb = k.bank()
            for kk in range(8):
                mm(PS[:, bb, 0:n], wza[:, kk, jt * 128:(jt + 1) * 128], hxT[:, kk, c0:c0 + n], kk == 0, kk == 7,
                   r=[r_wza, k.r_hx[c]], pw=[PB[bb]])
            act(k.sza[:, jt, 0:n], PS[:, bb, 0:n], AF.Silu, r=[PB[bb]], pw=[rm("sza")])
        items = []
        groups = []
        for qb in range(n // 128):
            for g in range(2):
                if c < 4:
                    qbg = c * 4 + qb
                    tiles = [(qbg * 128, qbg, 2 if qbg == 0 else 0), ((qbg + 1) * 128, qbg + 1, None),
                             ((qbg + 2) * 128, qbg + 2, 3 if qbg == 15 else 1), (2304, 18, None), (2432, 19, None)]
                else:
                    tiles = [(2304, 18, None), (2432, 19, None)]
                gi = gctr[0]
                gctr[0] += 1
                ob = gi % 4
                sb0 = 4 + 2 * (gi % 2)
                groups.append((qb, g, ob, gi))
                for ti, (koff, vt, mi) in enumerate(tiles):
                    sb0 = 4 + 2 * ((len(items)) % 2)
                    it = dict(sb=(sb0, 2),
                              s_mms=[(PS[:, sb0 + e, 0:256], k.kT_A[e * 64:(e + 1) * 64, g, koff:koff + 128],
                                      k.qA[e * 64:(e + 1) * 64, 2 * g:2 * g + 2, qb * 128:(qb + 1) * 128])
                                     for e in range(2)],
                              mask=(None if mi is None else [(PS[:, sb0 + e, 0:256], k.masks[:, mi, 0:256])
                                                             for e in range(2)]),
                              s_view=PS[:, sb0:sb0 + 2, 0:256],
                              p_view=(lambda p: p[:, 0:512].rearrange("p (e c) -> p e c", e=2)),
                              r=[rm("kTA"), rm("qA")],
                              pv=[(PS[:, ob, 0:256], k.V_A[:, vt, g, 64:192], (lambda p: p[:, 0:256])),
                                  (PS[:, ob, 256:512], k.V_A[:, vt, g, 0:128], (lambda p: p[:, 256:512]))],
                              rv=[rm("VA")], ob=ob, start=(ti == 0), stop=(ti == len(tiles) - 1))
                    items.append(it)

        def normalize(qb, g, ob, gi, c0=c0):
            ta, tb = (TMP[0], TMP[1]) if gi % 2 == 0 else (TMP[2], TMP[3])
            ra, rb = (r_tmp[0], r_tmp[1]) if gi % 2 == 0 else (r_tmp[2], r_tmp[3])
            tok0 = c0 + qb * 128
            for e in range(2):
                o_lo, r_lo = (0, 64) if e == 0 else (64, 0)
                cs_ = slice(e * 256, (e + 1) * 256)
                for jj in range(2):
                    h = 4 * g + 2 * jj + e
                    cj = slice(e * 256 + jj * 128, e * 256 + (jj + 1) * 128)
                    ts("dve", ta[o_lo:o_lo + 64, cj], PS[r_lo:r_lo + 64, ob, cj], k.esink[r_lo:r_lo + 64, l, h:h + 1],
                       ALU.add, r=[PB[ob], rm("esink")], pw=[ra])
                act(ta[o_lo:o_lo + 64, cs_], ta[o_lo:o_lo + 64, cs_], AF.Ln, r=[], w=[ra])
                act(ta[o_lo:o_lo + 64, cs_], ta[o_lo:o_lo + 64, cs_], AF.Exp, r=[], w=[ra], scale=-1.0)
                tt("dve", tb[o_lo:o_lo + 64, cs_], PS[o_lo:o_lo + 64, ob, cs_], ta[o_lo:o_lo + 64, cs_], ALU.mult,
                   r=[PB[ob], ra], pw=[rb])
                tt("dve", k.Y_A[o_lo:o_lo + 64, 2 * g:2 * g + 2, tok0:tok0 + 128],
                   tb[o_lo:o_lo + 64, cs_].rearrange("p (j q) -> p j q", j=2),
                   k.sza[o_lo:o_lo + 64, 2 * g:2 * g + 2, qb * 128:(qb + 1) * 128], ALU.mult,
                   r=[rb, rm("sza")], pw=[rm("YA")])

        per = len(items) // len(groups)
        for gidx in range(len(groups)):
            items[gidx * per + per - 1]["after_pv"] = (lambda a=groups[gidx]: normalize(*a))
        _attn_core(k, S_, items, SCALE_A)


def _phaseD_zb(k, S_, l, nchunks):
    PS, PB, hxT = k.PS, k.PB, k.hxT
    wzb, r_wzb = k.load_w(k.win_src(l, "ZB", 0, 512), 512)
    for c in range(nchunks):
        c0, n = CHUNKS[c]
        for jt in range(4):
            bb = k.bank()
            for kk in range(8):
                k.mm(PS[:, bb, 0:n], wzb[:, kk, jt * 128:(jt + 1) * 128], hxT[:, kk, c0:c0 + n], kk == 0, kk == 7,
                     r=[r_wzb, k.r_hx[c]], pw=[PB[bb]])
            sq = k.sqb[:, jt % 3, 0:n]
            k.act(sq, PS[:, bb, 0:n], AF.Silu, r=[PB[bb]], w=[k.rm("sqb%d" % (jt % 3))])
            k.tt("dve", k.Y_B[:, jt, c0:c0 + n], k.Y_B[:, jt, c0:c0 + n], sq, ALU.mult,
                 r=[k.rm("sqb%d" % (jt % 3))], pw=[k.rm("YB")])


def _phaseD_C(k, S_, l, nchunks):
    PS, PB, hxT = k.PS, k.PB, k.hxT
    mm, act, tt, ts, stt, cp, rm = k.mm, k.act, k.tt, k.ts, k.stt, k.cp, k.rm
    TMP, r_tmp = k.TMP, k.r_tmp
    zrow = k.zrow
    for ct in range(4):
        wc, r_wc = k.load_w(k.win_src(l, "C", ct * 512, 512), 512)
        cp("dve", zrow[:, 0:1], k.zhalo[:, ct, 0:1], r=[rm("zhalo")], pw=[rm("zrow")])
        cp("dve", zrow[:, 2049:2050], k.zhalo[:, ct, 1:2], r=[rm("zhalo")], pw=[rm("zrow")])
        S_.op("dve", lambda e: e.memset(zrow[:, 2050:2051], 0.0), pw=[rm("zrow")])
        S_.op("dve", lambda e: e.memset(zrow[:, 2307:2308], 0.0), pw=[rm("zrow")])
        zo = lambda c: (1 + CHUNKS[c][0]) if c < 4 else 2051
        for c in range(nchunks):
            c0, n = CHUNKS[c]
            bcc, buc = k.bank(), k.bank()
            for (bb, co) in ((bcc, 128), (buc, 256)):
                for kk in range(8):
                    mm(PS[:, bb, 0:n], wc[:, kk, co:co + 128], hxT[:, kk, c0:c0 + n], kk == 0, kk == 7,
                       r=[r_wc, k.r_hx[c]], pw=[PB[bb]])
            cp("act", TMP[0][:, 0:n], PS[:, bcc, 0:n], r=[PB[bcc]], w=[r_tmp[0]])
            tt("dve", zrow[:, zo(c):zo(c) + n], TMP[0][:, 0:n], PS[:, buc, 0:n], ALU.mult,
               r=[r_tmp[0], PB[buc]], pw=[rm("zrow")])
        for c in range(nchunks):
            c0, n = CHUNKS[c]
            z0 = zo(c)
            bbc, bzc = k.bank(), k.bank()
            for (bb, co) in ((bbc, 0), (bzc, 384)):
                for kk in range(8):
                    mm(PS[:, bb, 0:n], wc[:, kk, co:co + 128], hxT[:, kk, c0:c0 + n], kk == 0, kk == 7,
                       r=[r_wc, k.r_hx[c]], pw=[PB[bb]])
            act(k.sqb[:, 0, 0:n], PS[:, bzc, 0:n], AF.Silu, r=[PB[bzc]], w=[rm("sqb0")])
            ts("dve", TMP[1][:, 0:n], zrow[:, z0 - 1:z0 - 1 + n], k.convw[:, l, ct, 0:1], ALU.mult,
               r=[rm("zrow")], w=[r_tmp[1]])
            stt(TMP[1][:, 0:n], zrow[:, z0:z0 + n], k.convw[:, l, ct, 1:2], TMP[1][:, 0:n], ALU.mult, ALU.add,
                r=[rm("zrow")], w=[r_tmp[1]])
            stt(TMP[1][:, 0:n], zrow[:, z0 + 1:z0 + 1 + n], k.convw[:, l, ct, 2:3], TMP[1][:, 0:n], ALU.mult, ALU.add,
                r=[rm("zrow")], w=[r_tmp[1]])
            tt("dve", TMP[2][:, 0:n], TMP[1][:, 0:n], PS[:, bbc, 0:n], ALU.mult, r=[r_tmp[1], PB[bbc]], w=[r_tmp[2]])
            tt("dve", k.Y_C[:, ct, c0:c0 + n], TMP[2][:, 0:n], k.sqb[:, 0, 0:n], ALU.mult,
               r=[r_tmp[2], rm("sqb0")], pw=[rm("YC")])


def _phaseE_merge(k, S_, l, nchunks):
    PS, PB, hxT = k.PS, k.PB, k.hxT
    mm, act, tt, rm = k.mm, k.act, k.tt, k.rm
    TMP, r_tmp = k.TMP, k.r_tmp
    Y = [k.Y_A, k.Y_B, k.Y_C]
    rY = [rm("YA"), rm("YB"), rm("YC")]
    for mt in range(8):
        wg, r_wg = k.load_w(k.win_src(l, "G", mt * 384, 384), 384)
        wbrb = k.wbrb2[mt % 2]
        r_wbrb = rm("wbrb%d" % (mt % 2))
        S_.dma("pool", wbrb[:], k.d_wbr.ap()[l].rearrange("b (kc p) c -> p b kc c", p=128)[:, :, :, mt * 128:(mt + 1) * 128],
               w=[r_wbrb])
        for c in range(nchunks):
            c0, n = CHUNKS[c]
            bp = [k.bank() for _ in range(3)]
            bg = [k.bank() for _ in range(3)]
            for br in range(3):
                for kc in range(4):
                    mm(PS[:, bp[br], 0:n], wbrb[:, br, kc, :], Y[br][:, kc, c0:c0 + n], kc == 0, kc == 3,
                       r=[r_wbrb, rY[br]], pw=[PB[bp[br]]])
            for br in range(3):
                for kk in range(8):
                    mm(PS[:, bg[br], 0:n], wg[:, kk, br * 128:(br + 1) * 128], hxT[:, kk, c0:c0 + n], kk == 0, kk == 7,
                       r=[r_wg, k.r_hx[c]], pw=[PB[bg[br]]])
            for br in range(3):
                act(TMP[br][:, 0:n], PS[:, bg[br], 0:n], AF.Sigmoid, r=[PB[bg[br]]], w=[r_tmp[br]])
            for br in range(3):
                tt("dve", TMP[br][:, 0:n], TMP[br][:, 0:n], PS[:, bp[br], 0:n], ALU.mult, r=[PB[bp[br]]], w=[r_tmp[br]])
            tt("dve", TMP[0][:, 0:n], TMP[0][:, 0:n], TMP[1][:, 0:n], ALU.add, r=[r_tmp[1]], w=[r_tmp[0]])
            tt("dve", k.mT[:, mt, c0:c0 + n], TMP[0][:, 0:n], TMP[2][:, 0:n], ALU.add, r=[r_tmp[0], r_tmp[2]],
               pw=[rm("mT")])


def _phaseF_out(k, S_, l, ntiles):
    PS, PB = k.PS, k.PB
    mm, act, tt, ts, stt, rm = k.mm, k.act, k.tt, k.ts, k.stt, k.rm
    TMP, r_tmp = k.TMP, k.r_tmp
    last = (l == DEPTH - 1)
    wsrc = k.d_wmod.ap()[l].rearrange("(k p) c -> p k c", p=128)
    scs3 = k.scs[:].rearrange("p (k s) -> p k s", s=2)
    for n_ in range(2):
        wt, rw = k.load_w(wsrc[:, :, 2048 + n_ * 512:2048 + n_ * 512 + 512], 512)
        bg = k.bank()
        for kk in range(8):
            mm(PS[0:2, bg, 0:512], scs3[:, kk, :], wt[:, kk, :], kk == 0, kk == 7, r=[rw, rm("scs")], pw=[PB[bg]])
        S_.dma("sp", TMP[3][0:2, 0:512], k.d_bmodg.ap()[:, l, n_ * 512:(n_ + 1) * 512], w=[r_tmp[3]])
        tt("dve", TMP[2][0:2, 0:512], PS[0:2, bg, 0:512], TMP[3][0:2, 0:512], ALU.add, r=[PB[bg], r_tmp[3]], w=[r_tmp[2]])
        for which, Gt, rG in ((0, k.G_x, rm("Gx")), (1, k.G_c, rm("Gc"))):
            if which == 1 and ntiles <= 16:
                continue
            hs = slice(n_ * 512, (n_ + 1) * 512)
            S_.dma("sp", Gt[:, hs], k.d_gpostb.ap()[l][:, hs], pw=[rG])
            bb = k.bank()
            mm(PS[:, bb, 0:512], k.sel[0:2, which * 128:(which + 1) * 128], TMP[2][0:2, 0:512],
               True, True, r=[r_tmp[2], k.r_const], pw=[PB[bb]])
            tt("dve", Gt[:, hs], Gt[:, hs], PS[:, bb, 0:512], ALU.mult, r=[PB[bb], rG], pw=[rG])
    wo0, r_wo0 = k.load_w(k.d_wo.ap()[l].rearrange("(k p) c -> p k c", p=128)[:, :, 0:512], 512)
    wo1, r_wo1 = k.load_w(k.d_wo.ap()[l].rearrange("(k p) c -> p k c", p=128)[:, :, 512:1024], 512)
    wo = [(wo0, r_wo0), (wo1, r_wo1)]
    sm = k.small
    for i in range(ntiles):
        j = i % 2
        isx = i < 16
        Gt, rG = (k.G_x, rm("Gx")) if isx else (k.G_c, rm("Gc"))
        if isx:
            src = (k.d_x.ap() if l == 0 else k.d_x1.ap())[i * 128:(i + 1) * 128, :]
        else:
            src = k.d_ctx.ap()[(i - 16) * 128:(i - 15) * 128, :]
        p = i % 2
        xold = k.xoldb[p]
        rs = lambda n_: rm("%s%d" % (n_, p))
        q0 = 16 * p + 4
        S_.dma("sp", xold, src, r=([rm("x1d")] if (l > 0 and isx) else []), w=[rs("xold")])
        bo = [k.bank(), k.bank()]
        for hf in range(2):
            for kk in range(8):
                mm(PS[:, bo[hf], 0:512], k.mT[:, kk, i * 128:(i + 1) * 128], wo[hf][0][:, kk, :], kk == 0, kk == 7,
                   r=[wo[hf][1], rm("mT")], pw=[PB[bo[hf]]])
        for hf in range(2):
            act(k.junk2[p][:, hf * 512:(hf + 1) * 512], PS[:, bo[hf], 0:512], AF.Square, r=[PB[bo[hf]]],
                w=[rs("junkf%d" % hf), rs("ssq2%d" % hf)], accum_out=sm[:, q0 + hf:q0 + hf + 1])
        tt("dve", sm[:, q0 + 2:q0 + 3], sm[:, q0:q0 + 1], sm[:, q0 + 1:q0 + 2], ALU.add, r=[rs("ssq20"), rs("ssq21")],
           w=[rs("ssq2")])
        act(sm[:, q0 + 3:q0 + 4], sm[:, q0 + 2:q0 + 3], AF.Ln, r=[rs("ssq2")], w=[rs("ln2")], scale=1.0 / D,
            bias=k.epsc[:, 0:1])
        act(sm[:, q0 + 4:q0 + 5], sm[:, q0 + 3:q0 + 4], AF.Exp, r=[rs("ln2")], w=[rs("rstd2")], scale=-0.5)
        for hf in range(2):
            hs = slice(hf * 512, (hf + 1) * 512)
            tmpi = 2 * p + hf
            stt(TMP[tmpi][:, 0:512], PS[:, bo[hf], 0:512], sm[:, q0 + 4:q0 + 5], Gt[:, hs], ALU.mult, ALU.mult,
                r=[PB[bo[hf]], rs("rstd2"), rG], w=[r_tmp[tmpi]])
            tt("dve", k.xt[j][:, hs], TMP[tmpi][:, 0:512], xold[:, hs], ALU.add, r=[r_tmp[tmpi], rs("xold")],
               pw=[k.r_xt[j]])
        if isx:
            dst = (k.d_y.ap() if last else k.d_x1.ap())[i * 128:(i + 1) * 128, :]
            S_.dma("sp", dst, k.xt[j], r=[k.r_xt[j]], pw=[rm("yd") if last else rm("x1d")])
        if not last:
            _stage1_tile(k, S_, l + 1, i, k.xt[j], k.r_xt[j])


def _program(k, S_, upto):
    k.sctr = 0
    k.r_pbuf = [Res("pbuf%d" % i) for i in range(4)]
    if upto <= 0:
        return
    if upto == 1 and DBG_TILES:
        k.ntilesA = DBG_TILES
    _phaseA(k, S_, 0)
    k.dump("hxT0", k.hxT[:], [128, 8 * TT], BF16)
    if upto <= 1:
        return
    for l in range(DEPTH):
        nch = 5 if l == 0 else 4
        S_.barrier()
        _phaseB_kv(k, S_, l)
        _phaseB_qb(k, S_, l, nch)
        _phaseB_halo(k, S_, l)
        if l == 0:
            k.dump("kTA", k.kT_A, [128, 5120], BF16)
            k.dump("VA", k.V_A, [128, 7680], BF16)
            k.dump("qTB", k.qT_B[0:96])
            k.dump("kvnc", k.kvnc, [128, 512], BF16)
            k.dump("zhalo", k.zhalo[:], [128, 8], F32)
        if upto <= 2 and l == 0:
            return
        S_.barrier()
        _phaseC_mla(k, S_, l, with_ctx_q=(l == 0))
        if upto <= 3 and l == 0:
            k.dump("YB", k.Y_B, [128, 4 * TT], BF16)
            return
        S_.barrier()
        _phaseD_A(k, S_, l, nch)
        _phaseD_zb(k, S_, l, nch)
        if upto <= 4 and l == 0:
            k.dump("YB", k.Y_B, [128, 4 * TT], BF16)
            k.dump("YA", k.Y_A, [128, 4 * TT], BF16)
            return
        S_.barrier()
        _phaseD_C(k, S_, l, nch)
        if upto <= 5 and l == 0:
            k.dump("YC", k.Y_C, [128, 4 * TT], BF16)
            return
        S_.barrier()
        _phaseE_merge(k, S_, l, nch)
        if upto <= 6 and l == 0:
            k.dump("mT", k.mT, [128, 8 * TT], BF16)
            return
        S_.barrier()
        _phaseF_out(k, S_, l, 18 if l == 0 else 16)
        if upto <= 7 and l == 0:
            k.dump("hxT1", k.hxT[:], [128, 8 * TT], BF16)
            return


def kernel(**inputs):
    in_maps = prepare_inputs(**inputs)
    nc = build()
    res = run_bass_kernel_spmd(nc, in_maps, core_ids=list(range(NCORE)))
    out = np.zeros((2, S, D), np.float32)
    for core in range(NCORE):
        b, r = divmod(core, 4)
        out[b, r * T:(r + 1) * T] = np.asarray(res.results[core]["y"], np.float32)
    return out
```

```python
import os
import numpy as np
import ml_dtypes
import concourse.bass as bass
import concourse.mybir as mybir
from concourse.bass_utils import run_bass_kernel_spmd

F32 = mybir.dt.float32
BF16 = mybir.dt.bfloat16
AF = mybir.ActivationFunctionType
ALU = mybir.AluOpType
AP = bass.AP

D = 1024
S = 8192
L = 256
DEPTH = 2
NCORE = 8
T = 2048
TT = T + L
GRID_W = 64
EPS = 1e-6
SCALE_A = 64 ** -0.5
SCALE_B = 96 ** -0.5
NEG = -30000.0
DBG_TILES = 0
CHUNKS = [(0, 512), (512, 512), (1024, 512), (1536, 512), (2048, 256)]

O_QA, O_KA, O_VA, O_ZA = 0, 512, 640, 768
O_QL, O_KVL, O_KR, O_ZB = 1280, 1664, 1920, 1952
O_BC, O_CC, O_UC, O_ZC = 2464, 2976, 3488, 4000
O_GA, O_GB, O_GC = 4512, 5536, 6560


def _win_perm():
    cols = []
    off = {}

    def add(name, idx):
        off[name] = len(cols)
        cols.extend(list(idx))

    r64 = np.arange(64)
    rot64 = np.concatenate([r64[32:], r64[:32]])
    r32 = np.arange(32)
    rot32 = np.concatenate([r32[16:], r32[:16]])
    ka = []
    for g in range(2):
        base = O_KA + g * 64
        ka += list(base + r64) + list(base + r64)
        ka += list(base + rot64) + list(base + rot64)
    add("KA", ka)
    add("VKK", list(O_VA + np.arange(128)) + list(O_KVL + np.arange(256))
        + list(O_KR + r32) + list(O_KR + rot32))
    add("ZH", list(O_CC + np.arange(512)) + list(O_UC + np.arange(512)))
    add("QL", list(O_QL + np.arange(384)))
    qa = []
    for j in range(4):
        for e in range(2):
            qa += list(O_QA + (2 * j + e) * 64 + r64)
        for e in range(2):
            qa += list(O_QA + (2 * j + e) * 64 + rot64)
    add("QA", qa)
    add("ZA", list(O_ZA + np.arange(512)))
    add("ZB", list(O_ZB + np.arange(512)))
    cc = []
    for ct in range(4):
        for o in (O_BC, O_CC, O_UC, O_ZC):
            cc += list(o + ct * 128 + np.arange(128))
    add("C", cc)
    gg = []
    for mt in range(8):
        for o in (O_GA, O_GB, O_GC):
            gg += list(o + mt * 128 + np.arange(128))
    add("G", gg)
    return np.asarray(cols, dtype=np.int64), off


WIN_PERM, WOFF = _win_perm()
NCW = len(WIN_PERM)

PC_KVN = 0
PC_N1 = 4096
PC_KR = 0
PC_KAH = 512
PC_VAH = 1024
PC_ZH = 1280
PC_N2 = 1296


class Res:
    __slots__ = ("name", "writers", "readers", "excl", "last")

    def __init__(self, name, excl=False):
        self.name = name
        self.writers = []
        self.readers = []
        self.excl = excl
        self.last = {}


class Op:
    __slots__ = ("eng", "fn", "deps", "kind", "signaled", "sem", "val", "idx")


def _prune(lst):
    out = []
    seen = set()
    for o in reversed(lst):
        if o.kind != "c":
            out.append(o)
        elif o.eng not in seen:
            seen.add(o.eng)
            out.append(o)
    out.reverse()
    return out


class Sched:
    ENG = ("pe", "act", "dve", "pool", "sp")

    def __init__(self):
        self.prog = {e: [] for e in self.ENG}
        self.all = []
        self.pending_dma = []

    def op(self, eng, fn, r=(), w=(), pw=(), kind="c", extra=()):
        o = Op()
        o.eng, o.fn, o.kind, o.signaled, o.sem, o.val = eng, fn, kind, False, None, 0
        o.idx = len(self.all)
        deps = set(x for x in extra if x is not None)
        allres = list(r) + list(w) + list(pw)
        r = [x for x in r if not x.excl]
        w = [x for x in w if not x.excl]
        pw = [x for x in pw if not x.excl]
        for res in allres:
            if res.excl:
                for e2, o2 in res.last.items():
                    if e2 != eng:
                        deps.add(o2)
                res.last[eng] = o
        for res in r:
            deps.update(res.writers)
        for res in w:
            deps.update(res.writers)
            deps.update(res.readers)
        for res in pw:
            deps.update(res.readers)
        for res in r:
            res.readers.append(o)
            if len(res.readers) > 12:
                res.readers = _prune(res.readers)
        for res in w:
            res.writers = [o]
            res.readers = []
        for res in pw:
            if res.readers:
                res.writers = [o]
                res.readers = []
            else:
                res.writers.append(o)
                if len(res.writers) > 12:
                    res.writers = _prune(res.writers)
        deps.discard(o)
        o.deps = deps
        self.prog[eng].append(o)
        self.all.append(o)
        if kind != "c":
            self.pending_dma.append(o)
        return o

    def dma(self, eng, out, in_, r=(), w=(), pw=(), extra=(), **kw):
        return self.op(eng, lambda e: e.dma_start(out=out, in_=in_, **kw), r=r, w=w, pw=pw,
                       kind="d", extra=extra)

    def barrier(self):
        last = [self.prog[e][-1] for e in self.ENG if self.prog[e]]
        last += self.pending_dma
        self.pending_dma = []
        for e in self.ENG:
            self.op(e, None, extra=last)

    def emit(self, nc, stack):
        NS = 20
        for o in self.all:
            for d in o.deps:
                if d.kind == "c" and d.eng == "pe" and o.eng == "pe" and o.kind == "c":
                    continue
                d.signaled = True
        esem = {e: stack.enter_context(nc.semaphore("s_" + e)) for e in self.ENG}
        dsem = {e: [stack.enter_context(nc.semaphore("d_%s%d" % (e, i))) for i in range(NS)]
                for e in ("sp", "pool", "act")}
        ccsem = stack.enter_context(nc.semaphore("s_cc"))
        cnt = {e: 0 for e in self.ENG}
        dcnt = {e: 0 for e in dsem}
        dval = {}
        prev = {}
        ccn = 0
        for e in self.ENG:
            for o in self.prog[e]:
                if o.kind == "c":
                    if o.signaled and o.fn is not None:
                        cnt[e] += 1
                        o.sem, o.val = esem[e], cnt[e]
                    elif o.fn is None:
                        o.sem, o.val = None, 0
                elif o.kind == "d":
                    s = dsem[e][dcnt[e] % NS]
                    dcnt[e] += 1
                    prev[o] = dval.get(id(s), 0)
                    dval[id(s)] = prev[o] + 16
                    o.sem, o.val = s, dval[id(s)]
                else:
                    ccn += 1
                    o.sem, o.val = ccsem, ccn
        self.n_inst = {e: len(self.prog[e]) for e in self.ENG}
        block = stack.enter_context(nc.Block())
        sched = self

        def run(e, engine):
            waited = {}
            for o in sched.prog[e]:
                waits = {}
                for d in o.deps:
                    if d.kind == "c" and d.eng == "pe" and o.eng == "pe" and o.kind == "c":
                        continue
                    if d.sem is None:
                        continue
                    k = id(d.sem)
                    if k not in waits or waits[k][1] < d.val:
                        waits[k] = (d.sem, d.val)
                if o.kind == "d" and prev[o] > 0:
                    k = id(o.sem)
                    if k not in waits or waits[k][1] < prev[o]:
                        waits[k] = (o.sem, prev[o])
                for k, (s, v) in waits.items():
                    if waited.get(k, 0) < v:
                        engine.wait_ge(s, v)
                        waited[k] = v
                if o.fn is None:
                    continue
                inst = o.fn(engine)
                if o.kind == "d":
                    inst.then_inc(o.sem, 16)
                elif o.kind == "cc":
                    inst.then_inc(o.sem)
                elif o.signaled:
                    inst.then_inc(o.sem, 1)

        @block.tensor
        def _(te):
            run("pe", te)

        @block.scalar
        def _(sc):
            run("act", sc)

        @block.vector
        def _(ve):
            run("dve", ve)

        @block.gpsimd
        def _(gp):
            run("pool", gp)

        @block.sync
        def _(sy):
            run("sp", sy)


def _rope_tables(rank):
    g = rank * T + np.arange(T)
    row = (g // GRID_W).astype(np.float64)
    col = (g % GRID_W).astype(np.float64)

    def tab(rot_dim):
        axis_dim = rot_dim // 2
        inv = 10000.0 ** (-np.arange(0, axis_dim, 2, dtype=np.float64) / axis_dim)
        ang = np.concatenate([row[:, None] * inv, col[:, None] * inv], axis=-1)
        half = rot_dim // 2
        d = np.arange(rot_dim)
        c = np.cos(ang)[:, d % half].T
        s = np.sin(ang)[:, d % half].T
        s = np.where((d < half)[:, None], -s, s)
        cfull = np.ones((rot_dim, TT)); sfull = np.zeros((rot_dim, TT))
        cfull[:, :T] = c; sfull[:, :T] = s
        return cfull, sfull

    ca, sa = tab(64)
    cb, sb = tab(32)
    cosA = np.concatenate([ca, ca], 0).astype(np.float32)
    sinA = np.concatenate([sa, sa], 0).astype(np.float32)
    cosB = np.zeros((128, TT), np.float32); sinB = np.zeros((128, TT), np.float32)
    cosB[0:32] = cb; cosB[64:96] = cb
    sinB[0:32] = sb; sinB[64:96] = sb
    return cosA, sinA, cosB, sinB


def _masks(rank):
    kk = np.arange(128)[:, None]
    qq = np.arange(128)[None, :]
    mp = np.where(kk >= qq, 0.0, NEG).astype(np.float32)
    mn = np.where(kk <= qq, 0.0, NEG).astype(np.float32)
    allneg = np.full((128, 128), NEG, np.float32)
    m = np.stack([np.tile(mp, (1, 4)), np.tile(mn, (1, 4)),
                  np.tile(mp if rank > 0 else allneg, (1, 4)),
                  np.tile(mn if rank < 3 else allneg, (1, 4))], axis=1)
    return m.astype(ml_dtypes.bfloat16)


def _fm(v, k):
    return np.ascontiguousarray(np.asarray(v, np.float32).reshape(k, 128).T)


def prepare_inputs(x, c, ctx, c_ctx, w_mod, b_mod, g_pre, g_post, w_in, sink,
                   g_qa, w_qb, g_kva, w_kvb, conv_w, w_branch, w_o):
    f = lambda a: np.asarray(a, np.float32)
    x, c, ctx, c_ctx = f(x), f(c), f(ctx), f(c_ctx)
    w_mod, b_mod, g_pre, g_post = f(w_mod), f(b_mod), f(g_pre), f(g_post)
    w_in, sink, g_qa, w_qb, g_kva, w_kvb = f(w_in), f(sink), f(g_qa), f(w_qb), f(g_kva), f(w_kvb)
    conv_w, w_branch, w_o = f(conv_w), f(w_branch), f(w_o)
    shared = {}
    shared["wmod"] = np.ascontiguousarray(w_mod)
    shared["bmodT"] = np.ascontiguousarray(np.stack([_fm(b_mod[l, :2048], 16) for l in range(DEPTH)], 1))
    shared["bmodg"] = np.ascontiguousarray(np.stack([np.stack([b_mod[l, 2048:], b_mod[l, 2048:]], 0)
                                                     for l in range(DEPTH)], 1))
    shared["gpreT"] = np.ascontiguousarray(np.stack([_fm(g_pre[l], 8) for l in range(DEPTH)], 1))
    shared["gpostb"] = np.ascontiguousarray(np.broadcast_to(g_post[:, None, :], (DEPTH, 128, D)))
    shared["win"] = np.ascontiguousarray(w_in[:, :, WIN_PERM])
    shared["sinkb"] = np.ascontiguousarray(np.broadcast_to(sink[None, :, :], (128, DEPTH, 8)))
    shared["gqaT"] = np.ascontiguousarray(np.stack([_fm(g_qa[l], 3) for l in range(DEPTH)], 1))
    shared["gkvaT"] = np.ascontiguousarray(np.stack([_fm(g_kva[l], 2) for l in range(DEPTH)], 1))
    r32 = np.arange(32)
    qcols = []
    for h in range(8):
        base = h * 96
        qcols += list(base + np.arange(96))
        qcols += list(base + np.arange(64)) + list(base + 64 + np.concatenate([r32[16:], r32[:16]]))
    shared["wqb"] = np.ascontiguousarray(w_qb[:, :, np.asarray(qcols)])
    kcols = [h * 128 + i for h in range(8) for i in range(64)]
    vcols = [h * 128 + 64 + i for h in range(8) for i in range(64)]
    shared["wkvb"] = np.ascontiguousarray(w_kvb[:, :, np.asarray(kcols + vcols)])
    cw = np.zeros((128, DEPTH, 4, 3), np.float32)
    for l in range(DEPTH):
        for k in range(3):
            cw[:, l, :, k] = conv_w[l, k].reshape(4, 128).T
    shared["convw"] = cw
    shared["wbr"] = np.ascontiguousarray(w_branch)
    shared["wo"] = np.ascontiguousarray(w_o)
    shared["ident"] = np.eye(128, dtype=np.float32)
    shared["identb"] = np.eye(128, dtype=np.float32).astype(ml_dtypes.bfloat16)
    shared["onesb"] = np.ones((128, 128), np.float32).astype(ml_dtypes.bfloat16)
    sel = np.zeros((2, 2, 128), np.float32)
    sel[0, 0, :] = 1.0
    sel[1, 1, :] = 1.0
    shared["sel"] = sel
    in_maps = []
    for core in range(NCORE):
        b, r = core // 4, core % 4
        m = dict(shared)
        m["x"] = np.ascontiguousarray(x[b, r * T:(r + 1) * T])
        m["ctx"] = np.ascontiguousarray(ctx[b])
        cs = np.zeros((128, 8, 2), np.float32)
        cs[:, :, 0] = _fm(c[b], 8)
        cs[:, :, 1] = _fm(c_ctx, 8)
        m["cs"] = cs.reshape(128, 16)
        cosA, sinA, cosB, sinB = _rope_tables(r)
        m["cosA"], m["sinA"], m["cosB"], m["sinB"] = cosA, sinA, cosB, sinB
        m["masks"] = _masks(r)
        oh = np.zeros((128, 8), np.float32)
        if r > 0:
            oh[:, r - 1] = 1.0
        if r < 3:
            oh[:, 4 + r + 1] = 1.0
        m["oh"] = oh
        in_maps.append(m)
    return in_maps


class _K:
    pass


def build(upto=99, dbg=()):
    from contextlib import ExitStack
    nc = bass.Bass("TRN2", target_bir_lowering=False)
    S_ = Sched()
    k = _K()
    stack = ExitStack()
    with stack:
        _build_body(nc, S_, k, stack, upto, dbg)
        _program(k, S_, upto)
        S_.barrier()
        for (dd, ap_) in k.dumps:
            S_.dma("sp", dd.ap(), ap_)
        S_.barrier()
        S_.emit(nc, stack)
    k.S = S_
    build.last = k
    return nc


def _din(nc, name, shape, dt=F32):
    return nc.dram_tensor(name, list(shape), dt, kind="ExternalInput")


def _build_body(nc, S_, k, stack, upto, dbg):
    sb = lambda name, shape, dt: stack.enter_context(nc.sbuf_tensor("s_" + name, list(shape), dt))
    d_x = _din(nc, "x", [T, D]); d_ctx = _din(nc, "ctx", [L, D]); d_cs = _din(nc, "cs", [128, 16])
    d_wmod = _din(nc, "wmod", [DEPTH, D, 3 * D]); d_bmodT = _din(nc, "bmodT", [128, DEPTH, 16])
    d_bmodg = _din(nc, "bmodg", [2, DEPTH, D]); d_gpreT = _din(nc, "gpreT", [128, DEPTH, 8])
    d_gpostb = _din(nc, "gpostb", [DEPTH, 128, D]); d_win = _din(nc, "win", [DEPTH, D, NCW])
    d_sinkb = _din(nc, "sinkb", [128, DEPTH, 8]); d_gqaT = _din(nc, "gqaT", [128, DEPTH, 3])
    d_gkvaT = _din(nc, "gkvaT", [128, DEPTH, 2]); d_wqb = _din(nc, "wqb", [DEPTH, 384, 1536])
    d_wkvb = _din(nc, "wkvb", [DEPTH, 256, 1024]); d_convw = _din(nc, "convw", [128, DEPTH, 4, 3])
    d_wbr = _din(nc, "wbr", [DEPTH, 3, 512, D]); d_wo = _din(nc, "wo", [DEPTH, D, D])
    d_tab = [_din(nc, n, [128, TT]) for n in ("cosA", "sinA", "cosB", "sinB")]
    d_masks = _din(nc, "masks", [128, 4, 512], BF16); d_oh = _din(nc, "oh", [128, 8])
    d_ident = _din(nc, "ident", [128, 128]); d_identb = _din(nc, "identb", [128, 128], BF16)
    d_onesb = _din(nc, "onesb", [128, 128], BF16); d_sel = _din(nc, "sel", [2, 2, 128])
    d_y = nc.dram_tensor("y", [T, D], F32, kind="ExternalOutput")
    d_snd1 = nc.dram_tensor("snd1", [128, PC_N1], BF16)
    d_rcv1 = nc.dram_tensor("rcv1", [512, PC_N1], BF16)
    d_snd = nc.dram_tensor("snd2", [128, PC_N2], BF16)
    d_rcv = nc.dram_tensor("rcv2", [512, PC_N2], BF16)
    d_x1 = nc.dram_tensor("x1", [T, D], F32)

    ident = sb("ident", [128, 128], F32); identb = sb("identb", [128, 128], BF16)
    onesb = sb("onesb", [128, 128], BF16); sel = sb("sel", [2, 256], F32)
    masks = sb("masksb", [128, 4, 512], BF16); oh = sb("oh", [128, 8], F32)
    cs = sb("cs", [128, 16], F32); scs = sb("scs", [128, 16], BF16)
    bmodT = sb("bmodT", [128, DEPTH, 16], F32)
    gpreT = sb("gpreT", [128, DEPTH, 8], F32); sinkb = sb("sinkb", [128, DEPTH, 8], F32)
    esink = sb("esink", [128, DEPTH, 8], F32)
    gqaT = sb("gqaT", [128, DEPTH, 3], F32); gkvaT = sb("gkvaT", [128, DEPTH, 2], F32)
    convw = sb("convw", [128, DEPTH, 4, 3], F32)
    modT = sb("modT", [128, 16, 2], F32)
    Amod = sb("Amod", [128, DEPTH, 8, 2], F32); Bmod = sb("Bmod", [128, DEPTH, 8, 2], F32)
    small = sb("small", [128, 32], F32)
    epsc = sb("epsc", [128, 1], F32)
    hxT = sb("hxT", [128, 8, TT], BF16)
    R1 = sb("R1", [128, 9216], BF16)
    R2 = sb("R2", [128, 18432], BF16)
    R3 = sb("R3", [128, 20736], BF16)
    WB = [sb("wb%d" % i, [128, 8, 512], BF16) for i in range(3)]
    wbrb2 = [sb("wbrb%d" % i, [128, 3, 4, 128], BF16) for i in range(2)]
    TMP = [sb("tmp%d" % i, [128, 512], F32) for i in range(8)]
    sqb = sb("sqb", [128, 3, 512], BF16); qnb = sb("qnb", [128, 3, 512], BF16)
    junk = sqb[:, 0:2, :].rearrange("p a b -> p (a b)")
    tabs = [sb("tab%d" % i, [128, 512], F32) for i in range(4)]
    kvst = sb("kvst", [128, 2, 512], BF16); krst = sb("krst", [32, 512], BF16)
    zh = sb("zh", [128, 8], F32); zhalo = sb("zhalo", [128, 4, 2], F32)
    zhb = sb("zhb", [128, 16], BF16); hzf = sb("hzf", [128, 4, 8], F32)
    PS = stack.enter_context(nc.psum_tensor("ps", [128, 8, 512], F32))

    def v32(R, off_bf, n_f32):
        return R[:, off_bf:off_bf + 2 * n_f32].bitcast(F32)
    xt = [v32(R1, 0, 1024), v32(R1, 2048, 1024)]
    xnb = [v32(R2, 4096, 1024), v32(R2, 6144, 1024)]
    xoldb = [v32(R2, 8192, 1024), v32(R2, 10240, 1024), v32(R2, 16384, 1024)]
    xhlb = [R2[:, 12288:14336].rearrange("p (a b) -> p a b", a=2), R2[:, 14336:16384].rearrange("p (a b) -> p a b", a=2)]
    junk2 = [junk, qnb[:, 0:2, :].rearrange("p a b -> p (a b)")]
    wqb = R1[:, 0:4608].rearrange("p (j c) -> p j c", j=3)
    hal_k = R1[:, 4608:4608 + 2048].rearrange("p (r c) -> p r c", r=4)
    hal_v = R1[:, 6656:6656 + 1024].rearrange("p (r c) -> p r c", r=4)
    hal_z = R1[:, 7680:7680 + 64].rearrange("p (r c) -> p r c", r=4)
    hal_zf = R1[:, 7680:7680 + 64].bitcast(F32).rearrange("p (r c) -> p r c", r=4)
    Y_B = R1[:, 0:9216].rearrange("p (j t) -> p j t", j=4)
    qT_B = R2[:, 0:18432].rearrange("p (h t) -> p h t", h=8)
    Y_A = R2[:, 0:9216].rearrange("p (j t) -> p j t", j=4)
    Y_C = R2[:, 9216:18432].rearrange("p (j t) -> p j t", j=4)
    G_x = v32(R2, 0, 1024); G_c = v32(R2, 2048, 1024)
    o3 = 0
    kT_A = R3[:, o3:o3 + 5120].rearrange("p (g t) -> p g t", g=2); o3 += 5120
    V_A = R3[:, o3:o3 + 7680].rearrange("p (t g c) -> p t g c", t=20, g=2); o3 += 7680
    o3m = o3
    wkvb = R3[:, o3:o3 + 2048].rearrange("p (j c) -> p j c", j=2); o3 += 2048
    kvbuf = []
    for i in range(2):
        kvbuf.append(R3[:, o3:o3 + 1024].rearrange("p (j c) -> p j c", j=2)); o3 += 1024
    Kbuf = []
    for i in range(2):
        Kbuf.append(R3[:, o3:o3 + 512]); o3 += 512
    Vbuf = []
    for i in range(2):
        Vbuf.append(R3[:, o3:o3 + 768].rearrange("p (t c) -> p t c", t=4)); o3 += 768
    Kc = R3[:, o3:o3 + 256]; o3 += 256
    Vc = R3[:, o3:o3 + 384].rearrange("p (t c) -> p t c", t=2); o3 += 384
    kvnc = R3[:, o3:o3 + 512].rearrange("p (j c) -> p j c", j=2); o3 += 512
    assert o3 <= 20736, o3
    qA = R3[:, o3m:o3m + 2048].rearrange("p (j c) -> p j c", j=4)
    sza = R3[:, o3m + 2048:o3m + 4096].rearrange("p (j c) -> p j c", j=4)
    zrow = R3[:, o3m:o3m + 4640].bitcast(F32)
    mT = R3[:, 0:18432].rearrange("p (j t) -> p j t", j=8)
    pbuf = [TMP[4 + i][:, :].bitcast(BF16)[:, 0:512] for i in range(4)]

    R = lambda n: Res(n)
    r_const = R("const")
    PB = [Res("pb%d" % i, excl=True) for i in range(8)]
    r_hx = [R("hx%d" % c) for c in range(5)]
    r_wb = [R("wb%d" % i) for i in range(3)]
    r_tmp = [R("tmp%d" % i) for i in range(8)]
    r_tab = [R("tab%d" % i) for i in range(4)]
    r_misc = {}
    r_xt = [R("xt0"), R("xt1")]

    def rm(name):
        if name not in r_misc:
            r_misc[name] = Res(name)
        return r_misc[name]

    def mm(out, lhsT, rhs, start, stop, r=(), pw=(), w=()):
        return S_.op("pe", lambda e: e.matmul(out, lhsT=lhsT, rhs=rhs, start=start, stop=stop), r=r, pw=pw, w=w)

    def act(out, in_, func, r=(), w=(), pw=(), scale=None, bias=None, accum_out=None):
        kw = {}
        if scale is not None:
            kw["scale"] = scale
        if bias is not None:
            kw["bias"] = bias
        if accum_out is not None:
            kw["accum_out"] = accum_out
        return S_.op("act", lambda e: e.activation(out=out, in_=in_, func=func, **kw), r=r, w=w, pw=pw)

    def tt(eng, out, in0, in1, op, r=(), w=(), pw=()):
        return S_.op(eng, lambda e: e.tensor_tensor(out=out, in0=in0, in1=in1, op=op), r=r, w=w, pw=pw)

    def ts(eng, out, in0, s1, op0, s2=None, op1=None, r=(), w=(), pw=()):
        if op1 is None:
            return S_.op(eng, lambda e: e.tensor_scalar(out=out, in0=in0, scalar1=s1, scalar2=None, op0=op0),
                         r=r, w=w, pw=pw)
        return S_.op(eng, lambda e: e.tensor_scalar(out=out, in0=in0, scalar1=s1, scalar2=s2, op0=op0, op1=op1),
                     r=r, w=w, pw=pw)

    def stt(out, in0, scalar, in1, op0, op1, r=(), w=(), pw=()):
        return S_.op("dve", lambda e: e.scalar_tensor_tensor(out=out, in0=in0, scalar=scalar, in1=in1,
                                                            op0=op0, op1=op1), r=r, w=w, pw=pw)

    def cp(eng, out, in_, r=(), w=(), pw=()):
        if eng == "act":
            return act(out, in_, AF.Copy, r=r, w=w, pw=pw)
        return S_.op(eng, lambda e: e.tensor_copy(out=out, in_=in_), r=r, w=w, pw=pw)

    wb_next = [0]

    def load_w(src_ap, ncols, kparts=8):
        i = wb_next[0] % 3
        wb_next[0] += 1
        dst = WB[i][:, 0:kparts, 0:ncols]
        S_.dma("pool", dst, src_ap, w=[r_wb[i]])
        return WB[i], r_wb[i]

    def win_src(l, name, c0, ncols):
        a = d_win.ap()[l].rearrange("(k p) c -> p k c", p=128)
        o = WOFF[name] + c0
        return a[:, :, o:o + ncols]

    pb_next = [0]

    def bank():
        b = pb_next[0] % 8
        pb_next[0] += 1
        return b

    k.dumps = []

    def dump(name, ap_sbuf, shape=None, dt=None):
        if name in dbg:
            ap_ = ap_sbuf if isinstance(ap_sbuf, AP) else ap_sbuf[:]
            dd = nc.dram_tensor("dbg_" + name, list(ap_.shape), ap_.dtype, kind="ExternalOutput")
            k.dumps.append((dd, ap_))

    for (dst, src) in ((ident[:], d_ident.ap()), (identb[:], d_identb.ap()), (onesb[:], d_onesb.ap()),
                       (sel[:], d_sel.ap().rearrange("k w m -> k (w m)")), (masks[:], d_masks.ap()),
                       (oh[:], d_oh.ap()), (cs[:], d_cs.ap()), (bmodT[:], d_bmodT.ap()),
                       (gpreT[:], d_gpreT.ap()), (sinkb[:], d_sinkb.ap()),
                       (gqaT[:], d_gqaT.ap()), (gkvaT[:], d_gkvaT.ap()), (convw[:], d_convw.ap())):
        S_.dma("sp", dst, src, pw=[r_const])
    S_.op("pool", lambda e: e.memset(epsc[:], EPS), pw=[r_const])
    S_.barrier()
    act(scs[:], cs[:], AF.Silu, w=[rm("scs")])
    act(esink[:], sinkb[:], AF.Exp, w=[rm("esink")])
    scs3 = scs[:].rearrange("p (k s) -> p k s", s=2)
    def mod_layer(l):
        wsrc = d_wmod.ap()[l].rearrange("(k p) c -> p k c", p=128)
        bm = bank()
        for j in range(16):
            if j % 4 == 0:
                wt, rw = load_w(wsrc[:, :, (j // 4) * 512:(j // 4) * 512 + 512], 512)
            for kk in range(8):
                mm(PS[:, bm, j * 2:j * 2 + 2], wt[:, kk, (j % 4) * 128:(j % 4) * 128 + 128], scs3[:, kk, :],
                   kk == 0, kk == 7, r=[rw, rm("scs")], pw=[PB[bm]])
        bmb = AP(bmodT[:].tensor, bmodT[:, l, :].offset, [list(bmodT[:].ap[0]), [1, 16], [0, 2]])
        tt("dve", modT[:], PS[:, bm, 0:32].rearrange("p (j s) -> p j s", s=2), bmb, ALU.add,
           r=[PB[bm]], w=[rm("modT")])
        ts("dve", modT[:, 8:16, :], modT[:, 8:16, :], 1.0, ALU.add, r=[], w=[rm("modT")])
        gpb = AP(gpreT[:].tensor, gpreT[:, l, :].offset, [list(gpreT[:].ap[0]), [1, 8], [0, 2]])
        tt("dve", Amod[:, l], modT[:, 8:16, :], gpb, ALU.mult, r=[rm("modT")], pw=[rm("AB%d" % l)])
        cp("dve", Bmod[:, l], modT[:, 0:8, :], r=[rm("modT")], pw=[rm("AB%d" % l)])
    mod_layer(0)
    k.__dict__.update(locals())


def _stage1_tile(k, S_, l, i, xtile, r_xt):
    PS, PB, hxT, ident = k.PS, k.PB, k.hxT, k.ident
    act, ts, mm, rm = k.act, k.ts, k.mm, k.rm
    s = 0 if i < 16 else 1
    c = i // 4 if i < 16 else 4
    col0 = i * 128
    p = i % 2
    sm0 = 16 * p
    ssq = k.small[:, sm0 + 0:sm0 + 1]
    lnv = k.small[:, sm0 + 1:sm0 + 2]
    rstd = k.small[:, sm0 + 2:sm0 + 3]
    xn = k.xnb[p]
    rs = lambda n_: rm("%s%d" % (n_, p))
    act(k.junk2[p], xtile, AF.Square, r=[r_xt], w=[rs("junk"), rs("ssq")], accum_out=ssq)
    act(lnv, ssq, AF.Ln, r=[rs("ssq")], w=[rs("lnv")], scale=1.0 / D, bias=k.epsc[:, 0:1])
    act(rstd, lnv, AF.Exp, r=[rs("lnv")], w=[rs("rstd")], scale=-0.5)
    ts("dve", xn, xtile, rstd, ALU.mult, r=[r_xt, rs("rstd")], w=[rs("xn")])
    hi, lo = k.xhlb[p][:, 0, :], k.xhlb[p][:, 1, :]
    k.cp("pool", hi, xn, r=[rs("xn")], w=[rs("xhi")])
    k.tt("dve", lo, xn, hi, ALU.subtract, r=[rs("xn"), rs("xhi")], w=[rs("xlo")])
    b0 = k.bank()
    b1 = k.bank()
    for kk in range(8):
        bb = b0 if kk < 4 else b1
        o_ = PS[:, bb, (kk % 4) * 128:(kk % 4) * 128 + 128]
        mm(o_, hi[:, kk * 128:(kk + 1) * 128], k.identb[:], True, False, r=[rs("xhi"), k.r_const], pw=[PB[bb]])
        mm(o_, lo[:, kk * 128:(kk + 1) * 128], k.identb[:], False, True, r=[rs("xlo"), k.r_const], pw=[PB[bb]])
    for kk in range(8):
        bb = b0 if kk < 4 else b1
        src = PS[:, bb, (kk % 4) * 128:(kk % 4) * 128 + 128]
        dst = hxT[:, kk, col0:col0 + 128]
        if kk < 4:
            act(dst, src, AF.Identity, r=[PB[bb], rm("AB%d" % l)], pw=[k.r_hx[c]],
                scale=k.Amod[:, l, kk, s:s + 1], bias=k.Bmod[:, l, kk, s:s + 1])
        else:
            ts("dve", dst, src, k.Amod[:, l, kk, s:s + 1], ALU.mult, s2=k.Bmod[:, l, kk, s:s + 1], op1=ALU.add,
               r=[PB[bb], rm("AB%d" % l)], pw=[k.r_hx[c]])


def _phaseA(k, S_, l):
    for i in range(getattr(k, "ntilesA", 18)):
        src = k.d_x.ap()[i * 128:(i + 1) * 128, :] if i < 16 else k.d_ctx.ap()[(i - 16) * 128:(i - 15) * 128, :]
        j = i % 2
        S_.dma("sp", k.xt[j], src, w=[k.r_xt[j]])
        _stage1_tile(k, S_, l, i, k.xt[j], k.r_xt[j])


def _load_tabs(k, S_, c0, n, which=(0, 1, 2, 3)):
    for ti in which:
        S_.dma("sp", k.tabs[ti][:, 0:n], k.d_tab[ti].ap()[:, c0:c0 + n], w=[k.r_tab[ti]])


def _rstd_bcast(k, S_, banks, nj, n, inv_n, out_tmp, r_out):
    PS, PB = k.PS, k.PB
    for j in range(nj):
        k.act(k.sqb[:, j, 0:n], PS[:, banks[j], 0:n], AF.Square, r=[PB[banks[j]]], pw=[k.rm("sqb")])
    bs = k.bank()
    for j in range(nj):
        k.mm(PS[:, bs, 0:n], k.onesb[:], k.sqb[:, j, 0:n], j == 0, j == nj - 1, r=[k.rm("sqb"), k.r_const],
             pw=[PB[bs]])
    k.act(out_tmp[:, 0:n], PS[:, bs, 0:n], AF.Ln, r=[PB[bs]], w=[r_out], scale=inv_n, bias=k.epsc[:, 0:1])
    k.act(out_tmp[:, 0:n], out_tmp[:, 0:n], AF.Exp, r=[], w=[r_out], scale=-0.5)


def _phaseB_kv(k, S_, l):
    PS, PB, hxT = k.PS, k.PB, k.hxT
    mm, act, tt, ts, stt, cp, rm = k.mm, k.act, k.tt, k.ts, k.stt, k.cp, k.rm
    TMP, r_tmp = k.TMP, k.r_tmp
    S_.op("pool", lambda e: e.memset(k.V_A[:, :, :, 0:64], 1.0), pw=[rm("VA")])
    S_.op("pool", lambda e: e.memset(k.V_A[:, :, :, 128:192], 1.0), pw=[rm("VA")])
    for i in range(2):
        S_.op("pool", (lambda vb: (lambda e: e.memset(vb[:, :, 0:64], 1.0)))(k.Vbuf[i]), pw=[rm("Vbuf%d" % i)])
        S_.op("pool", (lambda vb: (lambda e: e.memset(vb[:, :, 128:192], 1.0)))(k.Vbuf[i]), pw=[rm("Vbuf%d" % i)])
    S_.op("pool", lambda e: e.memset(k.Vc[:, :, 0:64], 1.0), pw=[rm("Vc")])
    S_.op("pool", lambda e: e.memset(k.Vc[:, :, 128:192], 1.0), pw=[rm("Vc")])
    wka, r_wka = k.load_w(k.win_src(l, "KA", 0, 512), 512)
    wvk, r_wvk = k.load_w(k.win_src(l, "VKK", 0, 448), 448)
    for c, (c0, n) in enumerate(CHUNKS):
        _load_tabs(k, S_, c0, n)
        rhs = [hxT[:, kk, c0:c0 + n] for kk in range(8)]
        kdst0 = 128 + c0 if c < 4 else 2304
        for g in range(2):
            bq, br = k.bank(), k.bank()
            for ti, bb in ((2 * g, bq), (2 * g + 1, br)):
                for kk in range(8):
                    mm(PS[:, bb, 0:n], wka[:, kk, ti * 128:(ti + 1) * 128], rhs[kk], kk == 0, kk == 7,
                       r=[r_wka, k.r_hx[c]], pw=[PB[bb]])
            tt("dve", TMP[0][:, 0:n], PS[:, bq, 0:n], k.tabs[0][:, 0:n], ALU.mult, r=[PB[bq], k.r_tab[0]], w=[r_tmp[0]])
            tt("dve", TMP[1][:, 0:n], PS[:, br, 0:n], k.tabs[1][:, 0:n], ALU.mult, r=[PB[br], k.r_tab[1]], w=[r_tmp[1]])
            tt("dve", k.kT_A[:, g, kdst0:kdst0 + n], TMP[0][:, 0:n], TMP[1][:, 0:n], ALU.add,
               r=[r_tmp[0], r_tmp[1]], pw=[rm("kTA")])
        bv = k.bank()
        nt = n // 128
        for t_ in range(nt):
            for kk in range(8):
                mm(PS[:, bv, t_ * 128:(t_ + 1) * 128], hxT[:, kk, c0 + t_ * 128:c0 + (t_ + 1) * 128], wvk[:, kk, 0:128],
                   kk == 0, kk == 7, r=[r_wvk, k.r_hx[c]], pw=[PB[bv]])
        vt0 = 1 + c * 4 if c < 4 else 18
        cp("dve", k.V_A[:, vt0:vt0 + nt, :, 64:128], PS[:, bv, 0:n].rearrange("p (t g d) -> p t g d", t=nt, g=2),
           r=[PB[bv]], pw=[rm("VA")])
        bk = [k.bank(), k.bank()]
        for j in range(2):
            for kk in range(8):
                mm(PS[:, bk[j], 0:n], wvk[:, kk, 128 + j * 128:256 + j * 128], rhs[kk], kk == 0, kk == 7,
                   r=[r_wvk, k.r_hx[c]], pw=[PB[bk[j]]])
        _rstd_bcast(k, S_, bk, 2, n, 1.0 / 256, TMP[2], r_tmp[2])
        for j in range(2):
            dst = k.kvst[:, j, 0:n] if c < 4 else k.kvnc[:, j, 0:n]
            stt(dst, PS[:, bk[j], 0:n], k.gkvaT[:, l, j:j + 1], TMP[2][:, 0:n], ALU.mult, ALU.mult,
                r=[PB[bk[j]], r_tmp[2]], pw=[rm("kvst") if c < 4 else rm("kvnc")])
        if c < 4:
            S_.dma("sp", k.d_snd1.ap()[:, PC_KVN:PC_KVN + 4096].rearrange("p (j t) -> p j t", j=2)[:, :, c0:c0 + n],
                   k.kvst[:, :, 0:n], r=[rm("kvst")], pw=[rm("snd1")])
        b1, b2 = k.bank(), k.bank()
        for (bb, co) in ((b1, 384), (b2, 416)):
            for kk in range(8):
                mm(PS[0:32, bb, 0:n], wvk[:, kk, co:co + 32], rhs[kk], kk == 0, kk == 7,
                   r=[r_wvk, k.r_hx[c]], pw=[PB[bb]])
        tt("dve", TMP[0][0:32, 0:n], PS[0:32, b1, 0:n], k.tabs[2][0:32, 0:n], ALU.mult, r=[PB[b1], k.r_tab[2]], w=[r_tmp[0]])
        tt("dve", TMP[1][0:32, 0:n], PS[0:32, b2, 0:n], k.tabs[3][0:32, 0:n], ALU.mult, r=[PB[b2], k.r_tab[3]], w=[r_tmp[1]])
        tt("dve", k.krst[:, 0:n], TMP[0][0:32, 0:n], TMP[1][0:32, 0:n], ALU.add, r=[r_tmp[0], r_tmp[1]], w=[rm("krst")])
        if c < 4:
            S_.dma("sp", k.d_snd.ap()[c * 32:(c + 1) * 32, PC_KR:PC_KR + 512], k.krst[:, 0:n], r=[rm("krst")], pw=[rm("snd")])
        else:
            S_.dma("sp", k.Kc[64:96, 0:n], k.krst[:, 0:n], r=[rm("krst")], pw=[rm("Kc")])
    bz = k.bank()
    for which in range(2):
        wz, r_wz = k.load_w(k.win_src(l, "ZH", which * 512, 512), 512)
        for ct in range(4):
            col = (which * 4 + ct) * 2
            for kk in range(8):
                mm(PS[:, bz, col:col + 2], wz[:, kk, ct * 128:(ct + 1) * 128], hxT[:, kk, 0:2048:2047],
                   kk == 0, kk == 7, r=[r_wz, k.r_hx[0], k.r_hx[3]], pw=[PB[bz]])
    cp("act", TMP[3][:, 0:8], PS[:, bz, 0:8], r=[PB[bz]], w=[r_tmp[3]])
    tt("dve", k.zh[:], TMP[3][:, 0:8], PS[:, bz, 8:16], ALU.mult, r=[PB[bz], r_tmp[3]], w=[rm("zh")])
    snd = k.d_snd.ap()
    for fl, off in ((0, 128), (1, 2048)):
        S_.dma("sp", snd[:, PC_KAH:PC_KAH + 512].rearrange("p (g f t) -> p g f t", g=2, f=2)[:, :, fl, :],
               k.kT_A[:, :, off:off + 128], r=[rm("kTA")], pw=[rm("snd")])
    for fl, vt in ((0, 1), (1, 16)):
        S_.dma("sp", snd[:, PC_VAH + fl * 128:PC_VAH + (fl + 1) * 128].rearrange("p (g d) -> p g d", g=2),
               k.V_A[:, vt, :, 64:128], r=[rm("VA")], pw=[rm("snd")])
    cp("dve", k.zhb[:, 0:8], k.zh[:], r=[rm("zh")], w=[rm("zhb")])
    tt("dve", k.zhb[:, 8:16], k.zh[:], k.zhb[:, 0:8], ALU.subtract, r=[rm("zh")], w=[rm("zhb")])
    S_.dma("sp", snd[:, PC_ZH:PC_ZH + 16], k.zhb[:], r=[rm("zhb")], pw=[rm("snd")])
    if os.environ.get("NOCC") == "1":
        return
    S_.op("pool", lambda e: e.collective_compute("AllGather", ALU.bypass,
                                                 replica_groups=[[0, 1, 2, 3], [4, 5, 6, 7]],
                                                 ins=[k.d_snd1.ap().opt()], outs=[k.d_rcv1.ap().opt()]),
          r=[rm("snd1")], w=[rm("rcv1")], kind="cc")
    S_.op("pool", lambda e: e.collective_compute("AllGather", ALU.bypass,
                                                 replica_groups=[[0, 1, 2, 3], [4, 5, 6, 7]],
                                                 ins=[k.d_snd.ap().opt()], outs=[k.d_rcv.ap().opt()]),
          r=[rm("snd")], w=[rm("rcv")], kind="cc")


def _phaseB_halo(k, S_, l):
    rm, stt, ts = k.rm, k.stt, k.ts
    rcv = k.d_rcv.ap().rearrange("(r p) c -> p r c", p=128)
    S_.dma("sp", k.hal_k, rcv[:, :, PC_KAH:PC_KAH + 512], r=[rm("rcv")], w=[rm("halk")])
    S_.dma("sp", k.hal_v, rcv[:, :, PC_VAH:PC_VAH + 256], r=[rm("rcv")], w=[rm("halv")])
    S_.dma("sp", k.hal_z, rcv[:, :, PC_ZH:PC_ZH + 16], r=[rm("rcv")], w=[rm("halz")])
    oh = k.oh

    def select(dst, srcs, ohbase, tmp, r_t, rsrc, rdst):
        ts("dve", tmp, srcs[0], oh[:, ohbase:ohbase + 1], ALU.mult, r=[rsrc, k.r_const], w=[r_t])
        for rr in range(1, 4):
            last = rr == 3
            stt(dst if last else tmp, srcs[rr], oh[:, ohbase + rr:ohbase + rr + 1], tmp, ALU.mult, ALU.add,
                r=[rsrc, k.r_const] + ([] if not last else [r_t]), w=([r_t] if not last else []),
                pw=([rdst] if last else []))

    hk = lambda rr, fl: k.hal_k[:, rr, :].rearrange("p (g f t) -> p g f t", g=2, f=2)[:, :, fl, :]
    t3 = lambda i: k.TMP[i][:, 0:256].rearrange("p (g t) -> p g t", g=2)
    select(k.kT_A[:, :, 0:128], [hk(rr, 1) for rr in range(4)], 0, t3(0), k.r_tmp[0], rm("halk"), rm("kTA"))
    select(k.kT_A[:, :, 2176:2304], [hk(rr, 0) for rr in range(4)], 4, t3(1), k.r_tmp[1], rm("halk"), rm("kTA"))
    hv = lambda rr, fl: k.hal_v[:, rr, fl * 128:(fl + 1) * 128].rearrange("p (g d) -> p g d", g=2)
    t4 = lambda i: k.TMP[i][:, 0:128].rearrange("p (g d) -> p g d", g=2)
    select(k.V_A[:, 0, :, 64:128], [hv(rr, 1) for rr in range(4)], 0, t4(2), k.r_tmp[2], rm("halv"), rm("VA"))
    select(k.V_A[:, 17, :, 64:128], [hv(rr, 0) for rr in range(4)], 4, t4(3), k.r_tmp[3], rm("halv"), rm("VA"))
    k.tt("dve", k.hzf[:], k.hal_z[:, :, 0:8], k.hal_z[:, :, 8:16], ALU.add, r=[rm("halz")], w=[rm("hzf")])
    hz = lambda rr, fl: k.hzf[:, rr, :].rearrange("p (c f) -> p c f", f=2)[:, :, fl]
    select(k.zhalo[:, :, 0], [hz(rr, 1) for rr in range(4)], 0, k.TMP[0][:, 256:260], k.r_tmp[0], rm("hzf"), rm("zhalo"))
    select(k.zhalo[:, :, 1], [hz(rr, 0) for rr in range(4)], 4, k.TMP[1][:, 256:260], k.r_tmp[1], rm("hzf"), rm("zhalo"))


def _phaseB_qb(k, S_, l, nchunks):
    PS, PB, hxT = k.PS, k.PB, k.hxT
    mm, act, tt, stt, cp, rm = k.mm, k.act, k.tt, k.stt, k.cp, k.rm
    TMP, r_tmp = k.TMP, k.r_tmp
    S_.dma("pool", k.wqb, k.d_wqb.ap()[l].rearrange("(j p) c -> p j c", p=128), w=[rm("wqb")])
    wql, r_wql = k.load_w(k.win_src(l, "QL", 0, 384), 384)
    for c in range(nchunks):
        c0, n = CHUNKS[c]
        _load_tabs(k, S_, c0, n, which=(2, 3))
        bq = [k.bank(), k.bank(), k.bank()]
        for j in range(3):
            for kk in range(8):
                mm(PS[:, bq[j], 0:n], wql[:, kk, j * 128:(j + 1) * 128], hxT[:, kk, c0:c0 + n], kk == 0, kk == 7,
                   r=[r_wql, k.r_hx[c]], pw=[PB[bq[j]]])
        _rstd_bcast(k, S_, bq, 3, n, 1.0 / 384, TMP[2], r_tmp[2])
        for j in range(3):
            stt(k.qnb[:, j, 0:n], PS[:, bq[j], 0:n], k.gqaT[:, l, j:j + 1], TMP[2][:, 0:n], ALU.mult, ALU.mult,
                r=[PB[bq[j]], r_tmp[2]], pw=[rm("qnb")])
        for h in range(8):
            b1, b2 = k.bank(), k.bank()
            for (bb, co) in ((b1, h * 192), (b2, h * 192 + 96)):
                for j in range(3):
                    mm(PS[0:96, bb, 0:n], k.wqb[:, j, co:co + 96], k.qnb[:, j, 0:n], j == 0, j == 2,
                       r=[rm("wqb"), rm("qnb")], pw=[PB[bb]])
            cp("act", k.qT_B[0:64, h, c0:c0 + n], PS[0:64, b1, 0:n], r=[PB[b1]], pw=[rm("qTB")])
            ta, tb = (0, 1) if h % 2 == 0 else (3, 5)
            tt("dve", TMP[ta][64:96, 0:n], PS[64:96, b1, 0:n], k.tabs[2][64:96, 0:n], ALU.mult,
               r=[PB[b1], k.r_tab[2]], w=[r_tmp[ta]])
            tt("dve", TMP[tb][64:96, 0:n], PS[64:96, b2, 0:n], k.tabs[3][64:96, 0:n], ALU.mult,
               r=[PB[b2], k.r_tab[3]], w=[r_tmp[tb]])
            tt("dve", k.qT_B[64:96, h, c0:c0 + n], TMP[ta][64:96, 0:n], TMP[tb][64:96, 0:n], ALU.add,
               r=[r_tmp[ta], r_tmp[tb]], pw=[rm("qTB")])


def _attn_core(k, S_, items, scale):
    PS, PB = k.PS, k.PB
    LOOK = 2
    pend = []
    for it in items:
        if it.get("before") is not None:
            it["before"]()
        sb0, nb = it["sb"]
        pb_i = k.sctr % 4
        k.sctr += 1
        pws = [PB[sb0 + i] for i in range(nb)]
        first = True
        if it.get("mask") is not None:
            for (o_ap, m_ap) in it["mask"]:
                k.mm(o_ap, k.identb[:], m_ap, True, False, r=[k.r_const], pw=pws)
            first = False
        for (o_ap, lhsT, rhs) in it["s_mms"]:
            k.mm(o_ap, lhsT, rhs, first, True, r=it["r"], pw=pws)
        pv_ = it["p_view"](k.pbuf[pb_i])
        k.act(pv_, it["s_view"], AF.Exp, r=pws, w=[k.r_pbuf[pb_i]], scale=scale)

        def mk(it=it, pb_i=pb_i):
            def f():
                st = it["start"]
                npv = len(it["pv"])
                for pi, (o_ap, lhsT, rhs_fn) in enumerate(it["pv"]):
                    k.mm(o_ap, lhsT, rhs_fn(k.pbuf[pb_i]), st, it["stop"] and pi == npv - 1,
                         r=it["rv"] + [k.r_pbuf[pb_i]], pw=[PB[it["ob"]]])
                    st = False
                if it.get("after_pv") is not None:
                    it["after_pv"]()
            return f
        pend.append(mk())
        if len(pend) > LOOK:
            pend.pop(0)()
        if it.get("after") is not None:
            it["after"]()
    while pend:
        pend.pop(0)()


def _phaseC_mla(k, S_, l, with_ctx_q):
    PS, PB = k.PS, k.PB
    mm, act, tt, cp, rm = k.mm, k.act, k.tt, k.cp, k.rm
    S_.dma("pool", k.wkvb, k.d_wkvb.ap()[l].rearrange("(j p) c -> p j c", p=128), w=[rm("wkvb")])
    rcv = k.d_rcv.ap()
    rcv1 = k.d_rcv1.ap()
    EB = 7
    kchunks = [(rr, cc) for rr in range(4) for cc in range(4)] + [None]
    rec = k.TMP[0]
    def mk_head(h):
        def load_kv(idx):
            if idx >= len(kchunks) or kchunks[idx] is None:
                return
            rr, cc = kchunks[idx]
            i = idx % 2
            S_.dma("sp", k.kvbuf[i], rcv1[rr * 128:(rr + 1) * 128, PC_KVN:PC_KVN + 4096].rearrange(
                "p (j t) -> p j t", j=2)[:, :, cc * 512:(cc + 1) * 512], r=[rm("rcv1")], w=[rm("kvbuf%d" % i)])

        def load_kr(idx):
            if idx >= len(kchunks) or kchunks[idx] is None:
                return
            rr, cc = kchunks[idx]
            i = idx % 2
            S_.dma("sp", k.Kbuf[i][64:96, :], rcv[rr * 128 + cc * 32:rr * 128 + cc * 32 + 32, PC_KR:PC_KR + 512],
                   r=[rm("rcv")], pw=[rm("Kbuf%d" % i)])

        def expand_k(idx):
            kc = kchunks[idx]
            i = idx % 2
            src, rs, n, dst, rd = ((k.kvbuf[i], rm("kvbuf%d" % i), 512, k.Kbuf[i], rm("Kbuf%d" % i)) if kc is not None
                                   else (k.kvnc, rm("kvnc"), 256, k.Kc, rm("Kc")))
            for j in range(2):
                mm(PS[0:64, EB, 0:n], k.wkvb[:, j, h * 64:(h + 1) * 64], src[:, j, 0:n], j == 0, j == 1,
                   r=[rm("wkvb"), rs], pw=[PB[EB]])
            cp("dve", dst[0:64, 0:n], PS[0:64, EB, 0:n], r=[PB[EB]], pw=[rd])

        def expand_v(idx):
            kc = kchunks[idx]
            i = idx % 2
            src, rs, nt, dst, rd = ((k.kvbuf[i], rm("kvbuf%d" % i), 4, k.Vbuf[i], rm("Vbuf%d" % i)) if kc is not None
                                    else (k.kvnc, rm("kvnc"), 2, k.Vc, rm("Vc")))
            for t_ in range(nt):
                for j in range(2):
                    mm(PS[:, EB, t_ * 64:(t_ + 1) * 64], src[:, j, t_ * 128:(t_ + 1) * 128],
                       k.wkvb[:, j, 512 + h * 64:512 + (h + 1) * 64], j == 0, j == 1,
                       r=[rm("wkvb"), rs], pw=[PB[EB]])
            cp("dve", dst[:, 0:nt, 64:128], PS[:, EB, 0:nt * 64].rearrange("p (t d) -> p t d", t=nt),
               r=[PB[EB]], pw=[rd])
        return load_kv, load_kr, expand_k, expand_v

    heads = [mk_head(h) for h in range(8)]
    for h in range(8):
        e = h % 2
        vsl = slice(64, 192) if e == 0 else slice(0, 128)
        o_lo, r_lo = (0, 64) if e == 0 else (64, 0)
        load_kv, load_kr, expand_k, expand_v = heads[h]
        nxt = heads[h + 1] if h + 1 < 8 else None
        if h == 0:
            load_kv(0)
            load_kv(1)
            load_kr(0)
            expand_k(0)
            expand_v(0)
        items = []
        for idx, kc in enumerate(kchunks):
            i = idx % 2
            if kc is not None:
                Kt, rK, Vt, rV, nt = k.Kbuf[i], rm("Kbuf%d" % i), k.Vbuf[i], rm("Vbuf%d" % i), 4
            else:
                Kt, rK, Vt, rV, nt = k.Kc, rm("Kc"), k.Vc, rm("Vc"), 2
            cnt = 0
            for qc in range(4):
                for kt in range(nt):
                    sbank = 4 + (len(items) % 3)
                    it = dict(sb=(sbank, 1),
                              s_mms=[(PS[:, sbank, 0:512], Kt[0:96, kt * 128:(kt + 1) * 128],
                                      k.qT_B[0:96, h, qc * 512:(qc + 1) * 512])],
                              s_view=PS[:, sbank, 0:512], p_view=(lambda p: p[:, 0:512]),
                              r=[rK, rm("qTB")],
                              pv=[(PS[:, qc, 0:512], Vt[:, kt, vsl], (lambda p: p[:, 0:512]))], rv=[rV],
                              ob=qc, start=(idx == 0 and kt == 0), stop=(idx == len(kchunks) - 1 and kt == nt - 1))
                    cnt += 1
                    if cnt == 1:
                        def bef(idx=idx):
                            load_kv(idx + 2)
                            load_kr(idx + 1)
                            if idx == len(kchunks) - 1 and nxt is not None:
                                nxt[0](0)
                                nxt[0](1)
                                nxt[1](0)
                        it["before"] = bef
                    if idx + 1 < len(kchunks):
                        if cnt == 2:
                            it["after"] = (lambda idx=idx: expand_k(idx + 1))
                        elif cnt == 6:
                            it["after"] = (lambda idx=idx: expand_v(idx + 1))
                    elif nxt is not None:
                        if cnt == 2:
                            it["after"] = (lambda: nxt[2](0))
                        elif cnt == 6:
                            it["after"] = (lambda: nxt[3](0))
                    items.append(it)
        _attn_core(k, S_, items, SCALE_B)
        for qc in range(4):
            act(rec[o_lo:o_lo + 64, 0:512], PS[r_lo:r_lo + 64, qc, 0:512], AF.Ln, r=[PB[qc]], w=[k.r_tmp[0]])
            act(rec[o_lo:o_lo + 64, 0:512], rec[o_lo:o_lo + 64, 0:512], AF.Exp, r=[], w=[k.r_tmp[0]], scale=-1.0)
            tt("dve", k.Y_B[o_lo:o_lo + 64, h // 2, qc * 512:(qc + 1) * 512], PS[o_lo:o_lo + 64, qc, 0:512],
               rec[o_lo:o_lo + 64, 0:512], ALU.mult, r=[PB[qc], k.r_tmp[0]], pw=[rm("YB")])
        if with_ctx_q:
            items = []
            for kt in range(2):
                sbank = 4 + kt
                items.append(dict(sb=(sbank, 1),
                                  s_mms=[(PS[:, sbank, 0:256], k.Kc[0:96, kt * 128:(kt + 1) * 128],
                                          k.qT_B[0:96, h, 2048:2304])],
                                  s_view=PS[:, sbank, 0:256], p_view=(lambda p: p[:, 0:256]),
                                  r=[rm("Kc"), rm("qTB")],
                                  pv=[(PS[:, 6, 0:256], k.Vc[:, kt, vsl], (lambda p: p[:, 0:256]))], rv=[rm("Vc")],
                                  ob=6, start=(kt == 0), stop=(kt == 1)))
            _attn_core(k, S_, items, SCALE_B)
            act(rec[o_lo:o_lo + 64, 0:256], PS[r_lo:r_lo + 64, 6, 0:256], AF.Ln, r=[PB[6]], w=[k.r_tmp[0]])
            act(rec[o_lo:o_lo + 64, 0:256], rec[o_lo:o_lo + 64, 0:256], AF.Exp, r=[], w=[k.r_tmp[0]], scale=-1.0)
            tt("dve", k.Y_B[o_lo:o_lo + 64, h // 2, 2048:2304], PS[o_lo:o_lo + 64, 6, 0:256],
               rec[o_lo:o_lo + 64, 0:256], ALU.mult, r=[PB[6], k.r_tmp[0]], pw=[rm("YB")])


def _phaseD_A(k, S_, l, nchunks, pre=None):
    PS, PB, hxT = k.PS, k.PB, k.hxT
    mm, act, tt, ts, rm = k.mm, k.act, k.tt, k.ts, k.rm
    TMP, r_tmp = k.TMP, k.r_tmp
    if pre is None:
        pre = _prefetch_A(k, l)
    wq = [pre[0], pre[1]]
    wza, r_wza = pre[2]
    gctr = [0]
    for c in range(nchunks):
        c0, n = CHUNKS[c]
        _load_tabs(k, S_, c0, n, which=(0, 1))
        for j in range(4):
            wt, rw = wq[j // 2]
            base = (j % 2) * 256
            bq, br = k.bank(), k.bank()
            for (bb, co) in ((bq, base), (br, base + 128)):
                for kk in range(8):
                    mm(PS[:, bb, 0:n], wt[:, kk, co:co + 128], hxT[:, kk, c0:c0 + n], kk == 0, kk == 7,
                       r=[rw, k.r_hx[c]], pw=[PB[bb]])
            tt("dve", TMP[0][:, 0:n], PS[:, bq, 0:n], k.tabs[0][:, 0:n], ALU.mult, r=[PB[bq], k.r_tab[0]], w=[r_tmp[0]])
            tt("dve", TMP[1][:, 0:n], PS[:, br, 0:n], k.tabs[1][:, 0:n], ALU.mult, r=[PB[br], k.r_tab[1]], w=[r_tmp[1]])
            tt("dve", k.qA[:, j, 0:n], TMP[0][:, 0:n], TMP[1][:, 0:n], ALU.add, r=[r_tmp[0], r_tmp[1]], pw=[rm("qA")])
        for jt in range(4):
            bb = k.bank()
            for kk in range(8):
                mm(PS[:, bb, 0:n], wza[:, kk, jt * 128:(jt + 1) * 128], hxT[:, kk, c0:c0 + n], kk == 0, kk == 7,
                   r=[r_wza, k.r_hx[c]], pw=[PB[bb]])
            act(k.sza[:, jt, 0:n], PS[:, bb, 0:n], AF.Silu, r=[PB[bb]], pw=[rm("sza")])
        items = []
        groups = []
        for qb in range(n // 128):
            for g in range(2):
                if c < 4:
                    qbg = c * 4 + qb
                    tiles = [(qbg * 128, qbg, 2 if qbg == 0 else 0), ((qbg + 1) * 128, qbg + 1, None),
                             ((qbg + 2) * 128, qbg + 2, 3 if qbg == 15 else 1), (2304, 18, None), (2432, 19, None)]
                else:
                    tiles = [(2304, 18, None), (2432, 19, None)]
                gi = gctr[0]
                gctr[0] += 1
                ob = gi % 4
                sb0 = 4 + 2 * (gi % 2)
                groups.append((qb, g, ob, gi))
                for ti, (koff, vt, mi) in enumerate(tiles):
                    sb0 = 4 + 2 * ((len(items)) % 2)
                    it = dict(sb=(sb0, 2),
                              s_mms=[(PS[:, sb0 + e, 0:256], k.kT_A[e * 64:(e + 1) * 64, g, koff:koff + 128],
                                      k.qA[e * 64:(e + 1) * 64, 2 * g:2 * g + 2, qb * 128:(qb + 1) * 128])
                                     for e in range(2)],
                              mask=(None if mi is None else [(PS[:, sb0 + e, 0:256], k.masks[:, mi, 0:256])
                                                             for e in range(2)]),
                              s_view=PS[:, sb0:sb0 + 2, 0:256],
                              p_view=(lambda p: p[:, 0:512].rearrange("p (e c) -> p e c", e=2)),
                              r=[rm("kTA"), rm("qA")],
                              pv=[(PS[:, ob, 0:256], k.V_A[:, vt, g, 64:192], (lambda p: p[:, 0:256])),
                                  (PS[:, ob, 256:512], k.V_A[:, vt, g, 0:128], (lambda p: p[:, 256:512]))],
                              rv=[rm("VA")], ob=ob, start=(ti == 0), stop=(ti == len(tiles) - 1))
                    items.append(it)

        def normalize(qb, g, ob, gi, c0=c0):
            ta, tb = (TMP[0], TMP[1]) if gi % 2 == 0 else (TMP[2], TMP[3])
            ra, rb = (r_tmp[0], r_tmp[1]) if gi % 2 == 0 else (r_tmp[2], r_tmp[3])
            tok0 = c0 + qb * 128
            for e in range(2):
                o_lo, r_lo = (0, 64) if e == 0 else (64, 0)
                cs_ = slice(e * 256, (e + 1) * 256)
                for jj in range(2):
                    h = 4 * g + 2 * jj + e
                    cj = slice(e * 256 + jj * 128, e * 256 + (jj + 1) * 128)
                    ts("dve", ta[o_lo:o_lo + 64, cj], PS[r_lo:r_lo + 64, ob, cj], k.esink[r_lo:r_lo + 64, l, h:h + 1],
                       ALU.add, r=[PB[ob], rm("esink")], pw=[ra])
                act(ta[o_lo:o_lo + 64, cs_], ta[o_lo:o_lo + 64, cs_], AF.Ln, r=[], w=[ra])
                act(ta[o_lo:o_lo + 64, cs_], ta[o_lo:o_lo + 64, cs_], AF.Exp, r=[], w=[ra], scale=-1.0)
                tt("dve", tb[o_lo:o_lo + 64, cs_], PS[o_lo:o_lo + 64, ob, cs_], ta[o_lo:o_lo + 64, cs_], ALU.mult,
                   r=[PB[ob], ra], pw=[rb])
                tt("dve", k.Y_A[o_lo:o_lo + 64, 2 * g:2 * g + 2, tok0:tok0 + 128],
                   tb[o_lo:o_lo + 64, cs_].rearrange("p (j q) -> p j q", j=2),
                   k.sza[o_lo:o_lo + 64, 2 * g:2 * g + 2, qb * 128:(qb + 1) * 128], ALU.mult,
                   r=[rb, rm("sza")], pw=[rm("YA")])

        per = len(items) // len(groups)
        for gidx in range(len(groups)):
            items[gidx * per + per - 1]["after_pv"] = (lambda a=groups[gidx]: normalize(*a))
        _attn_core(k, S_, items, SCALE_A)


def _prefetch_A(k, l):
    return [k.load_w(k.win_src(l, "QA", 0, 512), 512), k.load_w(k.win_src(l, "QA", 512, 512), 512),
            k.load_w(k.win_src(l, "ZA", 0, 512), 512)]


def _phaseD_zb(k, S_, l, nchunks):
    PS, PB, hxT = k.PS, k.PB, k.hxT
    wzb, r_wzb = k.load_w(k.win_src(l, "ZB", 0, 512), 512)
    for c in range(nchunks):
        c0, n = CHUNKS[c]
        for jt in range(4):
            bb = k.bank()
            for kk in range(8):
                k.mm(PS[:, bb, 0:n], wzb[:, kk, jt * 128:(jt + 1) * 128], hxT[:, kk, c0:c0 + n], kk == 0, kk == 7,
                     r=[r_wzb, k.r_hx[c]], pw=[PB[bb]])
            sq = k.sqb[:, jt % 3, 0:n]
            k.act(sq, PS[:, bb, 0:n], AF.Silu, r=[PB[bb]], w=[k.rm("sqb%d" % (jt % 3))])
            k.tt("dve", k.Y_B[:, jt, c0:c0 + n], k.Y_B[:, jt, c0:c0 + n], sq, ALU.mult,
                 r=[k.rm("sqb%d" % (jt % 3))], pw=[k.rm("YB")])


def _phaseD_C(k, S_, l, nchunks):
    PS, PB, hxT = k.PS, k.PB, k.hxT
    mm, act, tt, ts, stt, cp, rm = k.mm, k.act, k.tt, k.ts, k.stt, k.cp, k.rm
    TMP, r_tmp = k.TMP, k.r_tmp
    zrow = k.zrow
    for ct in range(4):
        wc, r_wc = k.load_w(k.win_src(l, "C", ct * 512, 512), 512)
        cp("dve", zrow[:, 0:1], k.zhalo[:, ct, 0:1], r=[rm("zhalo")], pw=[rm("zrow")])
        cp("dve", zrow[:, 2049:2050], k.zhalo[:, ct, 1:2], r=[rm("zhalo")], pw=[rm("zrow")])
        S_.op("dve", lambda e: e.memset(zrow[:, 2050:2051], 0.0), pw=[rm("zrow")])
        S_.op("dve", lambda e: e.memset(zrow[:, 2307:2308], 0.0), pw=[rm("zrow")])
        zo = lambda c: (1 + CHUNKS[c][0]) if c < 4 else 2051
        for c in range(nchunks):
            c0, n = CHUNKS[c]
            bcc, buc = k.bank(), k.bank()
            for (bb, co) in ((bcc, 128), (buc, 256)):
                for kk in range(8):
                    mm(PS[:, bb, 0:n], wc[:, kk, co:co + 128], hxT[:, kk, c0:c0 + n], kk == 0, kk == 7,
                       r=[r_wc, k.r_hx[c]], pw=[PB[bb]])
            cp("act", TMP[0][:, 0:n], PS[:, bcc, 0:n], r=[PB[bcc]], w=[r_tmp[0]])
            tt("dve", zrow[:, zo(c):zo(c) + n], TMP[0][:, 0:n], PS[:, buc, 0:n], ALU.mult,
               r=[r_tmp[0], PB[buc]], pw=[rm("zrow")])
        for c in range(nchunks):
            c0, n = CHUNKS[c]
            z0 = zo(c)
            bbc, bzc = k.bank(), k.bank()
            for (bb, co) in ((bbc, 0), (bzc, 384)):
                for kk in range(8):
                    mm(PS[:, bb, 0:n], wc[:, kk, co:co + 128], hxT[:, kk, c0:c0 + n], kk == 0, kk == 7,
                       r=[r_wc, k.r_hx[c]], pw=[PB[bb]])
            act(k.sqb[:, 0, 0:n], PS[:, bzc, 0:n], AF.Silu, r=[PB[bzc]], w=[rm("sqb0")])
            ts("dve", TMP[1][:, 0:n], zrow[:, z0 - 1:z0 - 1 + n], k.convw[:, l, ct, 0:1], ALU.mult,
               r=[rm("zrow")], w=[r_tmp[1]])
            stt(TMP[1][:, 0:n], zrow[:, z0:z0 + n], k.convw[:, l, ct, 1:2], TMP[1][:, 0:n], ALU.mult, ALU.add,
                r=[rm("zrow")], w=[r_tmp[1]])
            stt(TMP[1][:, 0:n], zrow[:, z0 + 1:z0 + 1 + n], k.convw[:, l, ct, 2:3], TMP[1][:, 0:n], ALU.mult, ALU.add,
                r=[rm("zrow")], w=[r_tmp[1]])
            tt("dve", TMP[2][:, 0:n], TMP[1][:, 0:n], PS[:, bbc, 0:n], ALU.mult, r=[r_tmp[1], PB[bbc]], w=[r_tmp[2]])
            tt("dve", k.Y_C[:, ct, c0:c0 + n], TMP[2][:, 0:n], k.sqb[:, 0, 0:n], ALU.mult,
               r=[r_tmp[2], rm("sqb0")], pw=[rm("YC")])


def _phaseE_merge(k, S_, l, nchunks):
    PS, PB, hxT = k.PS, k.PB, k.hxT
    mm, act, tt, rm = k.mm, k.act, k.tt, k.rm
    TMP, r_tmp = k.TMP, k.r_tmp
    Y = [k.Y_A, k.Y_B, k.Y_C]
    rY = [rm("YA"), rm("YB"), rm("YC")]
    for mt in range(8):
        wg, r_wg = k.load_w(k.win_src(l, "G", mt * 384, 384), 384)
        wbrb = k.wbrb2[mt % 2]
        r_wbrb = rm("wbrb%d" % (mt % 2))
        S_.dma("pool", wbrb[:], k.d_wbr.ap()[l].rearrange("b (kc p) c -> p b kc c", p=128)[:, :, :, mt * 128:(mt + 1) * 128],
               w=[r_wbrb])
        for c in range(nchunks):
            c0, n = CHUNKS[c]
            bp = [k.bank() for _ in range(3)]
            bg = [k.bank() for _ in range(3)]
            for br in range(3):
                for kc in range(4):
                    mm(PS[:, bp[br], 0:n], wbrb[:, br, kc, :], Y[br][:, kc, c0:c0 + n], kc == 0, kc == 3,
                       r=[r_wbrb, rY[br]], pw=[PB[bp[br]]])
            for br in range(3):
                for kk in range(8):
                    mm(PS[:, bg[br], 0:n], wg[:, kk, br * 128:(br + 1) * 128], hxT[:, kk, c0:c0 + n], kk == 0, kk == 7,
                       r=[r_wg, k.r_hx[c]], pw=[PB[bg[br]]])
            for br in range(3):
                act(TMP[br][:, 0:n], PS[:, bg[br], 0:n], AF.Sigmoid, r=[PB[bg[br]]], w=[r_tmp[br]])
            for br in range(3):
                tt("dve", TMP[br][:, 0:n], TMP[br][:, 0:n], PS[:, bp[br], 0:n], ALU.mult, r=[PB[bp[br]]], w=[r_tmp[br]])
            tt("dve", TMP[0][:, 0:n], TMP[0][:, 0:n], TMP[1][:, 0:n], ALU.add, r=[r_tmp[1]], w=[r_tmp[0]])
            tt("dve", k.mT[:, mt, c0:c0 + n], TMP[0][:, 0:n], TMP[2][:, 0:n], ALU.add, r=[r_tmp[0], r_tmp[2]],
               pw=[rm("mT")])


def _phaseF_out(k, S_, l, ntiles):
    PS, PB = k.PS, k.PB
    mm, act, tt, ts, stt, rm = k.mm, k.act, k.tt, k.ts, k.stt, k.rm
    TMP, r_tmp = k.TMP, k.r_tmp
    last = (l == DEPTH - 1)
    wsrc = k.d_wmod.ap()[l].rearrange("(k p) c -> p k c", p=128)
    scs3 = k.scs[:].rearrange("p (k s) -> p k s", s=2)
    for n_ in range(2):
        wt, rw = k.load_w(wsrc[:, :, 2048 + n_ * 512:2048 + n_ * 512 + 512], 512)
        bg = k.bank()
        for kk in range(8):
            mm(PS[0:2, bg, 0:512], scs3[:, kk, :], wt[:, kk, :], kk == 0, kk == 7, r=[rw, rm("scs")], pw=[PB[bg]])
        S_.dma("sp", TMP[3][0:2, 0:512], k.d_bmodg.ap()[:, l, n_ * 512:(n_ + 1) * 512], w=[r_tmp[3]])
        tt("dve", TMP[2][0:2, 0:512], PS[0:2, bg, 0:512], TMP[3][0:2, 0:512], ALU.add, r=[PB[bg], r_tmp[3]], w=[r_tmp[2]])
        for which, Gt, rG in ((0, k.G_x, rm("Gx")), (1, k.G_c, rm("Gc"))):
            if which == 1 and ntiles <= 16:
                continue
            hs = slice(n_ * 512, (n_ + 1) * 512)
            S_.dma("sp", Gt[:, hs], k.d_gpostb.ap()[l][:, hs], pw=[rG])
            bb = k.bank()
            mm(PS[:, bb, 0:512], k.sel[0:2, which * 128:(which + 1) * 128], TMP[2][0:2, 0:512],
               True, True, r=[r_tmp[2], k.r_const], pw=[PB[bb]])
            tt("dve", Gt[:, hs], Gt[:, hs], PS[:, bb, 0:512], ALU.mult, r=[PB[bb], rG], pw=[rG])
    wo0, r_wo0 = k.load_w(k.d_wo.ap()[l].rearrange("(k p) c -> p k c", p=128)[:, :, 0:512], 512)
    wo1, r_wo1 = k.load_w(k.d_wo.ap()[l].rearrange("(k p) c -> p k c", p=128)[:, :, 512:1024], 512)
    wo = [(wo0, r_wo0), (wo1, r_wo1)]
    sm = k.small
    def load_xold(i_):
        if i_ >= ntiles:
            return
        if i_ < 16:
            src_ = (k.d_x.ap() if l == 0 else k.d_x1.ap())[i_ * 128:(i_ + 1) * 128, :]
        else:
            src_ = k.d_ctx.ap()[(i_ - 16) * 128:(i_ - 15) * 128, :]
        S_.dma("sp", k.xoldb[i_ % 3], src_, r=([rm("x1d")] if (l > 0 and i_ < 16) else []), w=[rm("xold%d" % (i_ % 3))])

    for i in range(ntiles):
        j = i % 2
        isx = i < 16
        Gt, rG = (k.G_x, rm("Gx")) if isx else (k.G_c, rm("Gc"))
        p = i % 2
        xold = k.xoldb[i % 3]
        r_xold = rm("xold%d" % (i % 3))
        rs = lambda n_: rm("%s%d" % (n_, p))
        q0 = 16 * p + 4
        if i == 0:
            load_xold(0)
            load_xold(1)
        load_xold(i + 2)
        bo = [k.bank(), k.bank()]
        for hf in range(2):
            for kk in range(8):
                mm(PS[:, bo[hf], 0:512], k.mT[:, kk, i * 128:(i + 1) * 128], wo[hf][0][:, kk, :], kk == 0, kk == 7,
                   r=[wo[hf][1], rm("mT")], pw=[PB[bo[hf]]])
        for hf in range(2):
            act(k.junk2[p][:, hf * 512:(hf + 1) * 512], PS[:, bo[hf], 0:512], AF.Square, r=[PB[bo[hf]]],
                w=[rs("junkf%d" % hf), rs("ssq2%d" % hf)], accum_out=sm[:, q0 + hf:q0 + hf + 1])
        tt("dve", sm[:, q0 + 2:q0 + 3], sm[:, q0:q0 + 1], sm[:, q0 + 1:q0 + 2], ALU.add, r=[rs("ssq20"), rs("ssq21")],
           w=[rs("ssq2")])
        act(sm[:, q0 + 3:q0 + 4], sm[:, q0 + 2:q0 + 3], AF.Ln, r=[rs("ssq2")], w=[rs("ln2")], scale=1.0 / D,
            bias=k.epsc[:, 0:1])
        act(sm[:, q0 + 4:q0 + 5], sm[:, q0 + 3:q0 + 4], AF.Exp, r=[rs("ln2")], w=[rs("rstd2")], scale=-0.5)
        for hf in range(2):
            hs = slice(hf * 512, (hf + 1) * 512)
            tmpi = 2 * p + hf
            stt(TMP[tmpi][:, 0:512], PS[:, bo[hf], 0:512], sm[:, q0 + 4:q0 + 5], Gt[:, hs], ALU.mult, ALU.mult,
                r=[PB[bo[hf]], rs("rstd2"), rG], w=[r_tmp[tmpi]])
            tt("dve", k.xt[j][:, hs], TMP[tmpi][:, 0:512], xold[:, hs], ALU.add, r=[r_tmp[tmpi], r_xold],
               pw=[k.r_xt[j]])
        if isx:
            dst = (k.d_y.ap() if last else k.d_x1.ap())[i * 128:(i + 1) * 128, :]
            S_.dma("sp", dst, k.xt[j], r=[k.r_xt[j]], pw=[rm("yd") if last else rm("x1d")])
        if not last:
            _stage1_tile(k, S_, l + 1, i, k.xt[j], k.r_xt[j])


def _program(k, S_, upto):
    k.sctr = 0
    k.r_pbuf = [Res("pbuf%d" % i) for i in range(4)]
    if upto <= 0:
        return
    if upto == 1 and DBG_TILES:
        k.ntilesA = DBG_TILES
    _phaseA(k, S_, 0)
    k.mod_layer(1)
    k.dump("hxT0", k.hxT[:], [128, 8 * TT], BF16)
    if upto <= 1:
        return
    for l in range(DEPTH):
        nch = 5 if l == 0 else 4
        S_.barrier()
        _phaseB_kv(k, S_, l)
        _phaseB_qb(k, S_, l, nch)
        _phaseB_halo(k, S_, l)
        if l == 0:
            k.dump("kTA", k.kT_A, [128, 5120], BF16)
            k.dump("VA", k.V_A, [128, 7680], BF16)
            k.dump("qTB", k.qT_B[0:96])
            k.dump("kvnc", k.kvnc, [128, 512], BF16)
            k.dump("zhalo", k.zhalo[:], [128, 8], F32)
        if upto <= 2 and l == 0:
            return
        S_.barrier()
        preA = _prefetch_A(k, l)
        _phaseC_mla(k, S_, l, with_ctx_q=(l == 0))
        if upto <= 3 and l == 0:
            k.dump("YB", k.Y_B, [128, 4 * TT], BF16)
            return
        S_.barrier()
        _phaseD_A(k, S_, l, nch, pre=preA)
        _phaseD_zb(k, S_, l, nch)
        if upto <= 4 and l == 0:
            k.dump("YB", k.Y_B, [128, 4 * TT], BF16)
            k.dump("YA", k.Y_A, [128, 4 * TT], BF16)
            return
        S_.barrier()
        _phaseD_C(k, S_, l, nch)
        if upto <= 5 and l == 0:
            k.dump("YC", k.Y_C, [128, 4 * TT], BF16)
            return
        S_.barrier()
        _phaseE_merge(k, S_, l, nch)
        if upto <= 6 and l == 0:
            k.dump("mT", k.mT, [128, 8 * TT], BF16)
            return
        S_.barrier()
        _phaseF_out(k, S_, l, 18 if l == 0 else 16)
        if upto <= 7 and l == 0:
            k.dump("hxT1", k.hxT[:], [128, 8 * TT], BF16)
            return


def kernel(**inputs):
    in_maps = prepare_inputs(**inputs)
    nc = build()
    res = run_bass_kernel_spmd(nc, in_maps, core_ids=list(range(NCORE)))
    out = np.zeros((2, S, D), np.float32)
    for core in range(NCORE):
        b, r = divmod(core, 4)
        out[b, r * T:(r + 1) * T] = np.asarray(res.results[core]["y"], np.float32)
    return out
```

```python
import os
import numpy as np
import ml_dtypes
import concourse.bass as bass
import concourse.mybir as mybir
from concourse.bass_utils import run_bass_kernel_spmd

F32 = mybir.dt.float32
BF16 = mybir.dt.bfloat16
AF = mybir.ActivationFunctionType
ALU = mybir.AluOpType
AP = bass.AP

D = 1024
S = 8192
L = 256
DEPTH = 2
NCORE = 8
T = 2048
TT = T + L
GRID_W = 64
EPS = 1e-6
SCALE_A = 64 ** -0.5
SCALE_B = 96 ** -0.5
NEG = -30000.0
DBG_TILES = 0
CHUNKS = [(0, 512), (512, 512), (1024, 512), (1536, 512), (2048, 256)]

O_QA, O_KA, O_VA, O_ZA = 0, 512, 640, 768
O_QL, O_KVL, O_KR, O_ZB = 1280, 1664, 1920, 1952
O_BC, O_CC, O_UC, O_ZC = 2464, 2976, 3488, 4000
O_GA, O_GB, O_GC = 4512, 5536, 6560


def _win_perm():
    cols = []
    off = {}

    def add(name, idx):
        off[name] = len(cols)
        cols.extend(list(idx))

    r64 = np.arange(64)
    rot64 = np.concatenate([r64[32:], r64[:32]])
    r32 = np.arange(32)
    rot32 = np.concatenate([r32[16:], r32[:16]])
    ka = []
    for g in range(2):
        base = O_KA + g * 64
        ka += list(base + r64) + list(base + r64)
        ka += list(base + rot64) + list(base + rot64)
    add("KA", ka)
    add("VKK", list(O_VA + np.arange(128)) + list(O_KVL + np.arange(256))
        + list(O_KR + r32) + list(O_KR + rot32))
    add("ZH", list(O_CC + np.arange(512)) + list(O_UC + np.arange(512)))
    add("QL", list(O_QL + np.arange(384)))
    qa = []
    for j in range(4):
        for e in range(2):
            qa += list(O_QA + (2 * j + e) * 64 + r64)
        for e in range(2):
            qa += list(O_QA + (2 * j + e) * 64 + rot64)
    add("QA", qa)
    add("ZA", list(O_ZA + np.arange(512)))
    add("ZB", list(O_ZB + np.arange(512)))
    cc = []
    for ct in range(4):
        for o in (O_BC, O_CC, O_UC, O_ZC):
            cc += list(o + ct * 128 + np.arange(128))
    add("C", cc)
    gg = []
    for mt in range(8):
        for o in (O_GA, O_GB, O_GC):
            gg += list(o + mt * 128 + np.arange(128))
    add("G", gg)
    return np.asarray(cols, dtype=np.int64), off


WIN_PERM, WOFF = _win_perm()
NCW = len(WIN_PERM)

PC_KVN = 0
PC_N1 = 4096
PC_KR = 0
PC_KAH = 512
PC_VAH = 1024
PC_ZH = 1280
PC_N2 = 1296


class Res:
    __slots__ = ("name", "writers", "readers", "excl", "last")

    def __init__(self, name, excl=False):
        self.name = name
        self.writers = []
        self.readers = []
        self.excl = excl
        self.last = {}


class Op:
    __slots__ = ("eng", "fn", "deps", "kind", "signaled", "sem", "val", "idx")


def _prune(lst):
    out = []
    seen = set()
    for o in reversed(lst):
        if o.kind != "c":
            out.append(o)
        elif o.eng not in seen:
            seen.add(o.eng)
            out.append(o)
    out.reverse()
    return out


class Sched:
    ENG = ("pe", "act", "dve", "pool", "sp")

    def __init__(self):
        self.prog = {e: [] for e in self.ENG}
        self.all = []
        self.pending_dma = []

    def op(self, eng, fn, r=(), w=(), pw=(), kind="c", extra=()):
        o = Op()
        o.eng, o.fn, o.kind, o.signaled, o.sem, o.val = eng, fn, kind, False, None, 0
        o.idx = len(self.all)
        deps = set(x for x in extra if x is not None)
        allres = list(r) + list(w) + list(pw)
        r = [x for x in r if not x.excl]
        w = [x for x in w if not x.excl]
        pw = [x for x in pw if not x.excl]
        for res in allres:
            if res.excl:
                for e2, o2 in res.last.items():
                    if e2 != eng:
                        deps.add(o2)
                res.last[eng] = o
        for res in r:
            deps.update(res.writers)
        for res in w:
            deps.update(res.writers)
            deps.update(res.readers)
        for res in pw:
            deps.update(res.readers)
        for res in r:
            res.readers.append(o)
            if len(res.readers) > 12:
                res.readers = _prune(res.readers)
        for res in w:
            res.writers = [o]
            res.readers = []
        for res in pw:
            if res.readers:
                res.writers = [o]
                res.readers = []
            else:
                res.writers.append(o)
                if len(res.writers) > 12:
                    res.writers = _prune(res.writers)
        deps.discard(o)
        o.deps = deps
        self.prog[eng].append(o)
        self.all.append(o)
        if kind != "c":
            self.pending_dma.append(o)
        return o

    def dma(self, eng, out, in_, r=(), w=(), pw=(), extra=(), **kw):
        return self.op(eng, lambda e: e.dma_start(out=out, in_=in_, **kw), r=r, w=w, pw=pw,
                       kind="d", extra=extra)

    def barrier(self):
        last = [self.prog[e][-1] for e in self.ENG if self.prog[e]]
        last += self.pending_dma
        self.pending_dma = []
        for e in self.ENG:
            self.op(e, None, extra=last)

    def emit(self, nc, stack):
        NS = 20
        for o in self.all:
            for d in o.deps:
                if d.kind == "c" and d.eng == "pe" and o.eng == "pe" and o.kind == "c":
                    continue
                d.signaled = True
        esem = {e: stack.enter_context(nc.semaphore("s_" + e)) for e in self.ENG}
        dsem = {e: [stack.enter_context(nc.semaphore("d_%s%d" % (e, i))) for i in range(NS)]
                for e in ("sp", "pool", "act")}
        ccsem = stack.enter_context(nc.semaphore("s_cc"))
        cnt = {e: 0 for e in self.ENG}
        dcnt = {e: 0 for e in dsem}
        dval = {}
        prev = {}
        ccn = 0
        for e in self.ENG:
            for o in self.prog[e]:
                if o.kind == "c":
                    if o.signaled and o.fn is not None:
                        cnt[e] += 1
                        o.sem, o.val = esem[e], cnt[e]
                    elif o.fn is None:
                        o.sem, o.val = None, 0
                elif o.kind == "d":
                    s = dsem[e][dcnt[e] % NS]
                    dcnt[e] += 1
                    prev[o] = dval.get(id(s), 0)
                    dval[id(s)] = prev[o] + 16
                    o.sem, o.val = s, dval[id(s)]
                else:
                    ccn += 1
                    o.sem, o.val = ccsem, ccn
        self.n_inst = {e: len(self.prog[e]) for e in self.ENG}
        block = stack.enter_context(nc.Block())
        sched = self

        def run(e, engine):
            waited = {}
            for o in sched.prog[e]:
                waits = {}
                for d in o.deps:
                    if d.kind == "c" and d.eng == "pe" and o.eng == "pe" and o.kind == "c":
                        continue
                    if d.sem is None:
                        continue
                    k = id(d.sem)
                    if k not in waits or waits[k][1] < d.val:
                        waits[k] = (d.sem, d.val)
                if o.kind == "d" and prev[o] > 0:
                    k = id(o.sem)
                    if k not in waits or waits[k][1] < prev[o]:
                        waits[k] = (o.sem, prev[o])
                for k, (s, v) in waits.items():
                    if waited.get(k, 0) < v:
                        engine.wait_ge(s, v)
                        waited[k] = v
                if o.fn is None:
                    continue
                inst = o.fn(engine)
                if o.kind == "d":
                    inst.then_inc(o.sem, 16)
                elif o.kind == "cc":
                    inst.then_inc(o.sem)
                elif o.signaled:
                    inst.then_inc(o.sem, 1)

        @block.tensor
        def _(te):
            run("pe", te)

        @block.scalar
        def _(sc):
            run("act", sc)

        @block.vector
        def _(ve):
            run("dve", ve)

        @block.gpsimd
        def _(gp):
            run("pool", gp)

        @block.sync
        def _(sy):
            run("sp", sy)


def _rope_tables(rank):
    g = rank * T + np.arange(T)
    row = (g // GRID_W).astype(np.float64)
    col = (g % GRID_W).astype(np.float64)

    def tab(rot_dim):
        axis_dim = rot_dim // 2
        inv = 10000.0 ** (-np.arange(0, axis_dim, 2, dtype=np.float64) / axis_dim)
        ang = np.concatenate([row[:, None] * inv, col[:, None] * inv], axis=-1)
        half = rot_dim // 2
        d = np.arange(rot_dim)
        c = np.cos(ang)[:, d % half].T
        s = np.sin(ang)[:, d % half].T
        s = np.where((d < half)[:, None], -s, s)
        cfull = np.ones((rot_dim, TT)); sfull = np.zeros((rot_dim, TT))
        cfull[:, :T] = c; sfull[:, :T] = s
        return cfull, sfull

    ca, sa = tab(64)
    cb, sb = tab(32)
    cosA = np.concatenate([ca, ca], 0).astype(np.float32)
    sinA = np.concatenate([sa, sa], 0).astype(np.float32)
    cosB = np.zeros((128, TT), np.float32); sinB = np.zeros((128, TT), np.float32)
    cosB[0:32] = cb; cosB[64:96] = cb
    sinB[0:32] = sb; sinB[64:96] = sb
    return cosA, sinA, cosB, sinB


def _masks(rank):
    kk = np.arange(128)[:, None]
    qq = np.arange(128)[None, :]
    mp = np.where(kk >= qq, 0.0, NEG).astype(np.float32)
    mn = np.where(kk <= qq, 0.0, NEG).astype(np.float32)
    allneg = np.full((128, 128), NEG, np.float32)
    m = np.stack([np.tile(mp, (1, 4)), np.tile(mn, (1, 4)),
                  np.tile(mp if rank > 0 else allneg, (1, 4)),
                  np.tile(mn if rank < 3 else allneg, (1, 4))], axis=1)
    return m.astype(ml_dtypes.bfloat16)


def _fm(v, k):
    return np.ascontiguousarray(np.asarray(v, np.float32).reshape(k, 128).T)


def prepare_inputs(x, c, ctx, c_ctx, w_mod, b_mod, g_pre, g_post, w_in, sink,
                   g_qa, w_qb, g_kva, w_kvb, conv_w, w_branch, w_o):
    f = lambda a: np.asarray(a, np.float32)
    x, c, ctx, c_ctx = f(x), f(c), f(ctx), f(c_ctx)
    w_mod, b_mod, g_pre, g_post = f(w_mod), f(b_mod), f(g_pre), f(g_post)
    w_in, sink, g_qa, w_qb, g_kva, w_kvb = f(w_in), f(sink), f(g_qa), f(w_qb), f(g_kva), f(w_kvb)
    conv_w, w_branch, w_o = f(conv_w), f(w_branch), f(w_o)
    shared = {}
    shared["wmod"] = np.ascontiguousarray(w_mod)
    shared["bmodT"] = np.ascontiguousarray(np.stack([_fm(b_mod[l, :2048], 16) for l in range(DEPTH)], 1))
    shared["bmodg"] = np.ascontiguousarray(np.stack([np.stack([b_mod[l, 2048:], b_mod[l, 2048:]], 0)
                                                     for l in range(DEPTH)], 1))
    shared["gpreT"] = np.ascontiguousarray(np.stack([_fm(g_pre[l], 8) for l in range(DEPTH)], 1))
    shared["gpostb"] = np.ascontiguousarray(np.broadcast_to(g_post[:, None, :], (DEPTH, 128, D)))
    shared["win"] = np.ascontiguousarray(w_in[:, :, WIN_PERM])
    shared["sinkb"] = np.ascontiguousarray(np.broadcast_to(sink[None, :, :], (128, DEPTH, 8)))
    shared["gqaT"] = np.ascontiguousarray(np.stack([_fm(g_qa[l], 3) for l in range(DEPTH)], 1))
    shared["gkvaT"] = np.ascontiguousarray(np.stack([_fm(g_kva[l], 2) for l in range(DEPTH)], 1))
    r32 = np.arange(32)
    qcols = []
    for h in range(8):
        base = h * 96
        qcols += list(base + np.arange(96))
        qcols += list(base + np.arange(64)) + list(base + 64 + np.concatenate([r32[16:], r32[:16]]))
    shared["wqb"] = np.ascontiguousarray(w_qb[:, :, np.asarray(qcols)])
    kcols = [h * 128 + i for h in range(8) for i in range(64)]
    vcols = [h * 128 + 64 + i for h in range(8) for i in range(64)]
    shared["wkvb"] = np.ascontiguousarray(w_kvb[:, :, np.asarray(kcols + vcols)])
    cw = np.zeros((128, DEPTH, 4, 3), np.float32)
    for l in range(DEPTH):
        for k in range(3):
            cw[:, l, :, k] = conv_w[l, k].reshape(4, 128).T
    shared["convw"] = cw
    shared["wbr"] = np.ascontiguousarray(w_branch)
    shared["wo"] = np.ascontiguousarray(w_o)
    shared["ident"] = np.eye(128, dtype=np.float32)
    shared["identb"] = np.eye(128, dtype=np.float32).astype(ml_dtypes.bfloat16)
    shared["onesb"] = np.ones((128, 128), np.float32).astype(ml_dtypes.bfloat16)
    sel = np.zeros((2, 2, 128), np.float32)
    sel[0, 0, :] = 1.0
    sel[1, 1, :] = 1.0
    shared["sel"] = sel
    in_maps = []
    for core in range(NCORE):
        b, r = core // 4, core % 4
        m = dict(shared)
        m["x"] = np.ascontiguousarray(x[b, r * T:(r + 1) * T])
        m["ctx"] = np.ascontiguousarray(ctx[b])
        cs = np.zeros((128, 8, 2), np.float32)
        cs[:, :, 0] = _fm(c[b], 8)
        cs[:, :, 1] = _fm(c_ctx, 8)
        m["cs"] = cs.reshape(128, 16)
        cosA, sinA, cosB, sinB = _rope_tables(r)
        m["cosA"], m["sinA"], m["cosB"], m["sinB"] = cosA, sinA, cosB, sinB
        m["masks"] = _masks(r)
        oh = np.zeros((128, 8), np.float32)
        if r > 0:
            oh[:, r - 1] = 1.0
        if r < 3:
            oh[:, 4 + r + 1] = 1.0
        m["oh"] = oh
        in_maps.append(m)
    return in_maps


class _K:
    pass


def build(upto=99, dbg=()):
    from contextlib import ExitStack
    nc = bass.Bass("TRN2", target_bir_lowering=False)
    S_ = Sched()
    k = _K()
    stack = ExitStack()
    with stack:
        _build_body(nc, S_, k, stack, upto, dbg)
        _program(k, S_, upto)
        S_.barrier()
        for (dd, ap_) in k.dumps:
            S_.dma("sp", dd.ap(), ap_)
        S_.barrier()
        S_.emit(nc, stack)
    k.S = S_
    build.last = k
    return nc


def _din(nc, name, shape, dt=F32):
    return nc.dram_tensor(name, list(shape), dt, kind="ExternalInput")


def _build_body(nc, S_, k, stack, upto, dbg):
    sb = lambda name, shape, dt: stack.enter_context(nc.sbuf_tensor("s_" + name, list(shape), dt))
    d_x = _din(nc, "x", [T, D]); d_ctx = _din(nc, "ctx", [L, D]); d_cs = _din(nc, "cs", [128, 16])
    d_wmod = _din(nc, "wmod", [DEPTH, D, 3 * D]); d_bmodT = _din(nc, "bmodT", [128, DEPTH, 16])
    d_bmodg = _din(nc, "bmodg", [2, DEPTH, D]); d_gpreT = _din(nc, "gpreT", [128, DEPTH, 8])
    d_gpostb = _din(nc, "gpostb", [DEPTH, 128, D]); d_win = _din(nc, "win", [DEPTH, D, NCW])
    d_sinkb = _din(nc, "sinkb", [128, DEPTH, 8]); d_gqaT = _din(nc, "gqaT", [128, DEPTH, 3])
    d_gkvaT = _din(nc, "gkvaT", [128, DEPTH, 2]); d_wqb = _din(nc, "wqb", [DEPTH, 384, 1536])
    d_wkvb = _din(nc, "wkvb", [DEPTH, 256, 1024]); d_convw = _din(nc, "convw", [128, DEPTH, 4, 3])
    d_wbr = _din(nc, "wbr", [DEPTH, 3, 512, D]); d_wo = _din(nc, "wo", [DEPTH, D, D])
    d_tab = [_din(nc, n, [128, TT]) for n in ("cosA", "sinA", "cosB", "sinB")]
    d_masks = _din(nc, "masks", [128, 4, 512], BF16); d_oh = _din(nc, "oh", [128, 8])
    d_ident = _din(nc, "ident", [128, 128]); d_identb = _din(nc, "identb", [128, 128], BF16)
    d_onesb = _din(nc, "onesb", [128, 128], BF16); d_sel = _din(nc, "sel", [2, 2, 128])
    d_y = nc.dram_tensor("y", [T, D], F32, kind="ExternalOutput")
    d_snd1 = nc.dram_tensor("snd1", [128, PC_N1], BF16)
    d_rcv1 = nc.dram_tensor("rcv1", [512, PC_N1], BF16)
    d_snd = nc.dram_tensor("snd2", [128, PC_N2], BF16)
    d_rcv = nc.dram_tensor("rcv2", [512, PC_N2], BF16)
    d_x1 = nc.dram_tensor("x1", [T, D], F32)

    ident = sb("ident", [128, 128], F32); identb = sb("identb", [128, 128], BF16)
    onesb = sb("onesb", [128, 128], BF16); sel = sb("sel", [2, 256], F32)
    masks = sb("masksb", [128, 4, 512], BF16); oh = sb("oh", [128, 8], F32)
    cs = sb("cs", [128, 16], F32); scs = sb("scs", [128, 16], BF16)
    bmodT = sb("bmodT", [128, DEPTH, 16], F32)
    gpreT = sb("gpreT", [128, DEPTH, 8], F32); sinkb = sb("sinkb", [128, DEPTH, 8], F32)
    esink = sb("esink", [128, DEPTH, 8], F32)
    gqaT = sb("gqaT", [128, DEPTH, 3], F32); gkvaT = sb("gkvaT", [128, DEPTH, 2], F32)
    convw = sb("convw", [128, DEPTH, 4, 3], F32)
    modT = sb("modT", [128, 16, 2], F32)
    Amod = sb("Amod", [128, DEPTH, 8, 2], F32); Bmod = sb("Bmod", [128, DEPTH, 8, 2], F32)
    small = sb("small", [128, 32], F32)
    epsc = sb("epsc", [128, 1], F32)
    hxT = sb("hxT", [128, 8, TT], BF16)
    R1 = sb("R1", [128, 9216], BF16)
    R2 = sb("R2", [128, 18432], BF16)
    R3 = sb("R3", [128, 20736], BF16)
    WB = [sb("wb%d" % i, [128, 8, 512], BF16) for i in range(3)]
    wbrb2 = [sb("wbrb%d" % i, [128, 3, 4, 128], BF16) for i in range(2)]
    TMP = [sb("tmp%d" % i, [128, 512], F32) for i in range(8)]
    sqb = sb("sqb", [128, 3, 512], BF16); qnb = sb("qnb", [128, 3, 512], BF16)
    junk = sqb[:, 0:2, :].rearrange("p a b -> p (a b)")
    tabs = [sb("tab%d" % i, [128, 512], F32) for i in range(4)]
    kvst = sb("kvst", [128, 2, 512], BF16); krst = sb("krst", [32, 512], BF16)
    zh = sb("zh", [128, 8], F32); zhalo = sb("zhalo", [128, 4, 2], F32)
    zhb = sb("zhb", [128, 16], BF16); hzf = sb("hzf", [128, 4, 8], F32)
    PS = stack.enter_context(nc.psum_tensor("ps", [128, 8, 512], F32))

    def v32(R, off_bf, n_f32):
        return R[:, off_bf:off_bf + 2 * n_f32].bitcast(F32)
    xt = [v32(R1, 0, 1024), v32(R1, 2048, 1024)]
    xnb = [v32(R2, 4096, 1024), v32(R2, 6144, 1024)]
    xoldb = [v32(R2, 8192, 1024), v32(R2, 10240, 1024), v32(R2, 16384, 1024)]
    xhlb = [R2[:, 12288:14336].rearrange("p (a b) -> p a b", a=2), R2[:, 14336:16384].rearrange("p (a b) -> p a b", a=2)]
    junk2 = [junk, qnb[:, 0:2, :].rearrange("p a b -> p (a b)")]
    wqb = R1[:, 0:4608].rearrange("p (j c) -> p j c", j=3)
    hal_k = R1[:, 4608:4608 + 2048].rearrange("p (r c) -> p r c", r=4)
    hal_v = R1[:, 6656:6656 + 1024].rearrange("p (r c) -> p r c", r=4)
    hal_z = R1[:, 7680:7680 + 64].rearrange("p (r c) -> p r c", r=4)
    hal_zf = R1[:, 7680:7680 + 64].bitcast(F32).rearrange("p (r c) -> p r c", r=4)
    Y_B = R1[:, 0:9216].rearrange("p (j t) -> p j t", j=4)
    qT_B = R2[:, 0:18432].rearrange("p (h t) -> p h t", h=8)
    Y_A = R2[:, 0:9216].rearrange("p (j t) -> p j t", j=4)
    Y_C = R2[:, 9216:18432].rearrange("p (j t) -> p j t", j=4)
    G_x = v32(R2, 0, 1024); G_c = v32(R2, 2048, 1024)
    o3 = 0
    kT_A = R3[:, o3:o3 + 5120].rearrange("p (g t) -> p g t", g=2); o3 += 5120
    V_A = R3[:, o3:o3 + 7680].rearrange("p (t g c) -> p t g c", t=20, g=2); o3 += 7680
    o3m = o3
    wkvb = R3[:, o3:o3 + 2048].rearrange("p (j c) -> p j c", j=2); o3 += 2048
    kvbuf = []
    for i in range(2):
        kvbuf.append(R3[:, o3:o3 + 1024].rearrange("p (j c) -> p j c", j=2)); o3 += 1024
    Kbuf = []
    for i in range(2):
        Kbuf.append(R3[:, o3:o3 + 512]); o3 += 512
    Vbuf = []
    for i in range(2):
        Vbuf.append(R3[:, o3:o3 + 768].rearrange("p (t c) -> p t c", t=4)); o3 += 768
    Kc = R3[:, o3:o3 + 256]; o3 += 256
    Vc = R3[:, o3:o3 + 384].rearrange("p (t c) -> p t c", t=2); o3 += 384
    kvnc = R3[:, o3:o3 + 512].rearrange("p (j c) -> p j c", j=2); o3 += 512
    assert o3 <= 20736, o3
    qA = R3[:, o3m:o3m + 2048].rearrange("p (j c) -> p j c", j=4)
    sza = R3[:, o3m + 2048:o3m + 4096].rearrange("p (j c) -> p j c", j=4)
    zrow = R3[:, o3m:o3m + 4640].bitcast(F32)
    mT = R3[:, 0:18432].rearrange("p (j t) -> p j t", j=8)
    pbuf = [TMP[4 + i][:, :].bitcast(BF16)[:, 0:512] for i in range(4)]

    R = lambda n: Res(n)
    r_const = R("const")
    PB = [Res("pb%d" % i, excl=True) for i in range(8)]
    r_hx = [R("hx%d" % c) for c in range(5)]
    r_wb = [R("wb%d" % i) for i in range(3)]
    r_tmp = [R("tmp%d" % i) for i in range(8)]
    r_tab = [R("tab%d" % i) for i in range(4)]
    r_misc = {}
    r_xt = [R("xt0"), R("xt1")]

    def rm(name):
        if name not in r_misc:
            r_misc[name] = Res(name)
        return r_misc[name]

    def mm(out, lhsT, rhs, start, stop, r=(), pw=(), w=()):
        return S_.op("pe", lambda e: e.matmul(out, lhsT=lhsT, rhs=rhs, start=start, stop=stop), r=r, pw=pw, w=w)

    def act(out, in_, func, r=(), w=(), pw=(), scale=None, bias=None, accum_out=None):
        kw = {}
        if scale is not None:
            kw["scale"] = scale
        if bias is not None:
            kw["bias"] = bias
        if accum_out is not None:
            kw["accum_out"] = accum_out
        return S_.op("act", lambda e: e.activation(out=out, in_=in_, func=func, **kw), r=r, w=w, pw=pw)

    def tt(eng, out, in0, in1, op, r=(), w=(), pw=()):
        return S_.op(eng, lambda e: e.tensor_tensor(out=out, in0=in0, in1=in1, op=op), r=r, w=w, pw=pw)

    def ts(eng, out, in0, s1, op0, s2=None, op1=None, r=(), w=(), pw=()):
        if op1 is None:
            return S_.op(eng, lambda e: e.tensor_scalar(out=out, in0=in0, scalar1=s1, scalar2=None, op0=op0),
                         r=r, w=w, pw=pw)
        return S_.op(eng, lambda e: e.tensor_scalar(out=out, in0=in0, scalar1=s1, scalar2=s2, op0=op0, op1=op1),
                     r=r, w=w, pw=pw)

    def stt(out, in0, scalar, in1, op0, op1, r=(), w=(), pw=()):
        return S_.op("dve", lambda e: e.scalar_tensor_tensor(out=out, in0=in0, scalar=scalar, in1=in1,
                                                            op0=op0, op1=op1), r=r, w=w, pw=pw)

    def cp(eng, out, in_, r=(), w=(), pw=()):
        if eng == "act":
            return act(out, in_, AF.Copy, r=r, w=w, pw=pw)
        return S_.op(eng, lambda e: e.tensor_copy(out=out, in_=in_), r=r, w=w, pw=pw)

    wb_next = [0]

    def load_w(src_ap, ncols, kparts=8):
        i = wb_next[0] % 3
        wb_next[0] += 1
        dst = WB[i][:, 0:kparts, 0:ncols]
        S_.dma("pool", dst, src_ap, w=[r_wb[i]])
        return WB[i], r_wb[i]

    def win_src(l, name, c0, ncols):
        a = d_win.ap()[l].rearrange("(k p) c -> p k c", p=128)
        o = WOFF[name] + c0
        return a[:, :, o:o + ncols]

    pb_next = [0]

    def bank():
        b = pb_next[0] % 8
        pb_next[0] += 1
        return b

    k.dumps = []

    def dump(name, ap_sbuf, shape=None, dt=None):
        if name in dbg:
            ap_ = ap_sbuf if isinstance(ap_sbuf, AP) else ap_sbuf[:]
            dd = nc.dram_tensor("dbg_" + name, list(ap_.shape), ap_.dtype, kind="ExternalOutput")
            k.dumps.append((dd, ap_))

    for (dst, src) in ((ident[:], d_ident.ap()), (identb[:], d_identb.ap()), (onesb[:], d_onesb.ap()),
                       (sel[:], d_sel.ap().rearrange("k w m -> k (w m)")), (masks[:], d_masks.ap()),
                       (oh[:], d_oh.ap()), (cs[:], d_cs.ap()), (bmodT[:], d_bmodT.ap()),
                       (gpreT[:], d_gpreT.ap()), (sinkb[:], d_sinkb.ap()),
                       (gqaT[:], d_gqaT.ap()), (gkvaT[:], d_gkvaT.ap()), (convw[:], d_convw.ap())):
        S_.dma("sp", dst, src, pw=[r_const])
    S_.op("pool", lambda e: e.memset(epsc[:], EPS), pw=[r_const])
    S_.barrier()
    act(scs[:], cs[:], AF.Silu, w=[rm("scs")])
    act(esink[:], sinkb[:], AF.Exp, w=[rm("esink")])
    scs3 = scs[:].rearrange("p (k s) -> p k s", s=2)
    def mod_layer(l):
        wsrc = d_wmod.ap()[l].rearrange("(k p) c -> p k c", p=128)
        bm = bank()
        for j in range(16):
            if j % 4 == 0:
                wt, rw = load_w(wsrc[:, :, (j // 4) * 512:(j // 4) * 512 + 512], 512)
            for kk in range(8):
                mm(PS[:, bm, j * 2:j * 2 + 2], wt[:, kk, (j % 4) * 128:(j % 4) * 128 + 128], scs3[:, kk, :],
                   kk == 0, kk == 7, r=[rw, rm("scs")], pw=[PB[bm]])
        bmb = AP(bmodT[:].tensor, bmodT[:, l, :].offset, [list(bmodT[:].ap[0]), [1, 16], [0, 2]])
        tt("dve", modT[:], PS[:, bm, 0:32].rearrange("p (j s) -> p j s", s=2), bmb, ALU.add,
           r=[PB[bm]], w=[rm("modT")])
        ts("dve", modT[:, 8:16, :], modT[:, 8:16, :], 1.0, ALU.add, r=[], w=[rm("modT")])
        gpb = AP(gpreT[:].tensor, gpreT[:, l, :].offset, [list(gpreT[:].ap[0]), [1, 8], [0, 2]])
        tt("dve", Amod[:, l], modT[:, 8:16, :], gpb, ALU.mult, r=[rm("modT")], pw=[rm("AB%d" % l)])
        cp("dve", Bmod[:, l], modT[:, 0:8, :], r=[rm("modT")], pw=[rm("AB%d" % l)])
    mod_layer(0)
    k.__dict__.update(locals())


def _stage1_tile(k, S_, l, i, xtile, r_xt):
    PS, PB, hxT, ident = k.PS, k.PB, k.hxT, k.ident
    act, ts, mm, rm = k.act, k.ts, k.mm, k.rm
    s = 0 if i < 16 else 1
    c = i // 4 if i < 16 else 4
    col0 = i * 128
    p = i % 2
    sm0 = 16 * p
    ssq = k.small[:, sm0 + 0:sm0 + 1]
    lnv = k.small[:, sm0 + 1:sm0 + 2]
    rstd = k.small[:, sm0 + 2:sm0 + 3]
    xn = k.xnb[p]
    rs = lambda n_: rm("%s%d" % (n_, p))
    act(k.junk2[p], xtile, AF.Square, r=[r_xt], w=[rs("junk"), rs("ssq")], accum_out=ssq)
    act(lnv, ssq, AF.Ln, r=[rs("ssq")], w=[rs("lnv")], scale=1.0 / D, bias=k.epsc[:, 0:1])
    act(rstd, lnv, AF.Exp, r=[rs("lnv")], w=[rs("rstd")], scale=-0.5)
    ts("dve", xn, xtile, rstd, ALU.mult, r=[r_xt, rs("rstd")], w=[rs("xn")])
    hi, lo = k.xhlb[p][:, 0, :], k.xhlb[p][:, 1, :]
    act(hi, xtile, AF.Copy, r=[r_xt, rs("rstd")], w=[rs("xhi")], scale=rstd)
    k.tt("dve", lo, xn, hi, ALU.subtract, r=[rs("xn"), rs("xhi")], w=[rs("xlo")])
    b0 = k.bank()
    b1 = k.bank()
    for kk in range(8):
        bb = b0 if kk < 4 else b1
        o_ = PS[:, bb, (kk % 4) * 128:(kk % 4) * 128 + 128]
        mm(o_, hi[:, kk * 128:(kk + 1) * 128], k.identb[:], True, False, r=[rs("xhi"), k.r_const], pw=[PB[bb]])
        mm(o_, lo[:, kk * 128:(kk + 1) * 128], k.identb[:], False, True, r=[rs("xlo"), k.r_const], pw=[PB[bb]])
    for kk in range(8):
        bb = b0 if kk < 4 else b1
        src = PS[:, bb, (kk % 4) * 128:(kk % 4) * 128 + 128]
        dst = hxT[:, kk, col0:col0 + 128]
        if kk < 4:
            act(dst, src, AF.Identity, r=[PB[bb], rm("AB%d" % l)], pw=[k.r_hx[c]],
                scale=k.Amod[:, l, kk, s:s + 1], bias=k.Bmod[:, l, kk, s:s + 1])
        else:
            ts("dve", dst, src, k.Amod[:, l, kk, s:s + 1], ALU.mult, s2=k.Bmod[:, l, kk, s:s + 1], op1=ALU.add,
               r=[PB[bb], rm("AB%d" % l)], pw=[k.r_hx[c]])


def _phaseA(k, S_, l):
    for i in range(getattr(k, "ntilesA", 18)):
        src = k.d_x.ap()[i * 128:(i + 1) * 128, :] if i < 16 else k.d_ctx.ap()[(i - 16) * 128:(i - 15) * 128, :]
        j = i % 2
        S_.dma("sp", k.xt[j], src, w=[k.r_xt[j]])
        _stage1_tile(k, S_, l, i, k.xt[j], k.r_xt[j])


def _load_tabs(k, S_, c0, n, which=(0, 1, 2, 3)):
    for ti in which:
        S_.dma("sp", k.tabs[ti][:, 0:n], k.d_tab[ti].ap()[:, c0:c0 + n], w=[k.r_tab[ti]])


def _rstd_bcast(k, S_, banks, nj, n, inv_n, out_tmp, r_out):
    PS, PB = k.PS, k.PB
    for j in range(nj):
        k.act(k.sqb[:, j, 0:n], PS[:, banks[j], 0:n], AF.Square, r=[PB[banks[j]]], pw=[k.rm("sqb")])
    bs = k.bank()
    for j in range(nj):
        k.mm(PS[:, bs, 0:n], k.onesb[:], k.sqb[:, j, 0:n], j == 0, j == nj - 1, r=[k.rm("sqb"), k.r_const],
             pw=[PB[bs]])
    k.act(out_tmp[:, 0:n], PS[:, bs, 0:n], AF.Ln, r=[PB[bs]], w=[r_out], scale=inv_n, bias=k.epsc[:, 0:1])
    k.act(out_tmp[:, 0:n], out_tmp[:, 0:n], AF.Exp, r=[], w=[r_out], scale=-0.5)


def _phaseB_kv(k, S_, l):
    PS, PB, hxT = k.PS, k.PB, k.hxT
    mm, act, tt, ts, stt, cp, rm = k.mm, k.act, k.tt, k.ts, k.stt, k.cp, k.rm
    TMP, r_tmp = k.TMP, k.r_tmp
    S_.op("pool", lambda e: e.memset(k.V_A[:, :, :, 0:64], 1.0), pw=[rm("VA")])
    S_.op("pool", lambda e: e.memset(k.V_A[:, :, :, 128:192], 1.0), pw=[rm("VA")])
    for i in range(2):
        S_.op("pool", (lambda vb: (lambda e: e.memset(vb[:, :, 0:64], 1.0)))(k.Vbuf[i]), pw=[rm("Vbuf%d" % i)])
        S_.op("pool", (lambda vb: (lambda e: e.memset(vb[:, :, 128:192], 1.0)))(k.Vbuf[i]), pw=[rm("Vbuf%d" % i)])
    S_.op("pool", lambda e: e.memset(k.Vc[:, :, 0:64], 1.0), pw=[rm("Vc")])
    S_.op("pool", lambda e: e.memset(k.Vc[:, :, 128:192], 1.0), pw=[rm("Vc")])
    wka, r_wka = k.load_w(k.win_src(l, "KA", 0, 512), 512)
    wvk, r_wvk = k.load_w(k.win_src(l, "VKK", 0, 448), 448)
    for c, (c0, n) in enumerate(CHUNKS):
        _load_tabs(k, S_, c0, n)
        rhs = [hxT[:, kk, c0:c0 + n] for kk in range(8)]
        kdst0 = 128 + c0 if c < 4 else 2304
        for g in range(2):
            bq, br = k.bank(), k.bank()
            for ti, bb in ((2 * g, bq), (2 * g + 1, br)):
                for kk in range(8):
                    mm(PS[:, bb, 0:n], wka[:, kk, ti * 128:(ti + 1) * 128], rhs[kk], kk == 0, kk == 7,
                       r=[r_wka, k.r_hx[c]], pw=[PB[bb]])
            tt("dve", TMP[0][:, 0:n], PS[:, bq, 0:n], k.tabs[0][:, 0:n], ALU.mult, r=[PB[bq], k.r_tab[0]], w=[r_tmp[0]])
            tt("dve", TMP[1][:, 0:n], PS[:, br, 0:n], k.tabs[1][:, 0:n], ALU.mult, r=[PB[br], k.r_tab[1]], w=[r_tmp[1]])
            tt("dve", k.kT_A[:, g, kdst0:kdst0 + n], TMP[0][:, 0:n], TMP[1][:, 0:n], ALU.add,
               r=[r_tmp[0], r_tmp[1]], pw=[rm("kTA")])
        bv = k.bank()
        nt = n // 128
        for t_ in range(nt):
            for kk in range(8):
                mm(PS[:, bv, t_ * 128:(t_ + 1) * 128], hxT[:, kk, c0 + t_ * 128:c0 + (t_ + 1) * 128], wvk[:, kk, 0:128],
                   kk == 0, kk == 7, r=[r_wvk, k.r_hx[c]], pw=[PB[bv]])
        vt0 = 1 + c * 4 if c < 4 else 18
        cp("dve", k.V_A[:, vt0:vt0 + nt, :, 64:128], PS[:, bv, 0:n].rearrange("p (t g d) -> p t g d", t=nt, g=2),
           r=[PB[bv]], pw=[rm("VA")])
        bk = [k.bank(), k.bank()]
        for j in range(2):
            for kk in range(8):
                mm(PS[:, bk[j], 0:n], wvk[:, kk, 128 + j * 128:256 + j * 128], rhs[kk], kk == 0, kk == 7,
                   r=[r_wvk, k.r_hx[c]], pw=[PB[bk[j]]])
        _rstd_bcast(k, S_, bk, 2, n, 1.0 / 256, TMP[2], r_tmp[2])
        for j in range(2):
            dst = k.kvst[:, j, 0:n] if c < 4 else k.kvnc[:, j, 0:n]
            stt(dst, PS[:, bk[j], 0:n], k.gkvaT[:, l, j:j + 1], TMP[2][:, 0:n], ALU.mult, ALU.mult,
                r=[PB[bk[j]], r_tmp[2]], pw=[rm("kvst") if c < 4 else rm("kvnc")])
        if c < 4:
            S_.dma("sp", k.d_snd1.ap()[:, PC_KVN:PC_KVN + 4096].rearrange("p (j t) -> p j t", j=2)[:, :, c0:c0 + n],
                   k.kvst[:, :, 0:n], r=[rm("kvst")], pw=[rm("snd1")])
        b1, b2 = k.bank(), k.bank()
        for (bb, co) in ((b1, 384), (b2, 416)):
            for kk in range(8):
                mm(PS[0:32, bb, 0:n], wvk[:, kk, co:co + 32], rhs[kk], kk == 0, kk == 7,
                   r=[r_wvk, k.r_hx[c]], pw=[PB[bb]])
        tt("dve", TMP[0][0:32, 0:n], PS[0:32, b1, 0:n], k.tabs[2][0:32, 0:n], ALU.mult, r=[PB[b1], k.r_tab[2]], w=[r_tmp[0]])
        tt("dve", TMP[1][0:32, 0:n], PS[0:32, b2, 0:n], k.tabs[3][0:32, 0:n], ALU.mult, r=[PB[b2], k.r_tab[3]], w=[r_tmp[1]])
        tt("dve", k.krst[:, 0:n], TMP[0][0:32, 0:n], TMP[1][0:32, 0:n], ALU.add, r=[r_tmp[0], r_tmp[1]], w=[rm("krst")])
        if c < 4:
            S_.dma("sp", k.d_snd.ap()[c * 32:(c + 1) * 32, PC_KR:PC_KR + 512], k.krst[:, 0:n], r=[rm("krst")], pw=[rm("snd")])
        else:
            S_.dma("sp", k.Kc[64:96, 0:n], k.krst[:, 0:n], r=[rm("krst")], pw=[rm("Kc")])
    bz = k.bank()
    for which in range(2):
        wz, r_wz = k.load_w(k.win_src(l, "ZH", which * 512, 512), 512)
        for ct in range(4):
            col = (which * 4 + ct) * 2
            for kk in range(8):
                mm(PS[:, bz, col:col + 2], wz[:, kk, ct * 128:(ct + 1) * 128], hxT[:, kk, 0:2048:2047],
                   kk == 0, kk == 7, r=[r_wz, k.r_hx[0], k.r_hx[3]], pw=[PB[bz]])
    cp("act", TMP[3][:, 0:8], PS[:, bz, 0:8], r=[PB[bz]], w=[r_tmp[3]])
    tt("dve", k.zh[:], TMP[3][:, 0:8], PS[:, bz, 8:16], ALU.mult, r=[PB[bz], r_tmp[3]], w=[rm("zh")])
    snd = k.d_snd.ap()
    for fl, off in ((0, 128), (1, 2048)):
        S_.dma("sp", snd[:, PC_KAH:PC_KAH + 512].rearrange("p (g f t) -> p g f t", g=2, f=2)[:, :, fl, :],
               k.kT_A[:, :, off:off + 128], r=[rm("kTA")], pw=[rm("snd")])
    for fl, vt in ((0, 1), (1, 16)):
        S_.dma("sp", snd[:, PC_VAH + fl * 128:PC_VAH + (fl + 1) * 128].rearrange("p (g d) -> p g d", g=2),
               k.V_A[:, vt, :, 64:128], r=[rm("VA")], pw=[rm("snd")])
    cp("dve", k.zhb[:, 0:8], k.zh[:], r=[rm("zh")], w=[rm("zhb")])
    tt("dve", k.zhb[:, 8:16], k.zh[:], k.zhb[:, 0:8], ALU.subtract, r=[rm("zh")], w=[rm("zhb")])
    S_.dma("sp", snd[:, PC_ZH:PC_ZH + 16], k.zhb[:], r=[rm("zhb")], pw=[rm("snd")])
    if os.environ.get("NOCC") == "1":
        return
    S_.op("pool", lambda e: e.collective_compute("AllGather", ALU.bypass,
                                                 replica_groups=[[0, 1, 2, 3], [4, 5, 6, 7]],
                                                 ins=[k.d_snd1.ap().opt()], outs=[k.d_rcv1.ap().opt()]),
          r=[rm("snd1")], w=[rm("rcv1")], kind="cc")
    S_.op("pool", lambda e: e.collective_compute("AllGather", ALU.bypass,
                                                 replica_groups=[[0, 1, 2, 3], [4, 5, 6, 7]],
                                                 ins=[k.d_snd.ap().opt()], outs=[k.d_rcv.ap().opt()]),
          r=[rm("snd")], w=[rm("rcv")], kind="cc")


def _phaseB_halo(k, S_, l):
    rm, stt, ts = k.rm, k.stt, k.ts
    rcv = k.d_rcv.ap().rearrange("(r p) c -> p r c", p=128)
    S_.dma("sp", k.hal_k, rcv[:, :, PC_KAH:PC_KAH + 512], r=[rm("rcv")], w=[rm("halk")])
    S_.dma("sp", k.hal_v, rcv[:, :, PC_VAH:PC_VAH + 256], r=[rm("rcv")], w=[rm("halv")])
    S_.dma("sp", k.hal_z, rcv[:, :, PC_ZH:PC_ZH + 16], r=[rm("rcv")], w=[rm("halz")])
    oh = k.oh

    def select(dst, srcs, ohbase, tmp, r_t, rsrc, rdst):
        ts("dve", tmp, srcs[0], oh[:, ohbase:ohbase + 1], ALU.mult, r=[rsrc, k.r_const], w=[r_t])
        for rr in range(1, 4):
            last = rr == 3
            stt(dst if last else tmp, srcs[rr], oh[:, ohbase + rr:ohbase + rr + 1], tmp, ALU.mult, ALU.add,
                r=[rsrc, k.r_const] + ([] if not last else [r_t]), w=([r_t] if not last else []),
                pw=([rdst] if last else []))

    hk = lambda rr, fl: k.hal_k[:, rr, :].rearrange("p (g f t) -> p g f t", g=2, f=2)[:, :, fl, :]
    t3 = lambda i: k.TMP[i][:, 0:256].rearrange("p (g t) -> p g t", g=2)
    select(k.kT_A[:, :, 0:128], [hk(rr, 1) for rr in range(4)], 0, t3(0), k.r_tmp[0], rm("halk"), rm("kTA"))
    select(k.kT_A[:, :, 2176:2304], [hk(rr, 0) for rr in range(4)], 4, t3(1), k.r_tmp[1], rm("halk"), rm("kTA"))
    hv = lambda rr, fl: k.hal_v[:, rr, fl * 128:(fl + 1) * 128].rearrange("p (g d) -> p g d", g=2)
    t4 = lambda i: k.TMP[i][:, 0:128].rearrange("p (g d) -> p g d", g=2)
    select(k.V_A[:, 0, :, 64:128], [hv(rr, 1) for rr in range(4)], 0, t4(2), k.r_tmp[2], rm("halv"), rm("VA"))
    select(k.V_A[:, 17, :, 64:128], [hv(rr, 0) for rr in range(4)], 4, t4(3), k.r_tmp[3], rm("halv"), rm("VA"))
    k.tt("dve", k.hzf[:], k.hal_z[:, :, 0:8], k.hal_z[:, :, 8:16], ALU.add, r=[rm("halz")], w=[rm("hzf")])
    hz = lambda rr, fl: k.hzf[:, rr, :].rearrange("p (c f) -> p c f", f=2)[:, :, fl]
    select(k.zhalo[:, :, 0], [hz(rr, 1) for rr in range(4)], 0, k.TMP[0][:, 256:260], k.r_tmp[0], rm("hzf"), rm("zhalo"))
    select(k.zhalo[:, :, 1], [hz(rr, 0) for rr in range(4)], 4, k.TMP[1][:, 256:260], k.r_tmp[1], rm("hzf"), rm("zhalo"))


def _phaseB_qb(k, S_, l, nchunks):
    PS, PB, hxT = k.PS, k.PB, k.hxT
    mm, act, tt, stt, cp, rm = k.mm, k.act, k.tt, k.stt, k.cp, k.rm
    TMP, r_tmp = k.TMP, k.r_tmp
    S_.dma("pool", k.wqb, k.d_wqb.ap()[l].rearrange("(j p) c -> p j c", p=128), w=[rm("wqb")])
    wql, r_wql = k.load_w(k.win_src(l, "QL", 0, 384), 384)
    for c in range(nchunks):
        c0, n = CHUNKS[c]
        _load_tabs(k, S_, c0, n, which=(2, 3))
        bq = [k.bank(), k.bank(), k.bank()]
        for j in range(3):
            for kk in range(8):
                mm(PS[:, bq[j], 0:n], wql[:, kk, j * 128:(j + 1) * 128], hxT[:, kk, c0:c0 + n], kk == 0, kk == 7,
                   r=[r_wql, k.r_hx[c]], pw=[PB[bq[j]]])
        _rstd_bcast(k, S_, bq, 3, n, 1.0 / 384, TMP[2], r_tmp[2])
        for j in range(3):
            stt(k.qnb[:, j, 0:n], PS[:, bq[j], 0:n], k.gqaT[:, l, j:j + 1], TMP[2][:, 0:n], ALU.mult, ALU.mult,
                r=[PB[bq[j]], r_tmp[2]], pw=[rm("qnb")])
        for h in range(8):
            b1, b2 = k.bank(), k.bank()
            for (bb, co) in ((b1, h * 192), (b2, h * 192 + 96)):
                for j in range(3):
                    mm(PS[0:96, bb, 0:n], k.wqb[:, j, co:co + 96], k.qnb[:, j, 0:n], j == 0, j == 2,
                       r=[rm("wqb"), rm("qnb")], pw=[PB[bb]])
            cp("act", k.qT_B[0:64, h, c0:c0 + n], PS[0:64, b1, 0:n], r=[PB[b1]], pw=[rm("qTB")])
            ta, tb = (0, 1) if h % 2 == 0 else (3, 5)
            tt("dve", TMP[ta][64:96, 0:n], PS[64:96, b1, 0:n], k.tabs[2][64:96, 0:n], ALU.mult,
               r=[PB[b1], k.r_tab[2]], w=[r_tmp[ta]])
            tt("dve", TMP[tb][64:96, 0:n], PS[64:96, b2, 0:n], k.tabs[3][64:96, 0:n], ALU.mult,
               r=[PB[b2], k.r_tab[3]], w=[r_tmp[tb]])
            tt("dve", k.qT_B[64:96, h, c0:c0 + n], TMP[ta][64:96, 0:n], TMP[tb][64:96, 0:n], ALU.add,
               r=[r_tmp[ta], r_tmp[tb]], pw=[rm("qTB")])


def _attn_core(k, S_, items, scale):
    PS, PB = k.PS, k.PB
    LOOK = 2
    pend = []
    for it in items:
        if it.get("before") is not None:
            it["before"]()
        sb0, nb = it["sb"]
        pb_i = k.sctr % 4
        k.sctr += 1
        pws = [PB[sb0 + i] for i in range(nb)]
        first = True
        if it.get("mask") is not None:
            for (o_ap, m_ap) in it["mask"]:
                k.mm(o_ap, k.identb[:], m_ap, True, False, r=[k.r_const], pw=pws)
            first = False
        for (o_ap, lhsT, rhs) in it["s_mms"]:
            k.mm(o_ap, lhsT, rhs, first, True, r=it["r"], pw=pws)
        pv_ = it["p_view"](k.pbuf[pb_i])
        k.act(pv_, it["s_view"], AF.Exp, r=pws, w=[k.r_pbuf[pb_i]], scale=scale)

        def mk(it=it, pb_i=pb_i):
            def f():
                st = it["start"]
                npv = len(it["pv"])
                for pi, (o_ap, lhsT, rhs_fn) in enumerate(it["pv"]):
                    k.mm(o_ap, lhsT, rhs_fn(k.pbuf[pb_i]), st, it["stop"] and pi == npv - 1,
                         r=it["rv"] + [k.r_pbuf[pb_i]], pw=[PB[it["ob"]]])
                    st = False
                if it.get("after_pv") is not None:
                    it["after_pv"]()
            return f
        pend.append(mk())
        if len(pend) > LOOK:
            pend.pop(0)()
        if it.get("after") is not None:
            it["after"]()
    while pend:
        pend.pop(0)()


def _phaseC_mla(k, S_, l, with_ctx_q):
    PS, PB = k.PS, k.PB
    mm, act, tt, cp, rm = k.mm, k.act, k.tt, k.cp, k.rm
    S_.dma("pool", k.wkvb, k.d_wkvb.ap()[l].rearrange("(j p) c -> p j c", p=128), w=[rm("wkvb")])
    rcv = k.d_rcv.ap()
    rcv1 = k.d_rcv1.ap()
    EB = 7
    kchunks = [(rr, cc) for rr in range(4) for cc in range(4)] + [None]
    rec = k.TMP[0]
    def mk_head(h):
        def load_kv(idx):
            if idx >= len(kchunks) or kchunks[idx] is None:
                return
            rr, cc = kchunks[idx]
            i = idx % 2
            S_.dma("sp", k.kvbuf[i], rcv1[rr * 128:(rr + 1) * 128, PC_KVN:PC_KVN + 4096].rearrange(
                "p (j t) -> p j t", j=2)[:, :, cc * 512:(cc + 1) * 512], r=[rm("rcv1")], w=[rm("kvbuf%d" % i)])

        def load_kr(idx):
            if idx >= len(kchunks) or kchunks[idx] is None:
                return
            rr, cc = kchunks[idx]
            i = idx % 2
            S_.dma("sp", k.Kbuf[i][64:96, :], rcv[rr * 128 + cc * 32:rr * 128 + cc * 32 + 32, PC_KR:PC_KR + 512],
                   r=[rm("rcv")], pw=[rm("Kbuf%d" % i)])

        def expand_k(idx):
            kc = kchunks[idx]
            i = idx % 2
            src, rs, n, dst, rd = ((k.kvbuf[i], rm("kvbuf%d" % i), 512, k.Kbuf[i], rm("Kbuf%d" % i)) if kc is not None
                                   else (k.kvnc, rm("kvnc"), 256, k.Kc, rm("Kc")))
            for j in range(2):
                mm(PS[0:64, EB, 0:n], k.wkvb[:, j, h * 64:(h + 1) * 64], src[:, j, 0:n], j == 0, j == 1,
                   r=[rm("wkvb"), rs], pw=[PB[EB]])
            cp("dve", dst[0:64, 0:n], PS[0:64, EB, 0:n], r=[PB[EB]], pw=[rd])

        def expand_v(idx):
            kc = kchunks[idx]
            i = idx % 2
            src, rs, nt, dst, rd = ((k.kvbuf[i], rm("kvbuf%d" % i), 4, k.Vbuf[i], rm("Vbuf%d" % i)) if kc is not None
                                    else (k.kvnc, rm("kvnc"), 2, k.Vc, rm("Vc")))
            for t_ in range(nt):
                for j in range(2):
                    mm(PS[:, EB, t_ * 64:(t_ + 1) * 64], src[:, j, t_ * 128:(t_ + 1) * 128],
                       k.wkvb[:, j, 512 + h * 64:512 + (h + 1) * 64], j == 0, j == 1,
                       r=[rm("wkvb"), rs], pw=[PB[EB]])
            cp("dve", dst[:, 0:nt, 64:128], PS[:, EB, 0:nt * 64].rearrange("p (t d) -> p t d", t=nt),
               r=[PB[EB]], pw=[rd])
        return load_kv, load_kr, expand_k, expand_v

    heads = [mk_head(h) for h in range(8)]
    for h in range(8):
        e = h % 2
        vsl = slice(64, 192) if e == 0 else slice(0, 128)
        o_lo, r_lo = (0, 64) if e == 0 else (64, 0)
        load_kv, load_kr, expand_k, expand_v = heads[h]
        nxt = heads[h + 1] if h + 1 < 8 else None
        if h == 0:
            load_kv(0)
            load_kv(1)
            load_kr(0)
            expand_k(0)
            expand_v(0)
        items = []
        for idx, kc in enumerate(kchunks):
            i = idx % 2
            if kc is not None:
                Kt, rK, Vt, rV, nt = k.Kbuf[i], rm("Kbuf%d" % i), k.Vbuf[i], rm("Vbuf%d" % i), 4
            else:
                Kt, rK, Vt, rV, nt = k.Kc, rm("Kc"), k.Vc, rm("Vc"), 2
            cnt = 0
            for qc in range(4):
                for kt in range(nt):
                    sbank = 4 + (len(items) % 3)
                    it = dict(sb=(sbank, 1),
                              s_mms=[(PS[:, sbank, 0:512], Kt[0:96, kt * 128:(kt + 1) * 128],
                                      k.qT_B[0:96, h, qc * 512:(qc + 1) * 512])],
                              s_view=PS[:, sbank, 0:512], p_view=(lambda p: p[:, 0:512]),
                              r=[rK, rm("qTB")],
                              pv=[(PS[:, qc, 0:512], Vt[:, kt, vsl], (lambda p: p[:, 0:512]))], rv=[rV],
                              ob=qc, start=(idx == 0 and kt == 0), stop=(idx == len(kchunks) - 1 and kt == nt - 1))
                    cnt += 1
                    if cnt == 1:
                        def bef(idx=idx):
                            load_kv(idx + 2)
                            load_kr(idx + 1)
                            if idx == len(kchunks) - 1 and nxt is not None:
                                nxt[0](0)
                                nxt[0](1)
                                nxt[1](0)
                        it["before"] = bef
                    if idx + 1 < len(kchunks):
                        if cnt == 2:
                            it["after"] = (lambda idx=idx: expand_k(idx + 1))
                        elif cnt == 6:
                            it["after"] = (lambda idx=idx: expand_v(idx + 1))
                    elif nxt is not None:
                        if cnt == 2:
                            it["after"] = (lambda: nxt[2](0))
                        elif cnt == 6:
                            it["after"] = (lambda: nxt[3](0))
                    items.append(it)
        _attn_core(k, S_, items, SCALE_B)
        for qc in range(4):
            act(rec[o_lo:o_lo + 64, 0:512], PS[r_lo:r_lo + 64, qc, 0:512], AF.Ln, r=[PB[qc]], w=[k.r_tmp[0]])
            act(rec[o_lo:o_lo + 64, 0:512], rec[o_lo:o_lo + 64, 0:512], AF.Exp, r=[], w=[k.r_tmp[0]], scale=-1.0)
            tt("dve", k.Y_B[o_lo:o_lo + 64, h // 2, qc * 512:(qc + 1) * 512], PS[o_lo:o_lo + 64, qc, 0:512],
               rec[o_lo:o_lo + 64, 0:512], ALU.mult, r=[PB[qc], k.r_tmp[0]], pw=[rm("YB")])
        if with_ctx_q:
            items = []
            for kt in range(2):
                sbank = 4 + kt
                items.append(dict(sb=(sbank, 1),
                                  s_mms=[(PS[:, sbank, 0:256], k.Kc[0:96, kt * 128:(kt + 1) * 128],
                                          k.qT_B[0:96, h, 2048:2304])],
                                  s_view=PS[:, sbank, 0:256], p_view=(lambda p: p[:, 0:256]),
                                  r=[rm("Kc"), rm("qTB")],
                                  pv=[(PS[:, 6, 0:256], k.Vc[:, kt, vsl], (lambda p: p[:, 0:256]))], rv=[rm("Vc")],
                                  ob=6, start=(kt == 0), stop=(kt == 1)))
            _attn_core(k, S_, items, SCALE_B)
            act(rec[o_lo:o_lo + 64, 0:256], PS[r_lo:r_lo + 64, 6, 0:256], AF.Ln, r=[PB[6]], w=[k.r_tmp[0]])
            act(rec[o_lo:o_lo + 64, 0:256], rec[o_lo:o_lo + 64, 0:256], AF.Exp, r=[], w=[k.r_tmp[0]], scale=-1.0)
            tt("dve", k.Y_B[o_lo:o_lo + 64, h // 2, 2048:2304], PS[o_lo:o_lo + 64, 6, 0:256],
               rec[o_lo:o_lo + 64, 0:256], ALU.mult, r=[PB[6], k.r_tmp[0]], pw=[rm("YB")])


def _phaseD_A(k, S_, l, nchunks, pre=None):
    PS, PB, hxT = k.PS, k.PB, k.hxT
    mm, act, tt, ts, rm = k.mm, k.act, k.tt, k.ts, k.rm
    TMP, r_tmp = k.TMP, k.r_tmp
    if pre is None:
        pre = _prefetch_A(k, l)
    wq = [pre[0], pre[1]]
    wza, r_wza = pre[2]
    gctr = [0]
    for c in range(nchunks):
        c0, n = CHUNKS[c]
        _load_tabs(k, S_, c0, n, which=(0, 1))
        for j in range(4):
            wt, rw = wq[j // 2]
            base = (j % 2) * 256
            bq, br = k.bank(), k.bank()
            for (bb, co) in ((bq, base), (br, base + 128)):
                for kk in range(8):
                    mm(PS[:, bb, 0:n], wt[:, kk, co:co + 128], hxT[:, kk, c0:c0 + n], kk == 0, kk == 7,
                       r=[rw, k.r_hx[c]], pw=[PB[bb]])
            tt("dve", TMP[0][:, 0:n], PS[:, bq, 0:n], k.tabs[0][:, 0:n], ALU.mult, r=[PB[bq], k.r_tab[0]], w=[r_tmp[0]])
            tt("dve", TMP[1][:, 0:n], PS[:, br, 0:n], k.tabs[1][:, 0:n], ALU.mult, r=[PB[br], k.r_tab[1]], w=[r_tmp[1]])
            tt("dve", k.qA[:, j, 0:n], TMP[0][:, 0:n], TMP[1][:, 0:n], ALU.add, r=[r_tmp[0], r_tmp[1]], pw=[rm("qA")])
        for jt in range(4):
            bb = k.bank()
            for kk in range(8):
                mm(PS[:, bb, 0:n], wza[:, kk, jt * 128:(jt + 1) * 128], hxT[:, kk, c0:c0 + n], kk == 0, kk == 7,
                   r=[r_wza, k.r_hx[c]], pw=[PB[bb]])
            act(k.sza[:, jt, 0:n], PS[:, bb, 0:n], AF.Silu, r=[PB[bb]], pw=[rm("sza")])
        items = []
        groups = []
        for qb in range(n // 128):
            for g in range(2):
                if c < 4:
                    qbg = c * 4 + qb
                    tiles = [(qbg * 128, qbg, 2 if qbg == 0 else 0), ((qbg + 1) * 128, qbg + 1, None),
                             ((qbg + 2) * 128, qbg + 2, 3 if qbg == 15 else 1), (2304, 18, None), (2432, 19, None)]
                else:
                    tiles = [(2304, 18, None), (2432, 19, None)]
                gi = gctr[0]
                gctr[0] += 1
                ob = gi % 2
                groups.append((qb, g, ob, gi))
                for ti, (koff, vt, mi) in enumerate(tiles):
                    sb0 = 2 + 2 * ((len(items)) % 3)
                    it = dict(sb=(sb0, 2),
                              s_mms=[(PS[:, sb0 + e, 0:256], k.kT_A[e * 64:(e + 1) * 64, g, koff:koff + 128],
                                      k.qA[e * 64:(e + 1) * 64, 2 * g:2 * g + 2, qb * 128:(qb + 1) * 128])
                                     for e in range(2)],
                              mask=(None if mi is None else [(PS[:, sb0 + e, 0:256], k.masks[:, mi, 0:256])
                                                             for e in range(2)]),
                              s_view=PS[:, sb0:sb0 + 2, 0:256],
                              p_view=(lambda p: p[:, 0:512].rearrange("p (e c) -> p e c", e=2)),
                              r=[rm("kTA"), rm("qA")],
                              pv=[(PS[:, ob, 0:256], k.V_A[:, vt, g, 64:192], (lambda p: p[:, 0:256])),
                                  (PS[:, ob, 256:512], k.V_A[:, vt, g, 0:128], (lambda p: p[:, 256:512]))],
                              rv=[rm("VA")], ob=ob, start=(ti == 0), stop=(ti == len(tiles) - 1))
                    items.append(it)

        def normalize(qb, g, ob, gi, c0=c0):
            ta, tb = (TMP[0], TMP[1]) if gi % 2 == 0 else (TMP[2], TMP[3])
            ra, rb = (r_tmp[0], r_tmp[1]) if gi % 2 == 0 else (r_tmp[2], r_tmp[3])
            tok0 = c0 + qb * 128
            for e in range(2):
                o_lo, r_lo = (0, 64) if e == 0 else (64, 0)
                cs_ = slice(e * 256, (e + 1) * 256)
                for jj in range(2):
                    h = 4 * g + 2 * jj + e
                    cj = slice(e * 256 + jj * 128, e * 256 + (jj + 1) * 128)
                    ts("dve", ta[o_lo:o_lo + 64, cj], PS[r_lo:r_lo + 64, ob, cj], k.esink[r_lo:r_lo + 64, l, h:h + 1],
                       ALU.add, r=[PB[ob], rm("esink")], pw=[ra])
                act(ta[o_lo:o_lo + 64, cs_], ta[o_lo:o_lo + 64, cs_], AF.Ln, r=[], w=[ra])
                act(ta[o_lo:o_lo + 64, cs_], ta[o_lo:o_lo + 64, cs_], AF.Exp, r=[], w=[ra], scale=-1.0)
                tt("dve", tb[o_lo:o_lo + 64, cs_], PS[o_lo:o_lo + 64, ob, cs_], ta[o_lo:o_lo + 64, cs_], ALU.mult,
                   r=[PB[ob], ra], pw=[rb])
                tt("dve", k.Y_A[o_lo:o_lo + 64, 2 * g:2 * g + 2, tok0:tok0 + 128],
                   tb[o_lo:o_lo + 64, cs_].rearrange("p (j q) -> p j q", j=2),
                   k.sza[o_lo:o_lo + 64, 2 * g:2 * g + 2, qb * 128:(qb + 1) * 128], ALU.mult,
                   r=[rb, rm("sza")], pw=[rm("YA")])

        per = len(items) // len(groups)
        for gidx in range(len(groups)):
            items[gidx * per + per - 1]["after_pv"] = (lambda a=groups[gidx]: normalize(*a))
        _attn_core(k, S_, items, SCALE_A)


def _prefetch_A(k, l):
    return [k.load_w(k.win_src(l, "QA", 0, 512), 512), k.load_w(k.win_src(l, "QA", 512, 512), 512),
            k.load_w(k.win_src(l, "ZA", 0, 512), 512)]


def _phaseD_zb(k, S_, l, nchunks):
    PS, PB, hxT = k.PS, k.PB, k.hxT
    wzb, r_wzb = k.load_w(k.win_src(l, "ZB", 0, 512), 512)
    for c in range(nchunks):
        c0, n = CHUNKS[c]
        for jt in range(4):
            bb = k.bank()
            for kk in range(8):
                k.mm(PS[:, bb, 0:n], wzb[:, kk, jt * 128:(jt + 1) * 128], hxT[:, kk, c0:c0 + n], kk == 0, kk == 7,
                     r=[r_wzb, k.r_hx[c]], pw=[PB[bb]])
            sq = k.sqb[:, jt % 3, 0:n]
            k.act(sq, PS[:, bb, 0:n], AF.Silu, r=[PB[bb]], w=[k.rm("sqb%d" % (jt % 3))])
            k.tt("dve", k.Y_B[:, jt, c0:c0 + n], k.Y_B[:, jt, c0:c0 + n], sq, ALU.mult,
                 r=[k.rm("sqb%d" % (jt % 3))], pw=[k.rm("YB")])


def _phaseD_C(k, S_, l, nchunks):
    PS, PB, hxT = k.PS, k.PB, k.hxT
    mm, act, tt, ts, stt, cp, rm = k.mm, k.act, k.tt, k.ts, k.stt, k.cp, k.rm
    TMP, r_tmp = k.TMP, k.r_tmp
    zrow = k.zrow
    for ct in range(4):
        wc, r_wc = k.load_w(k.win_src(l, "C", ct * 512, 512), 512)
        cp("dve", zrow[:, 0:1], k.zhalo[:, ct, 0:1], r=[rm("zhalo")], pw=[rm("zrow")])
        cp("dve", zrow[:, 2049:2050], k.zhalo[:, ct, 1:2], r=[rm("zhalo")], pw=[rm("zrow")])
        S_.op("dve", lambda e: e.memset(zrow[:, 2050:2051], 0.0), pw=[rm("zrow")])
        S_.op("dve", lambda e: e.memset(zrow[:, 2307:2308], 0.0), pw=[rm("zrow")])
        zo = lambda c: (1 + CHUNKS[c][0]) if c < 4 else 2051
        for c in range(nchunks):
            c0, n = CHUNKS[c]
            bcc, buc = k.bank(), k.bank()
            for (bb, co) in ((bcc, 128), (buc, 256)):
                for kk in range(8):
                    mm(PS[:, bb, 0:n], wc[:, kk, co:co + 128], hxT[:, kk, c0:c0 + n], kk == 0, kk == 7,
                       r=[r_wc, k.r_hx[c]], pw=[PB[bb]])
            cp("act", TMP[0][:, 0:n], PS[:, bcc, 0:n], r=[PB[bcc]], w=[r_tmp[0]])
            tt("dve", zrow[:, zo(c):zo(c) + n], TMP[0][:, 0:n], PS[:, buc, 0:n], ALU.mult,
               r=[r_tmp[0], PB[buc]], pw=[rm("zrow")])
        for c in range(nchunks):
            c0, n = CHUNKS[c]
            z0 = zo(c)
            bbc, bzc = k.bank(), k.bank()
            for (bb, co) in ((bbc, 0), (bzc, 384)):
                for kk in range(8):
                    mm(PS[:, bb, 0:n], wc[:, kk, co:co + 128], hxT[:, kk, c0:c0 + n], kk == 0, kk == 7,
                       r=[r_wc, k.r_hx[c]], pw=[PB[bb]])
            act(k.sqb[:, 0, 0:n], PS[:, bzc, 0:n], AF.Silu, r=[PB[bzc]], w=[rm("sqb0")])
            ts("dve", TMP[1][:, 0:n], zrow[:, z0 - 1:z0 - 1 + n], k.convw[:, l, ct, 0:1], ALU.mult,
               r=[rm("zrow")], w=[r_tmp[1]])
            stt(TMP[1][:, 0:n], zrow[:, z0:z0 + n], k.convw[:, l, ct, 1:2], TMP[1][:, 0:n], ALU.mult, ALU.add,
                r=[rm("zrow")], w=[r_tmp[1]])
            stt(TMP[1][:, 0:n], zrow[:, z0 + 1:z0 + 1 + n], k.convw[:, l, ct, 2:3], TMP[1][:, 0:n], ALU.mult, ALU.add,
                r=[rm("zrow")], w=[r_tmp[1]])
            tt("dve", TMP[2][:, 0:n], TMP[1][:, 0:n], PS[:, bbc, 0:n], ALU.mult, r=[r_tmp[1], PB[bbc]], w=[r_tmp[2]])
            tt("dve", k.Y_C[:, ct, c0:c0 + n], TMP[2][:, 0:n], k.sqb[:, 0, 0:n], ALU.mult,
               r=[r_tmp[2], rm("sqb0")], pw=[rm("YC")])


def _phaseE_merge(k, S_, l, nchunks):
    PS, PB, hxT = k.PS, k.PB, k.hxT
    mm, act, tt, rm = k.mm, k.act, k.tt, k.rm
    TMP, r_tmp = k.TMP, k.r_tmp
    Y = [k.Y_A, k.Y_B, k.Y_C]
    rY = [rm("YA"), rm("YB"), rm("YC")]
    for mt in range(8):
        wg, r_wg = k.load_w(k.win_src(l, "G", mt * 384, 384), 384)
        wbrb = k.wbrb2[mt % 2]
        r_wbrb = rm("wbrb%d" % (mt % 2))
        S_.dma("pool", wbrb[:], k.d_wbr.ap()[l].rearrange("b (kc p) c -> p b kc c", p=128)[:, :, :, mt * 128:(mt + 1) * 128],
               w=[r_wbrb])
        for c in range(nchunks):
            c0, n = CHUNKS[c]
            bp = [k.bank() for _ in range(3)]
            bg = [k.bank() for _ in range(3)]
            for br in range(3):
                for kc in range(4):
                    mm(PS[:, bp[br], 0:n], wbrb[:, br, kc, :], Y[br][:, kc, c0:c0 + n], kc == 0, kc == 3,
                       r=[r_wbrb, rY[br]], pw=[PB[bp[br]]])
            for br in range(3):
                for kk in range(8):
                    mm(PS[:, bg[br], 0:n], wg[:, kk, br * 128:(br + 1) * 128], hxT[:, kk, c0:c0 + n], kk == 0, kk == 7,
                       r=[r_wg, k.r_hx[c]], pw=[PB[bg[br]]])
            for br in range(3):
                act(TMP[br][:, 0:n], PS[:, bg[br], 0:n], AF.Sigmoid, r=[PB[bg[br]]], w=[r_tmp[br]])
            for br in range(3):
                tt("dve", TMP[br][:, 0:n], TMP[br][:, 0:n], PS[:, bp[br], 0:n], ALU.mult, r=[PB[bp[br]]], w=[r_tmp[br]])
            tt("dve", TMP[0][:, 0:n], TMP[0][:, 0:n], TMP[1][:, 0:n], ALU.add, r=[r_tmp[1]], w=[r_tmp[0]])
            tt("dve", k.mT[:, mt, c0:c0 + n], TMP[0][:, 0:n], TMP[2][:, 0:n], ALU.add, r=[r_tmp[0], r_tmp[2]],
               pw=[rm("mT")])


def _phaseF_out(k, S_, l, ntiles):
    PS, PB = k.PS, k.PB
    mm, act, tt, ts, stt, rm = k.mm, k.act, k.tt, k.ts, k.stt, k.rm
    TMP, r_tmp = k.TMP, k.r_tmp
    last = (l == DEPTH - 1)
    wsrc = k.d_wmod.ap()[l].rearrange("(k p) c -> p k c", p=128)
    scs3 = k.scs[:].rearrange("p (k s) -> p k s", s=2)
    for n_ in range(2):
        wt, rw = k.load_w(wsrc[:, :, 2048 + n_ * 512:2048 + n_ * 512 + 512], 512)
        bg = k.bank()
        for kk in range(8):
            mm(PS[0:2, bg, 0:512], scs3[:, kk, :], wt[:, kk, :], kk == 0, kk == 7, r=[rw, rm("scs")], pw=[PB[bg]])
        S_.dma("sp", TMP[3][0:2, 0:512], k.d_bmodg.ap()[:, l, n_ * 512:(n_ + 1) * 512], w=[r_tmp[3]])
        tt("dve", TMP[2][0:2, 0:512], PS[0:2, bg, 0:512], TMP[3][0:2, 0:512], ALU.add, r=[PB[bg], r_tmp[3]], w=[r_tmp[2]])
        for which, Gt, rG in ((0, k.G_x, rm("Gx")), (1, k.G_c, rm("Gc"))):
            if which == 1 and ntiles <= 16:
                continue
            hs = slice(n_ * 512, (n_ + 1) * 512)
            S_.dma("sp", Gt[:, hs], k.d_gpostb.ap()[l][:, hs], pw=[rG])
            bb = k.bank()
            mm(PS[:, bb, 0:512], k.sel[0:2, which * 128:(which + 1) * 128], TMP[2][0:2, 0:512],
               True, True, r=[r_tmp[2], k.r_const], pw=[PB[bb]])
            tt("dve", Gt[:, hs], Gt[:, hs], PS[:, bb, 0:512], ALU.mult, r=[PB[bb], rG], pw=[rG])
    wo0, r_wo0 = k.load_w(k.d_wo.ap()[l].rearrange("(k p) c -> p k c", p=128)[:, :, 0:512], 512)
    wo1, r_wo1 = k.load_w(k.d_wo.ap()[l].rearrange("(k p) c -> p k c", p=128)[:, :, 512:1024], 512)
    wo = [(wo0, r_wo0), (wo1, r_wo1)]
    sm = k.small
    def load_xold(i_):
        if i_ >= ntiles:
            return
        if i_ < 16:
            src_ = (k.d_x.ap() if l == 0 else k.d_x1.ap())[i_ * 128:(i_ + 1) * 128, :]
        else:
            src_ = k.d_ctx.ap()[(i_ - 16) * 128:(i_ - 15) * 128, :]
        S_.dma("sp", k.xoldb[i_ % 3], src_, r=([rm("x1d")] if (l > 0 and i_ < 16) else []), w=[rm("xold%d" % (i_ % 3))])

    for i in range(ntiles):
        j = i % 2
        isx = i < 16
        Gt, rG = (k.G_x, rm("Gx")) if isx else (k.G_c, rm("Gc"))
        p = i % 2
        xold = k.xoldb[i % 3]
        r_xold = rm("xold%d" % (i % 3))
        rs = lambda n_: rm("%s%d" % (n_, p))
        q0 = 16 * p + 4
        if i == 0:
            load_xold(0)
            load_xold(1)
        load_xold(i + 2)
        bo = [k.bank(), k.bank()]
        for hf in range(2):
            for kk in range(8):
                mm(PS[:, bo[hf], 0:512], k.mT[:, kk, i * 128:(i + 1) * 128], wo[hf][0][:, kk, :], kk == 0, kk == 7,
                   r=[wo[hf][1], rm("mT")], pw=[PB[bo[hf]]])
        for hf in range(2):
            act(k.junk2[p][:, hf * 512:(hf + 1) * 512], PS[:, bo[hf], 0:512], AF.Square, r=[PB[bo[hf]]],
                w=[rs("junkf%d" % hf), rs("ssq2%d" % hf)], accum_out=sm[:, q0 + hf:q0 + hf + 1])
        tt("dve", sm[:, q0 + 2:q0 + 3], sm[:, q0:q0 + 1], sm[:, q0 + 1:q0 + 2], ALU.add, r=[rs("ssq20"), rs("ssq21")],
           w=[rs("ssq2")])
        act(sm[:, q0 + 3:q0 + 4], sm[:, q0 + 2:q0 + 3], AF.Ln, r=[rs("ssq2")], w=[rs("ln2")], scale=1.0 / D,
            bias=k.epsc[:, 0:1])
        act(sm[:, q0 + 4:q0 + 5], sm[:, q0 + 3:q0 + 4], AF.Exp, r=[rs("ln2")], w=[rs("rstd2")], scale=-0.5)
        for hf in range(2):
            hs = slice(hf * 512, (hf + 1) * 512)
            tmpi = 2 * p + hf
            stt(TMP[tmpi][:, 0:512], PS[:, bo[hf], 0:512], sm[:, q0 + 4:q0 + 5], Gt[:, hs], ALU.mult, ALU.mult,
                r=[PB[bo[hf]], rs("rstd2"), rG], w=[r_tmp[tmpi]])
            tt("dve", k.xt[j][:, hs], TMP[tmpi][:, 0:512], xold[:, hs], ALU.add, r=[r_tmp[tmpi], r_xold],
               pw=[k.r_xt[j]])
        if isx:
            dst = (k.d_y.ap() if last else k.d_x1.ap())[i * 128:(i + 1) * 128, :]
            S_.dma("sp", dst, k.xt[j], r=[k.r_xt[j]], pw=[rm("yd") if last else rm("x1d")])
        if not last:
            _stage1_tile(k, S_, l + 1, i, k.xt[j], k.r_xt[j])


def _program(k, S_, upto):
    k.sctr = 0
    k.r_pbuf = [Res("pbuf%d" % i) for i in range(4)]
    if upto <= 0:
        return
    if upto == 1 and DBG_TILES:
        k.ntilesA = DBG_TILES
    _phaseA(k, S_, 0)
    k.mod_layer(1)
    k.dump("hxT0", k.hxT[:], [128, 8 * TT], BF16)
    if upto <= 1:
        return
    for l in range(DEPTH):
        nch = 5 if l == 0 else 4
        S_.barrier()
        _phaseB_kv(k, S_, l)
        _phaseB_qb(k, S_, l, nch)
        _phaseB_halo(k, S_, l)
        if l == 0:
            k.dump("kTA", k.kT_A, [128, 5120], BF16)
            k.dump("VA", k.V_A, [128, 7680], BF16)
            k.dump("qTB", k.qT_B[0:96])
            k.dump("kvnc", k.kvnc, [128, 512], BF16)
            k.dump("zhalo", k.zhalo[:], [128, 8], F32)
        if upto <= 2 and l == 0:
            return
        S_.barrier()
        preA = _prefetch_A(k, l)
        _phaseC_mla(k, S_, l, with_ctx_q=(l == 0))
        if upto <= 3 and l == 0:
            k.dump("YB", k.Y_B, [128, 4 * TT], BF16)
            return
        S_.barrier()
        _phaseD_A(k, S_, l, nch, pre=preA)
        _phaseD_zb(k, S_, l, nch)
        if upto <= 4 and l == 0:
            k.dump("YB", k.Y_B, [128, 4 * TT], BF16)
            k.dump("YA", k.Y_A, [128, 4 * TT], BF16)
            return
        S_.barrier()
        _phaseD_C(k, S_, l, nch)
        if upto <= 5 and l == 0:
            k.dump("YC", k.Y_C, [128, 4 * TT], BF16)
            return
        S_.barrier()
        _phaseE_merge(k, S_, l, nch)
        if upto <= 6 and l == 0:
            k.dump("mT", k.mT, [128, 8 * TT], BF16)
            return
        S_.barrier()
        _phaseF_out(k, S_, l, 18 if l == 0 else 16)
        if upto <= 7 and l == 0:
            k.dump("hxT1", k.hxT[:], [128, 8 * TT], BF16)
            return


def kernel(**inputs):
    in_maps = prepare_inputs(**inputs)
    nc = build()
    res = run_bass_kernel_spmd(nc, in_maps, core_ids=list(range(NCORE)))
    out = np.zeros((2, S, D), np.float32)
    for core in range(NCORE):
        b, r = divmod(core, 4)
        out[b, r * T:(r + 1) * T] = np.asarray(res.results[core]["y"], np.float32)
    return out
```

```python
import os
import numpy as np
import ml_dtypes
import concourse.bass as bass
import concourse.mybir as mybir
from concourse.bass_utils import run_bass_kernel_spmd

F32 = mybir.dt.float32
BF16 = mybir.dt.bfloat16
AF = mybir.ActivationFunctionType
ALU = mybir.AluOpType
AP = bass.AP

D = 1024
S = 8192
L = 256
DEPTH = 2
NCORE = 8
T = 2048
TT = T + L
GRID_W = 64
EPS = 1e-6
SCALE_A = 64 ** -0.5
SCALE_B = 96 ** -0.5
NEG = -30000.0
DBG_TILES = 0
CHUNKS = [(0, 512), (512, 512), (1024, 512), (1536, 512), (2048, 256)]

O_QA, O_KA, O_VA, O_ZA = 0, 512, 640, 768
O_QL, O_KVL, O_KR, O_ZB = 1280, 1664, 1920, 1952
O_BC, O_CC, O_UC, O_ZC = 2464, 2976, 3488, 4000
O_GA, O_GB, O_GC = 4512, 5536, 6560


def _win_perm():
    cols = []
    off = {}

    def add(name, idx):
        off[name] = len(cols)
        cols.extend(list(idx))

    r64 = np.arange(64)
    rot64 = np.concatenate([r64[32:], r64[:32]])
    r32 = np.arange(32)
    rot32 = np.concatenate([r32[16:], r32[:16]])
    ka = []
    for g in range(2):
        base = O_KA + g * 64
        ka += list(base + r64) + list(base + r64)
        ka += list(base + rot64) + list(base + rot64)
    add("KA", ka)
    add("VKK", list(O_VA + np.arange(128)) + list(O_KVL + np.arange(256))
        + list(O_KR + r32) + list(O_KR + rot32))
    add("ZH", list(O_CC + np.arange(512)) + list(O_UC + np.arange(512)))
    add("QL", list(O_QL + np.arange(384)))
    qa = []
    for j in range(4):
        for e in range(2):
            qa += list(O_QA + (2 * j + e) * 64 + r64)
        for e in range(2):
            qa += list(O_QA + (2 * j + e) * 64 + rot64)
    add("QA", qa)
    add("ZA", list(O_ZA + np.arange(512)))
    add("ZB", list(O_ZB + np.arange(512)))
    cc = []
    for ct in range(4):
        for o in (O_BC, O_CC, O_UC, O_ZC):
            cc += list(o + ct * 128 + np.arange(128))
    add("C", cc)
    gg = []
    for mt in range(8):
        for o in (O_GA, O_GB, O_GC):
            gg += list(o + mt * 128 + np.arange(128))
    add("G", gg)
    return np.asarray(cols, dtype=np.int64), off


WIN_PERM, WOFF = _win_perm()
NCW = len(WIN_PERM)

PC_KVN = 0
PC_N1 = 4096
PC_KR = 0
PC_KAH = 512
PC_VAH = 1024
PC_ZH = 1280
PC_N2 = 1296


class Res:
    __slots__ = ("name", "writers", "readers", "excl", "last")

    def __init__(self, name, excl=False):
        self.name = name
        self.writers = []
        self.readers = []
        self.excl = excl
        self.last = {}


class Op:
    __slots__ = ("eng", "fn", "deps", "kind", "signaled", "sem", "val", "idx")


def _prune(lst):
    out = []
    seen = set()
    for o in reversed(lst):
        if o.kind != "c":
            out.append(o)
        elif o.eng not in seen:
            seen.add(o.eng)
            out.append(o)
    out.reverse()
    return out


class Sched:
    ENG = ("pe", "act", "dve", "pool", "sp")

    def __init__(self):
        self.prog = {e: [] for e in self.ENG}
        self.all = []
        self.pending_dma = []

    def op(self, eng, fn, r=(), w=(), pw=(), kind="c", extra=()):
        o = Op()
        o.eng, o.fn, o.kind, o.signaled, o.sem, o.val = eng, fn, kind, False, None, 0
        o.idx = len(self.all)
        deps = set(x for x in extra if x is not None)
        allres = list(r) + list(w) + list(pw)
        r = [x for x in r if not x.excl]
        w = [x for x in w if not x.excl]
        pw = [x for x in pw if not x.excl]
        for res in allres:
            if res.excl:
                for e2, o2 in res.last.items():
                    if e2 != eng:
                        deps.add(o2)
                res.last[eng] = o
        for res in r:
            deps.update(res.writers)
        for res in w:
            deps.update(res.writers)
            deps.update(res.readers)
        for res in pw:
            deps.update(res.readers)
        for res in r:
            res.readers.append(o)
            if len(res.readers) > 12:
                res.readers = _prune(res.readers)
        for res in w:
            res.writers = [o]
            res.readers = []
        for res in pw:
            if res.readers:
                res.writers = [o]
                res.readers = []
            else:
                res.writers.append(o)
                if len(res.writers) > 12:
                    res.writers = _prune(res.writers)
        deps.discard(o)
        o.deps = deps
        self.prog[eng].append(o)
        self.all.append(o)
        if kind != "c":
            self.pending_dma.append(o)
        return o

    def dma(self, eng, out, in_, r=(), w=(), pw=(), extra=(), **kw):
        return self.op(eng, lambda e: e.dma_start(out=out, in_=in_, **kw), r=r, w=w, pw=pw,
                       kind="d", extra=extra)

    def barrier(self):
        last = [self.prog[e][-1] for e in self.ENG if self.prog[e]]
        last += self.pending_dma
        self.pending_dma = []
        for e in self.ENG:
            self.op(e, None, extra=last)

    def emit(self, nc, stack):
        NS = 20
        for o in self.all:
            for d in o.deps:
                if d.kind == "c" and d.eng == "pe" and o.eng == "pe" and o.kind == "c":
                    continue
                d.signaled = True
        esem = {e: stack.enter_context(nc.semaphore("s_" + e)) for e in self.ENG}
        dsem = {e: [stack.enter_context(nc.semaphore("d_%s%d" % (e, i))) for i in range(NS)]
                for e in ("sp", "pool", "act")}
        ccsem = stack.enter_context(nc.semaphore("s_cc"))
        cnt = {e: 0 for e in self.ENG}
        dcnt = {e: 0 for e in dsem}
        dval = {}
        prev = {}
        ccn = 0
        for e in self.ENG:
            for o in self.prog[e]:
                if o.kind == "c":
                    if o.signaled and o.fn is not None:
                        cnt[e] += 1
                        o.sem, o.val = esem[e], cnt[e]
                    elif o.fn is None:
                        o.sem, o.val = None, 0
                elif o.kind == "d":
                    s = dsem[e][dcnt[e] % NS]
                    dcnt[e] += 1
                    prev[o] = dval.get(id(s), 0)
                    dval[id(s)] = prev[o] + 16
                    o.sem, o.val = s, dval[id(s)]
                else:
                    ccn += 1
                    o.sem, o.val = ccsem, ccn
        self.n_inst = {e: len(self.prog[e]) for e in self.ENG}
        block = stack.enter_context(nc.Block())
        sched = self

        def run(e, engine):
            waited = {}
            for o in sched.prog[e]:
                waits = {}
                for d in o.deps:
                    if d.kind == "c" and d.eng == "pe" and o.eng == "pe" and o.kind == "c":
                        continue
                    if d.sem is None:
                        continue
                    k = id(d.sem)
                    if k not in waits or waits[k][1] < d.val:
                        waits[k] = (d.sem, d.val)
                if o.kind == "d" and prev[o] > 0:
                    k = id(o.sem)
                    if k not in waits or waits[k][1] < prev[o]:
                        waits[k] = (o.sem, prev[o])
                for k, (s, v) in waits.items():
                    if waited.get(k, 0) < v:
                        engine.wait_ge(s, v)
                        waited[k] = v
                if o.fn is None:
                    continue
                inst = o.fn(engine)
                if o.kind == "d":
                    inst.then_inc(o.sem, 16)
                elif o.kind == "cc":
                    inst.then_inc(o.sem)
                elif o.signaled:
                    inst.then_inc(o.sem, 1)

        @block.tensor
        def _(te):
            run("pe", te)

        @block.scalar
        def _(sc):
            run("act", sc)

        @block.vector
        def _(ve):
            run("dve", ve)

        @block.gpsimd
        def _(gp):
            run("pool", gp)

        @block.sync
        def _(sy):
            run("sp", sy)


def _rope_tables(rank):
    g = rank * T + np.arange(T)
    row = (g // GRID_W).astype(np.float64)
    col = (g % GRID_W).astype(np.float64)

    def tab(rot_dim):
        axis_dim = rot_dim // 2
        inv = 10000.0 ** (-np.arange(0, axis_dim, 2, dtype=np.float64) / axis_dim)
        ang = np.concatenate([row[:, None] * inv, col[:, None] * inv], axis=-1)
        half = rot_dim // 2
        d = np.arange(rot_dim)
        c = np.cos(ang)[:, d % half].T
        s = np.sin(ang)[:, d % half].T
        s = np.where((d < half)[:, None], -s, s)
        cfull = np.ones((rot_dim, TT)); sfull = np.zeros((rot_dim, TT))
        cfull[:, :T] = c; sfull[:, :T] = s
        return cfull, sfull

    ca, sa = tab(64)
    cb, sb = tab(32)
    cosA = np.concatenate([ca, ca], 0).astype(np.float32)
    sinA = np.concatenate([sa, sa], 0).astype(np.float32)
    cosB = np.zeros((128, TT), np.float32); sinB = np.zeros((128, TT), np.float32)
    cosB[0:32] = cb; cosB[64:96] = cb
    sinB[0:32] = sb; sinB[64:96] = sb
    return cosA, sinA, cosB, sinB


def _masks(rank):
    kk = np.arange(128)[:, None]
    qq = np.arange(128)[None, :]
    mp = np.where(kk >= qq, 0.0, NEG).astype(np.float32)
    mn = np.where(kk <= qq, 0.0, NEG).astype(np.float32)
    allneg = np.full((128, 128), NEG, np.float32)
    m = np.stack([np.tile(mp, (1, 4)), np.tile(mn, (1, 4)),
                  np.tile(mp if rank > 0 else allneg, (1, 4)),
                  np.tile(mn if rank < 3 else allneg, (1, 4))], axis=1)
    return m.astype(ml_dtypes.bfloat16)


def _fm(v, k):
    return np.ascontiguousarray(np.asarray(v, np.float32).reshape(k, 128).T)


def prepare_inputs(x, c, ctx, c_ctx, w_mod, b_mod, g_pre, g_post, w_in, sink,
                   g_qa, w_qb, g_kva, w_kvb, conv_w, w_branch, w_o):
    f = lambda a: np.asarray(a, np.float32)
    x, c, ctx, c_ctx = f(x), f(c), f(ctx), f(c_ctx)
    w_mod, b_mod, g_pre, g_post = f(w_mod), f(b_mod), f(g_pre), f(g_post)
    w_in, sink, g_qa, w_qb, g_kva, w_kvb = f(w_in), f(sink), f(g_qa), f(w_qb), f(g_kva), f(w_kvb)
    conv_w, w_branch, w_o = f(conv_w), f(w_branch), f(w_o)
    shared = {}
    shared["wmod"] = np.ascontiguousarray(w_mod)
    shared["bmodT"] = np.ascontiguousarray(np.stack([_fm(b_mod[l, :2048], 16) for l in range(DEPTH)], 1))
    shared["bmodg"] = np.ascontiguousarray(np.stack([np.stack([b_mod[l, 2048:], b_mod[l, 2048:]], 0)
                                                     for l in range(DEPTH)], 1))
    shared["gpreT"] = np.ascontiguousarray(np.stack([_fm(g_pre[l], 8) for l in range(DEPTH)], 1))
    shared["gpostb"] = np.ascontiguousarray(np.broadcast_to(g_post[:, None, :], (DEPTH, 128, D)))
    shared["win"] = np.ascontiguousarray(w_in[:, :, WIN_PERM])
    shared["sinkb"] = np.ascontiguousarray(np.broadcast_to(sink[None, :, :], (128, DEPTH, 8)))
    shared["gqaT"] = np.ascontiguousarray(np.stack([_fm(g_qa[l], 3) for l in range(DEPTH)], 1))
    shared["gkvaT"] = np.ascontiguousarray(np.stack([_fm(g_kva[l], 2) for l in range(DEPTH)], 1))
    r32 = np.arange(32)
    qcols = []
    for h in range(8):
        base = h * 96
        qcols += list(base + np.arange(96))
        qcols += list(base + np.arange(64)) + list(base + 64 + np.concatenate([r32[16:], r32[:16]]))
    shared["wqb"] = np.ascontiguousarray(w_qb[:, :, np.asarray(qcols)])
    kcols = [h * 128 + i for h in range(8) for i in range(64)]
    vcols = [h * 128 + 64 + i for h in range(8) for i in range(64)]
    shared["wkvb"] = np.ascontiguousarray(w_kvb[:, :, np.asarray(kcols + vcols)])
    cw = np.zeros((128, DEPTH, 4, 3), np.float32)
    for l in range(DEPTH):
        for k in range(3):
            cw[:, l, :, k] = conv_w[l, k].reshape(4, 128).T
    shared["convw"] = cw
    shared["wbr"] = np.ascontiguousarray(w_branch)
    shared["wo"] = np.ascontiguousarray(w_o)
    shared["ident"] = np.eye(128, dtype=np.float32)
    shared["identb"] = np.eye(128, dtype=np.float32).astype(ml_dtypes.bfloat16)
    shared["onesb"] = np.ones((128, 128), np.float32).astype(ml_dtypes.bfloat16)
    sel = np.zeros((2, 2, 128), np.float32)
    sel[0, 0, :] = 1.0
    sel[1, 1, :] = 1.0
    shared["sel"] = sel
    in_maps = []
    for core in range(NCORE):
        b, r = core // 4, core % 4
        m = dict(shared)
        m["x"] = np.ascontiguousarray(x[b, r * T:(r + 1) * T])
        m["ctx"] = np.ascontiguousarray(ctx[b])
        cs = np.zeros((128, 8, 2), np.float32)
        cs[:, :, 0] = _fm(c[b], 8)
        cs[:, :, 1] = _fm(c_ctx, 8)
        m["cs"] = cs.reshape(128, 16)
        cosA, sinA, cosB, sinB = _rope_tables(r)
        m["cosA"], m["sinA"], m["cosB"], m["sinB"] = cosA, sinA, cosB, sinB
        m["masks"] = _masks(r)
        oh = np.zeros((128, 8), np.float32)
        if r > 0:
            oh[:, r - 1] = 1.0
        if r < 3:
            oh[:, 4 + r + 1] = 1.0
        m["oh"] = oh
        in_maps.append(m)
    return in_maps


class _K:
    pass


def build(upto=99, dbg=()):
    from contextlib import ExitStack
    nc = bass.Bass("TRN2", target_bir_lowering=False)
    S_ = Sched()
    k = _K()
    stack = ExitStack()
    with stack:
        _build_body(nc, S_, k, stack, upto, dbg)
        _program(k, S_, upto)
        S_.barrier()
        for (dd, ap_) in k.dumps:
            S_.dma("sp", dd.ap(), ap_)
        S_.barrier()
        S_.emit(nc, stack)
    k.S = S_
    build.last = k
    return nc


def _din(nc, name, shape, dt=F32):
    return nc.dram_tensor(name, list(shape), dt, kind="ExternalInput")


def _build_body(nc, S_, k, stack, upto, dbg):
    sb = lambda name, shape, dt: stack.enter_context(nc.sbuf_tensor("s_" + name, list(shape), dt))
    d_x = _din(nc, "x", [T, D]); d_ctx = _din(nc, "ctx", [L, D]); d_cs = _din(nc, "cs", [128, 16])
    d_wmod = _din(nc, "wmod", [DEPTH, D, 3 * D]); d_bmodT = _din(nc, "bmodT", [128, DEPTH, 16])
    d_bmodg = _din(nc, "bmodg", [2, DEPTH, D]); d_gpreT = _din(nc, "gpreT", [128, DEPTH, 8])
    d_gpostb = _din(nc, "gpostb", [DEPTH, 128, D]); d_win = _din(nc, "win", [DEPTH, D, NCW])
    d_sinkb = _din(nc, "sinkb", [128, DEPTH, 8]); d_gqaT = _din(nc, "gqaT", [128, DEPTH, 3])
    d_gkvaT = _din(nc, "gkvaT", [128, DEPTH, 2]); d_wqb = _din(nc, "wqb", [DEPTH, 384, 1536])
    d_wkvb = _din(nc, "wkvb", [DEPTH, 256, 1024]); d_convw = _din(nc, "convw", [128, DEPTH, 4, 3])
    d_wbr = _din(nc, "wbr", [DEPTH, 3, 512, D]); d_wo = _din(nc, "wo", [DEPTH, D, D])
    d_tab = [_din(nc, n, [128, TT]) for n in ("cosA", "sinA", "cosB", "sinB")]
    d_masks = _din(nc, "masks", [128, 4, 512], BF16); d_oh = _din(nc, "oh", [128, 8])
    d_ident = _din(nc, "ident", [128, 128]); d_identb = _din(nc, "identb", [128, 128], BF16)
    d_onesb = _din(nc, "onesb", [128, 128], BF16); d_sel = _din(nc, "sel", [2, 2, 128])
    d_y = nc.dram_tensor("y", [T, D], F32, kind="ExternalOutput")
    d_snd1 = nc.dram_tensor("snd1", [128, PC_N1], BF16)
    d_rcv1 = nc.dram_tensor("rcv1", [512, PC_N1], BF16)
    d_snd = nc.dram_tensor("snd2", [128, PC_N2], BF16)
    d_rcv = nc.dram_tensor("rcv2", [512, PC_N2], BF16)
    d_x1 = nc.dram_tensor("x1", [T, D], F32)

    ident = sb("ident", [128, 128], F32); identb = sb("identb", [128, 128], BF16)
    onesb = sb("onesb", [128, 128], BF16); sel = sb("sel", [2, 256], F32)
    masks = sb("masksb", [128, 4, 512], BF16); oh = sb("oh", [128, 8], F32)
    cs = sb("cs", [128, 16], F32); scs = sb("scs", [128, 16], BF16)
    bmodT = sb("bmodT", [128, DEPTH, 16], F32)
    gpreT = sb("gpreT", [128, DEPTH, 8], F32); sinkb = sb("sinkb", [128, DEPTH, 8], F32)
    esink = sb("esink", [128, DEPTH, 8], F32)
    gqaT = sb("gqaT", [128, DEPTH, 3], F32); gkvaT = sb("gkvaT", [128, DEPTH, 2], F32)
    convw = sb("convw", [128, DEPTH, 4, 3], F32)
    modT = sb("modT", [128, 16, 2], F32)
    Amod = sb("Amod", [128, DEPTH, 8, 2], F32); Bmod = sb("Bmod", [128, DEPTH, 8, 2], F32)
    small = sb("small", [128, 64], F32)
    epsc = sb("epsc", [128, 1], F32)
    hxT = sb("hxT", [128, 8, TT], BF16)
    R1 = sb("R1", [128, 9216], BF16)
    R2 = sb("R2", [128, 18432], BF16)
    R3 = sb("R3", [128, 20736], BF16)
    WB = [sb("wb%d" % i, [128, 8, 512], BF16) for i in range(3)]
    wbrb2 = [sb("wbrb%d" % i, [128, 3, 4, 128], BF16) for i in range(2)]
    TMP = [sb("tmp%d" % i, [128, 512], F32) for i in range(8)]
    sqb = sb("sqb", [128, 3, 512], BF16); qnb = sb("qnb", [128, 3, 512], BF16)
    junk = sqb[:, 0:2, :].rearrange("p a b -> p (a b)")
    tabs = [sb("tab%d" % i, [128, 512], F32) for i in range(4)]
    kvst = sb("kvst", [128, 2, 512], BF16); krst = sb("krst", [32, 512], BF16)
    zh = sb("zh", [128, 8], F32); zhalo = sb("zhalo", [128, 4, 2], F32)
    zhb = sb("zhb", [128, 16], BF16); hzf = sb("hzf", [128, 4, 8], F32)
    PS = stack.enter_context(nc.psum_tensor("ps", [128, 8, 512], F32))

    def v32(R, off_bf, n_f32):
        return R[:, off_bf:off_bf + 2 * n_f32].bitcast(F32)
    xt = [v32(R1, 0, 1024), v32(R1, 2048, 1024)]
    xnb = [v32(R2, 4096, 1024), v32(R2, 6144, 1024)]
    xoldb = [v32(R2, 8192, 1024), v32(R2, 10240, 1024), v32(R2, 16384, 1024)]
    xhlb = [R2[:, 12288:14336].rearrange("p (a b) -> p a b", a=2), R2[:, 14336:16384].rearrange("p (a b) -> p a b", a=2)]
    junk2 = [junk, qnb[:, 0:2, :].rearrange("p a b -> p (a b)")]
    wqb = R1[:, 0:4608].rearrange("p (j c) -> p j c", j=3)
    hal_k = R1[:, 4608:4608 + 2048].rearrange("p (r c) -> p r c", r=4)
    hal_v = R1[:, 6656:6656 + 1024].rearrange("p (r c) -> p r c", r=4)
    hal_z = R1[:, 7680:7680 + 64].rearrange("p (r c) -> p r c", r=4)
    hal_zf = R1[:, 7680:7680 + 64].bitcast(F32).rearrange("p (r c) -> p r c", r=4)
    Y_B = R1[:, 0:9216].rearrange("p (j t) -> p j t", j=4)
    qT_B = R2[:, 0:18432].rearrange("p (h t) -> p h t", h=8)
    Y_A = R2[:, 0:9216].rearrange("p (j t) -> p j t", j=4)
    Y_C = R2[:, 9216:18432].rearrange("p (j t) -> p j t", j=4)
    G_x = v32(R2, 0, 1024); G_c = v32(R2, 2048, 1024)
    o3 = 0
    kT_A = R3[:, o3:o3 + 5120].rearrange("p (g t) -> p g t", g=2); o3 += 5120
    V_A = R3[:, o3:o3 + 7680].rearrange("p (t g c) -> p t g c", t=20, g=2); o3 += 7680
    o3m = o3
    wkvb = R3[:, o3:o3 + 2048].rearrange("p (j c) -> p j c", j=2); o3 += 2048
    kvbuf = []
    for i in range(2):
        kvbuf.append(R3[:, o3:o3 + 1024].rearrange("p (j c) -> p j c", j=2)); o3 += 1024
    Kbuf = []
    for i in range(2):
        Kbuf.append(R3[:, o3:o3 + 512]); o3 += 512
    Vbuf = []
    for i in range(2):
        Vbuf.append(R3[:, o3:o3 + 768].rearrange("p (t c) -> p t c", t=4)); o3 += 768
    Kc = R3[:, o3:o3 + 256]; o3 += 256
    Vc = R3[:, o3:o3 + 384].rearrange("p (t c) -> p t c", t=2); o3 += 384
    kvnc = R3[:, o3:o3 + 512].rearrange("p (j c) -> p j c", j=2); o3 += 512
    assert o3 <= 20736, o3
    qA = R3[:, o3m:o3m + 2048].rearrange("p (j c) -> p j c", j=4)
    sza = R3[:, o3m + 2048:o3m + 4096].rearrange("p (j c) -> p j c", j=4)
    zrow = R3[:, o3m:o3m + 4640].bitcast(F32)
    mT = R3[:, 0:18432].rearrange("p (j t) -> p j t", j=8)
    pbuf = [TMP[4 + i][:, :].bitcast(BF16)[:, 0:512] for i in range(4)]

    R = lambda n: Res(n)
    r_const = R("const")
    PB = [Res("pb%d" % i, excl=True) for i in range(8)]
    r_hx = [R("hx%d" % c) for c in range(5)]
    r_wb = [R("wb%d" % i) for i in range(3)]
    r_tmp = [R("tmp%d" % i) for i in range(8)]
    r_tab = [R("tab%d" % i) for i in range(4)]
    r_misc = {}
    r_xt = [R("xt0"), R("xt1")]

    def rm(name):
        if name not in r_misc:
            r_misc[name] = Res(name)
        return r_misc[name]

    def mm(out, lhsT, rhs, start, stop, r=(), pw=(), w=()):
        return S_.op("pe", lambda e: e.matmul(out, lhsT=lhsT, rhs=rhs, start=start, stop=stop), r=r, pw=pw, w=w)

    def act(out, in_, func, r=(), w=(), pw=(), scale=None, bias=None, accum_out=None):
        kw = {}
        if scale is not None:
            kw["scale"] = scale
        if bias is not None:
            kw["bias"] = bias
        if accum_out is not None:
            kw["accum_out"] = accum_out
        return S_.op("act", lambda e: e.activation(out=out, in_=in_, func=func, **kw), r=r, w=w, pw=pw)

    def tt(eng, out, in0, in1, op, r=(), w=(), pw=()):
        return S_.op(eng, lambda e: e.tensor_tensor(out=out, in0=in0, in1=in1, op=op), r=r, w=w, pw=pw)

    def ts(eng, out, in0, s1, op0, s2=None, op1=None, r=(), w=(), pw=()):
        if op1 is None:
            return S_.op(eng, lambda e: e.tensor_scalar(out=out, in0=in0, scalar1=s1, scalar2=None, op0=op0),
                         r=r, w=w, pw=pw)
        return S_.op(eng, lambda e: e.tensor_scalar(out=out, in0=in0, scalar1=s1, scalar2=s2, op0=op0, op1=op1),
                     r=r, w=w, pw=pw)

    def stt(out, in0, scalar, in1, op0, op1, r=(), w=(), pw=()):
        return S_.op("dve", lambda e: e.scalar_tensor_tensor(out=out, in0=in0, scalar=scalar, in1=in1,
                                                            op0=op0, op1=op1), r=r, w=w, pw=pw)

    def cp(eng, out, in_, r=(), w=(), pw=()):
        if eng == "act":
            return act(out, in_, AF.Copy, r=r, w=w, pw=pw)
        return S_.op(eng, lambda e: e.tensor_copy(out=out, in_=in_), r=r, w=w, pw=pw)

    wb_next = [0]

    def load_w(src_ap, ncols, kparts=8):
        i = wb_next[0] % 3
        wb_next[0] += 1
        dst = WB[i][:, 0:kparts, 0:ncols]
        S_.dma("pool", dst, src_ap, w=[r_wb[i]])
        return WB[i], r_wb[i]

    def win_src(l, name, c0, ncols):
        a = d_win.ap()[l].rearrange("(k p) c -> p k c", p=128)
        o = WOFF[name] + c0
        return a[:, :, o:o + ncols]

    pb_next = [0]

    def bank():
        b = pb_next[0] % 8
        pb_next[0] += 1
        return b

    k.dumps = []

    def dump(name, ap_sbuf, shape=None, dt=None):
        if name in dbg:
            ap_ = ap_sbuf if isinstance(ap_sbuf, AP) else ap_sbuf[:]
            dd = nc.dram_tensor("dbg_" + name, list(ap_.shape), ap_.dtype, kind="ExternalOutput")
            k.dumps.append((dd, ap_))

    for (dst, src) in ((ident[:], d_ident.ap()), (identb[:], d_identb.ap()), (onesb[:], d_onesb.ap()),
                       (sel[:], d_sel.ap().rearrange("k w m -> k (w m)")), (masks[:], d_masks.ap()),
                       (oh[:], d_oh.ap()), (cs[:], d_cs.ap()), (bmodT[:], d_bmodT.ap()),
                       (gpreT[:], d_gpreT.ap()), (sinkb[:], d_sinkb.ap()),
                       (gqaT[:], d_gqaT.ap()), (gkvaT[:], d_gkvaT.ap()), (convw[:], d_convw.ap())):
        S_.dma("sp", dst, src, pw=[r_const])
    S_.op("pool", lambda e: e.memset(epsc[:], EPS), pw=[r_const])
    S_.barrier()
    act(scs[:], cs[:], AF.Silu, w=[rm("scs")])
    act(esink[:], sinkb[:], AF.Exp, w=[rm("esink")])
    scs3 = scs[:].rearrange("p (k s) -> p k s", s=2)
    def mod_layer(l):
        wsrc = d_wmod.ap()[l].rearrange("(k p) c -> p k c", p=128)
        bm = bank()
        for j in range(16):
            if j % 4 == 0:
                wt, rw = load_w(wsrc[:, :, (j // 4) * 512:(j // 4) * 512 + 512], 512)
            for kk in range(8):
                mm(PS[:, bm, j * 2:j * 2 + 2], wt[:, kk, (j % 4) * 128:(j % 4) * 128 + 128], scs3[:, kk, :],
                   kk == 0, kk == 7, r=[rw, rm("scs")], pw=[PB[bm]])
        bmb = AP(bmodT[:].tensor, bmodT[:, l, :].offset, [list(bmodT[:].ap[0]), [1, 16], [0, 2]])
        tt("dve", modT[:], PS[:, bm, 0:32].rearrange("p (j s) -> p j s", s=2), bmb, ALU.add,
           r=[PB[bm]], w=[rm("modT")])
        ts("dve", modT[:, 8:16, :], modT[:, 8:16, :], 1.0, ALU.add, r=[], w=[rm("modT")])
        gpb = AP(gpreT[:].tensor, gpreT[:, l, :].offset, [list(gpreT[:].ap[0]), [1, 8], [0, 2]])
        tt("dve", Amod[:, l], modT[:, 8:16, :], gpb, ALU.mult, r=[rm("modT")], pw=[rm("AB%d" % l)])
        cp("dve", Bmod[:, l], modT[:, 0:8, :], r=[rm("modT")], pw=[rm("AB%d" % l)])
    mod_layer(0)
    k.__dict__.update(locals())


def _stage1_tile(k, S_, l, i, xtile, r_xt):
    _stage1_b(k, S_, l, i, *_stage1_a(k, S_, l, i, xtile, r_xt))


def _stage1_a(k, S_, l, i, xtile, r_xt):
    PS, PB, hxT, ident = k.PS, k.PB, k.hxT, k.ident
    act, ts, mm, rm = k.act, k.ts, k.mm, k.rm
    s = 0 if i < 16 else 1
    c = i // 4 if i < 16 else 4
    col0 = i * 128
    p = i % 2
    sm0 = 16 * (i % 4)
    ssq = k.small[:, sm0 + 0:sm0 + 1]
    lnv = k.small[:, sm0 + 1:sm0 + 2]
    rstd = k.small[:, sm0 + 2:sm0 + 3]
    xn = k.xnb[p]
    rs = lambda n_: rm("%s%d" % (n_, p))
    act(k.junk2[p], xtile, AF.Square, r=[r_xt], w=[rs("junk"), rs("ssq")], accum_out=ssq)
    act(lnv, ssq, AF.Ln, r=[rs("ssq")], w=[rs("lnv")], scale=1.0 / D, bias=k.epsc[:, 0:1])
    act(rstd, lnv, AF.Exp, r=[rs("lnv")], w=[rs("rstd")], scale=-0.5)
    ts("dve", xn, xtile, rstd, ALU.mult, r=[r_xt, rs("rstd")], w=[rs("xn")])
    hi, lo = k.xhlb[p][:, 0, :], k.xhlb[p][:, 1, :]
    act(hi, xtile, AF.Copy, r=[r_xt, rs("rstd")], w=[rs("xhi")], scale=rstd)
    k.tt("dve", lo, xn, hi, ALU.subtract, r=[rs("xn"), rs("xhi")], w=[rs("xlo")])
    b0 = k.bank()
    b1 = k.bank()
    for kk in range(8):
        bb = b0 if kk < 4 else b1
        o_ = PS[:, bb, (kk % 4) * 128:(kk % 4) * 128 + 128]
        mm(o_, hi[:, kk * 128:(kk + 1) * 128], k.identb[:], True, False, r=[rs("xhi"), k.r_const], pw=[PB[bb]])
        mm(o_, lo[:, kk * 128:(kk + 1) * 128], k.identb[:], False, True, r=[rs("xlo"), k.r_const], pw=[PB[bb]])
    return b0, b1


def _stage1_b(k, S_, l, i, b0, b1):
    PS, PB, hxT = k.PS, k.PB, k.hxT
    act, ts, rm = k.act, k.ts, k.rm
    s = 0 if i < 16 else 1
    c = i // 4 if i < 16 else 4
    col0 = i * 128
    for kk in range(8):
        bb = b0 if kk < 4 else b1
        src = PS[:, bb, (kk % 4) * 128:(kk % 4) * 128 + 128]
        dst = hxT[:, kk, col0:col0 + 128]
        if kk < 4:
            act(dst, src, AF.Identity, r=[PB[bb], rm("AB%d" % l)], pw=[k.r_hx[c]],
                scale=k.Amod[:, l, kk, s:s + 1], bias=k.Bmod[:, l, kk, s:s + 1])
        else:
            ts("dve", dst, src, k.Amod[:, l, kk, s:s + 1], ALU.mult, s2=k.Bmod[:, l, kk, s:s + 1], op1=ALU.add,
               r=[PB[bb], rm("AB%d" % l)], pw=[k.r_hx[c]])


def _phaseA(k, S_, l):
    nt = getattr(k, "ntilesA", 18)

    def load(i):
        if i >= nt:
            return
        src = k.d_x.ap()[i * 128:(i + 1) * 128, :] if i < 16 else k.d_ctx.ap()[(i - 16) * 128:(i - 15) * 128, :]
        S_.dma("sp", k.xt[i % 2], src, w=[k.r_xt[i % 2]])

    load(0)
    load(1)
    banks = {}
    for s_ in range(nt + 1):
        if s_ < nt:
            banks[s_] = _stage1_a(k, S_, l, s_, k.xt[s_ % 2], k.r_xt[s_ % 2])
            load(s_ + 2)
        if s_ >= 1:
            _stage1_b(k, S_, l, s_ - 1, *banks.pop(s_ - 1))


def _load_tabs(k, S_, c0, n, which=(0, 1, 2, 3)):
    for ti in which:
        S_.dma("sp", k.tabs[ti][:, 0:n], k.d_tab[ti].ap()[:, c0:c0 + n], w=[k.r_tab[ti]])


def _rstd_bcast(k, S_, banks, nj, n, inv_n, out_tmp, r_out):
    PS, PB = k.PS, k.PB
    for j in range(nj):
        k.act(k.sqb[:, j, 0:n], PS[:, banks[j], 0:n], AF.Square, r=[PB[banks[j]]], pw=[k.rm("sqb")])
    bs = k.bank()
    for j in range(nj):
        k.mm(PS[:, bs, 0:n], k.onesb[:], k.sqb[:, j, 0:n], j == 0, j == nj - 1, r=[k.rm("sqb"), k.r_const],
             pw=[PB[bs]])
    k.act(out_tmp[:, 0:n], PS[:, bs, 0:n], AF.Ln, r=[PB[bs]], w=[r_out], scale=inv_n, bias=k.epsc[:, 0:1])
    k.act(out_tmp[:, 0:n], out_tmp[:, 0:n], AF.Exp, r=[], w=[r_out], scale=-0.5)


def _phaseB_kv(k, S_, l):
    PS, PB, hxT = k.PS, k.PB, k.hxT
    mm, act, tt, ts, stt, cp, rm = k.mm, k.act, k.tt, k.ts, k.stt, k.cp, k.rm
    TMP, r_tmp = k.TMP, k.r_tmp
    S_.op("pool", lambda e: e.memset(k.V_A[:, :, :, 0:64], 1.0), pw=[rm("VA")])
    S_.op("pool", lambda e: e.memset(k.V_A[:, :, :, 128:192], 1.0), pw=[rm("VA")])
    for i in range(2):
        S_.op("pool", (lambda vb: (lambda e: e.memset(vb[:, :, 0:64], 1.0)))(k.Vbuf[i]), pw=[rm("Vbuf%d" % i)])
        S_.op("pool", (lambda vb: (lambda e: e.memset(vb[:, :, 128:192], 1.0)))(k.Vbuf[i]), pw=[rm("Vbuf%d" % i)])
    S_.op("pool", lambda e: e.memset(k.Vc[:, :, 0:64], 1.0), pw=[rm("Vc")])
    S_.op("pool", lambda e: e.memset(k.Vc[:, :, 128:192], 1.0), pw=[rm("Vc")])
    wka, r_wka = k.load_w(k.win_src(l, "KA", 0, 512), 512)
    wvk, r_wvk = k.load_w(k.win_src(l, "VKK", 0, 448), 448)
    for c, (c0, n) in enumerate(CHUNKS):
        _load_tabs(k, S_, c0, n)
        rhs = [hxT[:, kk, c0:c0 + n] for kk in range(8)]
        kdst0 = 128 + c0 if c < 4 else 2304
        for g in range(2):
            bq, br = k.bank(), k.bank()
            for ti, bb in ((2 * g, bq), (2 * g + 1, br)):
                for kk in range(8):
                    mm(PS[:, bb, 0:n], wka[:, kk, ti * 128:(ti + 1) * 128], rhs[kk], kk == 0, kk == 7,
                       r=[r_wka, k.r_hx[c]], pw=[PB[bb]])
            tt("dve", TMP[0][:, 0:n], PS[:, bq, 0:n], k.tabs[0][:, 0:n], ALU.mult, r=[PB[bq], k.r_tab[0]], w=[r_tmp[0]])
            tt("dve", TMP[1][:, 0:n], PS[:, br, 0:n], k.tabs[1][:, 0:n], ALU.mult, r=[PB[br], k.r_tab[1]], w=[r_tmp[1]])
            tt("dve", k.kT_A[:, g, kdst0:kdst0 + n], TMP[0][:, 0:n], TMP[1][:, 0:n], ALU.add,
               r=[r_tmp[0], r_tmp[1]], pw=[rm("kTA")])
        bv = k.bank()
        nt = n // 128
        for t_ in range(nt):
            for kk in range(8):
                mm(PS[:, bv, t_ * 128:(t_ + 1) * 128], hxT[:, kk, c0 + t_ * 128:c0 + (t_ + 1) * 128], wvk[:, kk, 0:128],
                   kk == 0, kk == 7, r=[r_wvk, k.r_hx[c]], pw=[PB[bv]])
        vt0 = 1 + c * 4 if c < 4 else 18
        cp("dve", k.V_A[:, vt0:vt0 + nt, :, 64:128], PS[:, bv, 0:n].rearrange("p (t g d) -> p t g d", t=nt, g=2),
           r=[PB[bv]], pw=[rm("VA")])
        bk = [k.bank(), k.bank()]
        for j in range(2):
            for kk in range(8):
                mm(PS[:, bk[j], 0:n], wvk[:, kk, 128 + j * 128:256 + j * 128], rhs[kk], kk == 0, kk == 7,
                   r=[r_wvk, k.r_hx[c]], pw=[PB[bk[j]]])
        _rstd_bcast(k, S_, bk, 2, n, 1.0 / 256, TMP[2], r_tmp[2])
        for j in range(2):
            dst = k.kvst[:, j, 0:n] if c < 4 else k.kvnc[:, j, 0:n]
            stt(dst, PS[:, bk[j], 0:n], k.gkvaT[:, l, j:j + 1], TMP[2][:, 0:n], ALU.mult, ALU.mult,
                r=[PB[bk[j]], r_tmp[2]], pw=[rm("kvst") if c < 4 else rm("kvnc")])
        if c < 4:
            S_.dma("sp", k.d_snd1.ap()[:, PC_KVN:PC_KVN + 4096].rearrange("p (j t) -> p j t", j=2)[:, :, c0:c0 + n],
                   k.kvst[:, :, 0:n], r=[rm("kvst")], pw=[rm("snd1")])
        b1, b2 = k.bank(), k.bank()
        for (bb, co) in ((b1, 384), (b2, 416)):
            for kk in range(8):
                mm(PS[0:32, bb, 0:n], wvk[:, kk, co:co + 32], rhs[kk], kk == 0, kk == 7,
                   r=[r_wvk, k.r_hx[c]], pw=[PB[bb]])
        tt("dve", TMP[0][0:32, 0:n], PS[0:32, b1, 0:n], k.tabs[2][0:32, 0:n], ALU.mult, r=[PB[b1], k.r_tab[2]], w=[r_tmp[0]])
        tt("dve", TMP[1][0:32, 0:n], PS[0:32, b2, 0:n], k.tabs[3][0:32, 0:n], ALU.mult, r=[PB[b2], k.r_tab[3]], w=[r_tmp[1]])
        tt("dve", k.krst[:, 0:n], TMP[0][0:32, 0:n], TMP[1][0:32, 0:n], ALU.add, r=[r_tmp[0], r_tmp[1]], w=[rm("krst")])
        if c < 4:
            S_.dma("sp", k.d_snd.ap()[c * 32:(c + 1) * 32, PC_KR:PC_KR + 512], k.krst[:, 0:n], r=[rm("krst")], pw=[rm("snd")])
        else:
            S_.dma("sp", k.Kc[64:96, 0:n], k.krst[:, 0:n], r=[rm("krst")], pw=[rm("Kc")])
    bz = k.bank()
    for which in range(2):
        wz, r_wz = k.load_w(k.win_src(l, "ZH", which * 512, 512), 512)
        for ct in range(4):
            col = (which * 4 + ct) * 2
            for kk in range(8):
                mm(PS[:, bz, col:col + 2], wz[:, kk, ct * 128:(ct + 1) * 128], hxT[:, kk, 0:2048:2047],
                   kk == 0, kk == 7, r=[r_wz, k.r_hx[0], k.r_hx[3]], pw=[PB[bz]])
    cp("act", TMP[3][:, 0:8], PS[:, bz, 0:8], r=[PB[bz]], w=[r_tmp[3]])
    tt("dve", k.zh[:], TMP[3][:, 0:8], PS[:, bz, 8:16], ALU.mult, r=[PB[bz], r_tmp[3]], w=[rm("zh")])
    snd = k.d_snd.ap()
    for fl, off in ((0, 128), (1, 2048)):
        S_.dma("sp", snd[:, PC_KAH:PC_KAH + 512].rearrange("p (g f t) -> p g f t", g=2, f=2)[:, :, fl, :],
               k.kT_A[:, :, off:off + 128], r=[rm("kTA")], pw=[rm("snd")])
    for fl, vt in ((0, 1), (1, 16)):
        S_.dma("sp", snd[:, PC_VAH + fl * 128:PC_VAH + (fl + 1) * 128].rearrange("p (g d) -> p g d", g=2),
               k.V_A[:, vt, :, 64:128], r=[rm("VA")], pw=[rm("snd")])
    cp("dve", k.zhb[:, 0:8], k.zh[:], r=[rm("zh")], w=[rm("zhb")])
    tt("dve", k.zhb[:, 8:16], k.zh[:], k.zhb[:, 0:8], ALU.subtract, r=[rm("zh")], w=[rm("zhb")])
    S_.dma("sp", snd[:, PC_ZH:PC_ZH + 16], k.zhb[:], r=[rm("zhb")], pw=[rm("snd")])
    if os.environ.get("NOCC") == "1":
        return
    S_.op("pool", lambda e: e.collective_compute("AllGather", ALU.bypass,
                                                 replica_groups=[[0, 1, 2, 3], [4, 5, 6, 7]],
                                                 ins=[k.d_snd1.ap().opt()], outs=[k.d_rcv1.ap().opt()]),
          r=[rm("snd1")], w=[rm("rcv1")], kind="cc")
    S_.op("pool", lambda e: e.collective_compute("AllGather", ALU.bypass,
                                                 replica_groups=[[0, 1, 2, 3], [4, 5, 6, 7]],
                                                 ins=[k.d_snd.ap().opt()], outs=[k.d_rcv.ap().opt()]),
          r=[rm("snd")], w=[rm("rcv")], kind="cc")


def _phaseB_halo(k, S_, l):
    rm, stt, ts = k.rm, k.stt, k.ts
    rcv = k.d_rcv.ap().rearrange("(r p) c -> p r c", p=128)
    S_.dma("sp", k.hal_k, rcv[:, :, PC_KAH:PC_KAH + 512], r=[rm("rcv")], w=[rm("halk")])
    S_.dma("sp", k.hal_v, rcv[:, :, PC_VAH:PC_VAH + 256], r=[rm("rcv")], w=[rm("halv")])
    S_.dma("sp", k.hal_z, rcv[:, :, PC_ZH:PC_ZH + 16], r=[rm("rcv")], w=[rm("halz")])
    oh = k.oh

    def select(dst, srcs, ohbase, tmp, r_t, rsrc, rdst):
        ts("dve", tmp, srcs[0], oh[:, ohbase:ohbase + 1], ALU.mult, r=[rsrc, k.r_const], w=[r_t])
        for rr in range(1, 4):
            last = rr == 3
            stt(dst if last else tmp, srcs[rr], oh[:, ohbase + rr:ohbase + rr + 1], tmp, ALU.mult, ALU.add,
                r=[rsrc, k.r_const] + ([] if not last else [r_t]), w=([r_t] if not last else []),
                pw=([rdst] if last else []))

    hk = lambda rr, fl: k.hal_k[:, rr, :].rearrange("p (g f t) -> p g f t", g=2, f=2)[:, :, fl, :]
    t3 = lambda i: k.TMP[i][:, 0:256].rearrange("p (g t) -> p g t", g=2)
    select(k.kT_A[:, :, 0:128], [hk(rr, 1) for rr in range(4)], 0, t3(0), k.r_tmp[0], rm("halk"), rm("kTA"))
    select(k.kT_A[:, :, 2176:2304], [hk(rr, 0) for rr in range(4)], 4, t3(1), k.r_tmp[1], rm("halk"), rm("kTA"))
    hv = lambda rr, fl: k.hal_v[:, rr, fl * 128:(fl + 1) * 128].rearrange("p (g d) -> p g d", g=2)
    t4 = lambda i: k.TMP[i][:, 0:128].rearrange("p (g d) -> p g d", g=2)
    select(k.V_A[:, 0, :, 64:128], [hv(rr, 1) for rr in range(4)], 0, t4(2), k.r_tmp[2], rm("halv"), rm("VA"))
    select(k.V_A[:, 17, :, 64:128], [hv(rr, 0) for rr in range(4)], 4, t4(3), k.r_tmp[3], rm("halv"), rm("VA"))
    k.tt("dve", k.hzf[:], k.hal_z[:, :, 0:8], k.hal_z[:, :, 8:16], ALU.add, r=[rm("halz")], w=[rm("hzf")])
    hz = lambda rr, fl: k.hzf[:, rr, :].rearrange("p (c f) -> p c f", f=2)[:, :, fl]
    select(k.zhalo[:, :, 0], [hz(rr, 1) for rr in range(4)], 0, k.TMP[0][:, 256:260], k.r_tmp[0], rm("hzf"), rm("zhalo"))
    select(k.zhalo[:, :, 1], [hz(rr, 0) for rr in range(4)], 4, k.TMP[1][:, 256:260], k.r_tmp[1], rm("hzf"), rm("zhalo"))


def _phaseB_qb(k, S_, l, nchunks):
    PS, PB, hxT = k.PS, k.PB, k.hxT
    mm, act, tt, stt, cp, rm = k.mm, k.act, k.tt, k.stt, k.cp, k.rm
    TMP, r_tmp = k.TMP, k.r_tmp
    S_.dma("pool", k.wqb, k.d_wqb.ap()[l].rearrange("(j p) c -> p j c", p=128), w=[rm("wqb")])
    wql, r_wql = k.load_w(k.win_src(l, "QL", 0, 384), 384)
    for c in range(nchunks):
        c0, n = CHUNKS[c]
        _load_tabs(k, S_, c0, n, which=(2, 3))
        bq = [k.bank(), k.bank(), k.bank()]
        for j in range(3):
            for kk in range(8):
                mm(PS[:, bq[j], 0:n], wql[:, kk, j * 128:(j + 1) * 128], hxT[:, kk, c0:c0 + n], kk == 0, kk == 7,
                   r=[r_wql, k.r_hx[c]], pw=[PB[bq[j]]])
        _rstd_bcast(k, S_, bq, 3, n, 1.0 / 384, TMP[2], r_tmp[2])
        for j in range(3):
            stt(k.qnb[:, j, 0:n], PS[:, bq[j], 0:n], k.gqaT[:, l, j:j + 1], TMP[2][:, 0:n], ALU.mult, ALU.mult,
                r=[PB[bq[j]], r_tmp[2]], pw=[rm("qnb")])
        for h in range(8):
            b1, b2 = k.bank(), k.bank()
            for (bb, co) in ((b1, h * 192), (b2, h * 192 + 96)):
                for j in range(3):
                    mm(PS[0:96, bb, 0:n], k.wqb[:, j, co:co + 96], k.qnb[:, j, 0:n], j == 0, j == 2,
                       r=[rm("wqb"), rm("qnb")], pw=[PB[bb]])
            cp("act", k.qT_B[0:64, h, c0:c0 + n], PS[0:64, b1, 0:n], r=[PB[b1]], pw=[rm("qTB")])
            ta, tb = (0, 1) if h % 2 == 0 else (3, 5)
            tt("dve", TMP[ta][64:96, 0:n], PS[64:96, b1, 0:n], k.tabs[2][64:96, 0:n], ALU.mult,
               r=[PB[b1], k.r_tab[2]], w=[r_tmp[ta]])
            tt("dve", TMP[tb][64:96, 0:n], PS[64:96, b2, 0:n], k.tabs[3][64:96, 0:n], ALU.mult,
               r=[PB[b2], k.r_tab[3]], w=[r_tmp[tb]])
            tt("dve", k.qT_B[64:96, h, c0:c0 + n], TMP[ta][64:96, 0:n], TMP[tb][64:96, 0:n], ALU.add,
               r=[r_tmp[ta], r_tmp[tb]], pw=[rm("qTB")])


def _attn_core(k, S_, items, scale):
    PS, PB = k.PS, k.PB
    LOOK = 2
    pend = []
    for it in items:
        if it.get("before") is not None:
            it["before"]()
        sb0, nb = it["sb"]
        pb_i = k.sctr % 4
        k.sctr += 1
        pws = [PB[sb0 + i] for i in range(nb)]
        first = True
        if it.get("mask") is not None:
            for (o_ap, m_ap) in it["mask"]:
                k.mm(o_ap, k.identb[:], m_ap, True, False, r=[k.r_const], pw=pws)
            first = False
        for (o_ap, lhsT, rhs) in it["s_mms"]:
            k.mm(o_ap, lhsT, rhs, first, True, r=it["r"], pw=pws)
        pv_ = it["p_view"](k.pbuf[pb_i])
        k.act(pv_, it["s_view"], AF.Exp, r=pws, w=[k.r_pbuf[pb_i]], scale=scale)

        def mk(it=it, pb_i=pb_i):
            def f():
                st = it["start"]
                npv = len(it["pv"])
                for pi, (o_ap, lhsT, rhs_fn) in enumerate(it["pv"]):
                    k.mm(o_ap, lhsT, rhs_fn(k.pbuf[pb_i]), st, it["stop"] and pi == npv - 1,
                         r=it["rv"] + [k.r_pbuf[pb_i]], pw=[PB[it["ob"]]])
                    st = False
                if it.get("after_pv") is not None:
                    it["after_pv"]()
            return f
        pend.append(mk())
        if len(pend) > LOOK:
            pend.pop(0)()
        if it.get("after") is not None:
            it["after"]()
    while pend:
        pend.pop(0)()


def _phaseC_mla(k, S_, l, with_ctx_q):
    PS, PB = k.PS, k.PB
    mm, act, tt, cp, rm = k.mm, k.act, k.tt, k.cp, k.rm
    S_.dma("pool", k.wkvb, k.d_wkvb.ap()[l].rearrange("(j p) c -> p j c", p=128), w=[rm("wkvb")])
    rcv = k.d_rcv.ap()
    rcv1 = k.d_rcv1.ap()
    EB = 7
    kchunks = [(rr, cc) for rr in range(4) for cc in range(4)] + [None]
    rec = k.TMP[0]
    def mk_head(h):
        def load_kv(idx):
            if idx >= len(kchunks) or kchunks[idx] is None:
                return
            rr, cc = kchunks[idx]
            i = idx % 2
            S_.dma("sp", k.kvbuf[i], rcv1[rr * 128:(rr + 1) * 128, PC_KVN:PC_KVN + 4096].rearrange(
                "p (j t) -> p j t", j=2)[:, :, cc * 512:(cc + 1) * 512], r=[rm("rcv1")], w=[rm("kvbuf%d" % i)])

        def load_kr(idx):
            if idx >= len(kchunks) or kchunks[idx] is None:
                return
            rr, cc = kchunks[idx]
            i = idx % 2
            S_.dma("sp", k.Kbuf[i][64:96, :], rcv[rr * 128 + cc * 32:rr * 128 + cc * 32 + 32, PC_KR:PC_KR + 512],
                   r=[rm("rcv")], pw=[rm("Kbuf%d" % i)])

        def expand_k(idx):
            kc = kchunks[idx]
            i = idx % 2
            src, rs, n, dst, rd = ((k.kvbuf[i], rm("kvbuf%d" % i), 512, k.Kbuf[i], rm("Kbuf%d" % i)) if kc is not None
                                   else (k.kvnc, rm("kvnc"), 256, k.Kc, rm("Kc")))
            for j in range(2):
                mm(PS[0:64, EB, 0:n], k.wkvb[:, j, h * 64:(h + 1) * 64], src[:, j, 0:n], j == 0, j == 1,
                   r=[rm("wkvb"), rs], pw=[PB[EB]])
            cp("dve", dst[0:64, 0:n], PS[0:64, EB, 0:n], r=[PB[EB]], pw=[rd])

        def expand_v(idx):
            kc = kchunks[idx]
            i = idx % 2
            src, rs, nt, dst, rd = ((k.kvbuf[i], rm("kvbuf%d" % i), 4, k.Vbuf[i], rm("Vbuf%d" % i)) if kc is not None
                                    else (k.kvnc, rm("kvnc"), 2, k.Vc, rm("Vc")))
            for t_ in range(nt):
                for j in range(2):
                    mm(PS[:, EB, t_ * 64:(t_ + 1) * 64], src[:, j, t_ * 128:(t_ + 1) * 128],
                       k.wkvb[:, j, 512 + h * 64:512 + (h + 1) * 64], j == 0, j == 1,
                       r=[rm("wkvb"), rs], pw=[PB[EB]])
            cp("dve", dst[:, 0:nt, 64:128], PS[:, EB, 0:nt * 64].rearrange("p (t d) -> p t d", t=nt),
               r=[PB[EB]], pw=[rd])
        return load_kv, load_kr, expand_k, expand_v

    heads = [mk_head(h) for h in range(8)]
    for h in range(8):
        e = h % 2
        vsl = slice(64, 192) if e == 0 else slice(0, 128)
        o_lo, r_lo = (0, 64) if e == 0 else (64, 0)
        load_kv, load_kr, expand_k, expand_v = heads[h]
        nxt = heads[h + 1] if h + 1 < 8 else None
        if h == 0:
            load_kv(0)
            load_kv(1)
            load_kr(0)
            expand_k(0)
            expand_v(0)
        items = []
        for idx, kc in enumerate(kchunks):
            i = idx % 2
            if kc is not None:
                Kt, rK, Vt, rV, nt = k.Kbuf[i], rm("Kbuf%d" % i), k.Vbuf[i], rm("Vbuf%d" % i), 4
            else:
                Kt, rK, Vt, rV, nt = k.Kc, rm("Kc"), k.Vc, rm("Vc"), 2
            cnt = 0
            for qc in range(4):
                for kt in range(nt):
                    sbank = 4 + (len(items) % 3)
                    it = dict(sb=(sbank, 1),
                              s_mms=[(PS[:, sbank, 0:512], Kt[0:96, kt * 128:(kt + 1) * 128],
                                      k.qT_B[0:96, h, qc * 512:(qc + 1) * 512])],
                              s_view=PS[:, sbank, 0:512], p_view=(lambda p: p[:, 0:512]),
                              r=[rK, rm("qTB")],
                              pv=[(PS[:, qc, 0:512], Vt[:, kt, vsl], (lambda p: p[:, 0:512]))], rv=[rV],
                              ob=qc, start=(idx == 0 and kt == 0), stop=(idx == len(kchunks) - 1 and kt == nt - 1))
                    cnt += 1
                    if cnt == 1:
                        def bef(idx=idx):
                            load_kv(idx + 2)
                            load_kr(idx + 1)
                            if idx == len(kchunks) - 1 and nxt is not None:
                                nxt[0](0)
                                nxt[0](1)
                                nxt[1](0)
                        it["before"] = bef
                    if idx + 1 < len(kchunks):
                        if cnt == 2:
                            it["after"] = (lambda idx=idx: expand_k(idx + 1))
                        elif cnt == 6:
                            it["after"] = (lambda idx=idx: expand_v(idx + 1))
                    elif nxt is not None:
                        if cnt == 2:
                            it["after"] = (lambda: nxt[2](0))
                        elif cnt == 6:
                            it["after"] = (lambda: nxt[3](0))
                    items.append(it)
        _attn_core(k, S_, items, SCALE_B)
        for qc in range(4):
            act(rec[o_lo:o_lo + 64, 0:512], PS[r_lo:r_lo + 64, qc, 0:512], AF.Ln, r=[PB[qc]], w=[k.r_tmp[0]])
            act(rec[o_lo:o_lo + 64, 0:512], rec[o_lo:o_lo + 64, 0:512], AF.Exp, r=[], w=[k.r_tmp[0]], scale=-1.0)
            tt("dve", k.Y_B[o_lo:o_lo + 64, h // 2, qc * 512:(qc + 1) * 512], PS[o_lo:o_lo + 64, qc, 0:512],
               rec[o_lo:o_lo + 64, 0:512], ALU.mult, r=[PB[qc], k.r_tmp[0]], pw=[rm("YB")])
        if with_ctx_q:
            items = []
            for kt in range(2):
                sbank = 4 + kt
                items.append(dict(sb=(sbank, 1),
                                  s_mms=[(PS[:, sbank, 0:256], k.Kc[0:96, kt * 128:(kt + 1) * 128],
                                          k.qT_B[0:96, h, 2048:2304])],
                                  s_view=PS[:, sbank, 0:256], p_view=(lambda p: p[:, 0:256]),
                                  r=[rm("Kc"), rm("qTB")],
                                  pv=[(PS[:, 6, 0:256], k.Vc[:, kt, vsl], (lambda p: p[:, 0:256]))], rv=[rm("Vc")],
                                  ob=6, start=(kt == 0), stop=(kt == 1)))
            _attn_core(k, S_, items, SCALE_B)
            act(rec[o_lo:o_lo + 64, 0:256], PS[r_lo:r_lo + 64, 6, 0:256], AF.Ln, r=[PB[6]], w=[k.r_tmp[0]])
            act(rec[o_lo:o_lo + 64, 0:256], rec[o_lo:o_lo + 64, 0:256], AF.Exp, r=[], w=[k.r_tmp[0]], scale=-1.0)
            tt("dve", k.Y_B[o_lo:o_lo + 64, h // 2, 2048:2304], PS[o_lo:o_lo + 64, 6, 0:256],
               rec[o_lo:o_lo + 64, 0:256], ALU.mult, r=[PB[6], k.r_tmp[0]], pw=[rm("YB")])


def _phaseD_A(k, S_, l, nchunks, pre=None):
    PS, PB, hxT = k.PS, k.PB, k.hxT
    mm, act, tt, ts, rm = k.mm, k.act, k.tt, k.ts, k.rm
    TMP, r_tmp = k.TMP, k.r_tmp
    if pre is None:
        pre = _prefetch_A(k, l)
    wq = [pre[0], pre[1]]
    wza, r_wza = pre[2]
    gctr = [0]
    for c in range(nchunks):
        c0, n = CHUNKS[c]
        _load_tabs(k, S_, c0, n, which=(0, 1))
        for j in range(4):
            wt, rw = wq[j // 2]
            base = (j % 2) * 256
            bq, br = k.bank(), k.bank()
            for (bb, co) in ((bq, base), (br, base + 128)):
                for kk in range(8):
                    mm(PS[:, bb, 0:n], wt[:, kk, co:co + 128], hxT[:, kk, c0:c0 + n], kk == 0, kk == 7,
                       r=[rw, k.r_hx[c]], pw=[PB[bb]])
            tt("dve", TMP[0][:, 0:n], PS[:, bq, 0:n], k.tabs[0][:, 0:n], ALU.mult, r=[PB[bq], k.r_tab[0]], w=[r_tmp[0]])
            tt("dve", TMP[1][:, 0:n], PS[:, br, 0:n], k.tabs[1][:, 0:n], ALU.mult, r=[PB[br], k.r_tab[1]], w=[r_tmp[1]])
            tt("dve", k.qA[:, j, 0:n], TMP[0][:, 0:n], TMP[1][:, 0:n], ALU.add, r=[r_tmp[0], r_tmp[1]], pw=[rm("qA")])
        for jt in range(4):
            bb = k.bank()
            for kk in range(8):
                mm(PS[:, bb, 0:n], wza[:, kk, jt * 128:(jt + 1) * 128], hxT[:, kk, c0:c0 + n], kk == 0, kk == 7,
                   r=[r_wza, k.r_hx[c]], pw=[PB[bb]])
            act(k.sza[:, jt, 0:n], PS[:, bb, 0:n], AF.Silu, r=[PB[bb]], pw=[rm("sza")])
        items = []
        groups = []
        for qb in range(n // 128):
            for g in range(2):
                if c < 4:
                    qbg = c * 4 + qb
                    tiles = [(qbg * 128, qbg, 2 if qbg == 0 else 0), ((qbg + 1) * 128, qbg + 1, None),
                             ((qbg + 2) * 128, qbg + 2, 3 if qbg == 15 else 1), (2304, 18, None), (2432, 19, None)]
                else:
                    tiles = [(2304, 18, None), (2432, 19, None)]
                gi = gctr[0]
                gctr[0] += 1
                ob = gi % 2
                groups.append((qb, g, ob, gi))
                for ti, (koff, vt, mi) in enumerate(tiles):
                    sb0 = 2 + 2 * ((len(items)) % 3)
                    it = dict(sb=(sb0, 2),
                              s_mms=[(PS[:, sb0 + e, 0:256], k.kT_A[e * 64:(e + 1) * 64, g, koff:koff + 128],
                                      k.qA[e * 64:(e + 1) * 64, 2 * g:2 * g + 2, qb * 128:(qb + 1) * 128])
                                     for e in range(2)],
                              mask=(None if mi is None else [(PS[:, sb0 + e, 0:256], k.masks[:, mi, 0:256])
                                                             for e in range(2)]),
                              s_view=PS[:, sb0:sb0 + 2, 0:256],
                              p_view=(lambda p: p[:, 0:512].rearrange("p (e c) -> p e c", e=2)),
                              r=[rm("kTA"), rm("qA")],
                              pv=[(PS[:, ob, 0:256], k.V_A[:, vt, g, 64:192], (lambda p: p[:, 0:256])),
                                  (PS[:, ob, 256:512], k.V_A[:, vt, g, 0:128], (lambda p: p[:, 256:512]))],
                              rv=[rm("VA")], ob=ob, start=(ti == 0), stop=(ti == len(tiles) - 1))
                    items.append(it)

        def normalize(qb, g, ob, gi, c0=c0):
            ta, tb = (TMP[0], TMP[1]) if gi % 2 == 0 else (TMP[2], TMP[3])
            ra, rb = (r_tmp[0], r_tmp[1]) if gi % 2 == 0 else (r_tmp[2], r_tmp[3])
            tok0 = c0 + qb * 128
            for e in range(2):
                o_lo, r_lo = (0, 64) if e == 0 else (64, 0)
                cs_ = slice(e * 256, (e + 1) * 256)
                for jj in range(2):
                    h = 4 * g + 2 * jj + e
                    cj = slice(e * 256 + jj * 128, e * 256 + (jj + 1) * 128)
                    ts("dve", ta[o_lo:o_lo + 64, cj], PS[r_lo:r_lo + 64, ob, cj], k.esink[r_lo:r_lo + 64, l, h:h + 1],
                       ALU.add, r=[PB[ob], rm("esink")], pw=[ra])
                act(ta[o_lo:o_lo + 64, cs_], ta[o_lo:o_lo + 64, cs_], AF.Ln, r=[], w=[ra])
                act(ta[o_lo:o_lo + 64, cs_], ta[o_lo:o_lo + 64, cs_], AF.Exp, r=[], w=[ra], scale=-1.0)
                tt("dve", tb[o_lo:o_lo + 64, cs_], PS[o_lo:o_lo + 64, ob, cs_], ta[o_lo:o_lo + 64, cs_], ALU.mult,
                   r=[PB[ob], ra], pw=[rb])
                tt("dve", k.Y_A[o_lo:o_lo + 64, 2 * g:2 * g + 2, tok0:tok0 + 128],
                   tb[o_lo:o_lo + 64, cs_].rearrange("p (j q) -> p j q", j=2),
                   k.sza[o_lo:o_lo + 64, 2 * g:2 * g + 2, qb * 128:(qb + 1) * 128], ALU.mult,
                   r=[rb, rm("sza")], pw=[rm("YA")])

        per = len(items) // len(groups)
        for gidx in range(len(groups)):
            items[gidx * per + per - 1]["after_pv"] = (lambda a=groups[gidx]: normalize(*a))
        _attn_core(k, S_, items, SCALE_A)


def _prefetch_A(k, l):
    return [k.load_w(k.win_src(l, "QA", 0, 512), 512), k.load_w(k.win_src(l, "QA", 512, 512), 512),
            k.load_w(k.win_src(l, "ZA", 0, 512), 512)]


def _phaseD_zb(k, S_, l, nchunks):
    PS, PB, hxT = k.PS, k.PB, k.hxT
    wzb, r_wzb = k.load_w(k.win_src(l, "ZB", 0, 512), 512)
    for c in range(nchunks):
        c0, n = CHUNKS[c]
        for jt in range(4):
            bb = k.bank()
            for kk in range(8):
                k.mm(PS[:, bb, 0:n], wzb[:, kk, jt * 128:(jt + 1) * 128], hxT[:, kk, c0:c0 + n], kk == 0, kk == 7,
                     r=[r_wzb, k.r_hx[c]], pw=[PB[bb]])
            sq = k.sqb[:, jt % 3, 0:n]
            k.act(sq, PS[:, bb, 0:n], AF.Silu, r=[PB[bb]], w=[k.rm("sqb%d" % (jt % 3))])
            k.tt("dve", k.Y_B[:, jt, c0:c0 + n], k.Y_B[:, jt, c0:c0 + n], sq, ALU.mult,
                 r=[k.rm("sqb%d" % (jt % 3))], pw=[k.rm("YB")])


def _phaseD_C(k, S_, l, nchunks):
    PS, PB, hxT = k.PS, k.PB, k.hxT
    mm, act, tt, ts, stt, cp, rm = k.mm, k.act, k.tt, k.ts, k.stt, k.cp, k.rm
    TMP, r_tmp = k.TMP, k.r_tmp
    zrow = k.zrow
    for ct in range(4):
        wc, r_wc = k.load_w(k.win_src(l, "C", ct * 512, 512), 512)
        cp("dve", zrow[:, 0:1], k.zhalo[:, ct, 0:1], r=[rm("zhalo")], pw=[rm("zrow")])
        cp("dve", zrow[:, 2049:2050], k.zhalo[:, ct, 1:2], r=[rm("zhalo")], pw=[rm("zrow")])
        S_.op("dve", lambda e: e.memset(zrow[:, 2050:2051], 0.0), pw=[rm("zrow")])
        S_.op("dve", lambda e: e.memset(zrow[:, 2307:2308], 0.0), pw=[rm("zrow")])
        zo = lambda c: (1 + CHUNKS[c][0]) if c < 4 else 2051
        for c in range(nchunks):
            c0, n = CHUNKS[c]
            bcc, buc = k.bank(), k.bank()
            for (bb, co) in ((bcc, 128), (buc, 256)):
                for kk in range(8):
                    mm(PS[:, bb, 0:n], wc[:, kk, co:co + 128], hxT[:, kk, c0:c0 + n], kk == 0, kk == 7,
                       r=[r_wc, k.r_hx[c]], pw=[PB[bb]])
            cp("act", TMP[0][:, 0:n], PS[:, bcc, 0:n], r=[PB[bcc]], w=[r_tmp[0]])
            tt("dve", zrow[:, zo(c):zo(c) + n], TMP[0][:, 0:n], PS[:, buc, 0:n], ALU.mult,
               r=[r_tmp[0], PB[buc]], pw=[rm("zrow")])
        for c in range(nchunks):
            c0, n = CHUNKS[c]
            z0 = zo(c)
            bbc, bzc = k.bank(), k.bank()
            for (bb, co) in ((bbc, 0), (bzc, 384)):
                for kk in range(8):
                    mm(PS[:, bb, 0:n], wc[:, kk, co:co + 128], hxT[:, kk, c0:c0 + n], kk == 0, kk == 7,
                       r=[r_wc, k.r_hx[c]], pw=[PB[bb]])
            act(k.sqb[:, 0, 0:n], PS[:, bzc, 0:n], AF.Silu, r=[PB[bzc]], w=[rm("sqb0")])
            ts("dve", TMP[1][:, 0:n], zrow[:, z0 - 1:z0 - 1 + n], k.convw[:, l, ct, 0:1], ALU.mult,
               r=[rm("zrow")], w=[r_tmp[1]])
            stt(TMP[1][:, 0:n], zrow[:, z0:z0 + n], k.convw[:, l, ct, 1:2], TMP[1][:, 0:n], ALU.mult, ALU.add,
                r=[rm("zrow")], w=[r_tmp[1]])
            stt(TMP[1][:, 0:n], zrow[:, z0 + 1:z0 + 1 + n], k.convw[:, l, ct, 2:3], TMP[1][:, 0:n], ALU.mult, ALU.add,
                r=[rm("zrow")], w=[r_tmp[1]])
            tt("dve", TMP[2][:, 0:n], TMP[1][:, 0:n], PS[:, bbc, 0:n], ALU.mult, r=[r_tmp[1], PB[bbc]], w=[r_tmp[2]])
            tt("dve", k.Y_C[:, ct, c0:c0 + n], TMP[2][:, 0:n], k.sqb[:, 0, 0:n], ALU.mult,
               r=[r_tmp[2], rm("sqb0")], pw=[rm("YC")])


def _phaseE_merge(k, S_, l, nchunks):
    PS, PB, hxT = k.PS, k.PB, k.hxT
    mm, act, tt, rm = k.mm, k.act, k.tt, k.rm
    TMP, r_tmp = k.TMP, k.r_tmp
    Y = [k.Y_A, k.Y_B, k.Y_C]
    rY = [rm("YA"), rm("YB"), rm("YC")]
    for mt in range(8):
        wg, r_wg = k.load_w(k.win_src(l, "G", mt * 384, 384), 384)
        wbrb = k.wbrb2[mt % 2]
        r_wbrb = rm("wbrb%d" % (mt % 2))
        S_.dma("pool", wbrb[:], k.d_wbr.ap()[l].rearrange("b (kc p) c -> p b kc c", p=128)[:, :, :, mt * 128:(mt + 1) * 128],
               w=[r_wbrb])
        for c in range(nchunks):
            c0, n = CHUNKS[c]
            bp = [k.bank() for _ in range(3)]
            bg = [k.bank() for _ in range(3)]
            for br in range(3):
                for kc in range(4):
                    mm(PS[:, bp[br], 0:n], wbrb[:, br, kc, :], Y[br][:, kc, c0:c0 + n], kc == 0, kc == 3,
                       r=[r_wbrb, rY[br]], pw=[PB[bp[br]]])
            for br in range(3):
                for kk in range(8):
                    mm(PS[:, bg[br], 0:n], wg[:, kk, br * 128:(br + 1) * 128], hxT[:, kk, c0:c0 + n], kk == 0, kk == 7,
                       r=[r_wg, k.r_hx[c]], pw=[PB[bg[br]]])
            for br in range(3):
                act(TMP[br][:, 0:n], PS[:, bg[br], 0:n], AF.Sigmoid, r=[PB[bg[br]]], w=[r_tmp[br]])
            for br in range(3):
                tt("dve", TMP[br][:, 0:n], TMP[br][:, 0:n], PS[:, bp[br], 0:n], ALU.mult, r=[PB[bp[br]]], w=[r_tmp[br]])
            tt("dve", TMP[0][:, 0:n], TMP[0][:, 0:n], TMP[1][:, 0:n], ALU.add, r=[r_tmp[1]], w=[r_tmp[0]])
            tt("dve", k.mT[:, mt, c0:c0 + n], TMP[0][:, 0:n], TMP[2][:, 0:n], ALU.add, r=[r_tmp[0], r_tmp[2]],
               pw=[rm("mT")])


def _phaseF_out(k, S_, l, ntiles):
    PS, PB = k.PS, k.PB
    mm, act, tt, ts, stt, rm = k.mm, k.act, k.tt, k.ts, k.stt, k.rm
    TMP, r_tmp = k.TMP, k.r_tmp
    last = (l == DEPTH - 1)
    wsrc = k.d_wmod.ap()[l].rearrange("(k p) c -> p k c", p=128)
    scs3 = k.scs[:].rearrange("p (k s) -> p k s", s=2)
    for n_ in range(2):
        wt, rw = k.load_w(wsrc[:, :, 2048 + n_ * 512:2048 + n_ * 512 + 512], 512)
        bg = k.bank()
        for kk in range(8):
            mm(PS[0:2, bg, 0:512], scs3[:, kk, :], wt[:, kk, :], kk == 0, kk == 7, r=[rw, rm("scs")], pw=[PB[bg]])
        S_.dma("sp", TMP[3][0:2, 0:512], k.d_bmodg.ap()[:, l, n_ * 512:(n_ + 1) * 512], w=[r_tmp[3]])
        tt("dve", TMP[2][0:2, 0:512], PS[0:2, bg, 0:512], TMP[3][0:2, 0:512], ALU.add, r=[PB[bg], r_tmp[3]], w=[r_tmp[2]])
        for which, Gt, rG in ((0, k.G_x, rm("Gx")), (1, k.G_c, rm("Gc"))):
            if which == 1 and ntiles <= 16:
                continue
            hs = slice(n_ * 512, (n_ + 1) * 512)
            S_.dma("sp", Gt[:, hs], k.d_gpostb.ap()[l][:, hs], pw=[rG])
            bb = k.bank()
            mm(PS[:, bb, 0:512], k.sel[0:2, which * 128:(which + 1) * 128], TMP[2][0:2, 0:512],
               True, True, r=[r_tmp[2], k.r_const], pw=[PB[bb]])
            tt("dve", Gt[:, hs], Gt[:, hs], PS[:, bb, 0:512], ALU.mult, r=[PB[bb], rG], pw=[rG])
    wo0, r_wo0 = k.load_w(k.d_wo.ap()[l].rearrange("(k p) c -> p k c", p=128)[:, :, 0:512], 512)
    wo1, r_wo1 = k.load_w(k.d_wo.ap()[l].rearrange("(k p) c -> p k c", p=128)[:, :, 512:1024], 512)
    wo = [(wo0, r_wo0), (wo1, r_wo1)]
    sm = k.small
    def load_xold(i_):
        if i_ >= ntiles:
            return
        if i_ < 16:
            src_ = (k.d_x.ap() if l == 0 else k.d_x1.ap())[i_ * 128:(i_ + 1) * 128, :]
        else:
            src_ = k.d_ctx.ap()[(i_ - 16) * 128:(i_ - 15) * 128, :]
        S_.dma("sp", k.xoldb[i_ % 3], src_, r=([rm("x1d")] if (l > 0 and i_ < 16) else []), w=[rm("xold%d" % (i_ % 3))])

    bos = {}
    s1b = {}

    def part_a(i):
        p = i % 2
        rs = lambda n_: rm("%s%d" % (n_, i % 4))
        q0 = 16 * (i % 4) + 4
        bo = [k.bank(), k.bank()]
        bos[i] = bo
        for hf in range(2):
            for kk in range(8):
                mm(PS[:, bo[hf], 0:512], k.mT[:, kk, i * 128:(i + 1) * 128], wo[hf][0][:, kk, :], kk == 0, kk == 7,
                   r=[wo[hf][1], rm("mT")], pw=[PB[bo[hf]]])
        for hf in range(2):
            act(k.junk2[p][:, hf * 512:(hf + 1) * 512], PS[:, bo[hf], 0:512], AF.Square, r=[PB[bo[hf]]],
                w=[rm("junkf%d%d" % (p, hf)), rs("ssq2%d" % hf)], accum_out=sm[:, q0 + hf:q0 + hf + 1])
        tt("dve", sm[:, q0 + 2:q0 + 3], sm[:, q0:q0 + 1], sm[:, q0 + 1:q0 + 2], ALU.add, r=[rs("ssq20"), rs("ssq21")],
           w=[rs("ssq2")])
        act(sm[:, q0 + 3:q0 + 4], sm[:, q0 + 2:q0 + 3], AF.Ln, r=[rs("ssq2")], w=[rs("ln2")], scale=1.0 / D,
            bias=k.epsc[:, 0:1])
        act(sm[:, q0 + 4:q0 + 5], sm[:, q0 + 3:q0 + 4], AF.Exp, r=[rs("ln2")], w=[rs("rstd2")], scale=-0.5)

    def part_b(i):
        j = i % 2
        p = i % 2
        isx = i < 16
        Gt, rG = (k.G_x, rm("Gx")) if isx else (k.G_c, rm("Gc"))
        rs = lambda n_: rm("%s%d" % (n_, i % 4))
        q0 = 16 * (i % 4) + 4
        xold = k.xoldb[i % 3]
        r_xold = rm("xold%d" % (i % 3))
        bo = bos.pop(i)
        for hf in range(2):
            hs = slice(hf * 512, (hf + 1) * 512)
            tmpi = 2 * p + hf
            stt(TMP[tmpi][:, 0:512], PS[:, bo[hf], 0:512], sm[:, q0 + 4:q0 + 5], Gt[:, hs], ALU.mult, ALU.mult,
                r=[PB[bo[hf]], rs("rstd2"), rG], w=[r_tmp[tmpi]])
            tt("dve", k.xt[j][:, hs], TMP[tmpi][:, 0:512], xold[:, hs], ALU.add, r=[r_tmp[tmpi], r_xold],
               pw=[k.r_xt[j]])
        if isx:
            dst = (k.d_y.ap() if last else k.d_x1.ap())[i * 128:(i + 1) * 128, :]
            S_.dma("sp", dst, k.xt[j], r=[k.r_xt[j]], pw=[rm("yd") if last else rm("x1d")])

    load_xold(0)
    for s_ in range(ntiles + 3):
        if s_ < ntiles:
            part_a(s_)
        if 0 <= s_ - 1 < ntiles:
            part_b(s_ - 1)
        load_xold(s_ + 1)
        if not last:
            if 0 <= s_ - 2 < ntiles:
                i_ = s_ - 2
                s1b[i_] = _stage1_a(k, S_, l + 1, i_, k.xt[i_ % 2], k.r_xt[i_ % 2])
            if 0 <= s_ - 3 < ntiles:
                i_ = s_ - 3
                _stage1_b(k, S_, l + 1, i_, *s1b.pop(i_))


def _program(k, S_, upto):
    k.sctr = 0
    k.r_pbuf = [Res("pbuf%d" % i) for i in range(4)]
    if upto <= 0:
        return
    if upto == 1 and DBG_TILES:
        k.ntilesA = DBG_TILES
    _phaseA(k, S_, 0)
    k.mod_layer(1)
    k.dump("hxT0", k.hxT[:], [128, 8 * TT], BF16)
    if upto <= 1:
        return
    for l in range(DEPTH):
        nch = 5 if l == 0 else 4
        S_.barrier()
        _phaseB_kv(k, S_, l)
        _phaseB_qb(k, S_, l, nch)
        _phaseB_halo(k, S_, l)
        if l == 0:
            k.dump("kTA", k.kT_A, [128, 5120], BF16)
            k.dump("VA", k.V_A, [128, 7680], BF16)
            k.dump("qTB", k.qT_B[0:96])
            k.dump("kvnc", k.kvnc, [128, 512], BF16)
            k.dump("zhalo", k.zhalo[:], [128, 8], F32)
        if upto <= 2 and l == 0:
            return
        S_.barrier()
        preA = _prefetch_A(k, l)
        _phaseC_mla(k, S_, l, with_ctx_q=(l == 0))
        if upto <= 3 and l == 0:
            k.dump("YB", k.Y_B, [128, 4 * TT], BF16)
            return
        S_.barrier()
        _phaseD_A(k, S_, l, nch, pre=preA)
        _phaseD_zb(k, S_, l, nch)
        if upto <= 4 and l == 0:
            k.dump("YB", k.Y_B, [128, 4 * TT], BF16)
            k.dump("YA", k.Y_A, [128, 4 * TT], BF16)
            return
        S_.barrier()
        _phaseD_C(k, S_, l, nch)
        if upto <= 5 and l == 0:
            k.dump("YC", k.Y_C, [128, 4 * TT], BF16)
            return
        S_.barrier()
        _phaseE_merge(k, S_, l, nch)
        if upto <= 6 and l == 0:
            k.dump("mT", k.mT, [128, 8 * TT], BF16)
            return
        S_.barrier()
        _phaseF_out(k, S_, l, 18 if l == 0 else 16)
        if upto <= 7 and l == 0:
            k.dump("hxT1", k.hxT[:], [128, 8 * TT], BF16)
            return


def kernel(**inputs):
    in_maps = prepare_inputs(**inputs)
    nc = build()
    res = run_bass_kernel_spmd(nc, in_maps, core_ids=list(range(NCORE)))
    out = np.zeros((2, S, D), np.float32)
    for core in range(NCORE):
        b, r = divmod(core, 4)
        out[b, r * T:(r + 1) * T] = np.asarray(res.results[core]["y"], np.float32)
    return out
```

```python
import os
import numpy as np
import ml_dtypes
import concourse.bass as bass
import concourse.mybir as mybir
from concourse.bass_utils import run_bass_kernel_spmd

F32 = mybir.dt.float32
BF16 = mybir.dt.bfloat16
AF = mybir.ActivationFunctionType
ALU = mybir.AluOpType
AP = bass.AP

D = 1024
S = 8192
L = 256
DEPTH = 2
NCORE = 8
T = 2048
TT = T + L
GRID_W = 64
EPS = 1e-6
SCALE_A = 64 ** -0.5
SCALE_B = 96 ** -0.5
NEG = -30000.0
DBG_TILES = 0
CHUNKS = [(0, 512), (512, 512), (1024, 512), (1536, 512), (2048, 256)]

O_QA, O_KA, O_VA, O_ZA = 0, 512, 640, 768
O_QL, O_KVL, O_KR, O_ZB = 1280, 1664, 1920, 1952
O_BC, O_CC, O_UC, O_ZC = 2464, 2976, 3488, 4000
O_GA, O_GB, O_GC = 4512, 5536, 6560


def _win_perm():
    cols = []
    off = {}

    def add(name, idx):
        off[name] = len(cols)
        cols.extend(list(idx))

    r64 = np.arange(64)
    rot64 = np.concatenate([r64[32:], r64[:32]])
    r32 = np.arange(32)
    rot32 = np.concatenate([r32[16:], r32[:16]])
    ka = []
    for g in range(2):
        base = O_KA + g * 64
        ka += list(base + r64) + list(base + r64)
        ka += list(base + rot64) + list(base + rot64)
    add("KA", ka)
    add("VKK", list(O_VA + np.arange(128)) + list(O_KVL + np.arange(256))
        + list(O_KR + r32) + list(O_KR + rot32))
    add("ZH", list(O_CC + np.arange(512)) + list(O_UC + np.arange(512)))
    add("QL", list(O_QL + np.arange(384)))
    qa = []
    for j in range(4):
        for e in range(2):
            qa += list(O_QA + (2 * j + e) * 64 + r64)
        for e in range(2):
            qa += list(O_QA + (2 * j + e) * 64 + rot64)
    add("QA", qa)
    add("ZA", list(O_ZA + np.arange(512)))
    add("ZB", list(O_ZB + np.arange(512)))
    cc = []
    for ct in range(4):
        for o in (O_BC, O_CC, O_UC, O_ZC):
            cc += list(o + ct * 128 + np.arange(128))
    add("C", cc)
    gg = []
    for mt in range(8):
        for o in (O_GA, O_GB, O_GC):
            gg += list(o + mt * 128 + np.arange(128))
    add("G", gg)
    return np.asarray(cols, dtype=np.int64), off


WIN_PERM, WOFF = _win_perm()
NCW = len(WIN_PERM)

PC_KVN = 0
PC_N1 = 4096
PC_KR = 0
PC_KAH = 512
PC_VAH = 1024
PC_ZH = 1280
PC_N2 = 1296


class Res:
    __slots__ = ("name", "writers", "readers", "excl", "last")

    def __init__(self, name, excl=False):
        self.name = name
        self.writers = []
        self.readers = []
        self.excl = excl
        self.last = {}


class Op:
    __slots__ = ("eng", "fn", "deps", "kind", "signaled", "sem", "val", "idx")


def _prune(lst):
    out = []
    seen = set()
    for o in reversed(lst):
        if o.kind != "c":
            out.append(o)
        elif o.eng not in seen:
            seen.add(o.eng)
            out.append(o)
    out.reverse()
    return out


class Sched:
    ENG = ("pe", "act", "dve", "pool", "sp")

    def __init__(self):
        self.prog = {e: [] for e in self.ENG}
        self.all = []
        self.pending_dma = []

    def op(self, eng, fn, r=(), w=(), pw=(), kind="c", extra=()):
        o = Op()
        o.eng, o.fn, o.kind, o.signaled, o.sem, o.val = eng, fn, kind, False, None, 0
        o.idx = len(self.all)
        deps = set(x for x in extra if x is not None)
        allres = list(r) + list(w) + list(pw)
        r = [x for x in r if not x.excl]
        w = [x for x in w if not x.excl]
        pw = [x for x in pw if not x.excl]
        for res in allres:
            if res.excl:
                for e2, o2 in res.last.items():
                    if e2 != eng:
                        deps.add(o2)
                res.last[eng] = o
        for res in r:
            deps.update(res.writers)
        for res in w:
            deps.update(res.writers)
            deps.update(res.readers)
        for res in pw:
            deps.update(res.readers)
        for res in r:
            res.readers.append(o)
            if len(res.readers) > 12:
                res.readers = _prune(res.readers)
        for res in w:
            res.writers = [o]
            res.readers = []
        for res in pw:
            if res.readers:
                res.writers = [o]
                res.readers = []
            else:
                res.writers.append(o)
                if len(res.writers) > 12:
                    res.writers = _prune(res.writers)
        deps.discard(o)
        o.deps = deps
        self.prog[eng].append(o)
        self.all.append(o)
        if kind != "c":
            self.pending_dma.append(o)
        return o

    def dma(self, eng, out, in_, r=(), w=(), pw=(), extra=(), **kw):
        return self.op(eng, lambda e: e.dma_start(out=out, in_=in_, **kw), r=r, w=w, pw=pw,
                       kind="d", extra=extra)

    def barrier(self):
        last = [self.prog[e][-1] for e in self.ENG if self.prog[e]]
        last += self.pending_dma
        self.pending_dma = []
        for e in self.ENG:
            self.op(e, None, extra=last)

    def emit(self, nc, stack):
        NS = 20
        for o in self.all:
            for d in o.deps:
                if d.kind == "c" and d.eng == "pe" and o.eng == "pe" and o.kind == "c":
                    continue
                d.signaled = True
        esem = {e: stack.enter_context(nc.semaphore("s_" + e)) for e in self.ENG}
        dsem = {e: [stack.enter_context(nc.semaphore("d_%s%d" % (e, i))) for i in range(NS)]
                for e in ("sp", "pool", "act")}
        ccsem = stack.enter_context(nc.semaphore("s_cc"))
        cnt = {e: 0 for e in self.ENG}
        dcnt = {e: 0 for e in dsem}
        dval = {}
        prev = {}
        ccn = 0
        for e in self.ENG:
            for o in self.prog[e]:
                if o.kind == "c":
                    if o.signaled and o.fn is not None:
                        cnt[e] += 1
                        o.sem, o.val = esem[e], cnt[e]
                    elif o.fn is None:
                        o.sem, o.val = None, 0
                elif o.kind == "d":
                    s = dsem[e][dcnt[e] % NS]
                    dcnt[e] += 1
                    prev[o] = dval.get(id(s), 0)
                    dval[id(s)] = prev[o] + 16
                    o.sem, o.val = s, dval[id(s)]
                else:
                    ccn += 1
                    o.sem, o.val = ccsem, ccn
        self.n_inst = {e: len(self.prog[e]) for e in self.ENG}
        block = stack.enter_context(nc.Block())
        sched = self

        def run(e, engine):
            waited = {}
            for o in sched.prog[e]:
                waits = {}
                for d in o.deps:
                    if d.kind == "c" and d.eng == "pe" and o.eng == "pe" and o.kind == "c":
                        continue
                    if d.sem is None:
                        continue
                    k = id(d.sem)
                    if k not in waits or waits[k][1] < d.val:
                        waits[k] = (d.sem, d.val)
                if o.kind == "d" and prev[o] > 0:
                    k = id(o.sem)
                    if k not in waits or waits[k][1] < prev[o]:
                        waits[k] = (o.sem, prev[o])
                for k, (s, v) in waits.items():
                    if waited.get(k, 0) < v:
                        engine.wait_ge(s, v)
                        waited[k] = v
                if o.fn is None:
                    continue
                inst = o.fn(engine)
                if o.kind == "d":
                    inst.then_inc(o.sem, 16)
                elif o.kind == "cc":
                    inst.then_inc(o.sem)
                elif o.signaled:
                    inst.then_inc(o.sem, 1)

        @block.tensor
        def _(te):
            run("pe", te)

        @block.scalar
        def _(sc):
            run("act", sc)

        @block.vector
        def _(ve):
            run("dve", ve)

        @block.gpsimd
        def _(gp):
            run("pool", gp)

        @block.sync
        def _(sy):
            run("sp", sy)


def _rope_tables(rank):
    g = rank * T + np.arange(T)
    row = (g // GRID_W).astype(np.float64)
    col = (g % GRID_W).astype(np.float64)

    def tab(rot_dim):
        axis_dim = rot_dim // 2
        inv = 10000.0 ** (-np.arange(0, axis_dim, 2, dtype=np.float64) / axis_dim)
        ang = np.concatenate([row[:, None] * inv, col[:, None] * inv], axis=-1)
        half = rot_dim // 2
        d = np.arange(rot_dim)
        c = np.cos(ang)[:, d % half].T
        s = np.sin(ang)[:, d % half].T
        s = np.where((d < half)[:, None], -s, s)
        cfull = np.ones((rot_dim, TT)); sfull = np.zeros((rot_dim, TT))
        cfull[:, :T] = c; sfull[:, :T] = s
        return cfull, sfull

    ca, sa = tab(64)
    cb, sb = tab(32)
    cosA = np.concatenate([ca, ca], 0).astype(np.float32)
    sinA = np.concatenate([sa, sa], 0).astype(np.float32)
    cosB = np.zeros((128, TT), np.float32); sinB = np.zeros((128, TT), np.float32)
    cosB[0:32] = cb; cosB[64:96] = cb
    sinB[0:32] = sb; sinB[64:96] = sb
    return cosA, sinA, cosB, sinB


def _masks(rank):
    kk = np.arange(128)[:, None]
    qq = np.arange(128)[None, :]
    mp = np.where(kk >= qq, 0.0, NEG).astype(np.float32)
    mn = np.where(kk <= qq, 0.0, NEG).astype(np.float32)
    allneg = np.full((128, 128), NEG, np.float32)
    m = np.stack([np.tile(mp, (1, 4)), np.tile(mn, (1, 4)),
                  np.tile(mp if rank > 0 else allneg, (1, 4)),
                  np.tile(mn if rank < 3 else allneg, (1, 4))], axis=1)
    return m.astype(ml_dtypes.bfloat16)


def _fm(v, k):
    return np.ascontiguousarray(np.asarray(v, np.float32).reshape(k, 128).T)


def prepare_inputs(x, c, ctx, c_ctx, w_mod, b_mod, g_pre, g_post, w_in, sink,
                   g_qa, w_qb, g_kva, w_kvb, conv_w, w_branch, w_o):
    f = lambda a: np.asarray(a, np.float32)
    x, c, ctx, c_ctx = f(x), f(c), f(ctx), f(c_ctx)
    w_mod, b_mod, g_pre, g_post = f(w_mod), f(b_mod), f(g_pre), f(g_post)
    w_in, sink, g_qa, w_qb, g_kva, w_kvb = f(w_in), f(sink), f(g_qa), f(w_qb), f(g_kva), f(w_kvb)
    conv_w, w_branch, w_o = f(conv_w), f(w_branch), f(w_o)
    shared = {}
    shared["wmod"] = np.ascontiguousarray(w_mod)
    shared["bmodT"] = np.ascontiguousarray(np.stack([_fm(b_mod[l, :2048], 16) for l in range(DEPTH)], 1))
    shared["bmodg"] = np.ascontiguousarray(np.stack([np.stack([b_mod[l, 2048:], b_mod[l, 2048:]], 0)
                                                     for l in range(DEPTH)], 1))
    shared["gpreT"] = np.ascontiguousarray(np.stack([_fm(g_pre[l], 8) for l in range(DEPTH)], 1))
    shared["gpostb"] = np.ascontiguousarray(np.broadcast_to(g_post[:, None, :], (DEPTH, 128, D)))
    shared["win"] = np.ascontiguousarray(w_in[:, :, WIN_PERM])
    shared["sinkb"] = np.ascontiguousarray(np.broadcast_to(sink[None, :, :], (128, DEPTH, 8)))
    shared["gqaT"] = np.ascontiguousarray(np.stack([_fm(g_qa[l], 3) for l in range(DEPTH)], 1))
    shared["gkvaT"] = np.ascontiguousarray(np.stack([_fm(g_kva[l], 2) for l in range(DEPTH)], 1))
    r32 = np.arange(32)
    qcols = []
    for h in range(8):
        base = h * 96
        qcols += list(base + np.arange(96))
        qcols += list(base + np.arange(64)) + list(base + 64 + np.concatenate([r32[16:], r32[:16]]))
    shared["wqb"] = np.ascontiguousarray(w_qb[:, :, np.asarray(qcols)])
    kcols = [h * 128 + i for h in range(8) for i in range(64)]
    vcols = [h * 128 + 64 + i for h in range(8) for i in range(64)]
    shared["wkvb"] = np.ascontiguousarray(w_kvb[:, :, np.asarray(kcols + vcols)])
    cw = np.zeros((128, DEPTH, 4, 3), np.float32)
    for l in range(DEPTH):
        for k in range(3):
            cw[:, l, :, k] = conv_w[l, k].reshape(4, 128).T
    shared["convw"] = cw
    shared["wbr"] = np.ascontiguousarray(w_branch)
    shared["wo"] = np.ascontiguousarray(w_o)
    shared["ident"] = np.eye(128, dtype=np.float32)
    shared["identb"] = np.eye(128, dtype=np.float32).astype(ml_dtypes.bfloat16)
    shared["onesb"] = np.ones((128, 128), np.float32).astype(ml_dtypes.bfloat16)
    sel = np.zeros((2, 2, 128), np.float32)
    sel[0, 0, :] = 1.0
    sel[1, 1, :] = 1.0
    shared["sel"] = sel
    in_maps = []
    for core in range(NCORE):
        b, r = core // 4, core % 4
        m = dict(shared)
        m["x"] = np.ascontiguousarray(x[b, r * T:(r + 1) * T])
        m["ctx"] = np.ascontiguousarray(ctx[b])
        cs = np.zeros((128, 8, 2), np.float32)
        cs[:, :, 0] = _fm(c[b], 8)
        cs[:, :, 1] = _fm(c_ctx, 8)
        m["cs"] = cs.reshape(128, 16)
        cosA, sinA, cosB, sinB = _rope_tables(r)
        m["cosA"], m["sinA"], m["cosB"], m["sinB"] = cosA, sinA, cosB, sinB
        m["masks"] = _masks(r)
        oh = np.zeros((128, 8), np.float32)
        if r > 0:
            oh[:, r - 1] = 1.0
        if r < 3:
            oh[:, 4 + r + 1] = 1.0
        m["oh"] = oh
        in_maps.append(m)
    return in_maps


class _K:
    pass


def build(upto=99, dbg=()):
    from contextlib import ExitStack
    nc = bass.Bass("TRN2", target_bir_lowering=False)
    S_ = Sched()
    k = _K()
    stack = ExitStack()
    with stack:
        _build_body(nc, S_, k, stack, upto, dbg)
        _program(k, S_, upto)
        S_.barrier()
        for (dd, ap_) in k.dumps:
            S_.dma("sp", dd.ap(), ap_)
        S_.barrier()
        S_.emit(nc, stack)
    k.S = S_
    build.last = k
    return nc


def _din(nc, name, shape, dt=F32):
    return nc.dram_tensor(name, list(shape), dt, kind="ExternalInput")


def _build_body(nc, S_, k, stack, upto, dbg):
    sb = lambda name, shape, dt: stack.enter_context(nc.sbuf_tensor("s_" + name, list(shape), dt))
    d_x = _din(nc, "x", [T, D]); d_ctx = _din(nc, "ctx", [L, D]); d_cs = _din(nc, "cs", [128, 16])
    d_wmod = _din(nc, "wmod", [DEPTH, D, 3 * D]); d_bmodT = _din(nc, "bmodT", [128, DEPTH, 16])
    d_bmodg = _din(nc, "bmodg", [2, DEPTH, D]); d_gpreT = _din(nc, "gpreT", [128, DEPTH, 8])
    d_gpostb = _din(nc, "gpostb", [DEPTH, 128, D]); d_win = _din(nc, "win", [DEPTH, D, NCW])
    d_sinkb = _din(nc, "sinkb", [128, DEPTH, 8]); d_gqaT = _din(nc, "gqaT", [128, DEPTH, 3])
    d_gkvaT = _din(nc, "gkvaT", [128, DEPTH, 2]); d_wqb = _din(nc, "wqb", [DEPTH, 384, 1536])
    d_wkvb = _din(nc, "wkvb", [DEPTH, 256, 1024]); d_convw = _din(nc, "convw", [128, DEPTH, 4, 3])
    d_wbr = _din(nc, "wbr", [DEPTH, 3, 512, D]); d_wo = _din(nc, "wo", [DEPTH, D, D])
    d_tab = [_din(nc, n, [128, TT]) for n in ("cosA", "sinA", "cosB", "sinB")]
    d_masks = _din(nc, "masks", [128, 4, 512], BF16); d_oh = _din(nc, "oh", [128, 8])
    d_ident = _din(nc, "ident", [128, 128]); d_identb = _din(nc, "identb", [128, 128], BF16)
    d_onesb = _din(nc, "onesb", [128, 128], BF16); d_sel = _din(nc, "sel", [2, 2, 128])
    d_y = nc.dram_tensor("y", [T, D], F32, kind="ExternalOutput")
    d_snd1 = nc.dram_tensor("snd1", [128, PC_N1], BF16)
    d_rcv1 = nc.dram_tensor("rcv1", [512, PC_N1], BF16)
    d_snd = nc.dram_tensor("snd2", [128, PC_N2], BF16)
    d_rcv = nc.dram_tensor("rcv2", [512, PC_N2], BF16)
    d_x1 = nc.dram_tensor("x1", [T, D], F32)

    ident = sb("ident", [128, 128], F32); identb = sb("identb", [128, 128], BF16)
    onesb = sb("onesb", [128, 128], BF16); sel = sb("sel", [2, 256], F32)
    masks = sb("masksb", [128, 4, 512], BF16); oh = sb("oh", [128, 8], F32)
    cs = sb("cs", [128, 16], F32); scs = sb("scs", [128, 16], BF16)
    bmodT = sb("bmodT", [128, DEPTH, 16], F32)
    gpreT = sb("gpreT", [128, DEPTH, 8], F32); sinkb = sb("sinkb", [128, DEPTH, 8], F32)
    esink = sb("esink", [128, DEPTH, 8], F32)
    gqaT = sb("gqaT", [128, DEPTH, 3], F32); gkvaT = sb("gkvaT", [128, DEPTH, 2], F32)
    convw = sb("convw", [128, DEPTH, 4, 3], F32)
    modT = sb("modT", [128, 16, 2], F32)
    Amod = sb("Amod", [128, DEPTH, 8, 2], F32); Bmod = sb("Bmod", [128, DEPTH, 8, 2], F32)
    small = sb("small", [128, 64], F32)
    epsc = sb("epsc", [128, 1], F32)
    hxT = sb("hxT", [128, 8, TT], BF16)
    R1 = sb("R1", [128, 9216], BF16)
    R2 = sb("R2", [128, 18432], BF16)
    R3 = sb("R3", [128, 20736], BF16)
    WB = [sb("wb%d" % i, [128, 8, 512], BF16) for i in range(3)]
    wbrb2 = [sb("wbrb%d" % i, [128, 3, 4, 128], BF16) for i in range(2)]
    TMP = [sb("tmp%d" % i, [128, 512], F32) for i in range(8)]
    sqb = sb("sqb", [128, 3, 512], BF16); qnb = sb("qnb", [128, 3, 512], BF16)
    junk = sqb[:, 0:2, :].rearrange("p a b -> p (a b)")
    tabs = [sb("tab%d" % i, [128, 512], F32) for i in range(4)]
    kvst = sb("kvst", [128, 2, 512], BF16); krst = sb("krst", [32, 512], BF16)
    zh = sb("zh", [128, 8], F32); zhalo = sb("zhalo", [128, 4, 2], F32)
    zhb = sb("zhb", [128, 16], BF16); hzf = sb("hzf", [128, 4, 8], F32)
    PS = stack.enter_context(nc.psum_tensor("ps", [128, 8, 512], F32))

    def v32(R, off_bf, n_f32):
        return R[:, off_bf:off_bf + 2 * n_f32].bitcast(F32)
    xt = [v32(R1, 0, 1024), v32(R1, 2048, 1024)]
    xnb = [v32(R2, 4096, 1024), v32(R2, 6144, 1024)]
    xoldb = [v32(R2, 8192, 1024), v32(R2, 10240, 1024), v32(R2, 16384, 1024)]
    xhlb = [R2[:, 12288:14336].rearrange("p (a b) -> p a b", a=2), R2[:, 14336:16384].rearrange("p (a b) -> p a b", a=2)]
    junk2 = [junk, qnb[:, 0:2, :].rearrange("p a b -> p (a b)")]
    wqb = R1[:, 0:4608].rearrange("p (j c) -> p j c", j=3)
    hal_k = R1[:, 4608:4608 + 2048].rearrange("p (r c) -> p r c", r=4)
    hal_v = R1[:, 6656:6656 + 1024].rearrange("p (r c) -> p r c", r=4)
    hal_z = R1[:, 7680:7680 + 64].rearrange("p (r c) -> p r c", r=4)
    hal_zf = R1[:, 7680:7680 + 64].bitcast(F32).rearrange("p (r c) -> p r c", r=4)
    Y_B = R1[:, 0:9216].rearrange("p (j t) -> p j t", j=4)
    qT_B = R2[:, 0:18432].rearrange("p (h t) -> p h t", h=8)
    Y_A = R2[:, 0:9216].rearrange("p (j t) -> p j t", j=4)
    Y_C = R2[:, 9216:18432].rearrange("p (j t) -> p j t", j=4)
    G_x = v32(R2, 0, 1024); G_c = v32(R2, 2048, 1024)
    o3 = 0
    kT_A = R3[:, o3:o3 + 5120].rearrange("p (g t) -> p g t", g=2); o3 += 5120
    V_A = R3[:, o3:o3 + 7680].rearrange("p (t g c) -> p t g c", t=20, g=2); o3 += 7680
    o3m = o3
    wkvb = R3[:, o3:o3 + 2048].rearrange("p (j c) -> p j c", j=2); o3 += 2048
    kvbuf = []
    for i in range(2):
        kvbuf.append(R3[:, o3:o3 + 1024].rearrange("p (j c) -> p j c", j=2)); o3 += 1024
    Kbuf = []
    for i in range(2):
        Kbuf.append(R3[:, o3:o3 + 512]); o3 += 512
    Vbuf = []
    for i in range(2):
        Vbuf.append(R3[:, o3:o3 + 768].rearrange("p (t c) -> p t c", t=4)); o3 += 768
    Kc = R3[:, o3:o3 + 256]; o3 += 256
    Vc = R3[:, o3:o3 + 384].rearrange("p (t c) -> p t c", t=2); o3 += 384
    kvnc = R3[:, o3:o3 + 512].rearrange("p (j c) -> p j c", j=2); o3 += 512
    assert o3 <= 20736, o3
    qA = R3[:, o3m:o3m + 2048].rearrange("p (j c) -> p j c", j=4)
    sza = R3[:, o3m + 2048:o3m + 4096].rearrange("p (j c) -> p j c", j=4)
    zrow = R3[:, o3m:o3m + 4640].bitcast(F32)
    mT = R3[:, 0:18432].rearrange("p (j t) -> p j t", j=8)
    pbuf = [TMP[4 + i][:, :].bitcast(BF16)[:, 0:512] for i in range(4)]

    R = lambda n: Res(n)
    r_const = R("const")
    PB = [Res("pb%d" % i, excl=True) for i in range(8)]
    r_hx = [R("hx%d" % c) for c in range(5)]
    r_wb = [R("wb%d" % i) for i in range(3)]
    r_tmp = [R("tmp%d" % i) for i in range(8)]
    r_tab = [R("tab%d" % i) for i in range(4)]
    r_misc = {}
    r_xt = [R("xt0"), R("xt1")]

    def rm(name):
        if name not in r_misc:
            r_misc[name] = Res(name)
        return r_misc[name]

    def mm(out, lhsT, rhs, start, stop, r=(), pw=(), w=()):
        return S_.op("pe", lambda e: e.matmul(out, lhsT=lhsT, rhs=rhs, start=start, stop=stop), r=r, pw=pw, w=w)

    def act(out, in_, func, r=(), w=(), pw=(), scale=None, bias=None, accum_out=None):
        kw = {}
        if scale is not None:
            kw["scale"] = scale
        if bias is not None:
            kw["bias"] = bias
        if accum_out is not None:
            kw["accum_out"] = accum_out
        return S_.op("act", lambda e: e.activation(out=out, in_=in_, func=func, **kw), r=r, w=w, pw=pw)

    def tt(eng, out, in0, in1, op, r=(), w=(), pw=()):
        return S_.op(eng, lambda e: e.tensor_tensor(out=out, in0=in0, in1=in1, op=op), r=r, w=w, pw=pw)

    def ts(eng, out, in0, s1, op0, s2=None, op1=None, r=(), w=(), pw=()):
        if op1 is None:
            return S_.op(eng, lambda e: e.tensor_scalar(out=out, in0=in0, scalar1=s1, scalar2=None, op0=op0),
                         r=r, w=w, pw=pw)
        return S_.op(eng, lambda e: e.tensor_scalar(out=out, in0=in0, scalar1=s1, scalar2=s2, op0=op0, op1=op1),
                     r=r, w=w, pw=pw)

    def stt(out, in0, scalar, in1, op0, op1, r=(), w=(), pw=()):
        return S_.op("dve", lambda e: e.scalar_tensor_tensor(out=out, in0=in0, scalar=scalar, in1=in1,
                                                            op0=op0, op1=op1), r=r, w=w, pw=pw)

    def cp(eng, out, in_, r=(), w=(), pw=()):
        if eng == "act":
            return act(out, in_, AF.Copy, r=r, w=w, pw=pw)
        return S_.op(eng, lambda e: e.tensor_copy(out=out, in_=in_), r=r, w=w, pw=pw)

    wb_next = [0]

    def load_w(src_ap, ncols, kparts=8):
        i = wb_next[0] % 3
        wb_next[0] += 1
        dst = WB[i][:, 0:kparts, 0:ncols]
        S_.dma("pool", dst, src_ap, w=[r_wb[i]])
        return WB[i], r_wb[i]

    def win_src(l, name, c0, ncols):
        a = d_win.ap()[l].rearrange("(k p) c -> p k c", p=128)
        o = WOFF[name] + c0
        return a[:, :, o:o + ncols]

    pb_next = [0]

    def bank():
        b = pb_next[0] % 8
        pb_next[0] += 1
        return b

    k.dumps = []

    def dump(name, ap_sbuf, shape=None, dt=None):
        if name in dbg:
            ap_ = ap_sbuf if isinstance(ap_sbuf, AP) else ap_sbuf[:]
            dd = nc.dram_tensor("dbg_" + name, list(ap_.shape), ap_.dtype, kind="ExternalOutput")
            k.dumps.append((dd, ap_))

    for (dst, src) in ((ident[:], d_ident.ap()), (identb[:], d_identb.ap()), (onesb[:], d_onesb.ap()),
                       (sel[:], d_sel.ap().rearrange("k w m -> k (w m)")), (masks[:], d_masks.ap()),
                       (oh[:], d_oh.ap()), (cs[:], d_cs.ap()), (bmodT[:], d_bmodT.ap()),
                       (gpreT[:], d_gpreT.ap()), (sinkb[:], d_sinkb.ap()),
                       (gqaT[:], d_gqaT.ap()), (gkvaT[:], d_gkvaT.ap()), (convw[:], d_convw.ap())):
        S_.dma("sp", dst, src, pw=[r_const])
    S_.op("pool", lambda e: e.memset(epsc[:], EPS), pw=[r_const])
    S_.barrier()
    act(scs[:], cs[:], AF.Silu, w=[rm("scs")])
    act(esink[:], sinkb[:], AF.Exp, w=[rm("esink")])
    scs3 = scs[:].rearrange("p (k s) -> p k s", s=2)
    def mod_layer(l):
        wsrc = d_wmod.ap()[l].rearrange("(k p) c -> p k c", p=128)
        bm = bank()
        for j in range(16):
            if j % 4 == 0:
                wt, rw = load_w(wsrc[:, :, (j // 4) * 512:(j // 4) * 512 + 512], 512)
            for kk in range(8):
                mm(PS[:, bm, j * 2:j * 2 + 2], wt[:, kk, (j % 4) * 128:(j % 4) * 128 + 128], scs3[:, kk, :],
                   kk == 0, kk == 7, r=[rw, rm("scs")], pw=[PB[bm]])
        bmb = AP(bmodT[:].tensor, bmodT[:, l, :].offset, [list(bmodT[:].ap[0]), [1, 16], [0, 2]])
        tt("dve", modT[:], PS[:, bm, 0:32].rearrange("p (j s) -> p j s", s=2), bmb, ALU.add,
           r=[PB[bm]], w=[rm("modT")])
        ts("dve", modT[:, 8:16, :], modT[:, 8:16, :], 1.0, ALU.add, r=[], w=[rm("modT")])
        gpb = AP(gpreT[:].tensor, gpreT[:, l, :].offset, [list(gpreT[:].ap[0]), [1, 8], [0, 2]])
        tt("dve", Amod[:, l], modT[:, 8:16, :], gpb, ALU.mult, r=[rm("modT")], pw=[rm("AB%d" % l)])
        cp("dve", Bmod[:, l], modT[:, 0:8, :], r=[rm("modT")], pw=[rm("AB%d" % l)])
    mod_layer(0)
    k.__dict__.update(locals())


def _stage1_tile(k, S_, l, i, xtile, r_xt):
    _stage1_b(k, S_, l, i, *_stage1_a(k, S_, l, i, xtile, r_xt))


def _stage1_a(k, S_, l, i, xtile, r_xt):
    PS, PB, hxT, ident = k.PS, k.PB, k.hxT, k.ident
    act, ts, mm, rm = k.act, k.ts, k.mm, k.rm
    s = 0 if i < 16 else 1
    c = i // 4 if i < 16 else 4
    col0 = i * 128
    p = i % 2
    sm0 = 16 * (i % 4)
    ssq = k.small[:, sm0 + 0:sm0 + 1]
    lnv = k.small[:, sm0 + 1:sm0 + 2]
    rstd = k.small[:, sm0 + 2:sm0 + 3]
    xn = k.xnb[p]
    rs = lambda n_: rm("%s%d" % (n_, p))
    act(k.junk2[p], xtile, AF.Square, r=[r_xt], w=[rs("junk"), rs("ssq")], accum_out=ssq)
    act(lnv, ssq, AF.Ln, r=[rs("ssq")], w=[rs("lnv")], scale=1.0 / D, bias=k.epsc[:, 0:1])
    act(rstd, lnv, AF.Exp, r=[rs("lnv")], w=[rs("rstd")], scale=-0.5)
    ts("dve", xn, xtile, rstd, ALU.mult, r=[r_xt, rs("rstd")], w=[rs("xn")])
    hi, lo = k.xhlb[p][:, 0, :], k.xhlb[p][:, 1, :]
    act(hi, xtile, AF.Copy, r=[r_xt, rs("rstd")], w=[rs("xhi")], scale=rstd)
    k.tt("dve", lo, xn, hi, ALU.subtract, r=[rs("xn"), rs("xhi")], w=[rs("xlo")])
    b0 = k.bank()
    b1 = k.bank()
    for kk in range(8):
        bb = b0 if kk < 4 else b1
        o_ = PS[:, bb, (kk % 4) * 128:(kk % 4) * 128 + 128]
        mm(o_, hi[:, kk * 128:(kk + 1) * 128], k.identb[:], True, False, r=[rs("xhi"), k.r_const], pw=[PB[bb]])
        mm(o_, lo[:, kk * 128:(kk + 1) * 128], k.identb[:], False, True, r=[rs("xlo"), k.r_const], pw=[PB[bb]])
    return b0, b1


def _stage1_b(k, S_, l, i, b0, b1):
    PS, PB, hxT = k.PS, k.PB, k.hxT
    act, ts, rm = k.act, k.ts, k.rm
    s = 0 if i < 16 else 1
    c = i // 4 if i < 16 else 4
    col0 = i * 128
    for kk in range(8):
        bb = b0 if kk < 4 else b1
        src = PS[:, bb, (kk % 4) * 128:(kk % 4) * 128 + 128]
        dst = hxT[:, kk, col0:col0 + 128]
        if kk < 4:
            act(dst, src, AF.Identity, r=[PB[bb], rm("AB%d" % l)], pw=[k.r_hx[c]],
                scale=k.Amod[:, l, kk, s:s + 1], bias=k.Bmod[:, l, kk, s:s + 1])
        else:
            ts("dve", dst, src, k.Amod[:, l, kk, s:s + 1], ALU.mult, s2=k.Bmod[:, l, kk, s:s + 1], op1=ALU.add,
               r=[PB[bb], rm("AB%d" % l)], pw=[k.r_hx[c]])


def _phaseA(k, S_, l):
    nt = getattr(k, "ntilesA", 18)

    def load(i):
        if i >= nt:
            return
        src = k.d_x.ap()[i * 128:(i + 1) * 128, :] if i < 16 else k.d_ctx.ap()[(i - 16) * 128:(i - 15) * 128, :]
        S_.dma("sp", k.xt[i % 2], src, w=[k.r_xt[i % 2]])

    load(0)
    load(1)
    banks = {}
    for s_ in range(nt + 1):
        if s_ < nt:
            banks[s_] = _stage1_a(k, S_, l, s_, k.xt[s_ % 2], k.r_xt[s_ % 2])
            load(s_ + 2)
        if s_ >= 1:
            _stage1_b(k, S_, l, s_ - 1, *banks.pop(s_ - 1))


def _load_tabs(k, S_, c0, n, which=(0, 1, 2, 3)):
    for ti in which:
        S_.dma("sp", k.tabs[ti][:, 0:n], k.d_tab[ti].ap()[:, c0:c0 + n], w=[k.r_tab[ti]])


def _rstd_bcast(k, S_, banks, nj, n, inv_n, out_tmp, r_out):
    PS, PB = k.PS, k.PB
    for j in range(nj):
        k.act(k.sqb[:, j, 0:n], PS[:, banks[j], 0:n], AF.Square, r=[PB[banks[j]]], pw=[k.rm("sqb")])
    bs = k.bank()
    for j in range(nj):
        k.mm(PS[:, bs, 0:n], k.onesb[:], k.sqb[:, j, 0:n], j == 0, j == nj - 1, r=[k.rm("sqb"), k.r_const],
             pw=[PB[bs]])
    k.act(out_tmp[:, 0:n], PS[:, bs, 0:n], AF.Ln, r=[PB[bs]], w=[r_out], scale=inv_n, bias=k.epsc[:, 0:1])
    k.act(out_tmp[:, 0:n], out_tmp[:, 0:n], AF.Exp, r=[], w=[r_out], scale=-0.5)


def _prefetch_B(k, l):
    return [k.load_w(k.win_src(l, "ZH", 0, 512), 512), k.load_w(k.win_src(l, "ZH", 512, 512), 512),
            k.load_w(k.win_src(l, "KA", 0, 512), 512)]


def _phaseB_kv(k, S_, l, pre=None):
    PS, PB, hxT = k.PS, k.PB, k.hxT
    mm, act, tt, ts, stt, cp, rm = k.mm, k.act, k.tt, k.ts, k.stt, k.cp, k.rm
    TMP, r_tmp = k.TMP, k.r_tmp
    S_.op("pool", lambda e: e.memset(k.V_A[:, :, :, 0:64], 1.0), pw=[rm("VA")])
    S_.op("pool", lambda e: e.memset(k.V_A[:, :, :, 128:192], 1.0), pw=[rm("VA")])
    for i in range(2):
        S_.op("pool", (lambda vb: (lambda e: e.memset(vb[:, :, 0:64], 1.0)))(k.Vbuf[i]), pw=[rm("Vbuf%d" % i)])
        S_.op("pool", (lambda vb: (lambda e: e.memset(vb[:, :, 128:192], 1.0)))(k.Vbuf[i]), pw=[rm("Vbuf%d" % i)])
    S_.op("pool", lambda e: e.memset(k.Vc[:, :, 0:64], 1.0), pw=[rm("Vc")])
    S_.op("pool", lambda e: e.memset(k.Vc[:, :, 128:192], 1.0), pw=[rm("Vc")])
    if pre is None:
        pre = _prefetch_B(k, l)
    bz = k.bank()
    for which in range(2):
        wz, r_wz = pre[which]
        for ct in range(4):
            col = (which * 4 + ct) * 2
            for kk in range(8):
                mm(PS[:, bz, col:col + 2], wz[:, kk, ct * 128:(ct + 1) * 128], hxT[:, kk, 0:2048:2047],
                   kk == 0, kk == 7, r=[r_wz, k.r_hx[0], k.r_hx[3]], pw=[PB[bz]])
    cp("act", TMP[3][:, 0:8], PS[:, bz, 0:8], r=[PB[bz]], w=[r_tmp[3]])
    tt("dve", k.zh[:], TMP[3][:, 0:8], PS[:, bz, 8:16], ALU.mult, r=[PB[bz], r_tmp[3]], w=[rm("zh")])
    wka, r_wka = pre[2]
    wvk, r_wvk = k.load_w(k.win_src(l, "VKK", 0, 448), 448)
    for c, (c0, n) in enumerate(CHUNKS):
        _load_tabs(k, S_, c0, n)
        rhs = [hxT[:, kk, c0:c0 + n] for kk in range(8)]
        kdst0 = 128 + c0 if c < 4 else 2304
        for g in range(2):
            bq, br = k.bank(), k.bank()
            for ti, bb in ((2 * g, bq), (2 * g + 1, br)):
                for kk in range(8):
                    mm(PS[:, bb, 0:n], wka[:, kk, ti * 128:(ti + 1) * 128], rhs[kk], kk == 0, kk == 7,
                       r=[r_wka, k.r_hx[c]], pw=[PB[bb]])
            tt("dve", TMP[0][:, 0:n], PS[:, bq, 0:n], k.tabs[0][:, 0:n], ALU.mult, r=[PB[bq], k.r_tab[0]], w=[r_tmp[0]])
            tt("dve", TMP[1][:, 0:n], PS[:, br, 0:n], k.tabs[1][:, 0:n], ALU.mult, r=[PB[br], k.r_tab[1]], w=[r_tmp[1]])
            tt("dve", k.kT_A[:, g, kdst0:kdst0 + n], TMP[0][:, 0:n], TMP[1][:, 0:n], ALU.add,
               r=[r_tmp[0], r_tmp[1]], pw=[rm("kTA")])
        bv = k.bank()
        nt = n // 128
        for t_ in range(nt):
            for kk in range(8):
                mm(PS[:, bv, t_ * 128:(t_ + 1) * 128], hxT[:, kk, c0 + t_ * 128:c0 + (t_ + 1) * 128], wvk[:, kk, 0:128],
                   kk == 0, kk == 7, r=[r_wvk, k.r_hx[c]], pw=[PB[bv]])
        vt0 = 1 + c * 4 if c < 4 else 18
        cp("dve", k.V_A[:, vt0:vt0 + nt, :, 64:128], PS[:, bv, 0:n].rearrange("p (t g d) -> p t g d", t=nt, g=2),
           r=[PB[bv]], pw=[rm("VA")])
        bk = [k.bank(), k.bank()]
        for j in range(2):
            for kk in range(8):
                mm(PS[:, bk[j], 0:n], wvk[:, kk, 128 + j * 128:256 + j * 128], rhs[kk], kk == 0, kk == 7,
                   r=[r_wvk, k.r_hx[c]], pw=[PB[bk[j]]])
        _rstd_bcast(k, S_, bk, 2, n, 1.0 / 256, TMP[2], r_tmp[2])
        for j in range(2):
            dst = k.kvst[:, j, 0:n] if c < 4 else k.kvnc[:, j, 0:n]
            stt(dst, PS[:, bk[j], 0:n], k.gkvaT[:, l, j:j + 1], TMP[2][:, 0:n], ALU.mult, ALU.mult,
                r=[PB[bk[j]], r_tmp[2]], pw=[rm("kvst") if c < 4 else rm("kvnc")])
        if c < 4:
            S_.dma("sp", k.d_snd1.ap()[:, PC_KVN:PC_KVN + 4096].rearrange("p (j t) -> p j t", j=2)[:, :, c0:c0 + n],
                   k.kvst[:, :, 0:n], r=[rm("kvst")], pw=[rm("snd1")])
        b1, b2 = k.bank(), k.bank()
        for (bb, co) in ((b1, 384), (b2, 416)):
            for kk in range(8):
                mm(PS[0:32, bb, 0:n], wvk[:, kk, co:co + 32], rhs[kk], kk == 0, kk == 7,
                   r=[r_wvk, k.r_hx[c]], pw=[PB[bb]])
        tt("dve", TMP[0][0:32, 0:n], PS[0:32, b1, 0:n], k.tabs[2][0:32, 0:n], ALU.mult, r=[PB[b1], k.r_tab[2]], w=[r_tmp[0]])
        tt("dve", TMP[1][0:32, 0:n], PS[0:32, b2, 0:n], k.tabs[3][0:32, 0:n], ALU.mult, r=[PB[b2], k.r_tab[3]], w=[r_tmp[1]])
        tt("dve", k.krst[:, 0:n], TMP[0][0:32, 0:n], TMP[1][0:32, 0:n], ALU.add, r=[r_tmp[0], r_tmp[1]], w=[rm("krst")])
        if c < 4:
            S_.dma("sp", k.d_snd.ap()[c * 32:(c + 1) * 32, PC_KR:PC_KR + 512], k.krst[:, 0:n], r=[rm("krst")], pw=[rm("snd")])
        else:
            S_.dma("sp", k.Kc[64:96, 0:n], k.krst[:, 0:n], r=[rm("krst")], pw=[rm("Kc")])
    snd = k.d_snd.ap()
    for fl, off in ((0, 128), (1, 2048)):
        S_.dma("sp", snd[:, PC_KAH:PC_KAH + 512].rearrange("p (g f t) -> p g f t", g=2, f=2)[:, :, fl, :],
               k.kT_A[:, :, off:off + 128], r=[rm("kTA")], pw=[rm("snd")])
    for fl, vt in ((0, 1), (1, 16)):
        S_.dma("sp", snd[:, PC_VAH + fl * 128:PC_VAH + (fl + 1) * 128].rearrange("p (g d) -> p g d", g=2),
               k.V_A[:, vt, :, 64:128], r=[rm("VA")], pw=[rm("snd")])
    cp("dve", k.zhb[:, 0:8], k.zh[:], r=[rm("zh")], w=[rm("zhb")])
    tt("dve", k.zhb[:, 8:16], k.zh[:], k.zhb[:, 0:8], ALU.subtract, r=[rm("zh")], w=[rm("zhb")])
    S_.dma("sp", snd[:, PC_ZH:PC_ZH + 16], k.zhb[:], r=[rm("zhb")], pw=[rm("snd")])
    if os.environ.get("NOCC") == "1":
        return
    S_.op("pool", lambda e: e.collective_compute("AllGather", ALU.bypass,
                                                 replica_groups=[[0, 1, 2, 3], [4, 5, 6, 7]],
                                                 ins=[k.d_snd1.ap().opt()], outs=[k.d_rcv1.ap().opt()]),
          r=[rm("snd1")], w=[rm("rcv1")], kind="cc")
    S_.op("pool", lambda e: e.collective_compute("AllGather", ALU.bypass,
                                                 replica_groups=[[0, 1, 2, 3], [4, 5, 6, 7]],
                                                 ins=[k.d_snd.ap().opt()], outs=[k.d_rcv.ap().opt()]),
          r=[rm("snd")], w=[rm("rcv")], kind="cc")


def _phaseB_halo(k, S_, l):
    rm, stt, ts = k.rm, k.stt, k.ts
    rcv = k.d_rcv.ap().rearrange("(r p) c -> p r c", p=128)
    S_.dma("sp", k.hal_k, rcv[:, :, PC_KAH:PC_KAH + 512], r=[rm("rcv")], w=[rm("halk")])
    S_.dma("sp", k.hal_v, rcv[:, :, PC_VAH:PC_VAH + 256], r=[rm("rcv")], w=[rm("halv")])
    S_.dma("sp", k.hal_z, rcv[:, :, PC_ZH:PC_ZH + 16], r=[rm("rcv")], w=[rm("halz")])
    oh = k.oh

    def select(dst, srcs, ohbase, tmp, r_t, rsrc, rdst):
        ts("dve", tmp, srcs[0], oh[:, ohbase:ohbase + 1], ALU.mult, r=[rsrc, k.r_const], w=[r_t])
        for rr in range(1, 4):
            last = rr == 3
            stt(dst if last else tmp, srcs[rr], oh[:, ohbase + rr:ohbase + rr + 1], tmp, ALU.mult, ALU.add,
                r=[rsrc, k.r_const] + ([] if not last else [r_t]), w=([r_t] if not last else []),
                pw=([rdst] if last else []))

    hk = lambda rr, fl: k.hal_k[:, rr, :].rearrange("p (g f t) -> p g f t", g=2, f=2)[:, :, fl, :]
    t3 = lambda i: k.TMP[i][:, 0:256].rearrange("p (g t) -> p g t", g=2)
    select(k.kT_A[:, :, 0:128], [hk(rr, 1) for rr in range(4)], 0, t3(0), k.r_tmp[0], rm("halk"), rm("kTA"))
    select(k.kT_A[:, :, 2176:2304], [hk(rr, 0) for rr in range(4)], 4, t3(1), k.r_tmp[1], rm("halk"), rm("kTA"))
    hv = lambda rr, fl: k.hal_v[:, rr, fl * 128:(fl + 1) * 128].rearrange("p (g d) -> p g d", g=2)
    t4 = lambda i: k.TMP[i][:, 0:128].rearrange("p (g d) -> p g d", g=2)
    select(k.V_A[:, 0, :, 64:128], [hv(rr, 1) for rr in range(4)], 0, t4(2), k.r_tmp[2], rm("halv"), rm("VA"))
    select(k.V_A[:, 17, :, 64:128], [hv(rr, 0) for rr in range(4)], 4, t4(3), k.r_tmp[3], rm("halv"), rm("VA"))
    k.tt("dve", k.hzf[:], k.hal_z[:, :, 0:8], k.hal_z[:, :, 8:16], ALU.add, r=[rm("halz")], w=[rm("hzf")])
    hz = lambda rr, fl: k.hzf[:, rr, :].rearrange("p (c f) -> p c f", f=2)[:, :, fl]
    select(k.zhalo[:, :, 0], [hz(rr, 1) for rr in range(4)], 0, k.TMP[0][:, 256:260], k.r_tmp[0], rm("hzf"), rm("zhalo"))
    select(k.zhalo[:, :, 1], [hz(rr, 0) for rr in range(4)], 4, k.TMP[1][:, 256:260], k.r_tmp[1], rm("hzf"), rm("zhalo"))


def _phaseB_qb(k, S_, l, nchunks):
    PS, PB, hxT = k.PS, k.PB, k.hxT
    mm, act, tt, stt, cp, rm = k.mm, k.act, k.tt, k.stt, k.cp, k.rm
    TMP, r_tmp = k.TMP, k.r_tmp
    S_.dma("pool", k.wqb, k.d_wqb.ap()[l].rearrange("(j p) c -> p j c", p=128), w=[rm("wqb")])
    wql, r_wql = k.load_w(k.win_src(l, "QL", 0, 384), 384)
    for c in range(nchunks):
        c0, n = CHUNKS[c]
        _load_tabs(k, S_, c0, n, which=(2, 3))
        bq = [k.bank(), k.bank(), k.bank()]
        for j in range(3):
            for kk in range(8):
                mm(PS[:, bq[j], 0:n], wql[:, kk, j * 128:(j + 1) * 128], hxT[:, kk, c0:c0 + n], kk == 0, kk == 7,
                   r=[r_wql, k.r_hx[c]], pw=[PB[bq[j]]])
        _rstd_bcast(k, S_, bq, 3, n, 1.0 / 384, TMP[2], r_tmp[2])
        for j in range(3):
            stt(k.qnb[:, j, 0:n], PS[:, bq[j], 0:n], k.gqaT[:, l, j:j + 1], TMP[2][:, 0:n], ALU.mult, ALU.mult,
                r=[PB[bq[j]], r_tmp[2]], pw=[rm("qnb")])
        for h in range(8):
            b1, b2 = k.bank(), k.bank()
            for (bb, co) in ((b1, h * 192), (b2, h * 192 + 96)):
                for j in range(3):
                    mm(PS[0:96, bb, 0:n], k.wqb[:, j, co:co + 96], k.qnb[:, j, 0:n], j == 0, j == 2,
                       r=[rm("wqb"), rm("qnb")], pw=[PB[bb]])
            cp("act", k.qT_B[0:64, h, c0:c0 + n], PS[0:64, b1, 0:n], r=[PB[b1]], pw=[rm("qTB")])
            ta, tb = (0, 1) if h % 2 == 0 else (3, 5)
            tt("dve", TMP[ta][64:96, 0:n], PS[64:96, b1, 0:n], k.tabs[2][64:96, 0:n], ALU.mult,
               r=[PB[b1], k.r_tab[2]], w=[r_tmp[ta]])
            tt("dve", TMP[tb][64:96, 0:n], PS[64:96, b2, 0:n], k.tabs[3][64:96, 0:n], ALU.mult,
               r=[PB[b2], k.r_tab[3]], w=[r_tmp[tb]])
            tt("dve", k.qT_B[64:96, h, c0:c0 + n], TMP[ta][64:96, 0:n], TMP[tb][64:96, 0:n], ALU.add,
               r=[r_tmp[ta], r_tmp[tb]], pw=[rm("qTB")])


def _attn_core(k, S_, items, scale):
    PS, PB = k.PS, k.PB
    LOOK = 2
    pend = []
    for it in items:
        if it.get("before") is not None:
            it["before"]()
        sb0, nb = it["sb"]
        pb_i = k.sctr % 4
        k.sctr += 1
        pws = [PB[sb0 + i] for i in range(nb)]
        first = True
        if it.get("mask") is not None:
            for (o_ap, m_ap) in it["mask"]:
                k.mm(o_ap, k.identb[:], m_ap, True, False, r=[k.r_const], pw=pws)
            first = False
        for (o_ap, lhsT, rhs) in it["s_mms"]:
            k.mm(o_ap, lhsT, rhs, first, True, r=it["r"], pw=pws)
        pv_ = it["p_view"](k.pbuf[pb_i])
        k.act(pv_, it["s_view"], AF.Exp, r=pws, w=[k.r_pbuf[pb_i]], scale=scale)

        def mk(it=it, pb_i=pb_i):
            def f():
                st = it["start"]
                npv = len(it["pv"])
                for pi, (o_ap, lhsT, rhs_fn) in enumerate(it["pv"]):
                    k.mm(o_ap, lhsT, rhs_fn(k.pbuf[pb_i]), st, it["stop"] and pi == npv - 1,
                         r=it["rv"] + [k.r_pbuf[pb_i]], pw=[PB[it["ob"]]])
                    st = False
                if it.get("after_pv") is not None:
                    it["after_pv"]()
            return f
        pend.append(mk())
        if len(pend) > LOOK:
            pend.pop(0)()
        if it.get("after") is not None:
            it["after"]()
    while pend:
        pend.pop(0)()


def _phaseC_mla(k, S_, l, with_ctx_q):
    PS, PB = k.PS, k.PB
    mm, act, tt, cp, rm = k.mm, k.act, k.tt, k.cp, k.rm
    S_.dma("pool", k.wkvb, k.d_wkvb.ap()[l].rearrange("(j p) c -> p j c", p=128), w=[rm("wkvb")])
    rcv = k.d_rcv.ap()
    rcv1 = k.d_rcv1.ap()
    EB = 7
    kchunks = [(rr, cc) for rr in range(4) for cc in range(4)] + [None]
    rec = k.TMP[0]
    def mk_head(h):
        def load_kv(idx):
            if idx >= len(kchunks) or kchunks[idx] is None:
                return
            rr, cc = kchunks[idx]
            i = idx % 2
            S_.dma("sp", k.kvbuf[i], rcv1[rr * 128:(rr + 1) * 128, PC_KVN:PC_KVN + 4096].rearrange(
                "p (j t) -> p j t", j=2)[:, :, cc * 512:(cc + 1) * 512], r=[rm("rcv1")], w=[rm("kvbuf%d" % i)])

        def load_kr(idx):
            if idx >= len(kchunks) or kchunks[idx] is None:
                return
            rr, cc = kchunks[idx]
            i = idx % 2
            S_.dma("sp", k.Kbuf[i][64:96, :], rcv[rr * 128 + cc * 32:rr * 128 + cc * 32 + 32, PC_KR:PC_KR + 512],
                   r=[rm("rcv")], pw=[rm("Kbuf%d" % i)])

        def expand_k(idx):
            kc = kchunks[idx]
            i = idx % 2
            src, rs, n, dst, rd = ((k.kvbuf[i], rm("kvbuf%d" % i), 512, k.Kbuf[i], rm("Kbuf%d" % i)) if kc is not None
                                   else (k.kvnc, rm("kvnc"), 256, k.Kc, rm("Kc")))
            for j in range(2):
                mm(PS[0:64, EB, 0:n], k.wkvb[:, j, h * 64:(h + 1) * 64], src[:, j, 0:n], j == 0, j == 1,
                   r=[rm("wkvb"), rs], pw=[PB[EB]])
            cp("dve", dst[0:64, 0:n], PS[0:64, EB, 0:n], r=[PB[EB]], pw=[rd])

        def expand_v(idx):
            kc = kchunks[idx]
            i = idx % 2
            src, rs, nt, dst, rd = ((k.kvbuf[i], rm("kvbuf%d" % i), 4, k.Vbuf[i], rm("Vbuf%d" % i)) if kc is not None
                                    else (k.kvnc, rm("kvnc"), 2, k.Vc, rm("Vc")))
            for t_ in range(nt):
                for j in range(2):
                    mm(PS[:, EB, t_ * 64:(t_ + 1) * 64], src[:, j, t_ * 128:(t_ + 1) * 128],
                       k.wkvb[:, j, 512 + h * 64:512 + (h + 1) * 64], j == 0, j == 1,
                       r=[rm("wkvb"), rs], pw=[PB[EB]])
            cp("dve", dst[:, 0:nt, 64:128], PS[:, EB, 0:nt * 64].rearrange("p (t d) -> p t d", t=nt),
               r=[PB[EB]], pw=[rd])
        return load_kv, load_kr, expand_k, expand_v

    heads = [mk_head(h) for h in range(8)]
    for h in range(8):
        e = h % 2
        vsl = slice(64, 192) if e == 0 else slice(0, 128)
        o_lo, r_lo = (0, 64) if e == 0 else (64, 0)
        load_kv, load_kr, expand_k, expand_v = heads[h]
        nxt = heads[h + 1] if h + 1 < 8 else None
        if h == 0:
            load_kv(0)
            load_kv(1)
            load_kr(0)
            expand_k(0)
            expand_v(0)
        items = []
        for idx, kc in enumerate(kchunks):
            i = idx % 2
            if kc is not None:
                Kt, rK, Vt, rV, nt = k.Kbuf[i], rm("Kbuf%d" % i), k.Vbuf[i], rm("Vbuf%d" % i), 4
            else:
                Kt, rK, Vt, rV, nt = k.Kc, rm("Kc"), k.Vc, rm("Vc"), 2
            cnt = 0
            for qc in range(4):
                for kt in range(nt):
                    sbank = 4 + (len(items) % 3)
                    it = dict(sb=(sbank, 1),
                              s_mms=[(PS[:, sbank, 0:512], Kt[0:96, kt * 128:(kt + 1) * 128],
                                      k.qT_B[0:96, h, qc * 512:(qc + 1) * 512])],
                              s_view=PS[:, sbank, 0:512], p_view=(lambda p: p[:, 0:512]),
                              r=[rK, rm("qTB")],
                              pv=[(PS[:, qc, 0:512], Vt[:, kt, vsl], (lambda p: p[:, 0:512]))], rv=[rV],
                              ob=qc, start=(idx == 0 and kt == 0), stop=(idx == len(kchunks) - 1 and kt == nt - 1))
                    cnt += 1
                    if cnt == 1:
                        def bef(idx=idx):
                            load_kv(idx + 2)
                            load_kr(idx + 1)
                            if idx == len(kchunks) - 1 and nxt is not None:
                                nxt[0](0)
                                nxt[0](1)
                                nxt[1](0)
                        it["before"] = bef
                    if idx + 1 < len(kchunks):
                        if cnt == 2:
                            it["after"] = (lambda idx=idx: expand_k(idx + 1))
                        elif cnt == 6:
                            it["after"] = (lambda idx=idx: expand_v(idx + 1))
                    elif nxt is not None:
                        if cnt == 2:
                            it["after"] = (lambda: nxt[2](0))
                        elif cnt == 6:
                            it["after"] = (lambda: nxt[3](0))
                    items.append(it)
        _attn_core(k, S_, items, SCALE_B)
        for qc in range(4):
            act(rec[o_lo:o_lo + 64, 0:512], PS[r_lo:r_lo + 64, qc, 0:512], AF.Ln, r=[PB[qc]], w=[k.r_tmp[0]])
            act(rec[o_lo:o_lo + 64, 0:512], rec[o_lo:o_lo + 64, 0:512], AF.Exp, r=[], w=[k.r_tmp[0]], scale=-1.0)
            tt("dve", k.Y_B[o_lo:o_lo + 64, h // 2, qc * 512:(qc + 1) * 512], PS[o_lo:o_lo + 64, qc, 0:512],
               rec[o_lo:o_lo + 64, 0:512], ALU.mult, r=[PB[qc], k.r_tmp[0]], pw=[rm("YB")])
        if with_ctx_q:
            items = []
            for kt in range(2):
                sbank = 4 + kt
                items.append(dict(sb=(sbank, 1),
                                  s_mms=[(PS[:, sbank, 0:256], k.Kc[0:96, kt * 128:(kt + 1) * 128],
                                          k.qT_B[0:96, h, 2048:2304])],
                                  s_view=PS[:, sbank, 0:256], p_view=(lambda p: p[:, 0:256]),
                                  r=[rm("Kc"), rm("qTB")],
                                  pv=[(PS[:, 6, 0:256], k.Vc[:, kt, vsl], (lambda p: p[:, 0:256]))], rv=[rm("Vc")],
                                  ob=6, start=(kt == 0), stop=(kt == 1)))
            _attn_core(k, S_, items, SCALE_B)
            act(rec[o_lo:o_lo + 64, 0:256], PS[r_lo:r_lo + 64, 6, 0:256], AF.Ln, r=[PB[6]], w=[k.r_tmp[0]])
            act(rec[o_lo:o_lo + 64, 0:256], rec[o_lo:o_lo + 64, 0:256], AF.Exp, r=[], w=[k.r_tmp[0]], scale=-1.0)
            tt("dve", k.Y_B[o_lo:o_lo + 64, h // 2, 2048:2304], PS[o_lo:o_lo + 64, 6, 0:256],
               rec[o_lo:o_lo + 64, 0:256], ALU.mult, r=[PB[6], k.r_tmp[0]], pw=[rm("YB")])


def _phaseD_A(k, S_, l, nchunks, pre=None):
    PS, PB, hxT = k.PS, k.PB, k.hxT
    mm, act, tt, ts, rm = k.mm, k.act, k.tt, k.ts, k.rm
    TMP, r_tmp = k.TMP, k.r_tmp
    if pre is None:
        pre = _prefetch_A(k, l)
    wq = [pre[0], pre[1]]
    wza, r_wza = pre[2]
    gctr = [0]
    for c in range(nchunks):
        c0, n = CHUNKS[c]
        _load_tabs(k, S_, c0, n, which=(0, 1))
        for j in range(4):
            wt, rw = wq[j // 2]
            base = (j % 2) * 256
            bq, br = k.bank(), k.bank()
            for (bb, co) in ((bq, base), (br, base + 128)):
                for kk in range(8):
                    mm(PS[:, bb, 0:n], wt[:, kk, co:co + 128], hxT[:, kk, c0:c0 + n], kk == 0, kk == 7,
                       r=[rw, k.r_hx[c]], pw=[PB[bb]])
            tt("dve", TMP[0][:, 0:n], PS[:, bq, 0:n], k.tabs[0][:, 0:n], ALU.mult, r=[PB[bq], k.r_tab[0]], w=[r_tmp[0]])
            tt("dve", TMP[1][:, 0:n], PS[:, br, 0:n], k.tabs[1][:, 0:n], ALU.mult, r=[PB[br], k.r_tab[1]], w=[r_tmp[1]])
            tt("dve", k.qA[:, j, 0:n], TMP[0][:, 0:n], TMP[1][:, 0:n], ALU.add, r=[r_tmp[0], r_tmp[1]], pw=[rm("qA")])
        for jt in range(4):
            bb = k.bank()
            for kk in range(8):
                mm(PS[:, bb, 0:n], wza[:, kk, jt * 128:(jt + 1) * 128], hxT[:, kk, c0:c0 + n], kk == 0, kk == 7,
                   r=[r_wza, k.r_hx[c]], pw=[PB[bb]])
            act(k.sza[:, jt, 0:n], PS[:, bb, 0:n], AF.Silu, r=[PB[bb]], pw=[rm("sza")])
        items = []
        groups = []
        for qb in range(n // 128):
            for g in range(2):
                if c < 4:
                    qbg = c * 4 + qb
                    tiles = [(qbg * 128, qbg, 2 if qbg == 0 else 0), ((qbg + 1) * 128, qbg + 1, None),
                             ((qbg + 2) * 128, qbg + 2, 3 if qbg == 15 else 1), (2304, 18, None), (2432, 19, None)]
                else:
                    tiles = [(2304, 18, None), (2432, 19, None)]
                gi = gctr[0]
                gctr[0] += 1
                ob = gi % 2
                groups.append((qb, g, ob, gi))
                for ti, (koff, vt, mi) in enumerate(tiles):
                    sb0 = 2 + 2 * ((len(items)) % 3)
                    it = dict(sb=(sb0, 2),
                              s_mms=[(PS[:, sb0 + e, 0:256], k.kT_A[e * 64:(e + 1) * 64, g, koff:koff + 128],
                                      k.qA[e * 64:(e + 1) * 64, 2 * g:2 * g + 2, qb * 128:(qb + 1) * 128])
                                     for e in range(2)],
                              mask=(None if mi is None else [(PS[:, sb0 + e, 0:256], k.masks[:, mi, 0:256])
                                                             for e in range(2)]),
                              s_view=PS[:, sb0:sb0 + 2, 0:256],
                              p_view=(lambda p: p[:, 0:512].rearrange("p (e c) -> p e c", e=2)),
                              r=[rm("kTA"), rm("qA")],
                              pv=[(PS[:, ob, 0:256], k.V_A[:, vt, g, 64:192], (lambda p: p[:, 0:256])),
                                  (PS[:, ob, 256:512], k.V_A[:, vt, g, 0:128], (lambda p: p[:, 256:512]))],
                              rv=[rm("VA")], ob=ob, start=(ti == 0), stop=(ti == len(tiles) - 1))
                    items.append(it)

        def normalize(qb, g, ob, gi, c0=c0):
            ta, tb = (TMP[0], TMP[1]) if gi % 2 == 0 else (TMP[2], TMP[3])
            ra, rb = (r_tmp[0], r_tmp[1]) if gi % 2 == 0 else (r_tmp[2], r_tmp[3])
            tok0 = c0 + qb * 128
            for e in range(2):
                o_lo, r_lo = (0, 64) if e == 0 else (64, 0)
                cs_ = slice(e * 256, (e + 1) * 256)
                for jj in range(2):
                    h = 4 * g + 2 * jj + e
                    cj = slice(e * 256 + jj * 128, e * 256 + (jj + 1) * 128)
                    ts("dve", ta[o_lo:o_lo + 64, cj], PS[r_lo:r_lo + 64, ob, cj], k.esink[r_lo:r_lo + 64, l, h:h + 1],
                       ALU.add, r=[PB[ob], rm("esink")], pw=[ra])
                act(ta[o_lo:o_lo + 64, cs_], ta[o_lo:o_lo + 64, cs_], AF.Ln, r=[], w=[ra])
                act(ta[o_lo:o_lo + 64, cs_], ta[o_lo:o_lo + 64, cs_], AF.Exp, r=[], w=[ra], scale=-1.0)
                tt("dve", tb[o_lo:o_lo + 64, cs_], PS[o_lo:o_lo + 64, ob, cs_], ta[o_lo:o_lo + 64, cs_], ALU.mult,
                   r=[PB[ob], ra], pw=[rb])
                tt("dve", k.Y_A[o_lo:o_lo + 64, 2 * g:2 * g + 2, tok0:tok0 + 128],
                   tb[o_lo:o_lo + 64, cs_].rearrange("p (j q) -> p j q", j=2),
                   k.sza[o_lo:o_lo + 64, 2 * g:2 * g + 2, qb * 128:(qb + 1) * 128], ALU.mult,
                   r=[rb, rm("sza")], pw=[rm("YA")])

        per = len(items) // len(groups)
        for gidx in range(len(groups)):
            items[gidx * per + per - 1]["after_pv"] = (lambda a=groups[gidx]: normalize(*a))
        _attn_core(k, S_, items, SCALE_A)


def _prefetch_A(k, l):
    return [k.load_w(k.win_src(l, "QA", 0, 512), 512), k.load_w(k.win_src(l, "QA", 512, 512), 512),
            k.load_w(k.win_src(l, "ZA", 0, 512), 512)]


def _phaseD_zb(k, S_, l, nchunks):
    PS, PB, hxT = k.PS, k.PB, k.hxT
    wzb, r_wzb = k.load_w(k.win_src(l, "ZB", 0, 512), 512)
    for c in range(nchunks):
        c0, n = CHUNKS[c]
        for jt in range(4):
            bb = k.bank()
            for kk in range(8):
                k.mm(PS[:, bb, 0:n], wzb[:, kk, jt * 128:(jt + 1) * 128], hxT[:, kk, c0:c0 + n], kk == 0, kk == 7,
                     r=[r_wzb, k.r_hx[c]], pw=[PB[bb]])
            sq = k.sqb[:, jt % 3, 0:n]
            k.act(sq, PS[:, bb, 0:n], AF.Silu, r=[PB[bb]], w=[k.rm("sqb%d" % (jt % 3))])
            k.tt("dve", k.Y_B[:, jt, c0:c0 + n], k.Y_B[:, jt, c0:c0 + n], sq, ALU.mult,
                 r=[k.rm("sqb%d" % (jt % 3))], pw=[k.rm("YB")])


def _phaseD_C(k, S_, l, nchunks):
    PS, PB, hxT = k.PS, k.PB, k.hxT
    mm, act, tt, ts, stt, cp, rm = k.mm, k.act, k.tt, k.ts, k.stt, k.cp, k.rm
    TMP, r_tmp = k.TMP, k.r_tmp
    zrow = k.zrow
    for ct in range(4):
        wc, r_wc = k.load_w(k.win_src(l, "C", ct * 512, 512), 512)
        cp("dve", zrow[:, 0:1], k.zhalo[:, ct, 0:1], r=[rm("zhalo")], pw=[rm("zrow")])
        cp("dve", zrow[:, 2049:2050], k.zhalo[:, ct, 1:2], r=[rm("zhalo")], pw=[rm("zrow")])
        S_.op("dve", lambda e: e.memset(zrow[:, 2050:2051], 0.0), pw=[rm("zrow")])
        S_.op("dve", lambda e: e.memset(zrow[:, 2307:2308], 0.0), pw=[rm("zrow")])
        zo = lambda c: (1 + CHUNKS[c][0]) if c < 4 else 2051
        for c in range(nchunks):
            c0, n = CHUNKS[c]
            bcc, buc = k.bank(), k.bank()
            for (bb, co) in ((bcc, 128), (buc, 256)):
                for kk in range(8):
                    mm(PS[:, bb, 0:n], wc[:, kk, co:co + 128], hxT[:, kk, c0:c0 + n], kk == 0, kk == 7,
                       r=[r_wc, k.r_hx[c]], pw=[PB[bb]])
            cp("act", TMP[0][:, 0:n], PS[:, bcc, 0:n], r=[PB[bcc]], w=[r_tmp[0]])
            tt("dve", zrow[:, zo(c):zo(c) + n], TMP[0][:, 0:n], PS[:, buc, 0:n], ALU.mult,
               r=[r_tmp[0], PB[buc]], pw=[rm("zrow")])
        for c in range(nchunks):
            c0, n = CHUNKS[c]
            z0 = zo(c)
            bbc, bzc = k.bank(), k.bank()
            for (bb, co) in ((bbc, 0), (bzc, 384)):
                for kk in range(8):
                    mm(PS[:, bb, 0:n], wc[:, kk, co:co + 128], hxT[:, kk, c0:c0 + n], kk == 0, kk == 7,
                       r=[r_wc, k.r_hx[c]], pw=[PB[bb]])
            act(k.sqb[:, 0, 0:n], PS[:, bzc, 0:n], AF.Silu, r=[PB[bzc]], w=[rm("sqb0")])
            ts("dve", TMP[1][:, 0:n], zrow[:, z0 - 1:z0 - 1 + n], k.convw[:, l, ct, 0:1], ALU.mult,
               r=[rm("zrow")], w=[r_tmp[1]])
            stt(TMP[1][:, 0:n], zrow[:, z0:z0 + n], k.convw[:, l, ct, 1:2], TMP[1][:, 0:n], ALU.mult, ALU.add,
                r=[rm("zrow")], w=[r_tmp[1]])
            stt(TMP[1][:, 0:n], zrow[:, z0 + 1:z0 + 1 + n], k.convw[:, l, ct, 2:3], TMP[1][:, 0:n], ALU.mult, ALU.add,
                r=[rm("zrow")], w=[r_tmp[1]])
            tt("dve", TMP[2][:, 0:n], TMP[1][:, 0:n], PS[:, bbc, 0:n], ALU.mult, r=[r_tmp[1], PB[bbc]], w=[r_tmp[2]])
            tt("dve", k.Y_C[:, ct, c0:c0 + n], TMP[2][:, 0:n], k.sqb[:, 0, 0:n], ALU.mult,
               r=[r_tmp[2], rm("sqb0")], pw=[rm("YC")])


def _phaseE_merge(k, S_, l, nchunks):
    PS, PB, hxT = k.PS, k.PB, k.hxT
    mm, act, tt, rm = k.mm, k.act, k.tt, k.rm
    TMP, r_tmp = k.TMP, k.r_tmp
    Y = [k.Y_A, k.Y_B, k.Y_C]
    rY = [rm("YA"), rm("YB"), rm("YC")]
    for mt in range(8):
        wg, r_wg = k.load_w(k.win_src(l, "G", mt * 384, 384), 384)
        wbrb = k.wbrb2[mt % 2]
        r_wbrb = rm("wbrb%d" % (mt % 2))
        S_.dma("pool", wbrb[:], k.d_wbr.ap()[l].rearrange("b (kc p) c -> p b kc c", p=128)[:, :, :, mt * 128:(mt + 1) * 128],
               w=[r_wbrb])
        for c in range(nchunks):
            c0, n = CHUNKS[c]
            bp = [k.bank() for _ in range(3)]
            bg = [k.bank() for _ in range(3)]
            for br in range(3):
                for kc in range(4):
                    mm(PS[:, bp[br], 0:n], wbrb[:, br, kc, :], Y[br][:, kc, c0:c0 + n], kc == 0, kc == 3,
                       r=[r_wbrb, rY[br]], pw=[PB[bp[br]]])
            for br in range(3):
                for kk in range(8):
                    mm(PS[:, bg[br], 0:n], wg[:, kk, br * 128:(br + 1) * 128], hxT[:, kk, c0:c0 + n], kk == 0, kk == 7,
                       r=[r_wg, k.r_hx[c]], pw=[PB[bg[br]]])
            for br in range(3):
                act(TMP[br][:, 0:n], PS[:, bg[br], 0:n], AF.Sigmoid, r=[PB[bg[br]]], w=[r_tmp[br]])
            for br in range(3):
                tt("dve", TMP[br][:, 0:n], TMP[br][:, 0:n], PS[:, bp[br], 0:n], ALU.mult, r=[PB[bp[br]]], w=[r_tmp[br]])
            tt("dve", TMP[0][:, 0:n], TMP[0][:, 0:n], TMP[1][:, 0:n], ALU.add, r=[r_tmp[1]], w=[r_tmp[0]])
            tt("dve", k.mT[:, mt, c0:c0 + n], TMP[0][:, 0:n], TMP[2][:, 0:n], ALU.add, r=[r_tmp[0], r_tmp[2]],
               pw=[rm("mT")])


def _phaseF_out(k, S_, l, ntiles):
    PS, PB = k.PS, k.PB
    mm, act, tt, ts, stt, rm = k.mm, k.act, k.tt, k.ts, k.stt, k.rm
    TMP, r_tmp = k.TMP, k.r_tmp
    last = (l == DEPTH - 1)
    wsrc = k.d_wmod.ap()[l].rearrange("(k p) c -> p k c", p=128)
    scs3 = k.scs[:].rearrange("p (k s) -> p k s", s=2)
    for n_ in range(2):
        wt, rw = k.load_w(wsrc[:, :, 2048 + n_ * 512:2048 + n_ * 512 + 512], 512)
        bg = k.bank()
        for kk in range(8):
            mm(PS[0:2, bg, 0:512], scs3[:, kk, :], wt[:, kk, :], kk == 0, kk == 7, r=[rw, rm("scs")], pw=[PB[bg]])
        S_.dma("sp", TMP[3][0:2, 0:512], k.d_bmodg.ap()[:, l, n_ * 512:(n_ + 1) * 512], w=[r_tmp[3]])
        tt("dve", TMP[2][0:2, 0:512], PS[0:2, bg, 0:512], TMP[3][0:2, 0:512], ALU.add, r=[PB[bg], r_tmp[3]], w=[r_tmp[2]])
        for which, Gt, rG in ((0, k.G_x, rm("Gx")), (1, k.G_c, rm("Gc"))):
            if which == 1 and ntiles <= 16:
                continue
            hs = slice(n_ * 512, (n_ + 1) * 512)
            S_.dma("sp", Gt[:, hs], k.d_gpostb.ap()[l][:, hs], pw=[rG])
            bb = k.bank()
            mm(PS[:, bb, 0:512], k.sel[0:2, which * 128:(which + 1) * 128], TMP[2][0:2, 0:512],
               True, True, r=[r_tmp[2], k.r_const], pw=[PB[bb]])
            tt("dve", Gt[:, hs], Gt[:, hs], PS[:, bb, 0:512], ALU.mult, r=[PB[bb], rG], pw=[rG])
    wo0, r_wo0 = k.load_w(k.d_wo.ap()[l].rearrange("(k p) c -> p k c", p=128)[:, :, 0:512], 512)
    wo1, r_wo1 = k.load_w(k.d_wo.ap()[l].rearrange("(k p) c -> p k c", p=128)[:, :, 512:1024], 512)
    wo = [(wo0, r_wo0), (wo1, r_wo1)]
    sm = k.small
    def load_xold(i_):
        if i_ >= ntiles:
            return
        if i_ < 16:
            src_ = (k.d_x.ap() if l == 0 else k.d_x1.ap())[i_ * 128:(i_ + 1) * 128, :]
        else:
            src_ = k.d_ctx.ap()[(i_ - 16) * 128:(i_ - 15) * 128, :]
        S_.dma("sp", k.xoldb[i_ % 3], src_, r=([rm("x1d")] if (l > 0 and i_ < 16) else []), w=[rm("xold%d" % (i_ % 3))])

    bos = {}
    s1b = {}

    def part_a(i):
        p = i % 2
        rs = lambda n_: rm("%s%d" % (n_, i % 4))
        q0 = 16 * (i % 4) + 4
        bo = [k.bank(), k.bank()]
        bos[i] = bo
        for hf in range(2):
            for kk in range(8):
                mm(PS[:, bo[hf], 0:512], k.mT[:, kk, i * 128:(i + 1) * 128], wo[hf][0][:, kk, :], kk == 0, kk == 7,
                   r=[wo[hf][1], rm("mT")], pw=[PB[bo[hf]]])
        for hf in range(2):
            act(k.junk2[p][:, hf * 512:(hf + 1) * 512], PS[:, bo[hf], 0:512], AF.Square, r=[PB[bo[hf]]],
                w=[rm("junkf%d%d" % (p, hf)), rs("ssq2%d" % hf)], accum_out=sm[:, q0 + hf:q0 + hf + 1])
        tt("dve", sm[:, q0 + 2:q0 + 3], sm[:, q0:q0 + 1], sm[:, q0 + 1:q0 + 2], ALU.add, r=[rs("ssq20"), rs("ssq21")],
           w=[rs("ssq2")])
        act(sm[:, q0 + 3:q0 + 4], sm[:, q0 + 2:q0 + 3], AF.Ln, r=[rs("ssq2")], w=[rs("ln2")], scale=1.0 / D,
            bias=k.epsc[:, 0:1])
        act(sm[:, q0 + 4:q0 + 5], sm[:, q0 + 3:q0 + 4], AF.Exp, r=[rs("ln2")], w=[rs("rstd2")], scale=-0.5)

    def part_b(i):
        j = i % 2
        p = i % 2
        isx = i < 16
        Gt, rG = (k.G_x, rm("Gx")) if isx else (k.G_c, rm("Gc"))
        rs = lambda n_: rm("%s%d" % (n_, i % 4))
        q0 = 16 * (i % 4) + 4
        xold = k.xoldb[i % 3]
        r_xold = rm("xold%d" % (i % 3))
        bo = bos.pop(i)
        for hf in range(2):
            hs = slice(hf * 512, (hf + 1) * 512)
            tmpi = 2 * p + hf
            stt(TMP[tmpi][:, 0:512], PS[:, bo[hf], 0:512], sm[:, q0 + 4:q0 + 5], Gt[:, hs], ALU.mult, ALU.mult,
                r=[PB[bo[hf]], rs("rstd2"), rG], w=[r_tmp[tmpi]])
            tt("dve", k.xt[j][:, hs], TMP[tmpi][:, 0:512], xold[:, hs], ALU.add, r=[r_tmp[tmpi], r_xold],
               pw=[k.r_xt[j]])
        if isx:
            dst = (k.d_y.ap() if last else k.d_x1.ap())[i * 128:(i + 1) * 128, :]
            S_.dma("sp", dst, k.xt[j], r=[k.r_xt[j]], pw=[rm("yd") if last else rm("x1d")])

    load_xold(0)
    for s_ in range(ntiles + 3):
        if s_ < ntiles:
            part_a(s_)
        if 0 <= s_ - 1 < ntiles:
            part_b(s_ - 1)
        load_xold(s_ + 1)
        if not last:
            if 0 <= s_ - 2 < ntiles:
                i_ = s_ - 2
                s1b[i_] = _stage1_a(k, S_, l + 1, i_, k.xt[i_ % 2], k.r_xt[i_ % 2])
            if 0 <= s_ - 3 < ntiles:
                i_ = s_ - 3
                _stage1_b(k, S_, l + 1, i_, *s1b.pop(i_))


def _program(k, S_, upto):
    k.sctr = 0
    k.r_pbuf = [Res("pbuf%d" % i) for i in range(4)]
    if upto <= 0:
        return
    if upto == 1 and DBG_TILES:
        k.ntilesA = DBG_TILES
    _phaseA(k, S_, 0)
    k.mod_layer(1)
    k.dump("hxT0", k.hxT[:], [128, 8 * TT], BF16)
    if upto <= 1:
        return
    for l in range(DEPTH):
        nch = 5 if l == 0 else 4
        preB = _prefetch_B(k, l)
        S_.barrier()
        _phaseB_kv(k, S_, l, pre=preB)
        _phaseB_qb(k, S_, l, nch)
        _phaseB_halo(k, S_, l)
        if l == 0:
            k.dump("kTA", k.kT_A, [128, 5120], BF16)
            k.dump("VA", k.V_A, [128, 7680], BF16)
            k.dump("qTB", k.qT_B[0:96])
            k.dump("kvnc", k.kvnc, [128, 512], BF16)
            k.dump("zhalo", k.zhalo[:], [128, 8], F32)
        if upto <= 2 and l == 0:
            return
        S_.barrier()
        preA = _prefetch_A(k, l)
        _phaseC_mla(k, S_, l, with_ctx_q=(l == 0))
        if upto <= 3 and l == 0:
            k.dump("YB", k.Y_B, [128, 4 * TT], BF16)
            return
        S_.barrier()
        _phaseD_A(k, S_, l, nch, pre=preA)
        _phaseD_zb(k, S_, l, nch)
        if upto <= 4 and l == 0:
            k.dump("YB", k.Y_B, [128, 4 * TT], BF16)
            k.dump("YA", k.Y_A, [128, 4 * TT], BF16)
            return
        S_.barrier()
        _phaseD_C(k, S_, l, nch)
        if upto <= 5 and l == 0:
            k.dump("YC", k.Y_C, [128, 4 * TT], BF16)
            return
        S_.barrier()
        _phaseE_merge(k, S_, l, nch)
        if upto <= 6 and l == 0:
            k.dump("mT", k.mT, [128, 8 * TT], BF16)
            return
        S_.barrier()
        _phaseF_out(k, S_, l, 18 if l == 0 else 16)
        if upto <= 7 and l == 0:
            k.dump("hxT1", k.hxT[:], [128, 8 * TT], BF16)
            return


def kernel(**inputs):
    in_maps = prepare_inputs(**inputs)
    nc = build()
    res = run_bass_kernel_spmd(nc, in_maps, core_ids=list(range(NCORE)))
    out = np.zeros((2, S, D), np.float32)
    for core in range(NCORE):
        b, r = divmod(core, 4)
        out[b, r * T:(r + 1) * T] = np.asarray(res.results[core]["y"], np.float32)
    return out
```
